# Optimizing a Trainium2 kernel written in Bass

```python
import math
import jax, jax.numpy as jnp
from jax import lax
import numpy as np

D_MODEL = 1024
BATCH = 32
SEQ = 256
DEPTH = 4
DEC_BATCH = 8
DEC_SEQ = 2048
PAST_LEN = 512

F32 = jnp.float32
GRID_W = 64
N_MIXERS = 3
N_A = (DEPTH + 2) // 3
N_B = (DEPTH + 1) // 3
N_C = DEPTH // 3
EPS = 1e-6

S5_WIDTH = D_MODEL
S5_GROUP = 16
S5_GROUPS = S5_WIDTH // S5_GROUP
S5_STATE = 64
S5_DT_MIN = 1e-3
S5_DT_MAX = 1e-1

RET_HEADS = 8
RET_QK = D_MODEL
RET_V = 2 * D_MODEL
RET_DK = RET_QK // RET_HEADS
RET_DV = RET_V // RET_HEADS
RET_CHUNK = 128
ROPE_BASE = 10000.0

HY_WIDTH = 2 * D_MODEL
HY_ORDER = 2
HY_SHORT = 3
HY_BANDS = 16
HY_EMB = 2 * HY_BANDS + 1
HY_FILTER_WIDTH = 64
HY_SHORT_DECAY_PCT = 0.3
HY_LONG_DECAY_PCT = 1.5
HY_DECAY_TARGET = 1e-2

kernel_name = 'hybrid_s5_retnet_hyena_prefix_step'


def rmsnorm(x, g):
    xf = x.astype(F32)
    y = xf * lax.rsqrt(jnp.mean(xf * xf, axis=-1, keepdims=True) + EPS)
    return (y * g.astype(F32)).astype(x.dtype)


def ada_mod(cvec, w, b):
    m = (jax.nn.silu(cvec) @ w + b)[:, None, :]
    return m[..., :D_MODEL], m[..., D_MODEL:2 * D_MODEL], m[..., 2 * D_MODEL:]


def _cmul(ar, ai, br, bi):
    return ar * br - ai * bi, ar * bi + ai * br


def s5_scan_dir(u, lam_re, lam_im, log_step, b_re, b_im, c_re, c_im, h0_re, h0_im):
    lam_re = lam_re.astype(F32)
    lam_im = lam_im.astype(F32)
    dt = jnp.exp(log_step.astype(F32))[:, None]
    mag = jnp.exp(lam_re * dt)
    ab_re, ab_im = mag * jnp.cos(lam_im * dt), mag * jnp.sin(lam_im * dt)
    den = lam_re * lam_re + lam_im * lam_im
    nr, ni = ab_re - 1.0, ab_im
    f_re = (nr * lam_re + ni * lam_im) / den
    f_im = (ni * lam_re - nr * lam_im) / den
    bb_re, bb_im = _cmul(f_re[..., None], f_im[..., None], b_re.astype(F32), b_im.astype(F32))
    bu_re = jnp.einsum('blgh,gph->blgp', u, bb_re)
    bu_im = jnp.einsum('blgh,gph->blgp', u, bb_im)
    i_re, i_im = _cmul(ab_re, ab_im, h0_re.astype(F32), h0_im.astype(F32))
    bu_re = bu_re.at[:, 0].add(i_re)
    bu_im = bu_im.at[:, 0].add(i_im)
    L = u.shape[1]
    a_re = jnp.broadcast_to(ab_re, (1, L) + ab_re.shape)
    a_im = jnp.broadcast_to(ab_im, (1, L) + ab_im.shape)

    def combine(e1, e2):
        a1r, a1i, b1r, b1i = e1
        a2r, a2i, b2r, b2i = e2
        ar, ai = _cmul(a2r, a2i, a1r, a1i)
        br, bi = _cmul(a2r, a2i, b1r, b1i)
        return ar, ai, br + b2r, bi + b2i

    _, _, h_re, h_im = lax.associative_scan(combine, (a_re, a_im, bu_re, bu_im), axis=1)
    y = (jnp.einsum('blgp,ghp->blgh', h_re, c_re.astype(F32))
         - jnp.einsum('blgp,ghp->blgh', h_im, c_im.astype(F32)))
    return y, h_re[:, -1], h_im[:, -1]


def s5_branch(h, h0, w_in, lam_re, lam_im, log_step, b_re, b_im, c_re, c_im, d_skip, w_glu, b_glu, w_out):
    bsz, L, _ = h.shape
    proj = h @ w_in
    u, gate = proj[..., :S5_WIDTH], proj[..., S5_WIDTH:]
    uf = u.astype(F32)
    ug = uf.reshape(bsz, L, S5_GROUPS, S5_GROUP)
    y = d_skip.astype(F32) * uf
    states = []
    for d in range(2):
        ud = ug if d == 0 else ug[:, ::-1]
        yd, hr, hi = s5_scan_dir(ud, lam_re[d], lam_im[d], log_step[d], b_re[d], b_im[d],
                                 c_re[d], c_im[d], h0[:, d, 0], h0[:, d, 1])
        yd = yd if d == 0 else yd[:, ::-1]
        y = y + yd.reshape(bsz, L, S5_WIDTH)
        states.append(jnp.stack([hr, hi], axis=1))
    g = jax.nn.gelu(y)
    y = g * jax.nn.sigmoid(g @ w_glu.astype(F32) + b_glu.astype(F32))
    out = (y * jax.nn.silu(gate.astype(F32))).astype(h.dtype) @ w_out
    return out, jnp.stack(states, axis=1)


def rope_2d(x):
    L = x.shape[1]
    rows = L // GRID_W
    t = jnp.arange(rows * GRID_W)
    row, col = t // GRID_W, t % GRID_W
    half = RET_DK // 2
    nfreq = half // 2
    inv = ROPE_BASE ** (-jnp.arange(nfreq, dtype=F32) / nfreq)

    def rot(xs, pos):
        ang = pos.astype(F32)[:, None] * inv[None, :]
        cos, sin = jnp.cos(ang)[None, :, None, :], jnp.sin(ang)[None, :, None, :]
        x1, x2 = xs[..., :nfreq], xs[..., nfreq:]
        return jnp.concatenate([x1 * cos - x2 * sin, x1 * sin + x2 * cos], axis=-1)

    return jnp.concatenate([rot(x[..., :half], row), rot(x[..., half:], col)], axis=-1)


def retention_chunkwise(q, k, v, log_g, s0, inclusive):
    bsz, L, H, _ = q.shape
    C = RET_CHUNK
    N = L // C
    idx = jnp.arange(C, dtype=F32)
    diff = idx[:, None] - idx[None, :]
    mask = (diff >= 0) if inclusive else (diff > 0)
    d_intra = jnp.where(mask[None], jnp.exp(jnp.maximum(diff, 0.0)[None] * log_g[:, None, None]), 0.0)
    q_dec = jnp.exp((idx[:, None] + 1.0) * log_g[None, :])
    k_dec = jnp.exp((C - 1.0 - idx)[:, None] * log_g[None, :])
    c_dec = jnp.exp(C * log_g)

    def to_chunks(t):
        return jnp.moveaxis(t.reshape(bsz, N, C, H, t.shape[-1]), 1, 0)

    def step(S, inp):
        qc, kc, vc = inp
        att = jnp.einsum('bihd,bjhd->bhij', qc, kc) * d_intra[None]
        o = (jnp.einsum('bhij,bjhe->bihe', att, vc)
             + jnp.einsum('bihd,bhde->bihe', qc, S) * q_dec[None, :, :, None])
        S = S * c_dec[None, :, None, None] + jnp.einsum('bjhd,bjhe->bhde', kc * k_dec[None, :, :, None], vc)
        return S, o

    S, o = lax.scan(step, s0.astype(F32), (to_chunks(q), to_chunks(k), to_chunks(v)))
    return jnp.moveaxis(o, 0, 1).reshape(bsz, L, H, v.shape[-1]), S


def retention_branch(h, s0, grid_pos, w_in, decay_logit, w_out):
    bsz, L, _ = h.shape
    proj = h @ w_in
    q, k, v, gate = jnp.split(proj, [RET_QK, 2 * RET_QK, 2 * RET_QK + RET_V], axis=-1)
    q = q.astype(F32).reshape(bsz, L, RET_HEADS, RET_DK)
    k = k.astype(F32).reshape(bsz, L, RET_HEADS, RET_DK) * (RET_DK ** -0.5)
    v = v.astype(F32).reshape(bsz, L, RET_HEADS, RET_DV)
    if grid_pos:
        q, k = rope_2d(q), rope_2d(k)
    lg = jax.nn.log_sigmoid(decay_logit.astype(F32))
    o_f, s_f = retention_chunkwise(q, k, v, lg[0], s0[:, 0], True)
    o_b, s_b = retention_chunkwise(q[:, ::-1], k[:, ::-1], v[:, ::-1], lg[1], s0[:, 1], False)
    o = o_f + o_b[:, ::-1]
    o = o * lax.rsqrt(jnp.mean(o * o, axis=-1, keepdims=True) + EPS)
    out = (o.reshape(bsz, L, RET_V) * jax.nn.silu(gate.astype(F32))).astype(h.dtype) @ w_out
    return out, jnp.stack([s_f, s_b], axis=1)


def hyena_filters(L, w1, b1, w2, b2, w3):
    t = jnp.linspace(0.0, 1.0, L, dtype=F32)[:, None]
    w = 2.0 * math.pi * jnp.arange(L, dtype=F32)[:, None] / L
    f = jnp.linspace(1e-4, HY_BANDS - 1.0, HY_BANDS, dtype=F32)[None, :]
    feat = jnp.concatenate([t, jnp.cos(f * w), -jnp.sin(f * w)], axis=-1)
    z = jnp.sin(feat @ w1.astype(F32) + b1.astype(F32))
    z = jnp.sin(z @ w2.astype(F32) + b2.astype(F32))
    filt = (z @ w3.astype(F32)).reshape(L, 2, HY_ORDER, HY_WIDTH)
    max_decay = math.log(HY_DECAY_TARGET) / HY_SHORT_DECAY_PCT
    min_decay = math.log(HY_DECAY_TARGET) / HY_LONG_DECAY_PCT
    deltas = jnp.linspace(min_decay, max_decay, HY_WIDTH, dtype=F32)
    filt = filt * jnp.exp(-t * jnp.abs(deltas)[None, :])[:, None, None, :]
    filt = filt / (jnp.sum(jnp.abs(filt), axis=(0, 1), keepdims=True) + EPS)
    full = jnp.concatenate([filt[:, 0], jnp.zeros((1, HY_ORDER, HY_WIDTH), F32), filt[:0:-1, 1]], axis=0)
    return jnp.fft.rfft(full, axis=0)


def hyena_branch(h, w_in, conv_w, conv_b, f_w1, f_b1, f_w2, f_b2, f_w3, skip, w_out):
    bsz, L, _ = h.shape
    proj = h @ w_in
    xs, gate = proj[..., :3 * HY_WIDTH].astype(F32), proj[..., 3 * HY_WIDTH:]
    pad = HY_SHORT // 2
    xp = jnp.pad(xs, ((0, 0), (pad, pad), (0, 0)))
    cw = conv_w.astype(F32)
    xs = conv_b.astype(F32) + xp[:, 0:L] * cw[0]
    for j in range(1, HY_SHORT):
        xs = xs + xp[:, j:j + L] * cw[j]
    v, x1, x2 = jnp.split(xs, 3, axis=-1)
    k_hat = hyena_filters(L, f_w1, f_b1, f_w2, f_b2, f_w3)
    sk = skip.astype(F32)
    z = v
    for o, xg in enumerate((x1, x2)):
        z_hat = jnp.fft.rfft(z, n=2 * L, axis=1)
        zc = jnp.fft.irfft(z_hat * k_hat[None, :, o], n=2 * L, axis=1)[:, :L]
        z = xg * (zc + z * sk[o])
    return (z * jax.nn.silu(gate.astype(F32))).astype(h.dtype) @ w_out


def setup_inputs(seed: int = 0) -> dict:
    key = jax.random.key(seed)
    ks = iter(jax.random.split(key, 48))

    def nrm(shape, s):
        return s * jax.random.normal(next(ks), shape, F32)

    G, P, Hg = S5_GROUPS, S5_STATE, S5_GROUP
    n_idx = jnp.arange(P, dtype=F32)
    gam_logit = jnp.log(2.0 ** (5.0 + jnp.arange(RET_HEADS, dtype=F32)) - 1.0)
    return {
        'x_prompt': nrm((BATCH, SEQ, D_MODEL), 1.0),
        'x_sample': nrm((DEC_BATCH, DEC_SEQ, D_MODEL), 1.0),
        'c': nrm((DEC_BATCH, D_MODEL), 1.0),
        'state_s5': nrm((DEC_BATCH, N_A, 2, 2, G, P), 0.1),
        'state_ret': nrm((DEC_BATCH, N_B, 2, RET_HEADS, RET_DK, RET_DV), 0.5),
        'c_ctx': nrm((D_MODEL,), 1.0),
        'norm_g': 1.0 + nrm((DEPTH, D_MODEL), 0.02),
        'mod_w': nrm((DEPTH, D_MODEL, 3 * D_MODEL), 0.5 * D_MODEL ** -0.5),
        'mod_b': nrm((DEPTH, 3 * D_MODEL), 0.02),
        's5_w_in': nrm((N_A, D_MODEL, 2 * S5_WIDTH), D_MODEL ** -0.5),
        's5_lam_re': -0.5 + nrm((N_A, 2, G, P), 0.01),
        's5_lam_im': math.pi * n_idx + nrm((N_A, 2, G, P), 0.01),
        's5_log_step': jax.random.uniform(next(ks), (N_A, 2, G), F32, math.log(S5_DT_MIN), math.log(S5_DT_MAX)),
        's5_b_re': nrm((N_A, 2, G, P, Hg), (2.0 * Hg) ** -0.5),
        's5_b_im': nrm((N_A, 2, G, P, Hg), (2.0 * Hg) ** -0.5),
        's5_c_re': nrm((N_A, 2, G, Hg, P), P ** -0.5),
        's5_c_im': nrm((N_A, 2, G, Hg, P), P ** -0.5),
        's5_d': nrm((N_A, S5_WIDTH), 1.0),
        's5_w_glu': nrm((N_A, S5_WIDTH, S5_WIDTH), S5_WIDTH ** -0.5),
        's5_b_glu': nrm((N_A, S5_WIDTH), 0.02),
        's5_w_out': nrm((N_A, S5_WIDTH, D_MODEL), S5_WIDTH ** -0.5),
        'ret_w_in': nrm((N_B, D_MODEL, 2 * RET_QK + 2 * RET_V), D_MODEL ** -0.5),
        'ret_decay_logit': gam_logit + nrm((N_B, 2, RET_HEADS), 0.05),
        'ret_w_out': nrm((N_B, RET_V, D_MODEL), RET_V ** -0.5),
        'hy_w_in': nrm((N_C, D_MODEL, 4 * HY_WIDTH), D_MODEL ** -0.5),
        'hy_conv_w': nrm((N_C, HY_SHORT, 3 * HY_WIDTH), HY_SHORT ** -0.5),
        'hy_conv_b': nrm((N_C, 3 * HY_WIDTH), 0.02),
        'hy_f_w1': nrm((N_C, HY_EMB, HY_FILTER_WIDTH), HY_EMB ** -0.5),
        'hy_f_b1': nrm((N_C, HY_FILTER_WIDTH), 0.1),
        'hy_f_w2': nrm((N_C, HY_FILTER_WIDTH, HY_FILTER_WIDTH), HY_FILTER_WIDTH ** -0.5),
        'hy_f_b2': nrm((N_C, HY_FILTER_WIDTH), 0.1),
        'hy_f_w3': nrm((N_C, HY_FILTER_WIDTH, 2 * HY_ORDER * HY_WIDTH), HY_FILTER_WIDTH ** -0.5),
        'hy_skip': nrm((N_C, HY_ORDER, HY_WIDTH), 1.0),
        'hy_w_out': nrm((N_C, HY_WIDTH, D_MODEL), HY_WIDTH ** -0.5),
        'final_g': 1.0 + nrm((D_MODEL,), 0.02),
    }


def reference(x_prompt, x_sample, c, state_s5, state_ret, c_ctx, norm_g, mod_w, mod_b,
              s5_w_in, s5_lam_re, s5_lam_im, s5_log_step, s5_b_re, s5_b_im, s5_c_re, s5_c_im,
              s5_d, s5_w_glu, s5_b_glu, s5_w_out,
              ret_w_in, ret_decay_logit, ret_w_out,
              hy_w_in, hy_conv_w, hy_conv_b, hy_f_w1, hy_f_b1, hy_f_w2, hy_f_b2, hy_f_w3, hy_skip, hy_w_out,
              final_g):
    n_ctx = x_prompt.shape[0]
    xc, xl = x_prompt, x_sample
    s5_new, ret_new = [], []
    for i in range(DEPTH):
        kind, j = i % N_MIXERS, i // N_MIXERS
        sh_c, sc_c, gt_c = ada_mod(c_ctx[None, :], mod_w[i], mod_b[i])
        sh_l, sc_l, gt_l = ada_mod(c, mod_w[i], mod_b[i])
        hc = rmsnorm(xc, norm_g[i]) * (1.0 + sc_c) + sh_c
        hl = rmsnorm(xl, norm_g[i]) * (1.0 + sc_l) + sh_l
        if kind == 0:
            p = (s5_w_in[j], s5_lam_re[j], s5_lam_im[j], s5_log_step[j], s5_b_re[j], s5_b_im[j],
                 s5_c_re[j], s5_c_im[j], s5_d[j], s5_w_glu[j], s5_b_glu[j], s5_w_out[j])
            zero = jnp.zeros((n_ctx, 2, 2, S5_GROUPS, S5_STATE), F32)
            oc, st = s5_branch(hc, zero, *p)
            ol, _ = s5_branch(hl, state_s5[:, j], *p)
            s5_new.append(st)
        elif kind == 1:
            p = (ret_w_in[j], ret_decay_logit[j], ret_w_out[j])
            zero = jnp.zeros((n_ctx, 2, RET_HEADS, RET_DK, RET_DV), F32)
            oc, st = retention_branch(hc, zero, False, *p)
            ol, _ = retention_branch(hl, state_ret[:, j], True, *p)
            ret_new.append(st)
        else:
            p = (hy_w_in[j], hy_conv_w[j], hy_conv_b[j], hy_f_w1[j], hy_f_b1[j], hy_f_w2[j],
                 hy_f_b2[j], hy_f_w3[j], hy_skip[j], hy_w_out[j])
            oc = hyena_branch(hc, *p)
            ol = hyena_branch(hl, *p)
        xc = xc + gt_c * oc
        xl = xl + gt_l * ol
    y_prompt = rmsnorm(xc, final_g)
    y_sample = rmsnorm(xl, final_g)
    new_state_s5 = jnp.stack(s5_new, axis=1)
    new_state_ret = jnp.stack(ret_new, axis=1)
    return (y_prompt, y_sample, new_state_s5, new_state_ret)
```

```python
import math
import numpy as np
import ml_dtypes
import concourse.bass as bass
import concourse.mybir as mybir
from concourse.bass_utils import run_bass_kernel_spmd

F32 = mybir.dt.float32
BF16 = mybir.dt.bfloat16
I32 = mybir.dt.int32
ALU = mybir.AluOpType
AF = mybir.ActivationFunctionType
AX = mybir.AxisListType

D = 1024
LS = 2048
LP = 256
NPS = 4
NT = LS + NPS * LP
EPS = 1e-6
TWO_PI = 2.0 * math.pi
ARENA_W = 31500

SAME_ENG_SYNC = True


class _PEProxy:
    def __init__(self, eng):
        self.eng = eng
        self.stop = True

    def matmul(self, *a, **kw):
        self.stop = bool(kw.get("stop", True))
        return self.eng.matmul(*a, **kw)

    def transpose(self, *a, **kw):
        self.stop = True
        return self.eng.transpose(*a, **kw)


class Fw:
    def __init__(self, nc, n_dma_sems=20):
        self.nc = nc
        self.engs = {}
        for name in ("tensor", "vector", "scalar", "gpsimd", "sync"):
            e = getattr(nc, name)
            self.engs[name] = dict(eng=e, sem=nc.alloc_semaphore("s_" + name), count=0, seen={})
        self.dma_pool = {}
        for q in ("sync", "gpsimd", "scalar"):
            self.dma_pool[q] = dict(
                sems=[nc.alloc_semaphore(f"d_{q}_{i}") for i in range(n_dma_sems)],
                vals=[0] * n_dma_sems, nxt=0)
        self.bufs = {}
        self.sem_owner = {id(E["sem"]): name for name, E in self.engs.items()}

    def _st(self, key):
        s = self.bufs.get(key)
        if s is None:
            s = dict(w=None, r=[])
            self.bufs[key] = s
        return s

    def _deps(self, reads, writes):
        deps = []
        for k in reads:
            s = self._st(k)
            if s["w"] is not None:
                deps.append(s["w"])
        for k in writes:
            s = self._st(k)
            if s["w"] is not None:
                deps.append(s["w"])
            deps.extend(s["r"])
        return deps

    def _wait(self, E, deps):
        best = {}
        for (sem, val) in deps:
            if sem is E["sem"] and not SAME_ENG_SYNC:
                continue
            k = id(sem)
            if k not in best or best[k][1] < val:
                best[k] = (sem, val)
        for k, (sem, val) in best.items():
            if E["seen"].get(k, 0) < val:
                E["eng"].wait_ge(sem, val)
                E["seen"][k] = val

    def _mark(self, tok, reads, writes):
        for k in reads:
            r = self._st(k)["r"]
            r.append(tok)
            if len(r) > 12:
                best = {}
                for (sem, val) in r:
                    if id(sem) not in best or best[id(sem)][1] < val:
                        best[id(sem)] = (sem, val)
                r[:] = list(best.values())
        for k in writes:
            s = self._st(k)
            s["w"] = tok
            s["r"] = []

    def op(self, eng, fn, reads=(), writes=()):
        E = self.engs[eng]
        self._wait(E, self._deps(reads, writes))
        if eng == "tensor":
            px = _PEProxy(E["eng"])
            ins = fn(px)
            if not px.stop:
                E.setdefault("pend", []).append((tuple(reads), tuple(writes)))
                return ins
            pend = E.get("pend", [])
            E["pend"] = []
            E["count"] += 1
            ins.then_inc(E["sem"], 1)
            tok = (E["sem"], E["count"])
            for (r, w) in pend:
                self._mark(tok, r, w)
            self._mark(tok, reads, writes)
            return ins
        ins = fn(E["eng"])
        E["count"] += 1
        ins.then_inc(E["sem"], 1)
        self._mark((E["sem"], E["count"]), reads, writes)
        return ins

    def dma(self, q, out, in_, reads=(), writes=(), **kw):
        E = self.engs[q]
        P = self.dma_pool[q]
        i = P["nxt"]
        P["nxt"] = (i + 1) % len(P["sems"])
        sem = P["sems"][i]
        deps = self._deps(reads, writes)
        if P["vals"][i] > 0:
            deps.append((sem, P["vals"][i]))
        self._wait(E, deps)
        ins = E["eng"].dma_start(out=out, in_=in_, **kw)
        P["vals"][i] += 16
        ins.then_inc(sem, 16)
        tok = (sem, P["vals"][i])
        self._mark(tok, reads, writes)
        return tok

    def barrier(self):
        toks = [(E["sem"], E["count"]) for E in self.engs.values() if E["count"] > 0]
        for P in self.dma_pool.values():
            for sem, val in zip(P["sems"], P["vals"]):
                if val > 0:
                    toks.append((sem, val))
        for E in self.engs.values():
            self._wait(E, toks)

    def finish(self, out_keys):
        E = self.engs["sync"]
        deps = []
        for k in out_keys:
            s = self._st(k)
            if s["w"] is not None:
                deps.append(s["w"])
        self._wait(E, deps)


def AP(t, off, dims):
    return bass.AP(t.tensor if hasattr(t, "tensor") else t, off, [list(d) for d in dims])


class K:
    def __init__(self, layers=(0, 1, 2, 3), final=True, dbg=False):
        self.dbg = dbg
        self.layers = layers
        self.final = final
        nc = self.nc = bass.Bass("TRN2", target_bir_lowering=False)
        self.fw = Fw(nc)
        self.inputs = {}
        self.build()

    def din(self, name, shape, dt=F32):
        t = self.nc.dram_tensor(name, list(shape), dt, kind="ExternalInput").ap()
        self.inputs[name] = (tuple(shape), dt)
        return t

    def dout(self, name, shape, dt=F32):
        return self.nc.dram_tensor(name, list(shape), dt, kind="ExternalOutput").ap()

    def dscr(self, name, shape, dt=F32):
        return self.nc.dram_tensor(name, list(shape), dt, kind="Internal").ap()

    def sb(self, name, shape, dt=F32):
        return self.nc.alloc_sbuf_tensor(name, list(shape), dt).ap()

    def arena_reset(self):
        self.fw.barrier()
        self.aoff = 0

    def asb(self, name, shape, dt=F32):
        n = 1
        for x in shape[1:]:
            n *= x
        words = n if dt in (F32, I32) else (n + 1) // 2
        words = (words + 7) // 8 * 8
        assert self.aoff + words <= ARENA_W, (name, self.aoff, words)
        v = self.arena[0:shape[0], self.aoff:self.aoff + words]
        self.aoff += words
        if dt not in (F32,):
            v = v.bitcast(dt)
        v = v[:, 0:n]
        if len(shape) > 2:
            names = " ".join(f"a{i}" for i in range(len(shape) - 1))
            kw = {f"a{i}": shape[i + 1] for i in range(len(shape) - 2)}
            v = v.rearrange(f"p ({names}) -> p {names}", **kw)
        return v

    def V(self, fn, r=(), w=()):
        return self.fw.op("vector", fn, r, w)

    def G(self, fn, r=(), w=()):
        return self.fw.op("gpsimd", fn, r, w)

    def A(self, fn, r=(), w=()):
        return self.fw.op("scalar", fn, r, w)

    def T(self, fn, r=(), w=()):
        return self.fw.op("tensor", fn, r, w)

    def ld(self, out, in_, r=(), w=(), q="sync", **kw):
        if q == "sync" and r and "DRam" in type(out.tensor).__name__ and "DRam" not in type(in_.tensor).__name__:
            engs = set()
            for k in r:
                st = self.fw.bufs.get(k)
                if st is None or st["w"] is None:
                    continue
                engs.add(self.fw.sem_owner.get(id(st["w"][0]), "dma"))
            if len(engs) == 1:
                e = engs.pop()
                if e in ("scalar", "gpsimd"):
                    q = e
                elif e == "vector":
                    q = "scalar"
        return self.fw.dma(q, out, in_, r, w, **kw)

    def ldc(self, out, in_, r=(), w=()):
        return self.fw.dma("gpsimd", out, in_, r, w)

    def build(self):
        nc = self.nc
        self.xs = self.din("xs", [LS, D])
        self.xp = self.din("xp", [NPS * LP, D])
        self.cvec = self.din("cvec", [2, D])
        self.st5 = self.din("st5", [2, 128, 128])
        self.stret = self.din("stret", [2, 8, 128, 256])
        self.norm_g = self.din("norm_g", [4, D])
        self.mod_w = self.din("mod_w", [4, D, 3 * D])
        self.mod_b = self.din("mod_b", [4, 3 * D])
        self.s5_w_in = self.din("s5_w_in", [2, D, 2 * D])
        self.s5_lam_re = self.din("s5_lam_re", [2, 2, 64, 64])
        self.s5_lam_im = self.din("s5_lam_im", [2, 2, 64, 64])
        self.s5_log_step = self.din("s5_log_step", [2, 2, 64])
        self.s5_b_re = self.din("s5_b_re", [2, 2, 64, 64, 16])
        self.s5_b_im = self.din("s5_b_im", [2, 2, 64, 64, 16])
        self.s5_c_re = self.din("s5_c_re", [2, 2, 64, 16, 64])
        self.s5_c_im = self.din("s5_c_im", [2, 2, 64, 16, 64])
        self.s5_d = self.din("s5_d", [2, D])
        self.s5_w_glu = self.din("s5_w_glu", [2, D, D])
        self.s5_b_glu = self.din("s5_b_glu", [2, D])
        self.s5_w_out = self.din("s5_w_out", [2, D, D])
        self.final_g = self.din("final_g", [D])
        self.ret_decl()
        self.hy_decl()
        self.c_identb = self.din("c_identb", [128, 128], BF16)
        self.c_identf = self.din("c_identf", [128, 128])
        self.c_pmask = self.din("c_pmask", [128, 8])
        self.c_bdmask = self.din("c_bdmask", [128, 128])
        self.ys = self.dout("ys", [LS, D])
        self.yp = self.dout("yp", [NPS * LP, D])
        self.ns5 = self.dout("ns5", [NPS, 2, 128, 128])
        self.nret = self.dout("nret", [NPS, 2, 8, 128, 256])
        self.xres = (self.dout if self.dbg else self.dscr)("xres", [NT, D])
        self.gscr = self.dscr("gscr", [16, 128, NT], BF16)
        self.g2scr = self.dscr("g2scr", [8, 128, NT], BF16)
        self.Tsb_scr = self.dscr("Tsb_scr", [2, 8, 128, 2048], BF16)
        self.Kc_scr = self.dscr("Kc_scr", [2, 8, 128, 1920], BF16)
        self.CA_scr = self.dscr("CA_scr", [2, 8, 128, 2, 2304], BF16)
        self.identb = self.sb("identb", [128, 128], BF16)
        self.identf = self.sb("identf", [128, 128])
        self.pmask = self.sb("pmask", [128, 8])
        self.bdmask = self.sb("bdmask", [128, 128])
        self.hT = self.sb("hT", [128, 8, LS], BF16)
        self.arena = self.sb("arena", [128, ARENA_W])
        self.aoff = 0
        self.wgs = [self.sb(f"wgs{i}", [128, 8, 128], BF16) for i in range(2)]
        self.wst = [self.sb("wst0", [128, 8, 512], BF16)] * 2
        self.xt = [self.sb(f"xt{i}", [128, D]) for i in range(2)]
        self.xn = [self.sb(f"xn{i}", [128, D], BF16) for i in range(2)]
        self.sq = self.sb("sq", [128, D])
        self.stat = self.sb("stat", [128, 8])
        self.modT = self.sb("modT", [128, 4, 24, 2])
        self.gsc = self.sb("gsc", [128, 4, 8, 2])
        self.gt_bc = self.sb("gt_bc", [128, 2, D])
        self.cT = self.sb("cT", [128, 8, 2])
        self.cTb = self.sb("cTb", [128, 8, 2], BF16)
        self.cTrep = self.sb("cTrep", [128, 2, 8, 128], BF16)
        self.ngT = self.sb("ngT", [128, 4, 8])
        self.mbT = self.sb("mbT", [128, 4, 24])
        self.mbg = None
        self.fgb = self.sb("fgb", [128, D])
        self.ylt = [self.sb("ylt", [128, 16, 128], BF16)] * 2
        self.ps = [nc.alloc_psum_tensor(f"ps{i}", [128, 512], F32).ap() for i in range(8)]

        f = self.fw
        self.ld(self.identb, self.c_identb, w=["identb"])
        self.ld(self.identf, self.c_identf, w=["identf"])
        self.ld(self.pmask, self.c_pmask, w=["pmask"])
        self.ld(self.bdmask, self.c_bdmask, w=["bdmask"])
        self.ld(self.fgb, self.final_g.partition_broadcast(128), w=["fgb"])

        self.mod_stage()
        out_keys = []
        nl = len(self.layers)
        for li, i in enumerate(self.layers):
            last = (li == nl - 1) and self.final
            first = (li == 0)
            self.gate_table(i)
            for grp in (1, 0):
                kind = i % 3
                self.prologue(i, grp, first)
                if kind == 0:
                    kdim = self.s5_mixer(i // 3, grp)
                    w_out = self.s5_w_out[i // 3]
                elif kind == 1:
                    kdim = self.ret_mixer(grp)
                    w_out = self.ret_w_out[0]
                else:
                    kdim = self.hy_mixer(grp)
                    w_out = self.hy_w_out[0]
                self.epilogue(i, grp, kdim, w_out, last)
        out_keys = ["ys", "yp", "ns5", "nret", "xres"]
        f.finish(out_keys)

    def trange(self, grp):
        return (0, LS) if grp == 1 else (LS, NPS * LP)

    def mod_stage(self):
        for r in range(2):
            self.ld(self.cT[:, :, r], self.cvec[r].rearrange("(k p) -> p k", p=128), w=["cT"], allow_slow_non_contiguous=True)
        self.A(lambda e: e.activation(out=self.cTb, in_=self.cT, func=AF.Silu), ["cT"], ["cTb"])
        for r in range(2):
            self.V(lambda e: e.tensor_copy(out=self.cTrep[:, r], in_=self.cTb[:, :, r:r + 1].to_broadcast([128, 8, 128])),
                   ["cTb"], ["cTrep"])
        for l in range(4):
            self.ld(self.ngT[:, l, :], self.norm_g[l].rearrange("(k p) -> p k", p=128), w=["ngT"], allow_slow_non_contiguous=True)
            self.ld(self.mbT[:, l, :], self.mod_b[l].rearrange("(k p) -> p k", p=128), w=["mbT"], allow_slow_non_contiguous=True)
        for i in self.layers:
            for half in range(4):
                wt = self.wst[half % 2]
                wk = "wst0"
                self.ldc(wt, self.mod_w[i, :, half * 512:(half + 1) * 512].rearrange("(k p) n -> p k n", p=128), w=[wk])
                for cc in range(4):
                    ch = half * 4 + cc
                    pt = self.ps[0][:, 0:2]
                    for k in range(8):
                        self.T(lambda e: e.matmul(pt, lhsT=wt[:, k, cc * 128:(cc + 1) * 128], rhs=self.cTb[:, k, :],
                                                  start=(k == 0), stop=(k == 7)), [wk, "cTb"], ["ps0"])
                    self.V(lambda e: e.tensor_tensor(out=self.modT[:, i, ch, :], in0=pt,
                                                     in1=self.mbT[:, i, ch:ch + 1].to_broadcast([128, 2]), op=ALU.add),
                           ["ps0", "mbT"], ["modT"])
            self.V(lambda e: e.tensor_scalar(out=self.gsc[:, i], in0=self.modT[:, i, 8:16, :], scalar1=1.0, scalar2=None,
                                             op0=ALU.add), ["modT"], ["gsc"])
            self.V(lambda e: e.tensor_tensor(out=self.gsc[:, i], in0=self.gsc[:, i],
                                             in1=self.ngT[:, i, :].unsqueeze(2).to_broadcast([128, 8, 2]), op=ALU.mult),
                   ["gsc", "ngT"], ["gsc"])

    def gate_table(self, i):
        self.mbg = self.sq
        self.ld(self.mbg, self.mod_b[i, 2 * D:3 * D].partition_broadcast(128), w=["sq"])
        for half in range(2):
            wt = self.wst[half % 2]
            wk = "wst0"
            self.ldc(wt, self.mod_w[i, :, 2 * D + half * 512: 2 * D + (half + 1) * 512].rearrange("(k p) n -> p k n", p=128),
                     w=[wk])
            for r in range(2):
                pt = self.ps[1]
                for k in range(8):
                    self.T(lambda e: e.matmul(pt, lhsT=self.cTrep[:, r, k, :], rhs=wt[:, k, :],
                                              start=(k == 0), stop=(k == 7)), [wk, "cTrep"], ["ps1"])
                self.V(lambda e: e.tensor_tensor(out=self.gt_bc[:, r, half * 512:(half + 1) * 512], in0=pt,
                                                 in1=self.mbg[:, half * 512:(half + 1) * 512], op=ALU.add),
                       ["ps1", "sq"], ["gt_bc"])

    def prologue(self, i, grp, first):
        t0, n = self.trange(grp)
        for tt in range(n // 128):
            b = tt % 2
            xt, xn = self.xt[b], self.xn[b]
            if first:
                src = self.xs[tt * 128:(tt + 1) * 128, :] if grp == 1 else self.xp[tt * 128:(tt + 1) * 128, :]
                rk = []
            else:
                src = self.xres[t0 + tt * 128: t0 + (tt + 1) * 128, :]
                rk = ["xres"]
            self.ld(xt, src, r=rk, w=[f"xt{b}"])
            self.rms_scale(xt, f"xt{b}", xn, f"xn{b}")
            for k in range(8):
                pt = self.ps[2 + (k % 2)].bitcast(BF16)[:, 0:128]
                pk = f"ps{2 + (k % 2)}"
                self.T(lambda e: e.transpose(pt, xn[:, k * 128:(k + 1) * 128], self.identb), [f"xn{b}", "identb"], [pk])
                self.A(lambda e: e.activation(out=self.hT[:, k, tt * 128:(tt + 1) * 128], in_=pt, func=AF.Identity,
                                              scale=self.gsc[:, i, k, grp:grp + 1], bias=self.modT[:, i, k, grp:grp + 1]),
                       [pk, "gsc", "modT"], ["hT"])

    def rms_scale(self, xt, xk, out, ok, gtab=None):
        self.A(lambda e: e.activation(out=self.sq, in_=xt, func=AF.Square, accum_out=self.stat[:, 0:1]), [xk], ["sq", "stat"])
        self.V(lambda e: e.tensor_scalar(out=self.stat[:, 1:2], in0=self.stat[:, 0:1], scalar1=1.0 / D, scalar2=EPS,
                                         op0=ALU.mult, op1=ALU.add), ["stat"], ["stat"])
        self.A(lambda e: e.activation(out=self.stat[:, 3:4], in_=self.stat[:, 1:2], func=AF.Sqrt), ["stat"], ["stat"])
        self.V(lambda e: e.reciprocal(out=self.stat[:, 2:3], in_=self.stat[:, 3:4]), ["stat"], ["stat"])
        if gtab is None:
            self.V(lambda e: e.tensor_scalar(out=out, in0=xt, scalar1=self.stat[:, 2:3], scalar2=None, op0=ALU.mult),
                   [xk, "stat"], [ok])
        else:
            self.V(lambda e: e.scalar_tensor_tensor(out=out, in0=xt, scalar=self.stat[:, 2:3], in1=gtab,
                                                    op0=ALU.mult, op1=ALU.mult), [xk, "stat", "fgb"], [ok])

    def epilogue(self, i, grp, kdim, w_out, last):
        t0, n = self.trange(grp)
        kc = kdim // 128
        if kdim > D:
            self.arena_reset()
            self.wres = self.asb("wres", [128, 16, 1024], BF16)
            wrk = ["wres"]
        else:
            self.wres = self.SS.bitcast(BF16).rearrange("p (k n) -> p k n", k=8)
            wrk = ["SS", "SS2"]
        self.ldc(self.wres[:, 0:kc, :], w_out.rearrange("(k p) n -> p k n", p=128), w=wrk)
        yb = self.xn
        for tt in range(n // 128):
            b = tt % 2
            xt = self.xt[b]
            src = self.xres[t0 + tt * 128: t0 + (tt + 1) * 128, :]
            if i == self.layers[0]:
                src = self.xs[tt * 128:(tt + 1) * 128, :] if grp == 1 else self.xp[tt * 128:(tt + 1) * 128, :]
                rk = []
            else:
                rk = ["xres"]
            self.ld(xt, src, r=rk, w=[f"xt{b}"])
            yt = self.ylt[b]
            ysrc, ykey = self.ysrc
            self.ld(yt[:, 0:kc, :], ysrc[0:kc, :, t0 + tt * 128: t0 + (tt + 1) * 128].rearrange("k p t -> p k t"),
                    r=[ykey], w=["ylt"], q="sync")
            for h in range(2):
                pt = self.ps[4 + h]
                for k in range(kc):
                    self.T(lambda e: e.matmul(pt, lhsT=yt[:, k, :], rhs=self.wres[:, k, h * 512:(h + 1) * 512],
                                              start=(k == 0), stop=(k == kc - 1)), ["ylt"] + wrk, [f"ps{4 + h}"])
                self.V(lambda e: e.tensor_tensor(out=self.sq[:, h * 512:(h + 1) * 512], in0=pt,
                                                 in1=self.gt_bc[:, grp, h * 512:(h + 1) * 512], op=ALU.mult),
                       [f"ps{4 + h}", "gt_bc"], ["sq"])
            self.V(lambda e: e.tensor_tensor(out=xt, in0=xt, in1=self.sq, op=ALU.add), [f"xt{b}", "sq"], [f"xt{b}"])
            if not last:
                self.ld(self.xres[t0 + tt * 128: t0 + (tt + 1) * 128, :], xt, r=[f"xt{b}"], w=["xres"])
            else:
                ot = self.xt[1 - b]
                self.rms_scale(xt, f"xt{b}", ot, f"xt{1 - b}", gtab=self.fgb)
                dst = self.ys if grp == 1 else self.yp
                self.ld(dst[tt * 128:(tt + 1) * 128, :], ot, r=[f"xt{1 - b}"], w=["ys" if grp == 1 else "yp"])

    def E(self, eng, fn, r=(), w=()):
        return self.fw.op(eng, fn, r, w)

    def sin_of(self, out, x, shift, eng="vector"):
        y, yi, yf = self.tr_y, self.tr_yi, self.tr_yf
        self.E(eng, lambda e: e.tensor_scalar(out=y, in0=x, scalar1=1.0 / TWO_PI, scalar2=shift / TWO_PI + 8.0,
                                              op0=ALU.mult, op1=ALU.add), ["trx"], ["try"])
        self.E(eng, lambda e: e.tensor_copy(out=yi, in_=y), ["try"], ["tryi"])
        self.E(eng, lambda e: e.tensor_copy(out=yf, in_=yi), ["tryi"], ["tryf"])
        self.E(eng, lambda e: e.tensor_tensor(out=y, in0=y, in1=yf, op=ALU.subtract), ["try", "tryf"], ["try"])
        self.E(eng, lambda e: e.tensor_scalar(out=yf, in0=y, scalar1=0.5, scalar2=None, op0=ALU.is_gt), ["try"], ["tryf"])
        self.E(eng, lambda e: e.tensor_tensor(out=y, in0=y, in1=yf, op=ALU.subtract), ["try", "tryf"], ["try"])
        self.E(eng, lambda e: e.tensor_scalar(out=yf, in0=y, scalar1=-0.5, scalar2=None, op0=ALU.is_lt), ["try"], ["tryf"])
        self.E(eng, lambda e: e.tensor_tensor(out=y, in0=y, in1=yf, op=ALU.add), ["try", "tryf"], ["try"])
        self.A(lambda e: e.activation(out=out, in_=y, func=AF.Sin, scale=6.283185), ["try"], ["trx"])

    def s5_alloc(self):
        sb = self.asb
        self.wgl = [sb(f"wgl{i}", [128, 8, 128], BF16) for i in range(2)]
        self.LR = sb("LR", [128, 128]); self.LI = sb("LI", [128, 128]); self.DT = sb("DT", [128, 128])
        self.ANG = sb("ANG", [128, 128]); self.AR = sb("AR", [128, 128])
        self.SN = sb("SN", [128, 128]); self.CS = sb("CS", [128, 128])
        self.tr_y = sb("tr_y", [128, 128]); self.tr_yi = sb("tr_yi", [128, 128], I32); self.tr_yf = sb("tr_yf", [128, 128])
        self.PR = sb("PR", [128, 9, 128]); self.PI = sb("PI", [128, 9, 128])
        self.FR = sb("FR", [128, 128]); self.FI = sb("FI", [128, 128])
        self.t1 = sb("t1", [128, 256]); self.t2 = sb("t2", [128, 256]); self.t3 = sb("t3", [128, 256])
        self.BRk = sb("BRk", [128, 2, 8, 16]); self.BIk = sb("BIk", [128, 2, 8, 16])
        self.bbr = sb("bbr", [128, 2, 8, 16]); self.bbi = sb("bbi", [128, 2, 8, 16])
        self.CRn = sb("CRn", [128, 2, 2, 64]); self.CIn = sb("CIn", [128, 2, 2, 64])
        self.CRk = sb("CRk", [128, 2, 8, 16]); self.CIk = sb("CIk", [128, 2, 8, 16])
        self.BA = sb("BA", [128, 8, 2, 128]); self.CC = sb("CC", [128, 2, 128])
        self.CAr = sb("CAr", [128, 9, 2, 128], BF16); self.CAi = sb("CAi", [128, 9, 2, 128], BF16)
        self.Tsb = sb("Tsb", [128, 16, 128], BF16)
        self.LW = sb("LW", [128, 16, 2, 128], BF16)
        self.LM = sb("LM", [128, 4, 2, 2, 2, 128], BF16)
        self.Kc = sb("Kc", [128, 15, 128], BF16)
        self.SS = sb("SS", [128, 8 * 2 * 256]); self.HP = sb("HP", [128, 8 * 2 * 256], BF16)
        self.HH = sb("HH", [128, 64]); self.HT1 = sb("HT1", [128, 64]); self.HU1 = sb("HU1", [128, 64])
        self.A1 = sb("A1", [128, 64]); self.A2 = sb("A2", [128, 64])
        self.HL = sb("HL", [128, 256]); self.HLT1 = sb("HLT1", [128, 256]); self.HLU1 = sb("HLU1", [128, 256])
        self.A1f = sb("A1f", [128, 256]); self.A2f = sb("A2f", [128, 256])
        self.PWp = sb("PWp", [128, 256]); self.PW = sb("PW", [128, 256]); self.HST = sb("HST", [128, 256])
        self.Hc = sb("Hc", [128, 16]); self.Hc0 = sb("Hc0", [128, 16]); self.HcT = sb("HcT", [128, 16]); self.HcU = sb("HcU", [128, 16])
        self.B1 = sb("B1", [128, 16]); self.B2 = sb("B2", [128, 16])
        self.TC1 = sb("TC1", [128, 512]); self.TC2 = sb("TC2", [128, 512])
        self.h0T = sb("h0T", [128, 2, 2, 32]); self.FS = sb("FS", [128, 4, 2, 2, 32]); self.FSo = sb("FSo", [128, 128])
        self.dcol = sb("dcol", [128, 8]); self.bgT = sb("bgT", [128, 8])
        self.u8 = sb("u8", [128, 8, LS // 8], BF16); self.g_k = sb("g_k", [128, LS], BF16)
        self.t1g = sb("t1g", [128, 256]); self.t2g = sb("t2g", [128, 256])
        self.gblk = self.wst[0]
        self.sgm = self.SS[:, 0:512]; self.slu = self.SS[:, 512:1024]; self.yb = [self.HP[:, i * 512:(i + 1) * 512] for i in range(2)]
        self.V(lambda e: e.memset(self.LM, 0.0), [], ["LM"])

    def s5_layer_prep(self, js):
        V, A = self.V, self.A
        for half in range(2):
            hs = slice(half * 64, half * 64 + 64)
            self.ld(self.LR[hs, :], self.s5_lam_re[js].rearrange("d g p -> p (d g)"), w=["LR"], allow_slow_non_contiguous=True)
            self.ld(self.LI[hs, :], self.s5_lam_im[js].rearrange("d g p -> p (d g)"), w=["LI"], allow_slow_non_contiguous=True)
        self.ld(self.DT, self.s5_log_step[js].rearrange("d g -> (d g)").partition_broadcast(128), w=["DT"])
        self.ld(self.dcol, self.s5_d[js].rearrange("(k p) -> p k", p=128), w=["dcol"], allow_slow_non_contiguous=True)
        self.ld(self.bgT, self.s5_b_glu[js].rearrange("(k p) -> p k", p=128), w=["bgT"], allow_slow_non_contiguous=True)
        A(lambda e: e.activation(out=self.DT, in_=self.DT, func=AF.Exp), ["DT"], ["DT"])
        V(lambda e: e.tensor_tensor(out=self.ANG, in0=self.LI, in1=self.DT, op=ALU.mult), ["LI", "DT"], ["trx", "ANG"])
        V(lambda e: e.tensor_tensor(out=self.AR, in0=self.LR, in1=self.DT, op=ALU.mult), ["LR", "DT"], ["AR"])
        A(lambda e: e.activation(out=self.AR, in_=self.AR, func=AF.Exp), ["AR"], ["AR"])
        self.sin_of(self.SN, self.ANG, 0.0)
        self.sin_of(self.CS, self.ANG, math.pi / 2)
        PR, PI = self.PR, self.PI
        V(lambda e: e.memset(PR[:, 0], 1.0), [], ["PR"])
        V(lambda e: e.memset(PI[:, 0], 0.0), [], ["PI"])
        V(lambda e: e.tensor_tensor(out=PR[:, 1], in0=self.AR, in1=self.CS, op=ALU.mult), ["AR", "trx"], ["PR"])
        V(lambda e: e.tensor_tensor(out=PI[:, 1], in0=self.AR, in1=self.SN, op=ALU.mult), ["AR", "trx"], ["PI"])
        t1, t2 = self.t1[:, 0:128], self.t2[:, 0:128]
        for m in range(2, 9):
            V(lambda e: e.tensor_tensor(out=t1, in0=PR[:, m - 1], in1=PR[:, 1], op=ALU.mult), ["PR"], ["t1"])
            V(lambda e: e.tensor_tensor(out=t2, in0=PI[:, m - 1], in1=PI[:, 1], op=ALU.mult), ["PI"], ["t2"])
            V(lambda e: e.tensor_tensor(out=PR[:, m], in0=t1, in1=t2, op=ALU.subtract), ["t1", "t2"], ["PR"])
            V(lambda e: e.tensor_tensor(out=t1, in0=PR[:, m - 1], in1=PI[:, 1], op=ALU.mult), ["PR", "PI"], ["t1"])
            V(lambda e: e.tensor_tensor(out=t2, in0=PI[:, m - 1], in1=PR[:, 1], op=ALU.mult), ["PR", "PI"], ["t2"])
            V(lambda e: e.tensor_tensor(out=PI[:, m], in0=t1, in1=t2, op=ALU.add), ["t1", "t2"], ["PI"])
        nr, den = self.SN, self.CS
        V(lambda e: e.tensor_scalar(out=nr, in0=PR[:, 1], scalar1=-1.0, scalar2=None, op0=ALU.add), ["PR"], ["trx"])
        V(lambda e: e.tensor_tensor(out=t1, in0=self.LR, in1=self.LR, op=ALU.mult), ["LR"], ["t1"])
        V(lambda e: e.tensor_tensor(out=t2, in0=self.LI, in1=self.LI, op=ALU.mult), ["LI"], ["t2"])
        V(lambda e: e.tensor_tensor(out=den, in0=t1, in1=t2, op=ALU.add), ["t1", "t2"], ["trx"])
        V(lambda e: e.reciprocal(out=den, in_=den), ["trx"], ["trx"])
        V(lambda e: e.tensor_tensor(out=t1, in0=nr, in1=self.LR, op=ALU.mult), ["trx", "LR"], ["t1"])
        V(lambda e: e.tensor_tensor(out=t2, in0=PI[:, 1], in1=self.LI, op=ALU.mult), ["PI", "LI"], ["t2"])
        V(lambda e: e.tensor_tensor(out=t1, in0=t1, in1=t2, op=ALU.add), ["t1", "t2"], ["t1"])
        V(lambda e: e.tensor_tensor(out=self.FR, in0=t1, in1=den, op=ALU.mult), ["t1", "trx"], ["FR"])
        V(lambda e: e.tensor_tensor(out=t1, in0=PI[:, 1], in1=self.LR, op=ALU.mult), ["PI", "LR"], ["t1"])
        V(lambda e: e.tensor_tensor(out=t2, in0=nr, in1=self.LI, op=ALU.mult), ["trx", "LI"], ["t2"])
        V(lambda e: e.tensor_tensor(out=t1, in0=t1, in1=t2, op=ALU.subtract), ["t1", "t2"], ["t1"])
        V(lambda e: e.tensor_tensor(out=self.FI, in0=t1, in1=den, op=ALU.mult), ["t1", "trx"], ["FI"])
        self.ld(self.FSo, self.st5[js], w=["FSo"])
        pt = self.ps[1][:, 0:128]
        self.T(lambda e: e.transpose(pt, self.FSo, self.identf), ["FSo", "identf"], ["ps1"])
        V(lambda e: e.tensor_copy(out=self.h0T.rearrange("p d x g -> p (d x g)"), in_=pt), ["ps1"], ["h0T"])

    def bcg(self, tab, m, k):
        a = tab[:, m, :].rearrange("p (d g) -> p d g", d=2)[:, :, 8 * k:8 * k + 8]
        return a.unsqueeze(3).to_broadcast([128, 2, 8, 16])

    def bcf(self, tab, k):
        a = tab.rearrange("p (d g) -> p d g", d=2)[:, :, 8 * k:8 * k + 8]
        return a.unsqueeze(3).to_broadcast([128, 2, 8, 16])

    def cmul(self, eng, outr, outi, ar, ai, br, bi, rk, wk, hs_r=slice(0, 128), hs_i=slice(0, 128), negi=False):
        if eng == "gpsimd":
            t1 = self.t1g.rearrange("p (d g h) -> p d g h", d=2, g=8)
            t2 = self.t2g.rearrange("p (d g h) -> p d g h", d=2, g=8)
            k1, k2 = "t1g", "t2g"
        else:
            t1 = self.t1.rearrange("p (d g h) -> p d g h", d=2, g=8)
            t2 = self.t2.rearrange("p (d g h) -> p d g h", d=2, g=8)
            k1, k2 = "t1", "t2"
        E = self.E
        s = hs_r
        E(eng, lambda e: e.tensor_tensor(out=t1[s], in0=ar[s], in1=br[s], op=ALU.mult), rk, [k1])
        E(eng, lambda e: e.tensor_tensor(out=t2[s], in0=ai[s], in1=bi[s], op=ALU.mult), rk, [k2])
        E(eng, lambda e: e.tensor_tensor(out=outr[s], in0=t1[s], in1=t2[s], op=ALU.subtract), [k1, k2], wk)
        s = hs_i
        E(eng, lambda e: e.tensor_tensor(out=t1[s], in0=ar[s], in1=bi[s], op=ALU.mult), rk, [k1])
        E(eng, lambda e: e.tensor_tensor(out=t2[s], in0=ai[s], in1=br[s], op=ALU.mult), rk, [k2])
        if negi:
            E(eng, lambda e: e.tensor_tensor(out=t1[s], in0=t1[s], in1=t2[s], op=ALU.add), [k1, k2], [k1])
            E(eng, lambda e: e.tensor_scalar(out=outi[s], in0=t1[s], scalar1=-1.0, scalar2=None, op0=ALU.mult), [k1], wk)
        else:
            E(eng, lambda e: e.tensor_tensor(out=outi[s], in0=t1[s], in1=t2[s], op=ALU.add), [k1, k2], wk)

    def s5_scan1(self, k, seng, SSv, HPv, SQs, ncg, ncs, nseq, A1s, A2s, grp):
        E = self.E
        HHv = lambda t: t.rearrange("p (x q d s) -> p x q d s", x=2, q=4, d=2)
        HHs, T1s, U1s = self.HH[:, 0:16 * nseq], self.HT1[:, 0:16 * nseq], self.HU1[:, 0:16 * nseq]
        if grp == 1:
            E(seng, lambda e: e.tensor_copy(out=HHv(HHs)[:, :, :, :, 0].rearrange("p x q d -> p d x q"),
                                            in_=self.h0T[:, :, :, 4 * k:4 * k + 4]), ["h0T"], ["HH"])
        else:
            E(seng, lambda e: e.memset(HHs, 0.0), [], ["HH"])
        hsw = AP(HHs, HHs.offset + 8 * nseq, [[HHs.ap[0][0], 128], [-8 * nseq, 2], [1, 8 * nseq]])
        hfl = HHs.rearrange("p (x r) -> p x r", x=2)
        a2f = A2s.rearrange("p (x r) -> p x r", x=2)
        u1f = U1s.rearrange("p (x r) -> p x r", x=2)
        hh4 = HHs.rearrange("p (xq d s) -> p xq d s", d=2, s=nseq)
        for i in range(ncs):
            def colap(t):
                return AP(t, t.offset + i, [[t.ap[0][0], 128], [SQs, 8], [ncg + ncs - 1 - 2 * i, 2], [ncs, nseq]])
            E(seng, lambda e: e.tensor_copy(out=colap(HPv), in_=hh4), ["HH"], ["HP"])
            E(seng, lambda e: e.tensor_tensor(out=T1s, in0=A1s, in1=HHs, op=ALU.mult), ["A1", "HH"], ["HT1"])
            E(seng, lambda e: e.tensor_tensor(out=u1f, in0=a2f, in1=hsw, op=ALU.mult), ["A2", "HH"], ["HU1"])
            E(seng, lambda e: e.tensor_tensor(out=T1s, in0=T1s, in1=U1s, op=ALU.add), ["HT1", "HU1"], ["HT1"])
            E(seng, lambda e: e.tensor_tensor(out=hh4, in0=T1s.rearrange("p (xq d s) -> p xq d s", d=2, s=nseq),
                                              in1=colap(SSv), op=ALU.add), ["HT1", "SS"], ["HH"])
        if grp == 0:
            for s in range(nseq):
                E(seng, lambda e: e.tensor_copy(out=self.FS[:, s, :, :, 4 * k:4 * k + 4],
                                                in_=HHv(HHs)[:, :, :, :, s].rearrange("p x q d -> p d x q")), ["HH"], ["FS"])

    def s5_scan2(self, k, seng, SSv, HPv, SQs, ncg, A1s, A2s):
        E = self.E
        MB = 16
        ps_ = SSv.ap[0][0]
        HL, T1, U1, A1f, A2f = self.HL, self.HLT1, self.HLU1, self.A1f, self.A2f
        PWp, PW, HST = self.PWp, self.PW, self.HST
        Hc, Hc0, HcT, HcU, B1, B2 = self.Hc, self.Hc0, self.HcT, self.HcU, self.B1, self.B2
        TC1, TC2 = self.TC1, self.TC2
        E(seng, lambda e: e.tensor_copy(out=A1f.rearrange("p (a b) -> p a b", b=MB), in_=A1s.unsqueeze(2).to_broadcast([128, 16, MB])), ["A1"], ["A1f"])
        E(seng, lambda e: e.tensor_copy(out=A2f.rearrange("p (a b) -> p a b", b=MB), in_=A2s.unsqueeze(2).to_broadcast([128, 16, MB])), ["A2"], ["A2f"])
        PWv = PWp.rearrange("p (x g i) -> p x g i", x=2, g=8)
        E(seng, lambda e: e.tensor_copy(out=PWv[:, 0, :, 0], in_=A1s[:, 0:8]), ["A1"], ["PWp"])
        E(seng, lambda e: e.tensor_copy(out=PWv[:, 1, :, 0], in_=A2s[:, 8:16]), ["A2"], ["PWp"])
        ln = 1
        t1 = TC1[:, 0:64].rearrange("p (g i) -> p g i", g=8)
        t2 = TC2[:, 0:64].rearrange("p (g i) -> p g i", g=8)
        while ln < MB:
            mr = PWv[:, 0, :, ln - 1:ln].to_broadcast([128, 8, ln])
            mi = PWv[:, 1, :, ln - 1:ln].to_broadcast([128, 8, ln])
            ar, ai = PWv[:, 0, :, 0:ln], PWv[:, 1, :, 0:ln]
            E(seng, lambda e: e.tensor_tensor(out=t1[:, :, 0:ln], in0=ar, in1=mr, op=ALU.mult), ["PWp"], ["TC1"])
            E(seng, lambda e: e.tensor_tensor(out=t2[:, :, 0:ln], in0=ai, in1=mi, op=ALU.mult), ["PWp"], ["TC2"])
            E(seng, lambda e: e.tensor_tensor(out=PWv[:, 0, :, ln:2 * ln], in0=t1[:, :, 0:ln], in1=t2[:, :, 0:ln], op=ALU.subtract), ["TC1", "TC2"], ["PWp"])
            E(seng, lambda e: e.tensor_tensor(out=t1[:, :, 0:ln], in0=ar, in1=mi, op=ALU.mult), ["PWp"], ["TC1"])
            E(seng, lambda e: e.tensor_tensor(out=t2[:, :, 0:ln], in0=ai, in1=mr, op=ALU.mult), ["PWp"], ["TC2"])
            E(seng, lambda e: e.tensor_tensor(out=PWv[:, 1, :, ln:2 * ln], in0=t1[:, :, 0:ln], in1=t2[:, :, 0:ln], op=ALU.add), ["TC1", "TC2"], ["PWp"])
            ln *= 2
        PW5 = PW.rearrange("p (x q d i) -> p x q d i", x=2, q=4, d=2)
        PWp5 = PWp.rearrange("p (x q d i) -> p x q d i", x=2, q=4, d=2)
        for x in range(2):
            E(seng, lambda e: e.tensor_copy(out=PW5[:, x, :, 0, :], in_=PWp5[:, x, :, 0, :]), ["PWp"], ["PW"])
            E(seng, lambda e: e.tensor_copy(out=PW5[:, x, :, 1, :], in_=PWp5[:, x, :, 1, ::-1]), ["PWp"], ["PW"])
        B1v = B1.rearrange("p (x g) -> p x g", x=2)
        B2v = B2.rearrange("p (x g) -> p x g", x=2)
        for x in range(2):
            E(seng, lambda e: e.tensor_copy(out=B1v[:, x, :], in_=PWv[:, 0, :, MB - 1]), ["PWp"], ["B1"])
        E(seng, lambda e: e.tensor_scalar(out=B2v[:, 0, :], in0=PWv[:, 1, :, MB - 1], scalar1=-1.0, scalar2=None, op0=ALU.mult), ["PWp"], ["B2"])
        E(seng, lambda e: e.tensor_copy(out=B2v[:, 1, :], in_=PWv[:, 1, :, MB - 1]), ["PWp"], ["B2"])
        E(seng, lambda e: e.memset(HL, 0.0), [], ["HL"])
        hsw = AP(HL, HL.offset + 128, [[HL.ap[0][0], 128], [-128, 2], [1, 128]])
        a2f = A2f.rearrange("p (x r) -> p x r", x=2)
        u1f = U1.rearrange("p (x r) -> p x r", x=2)
        hl4 = HL.rearrange("p (g d b) -> p g d b", g=8, d=2)
        t14 = T1.rearrange("p (g d b) -> p g d b", g=8, d=2)
        for i in range(MB):
            col = AP(SSv, SSv.offset + i, [[ps_, 128], [SQs, 8], [ncg + MB - 1 - 2 * i, 2], [MB, MB]])
            E(seng, lambda e: e.tensor_tensor(out=T1, in0=A1f, in1=HL, op=ALU.mult), ["A1f", "HL"], ["HLT1"])
            E(seng, lambda e: e.tensor_tensor(out=u1f, in0=a2f, in1=hsw, op=ALU.mult), ["A2f", "HL"], ["HLU1"])
            E(seng, lambda e: e.tensor_tensor(out=T1, in0=T1, in1=U1, op=ALU.add), ["HLT1", "HLU1"], ["HLT1"])
            E(seng, lambda e: e.tensor_tensor(out=hl4, in0=t14, in1=col, op=ALU.add), ["HLT1", "SS"], ["HL"])
            E(seng, lambda e: e.tensor_copy(out=col, in_=hl4), ["HL"], ["SS"])
        E(seng, lambda e: e.tensor_copy(out=Hc.rearrange("p (x q d) -> p d x q", x=2, q=4), in_=self.h0T[:, :, :, 4 * k:4 * k + 4]), ["h0T"], ["Hc"])
        E(seng, lambda e: e.tensor_copy(out=Hc0, in_=Hc), ["Hc"], ["Hc0"])
        hcsw = AP(Hc, Hc.offset + 8, [[Hc.ap[0][0], 128], [-8, 2], [1, 8]])
        b2f = B2.rearrange("p (x r) -> p x r", x=2)
        hcuf = HcU.rearrange("p (x r) -> p x r", x=2)
        hc2 = Hc.rearrange("p (g d) -> p g d", d=2)
        hct2 = HcT.rearrange("p (g d) -> p g d", d=2)
        for b in range(MB):
            hpos = AP(HST, HST.offset + b, [[HST.ap[0][0], 128], [2 * MB, 8], [MB + MB - 1 - 2 * b, 2]])
            E(seng, lambda e: e.tensor_copy(out=hpos, in_=hc2), ["Hc"], ["HST"])
            if b == MB - 1:
                break
            send = AP(SSv, SSv.offset + MB * b + MB - 1, [[ps_, 128], [SQs, 8], [ncg + MB * (MB - 1 - b) - (MB * b + MB - 1), 2]])
            E(seng, lambda e: e.tensor_tensor(out=HcT, in0=B1, in1=Hc, op=ALU.mult), ["B1", "Hc"], ["HcT"])
            E(seng, lambda e: e.tensor_tensor(out=hcuf, in0=b2f, in1=hcsw, op=ALU.mult), ["B2", "Hc"], ["HcU"])
            E(seng, lambda e: e.tensor_tensor(out=HcT, in0=HcT, in1=HcU, op=ALU.add), ["HcT", "HcU"], ["HcT"])
            E(seng, lambda e: e.tensor_tensor(out=hc2, in0=hct2, in1=send, op=ALU.add), ["HcT", "SS"], ["Hc"])
        HST4 = HST.rearrange("p (x q d b) -> p x q d b", x=2, q=4, d=2)
        c1 = TC1.rearrange("p (q b i) -> p q b i", q=2, b=MB)
        c2 = TC2.rearrange("p (q b i) -> p q b i", q=2, b=MB)
        for d in range(2):
          for qh in range(2):
            qs = slice(2 * qh, 2 * qh + 2)
            pr = PW5[:, 0, qs, d, :].unsqueeze(2).to_broadcast([128, 2, MB, MB])
            pi = PW5[:, 1, qs, d, :].unsqueeze(2).to_broadcast([128, 2, MB, MB])
            hr = HST4[:, 0, qs, d, :].unsqueeze(3).to_broadcast([128, 2, MB, MB])
            hi = HST4[:, 1, qs, d, :].unsqueeze(3).to_broadcast([128, 2, MB, MB])
            sre = AP(SSv, SSv.offset + 2 * qh * SQs + d * ncg, [[ps_, 128], [SQs, 2], [MB, MB], [1, MB]])
            sim = AP(SSv, SSv.offset + (4 + 2 * qh) * SQs + d * ncg, [[ps_, 128], [SQs, 2], [MB, MB], [1, MB]])
            E(seng, lambda e: e.tensor_tensor(out=c1, in0=pr, in1=hr, op=ALU.mult), ["PW", "HST"], ["TC1"])
            E(seng, lambda e: e.tensor_tensor(out=c2, in0=pi, in1=hi, op=ALU.mult), ["PW", "HST"], ["TC2"])
            E(seng, lambda e: e.tensor_tensor(out=c1, in0=c1, in1=c2, op=ALU.subtract), ["TC1", "TC2"], ["TC1"])
            E(seng, lambda e: e.tensor_tensor(out=sre, in0=sre, in1=c1, op=ALU.add), ["SS", "TC1"], ["SS"])
            E(seng, lambda e: e.tensor_tensor(out=c1, in0=pr, in1=hi, op=ALU.mult), ["PW", "HST"], ["TC1"])
            E(seng, lambda e: e.tensor_tensor(out=c2, in0=pi, in1=hr, op=ALU.mult), ["PW", "HST"], ["TC2"])
            E(seng, lambda e: e.tensor_tensor(out=c1, in0=c1, in1=c2, op=ALU.add), ["TC1", "TC2"], ["TC1"])
            E(seng, lambda e: e.tensor_tensor(out=sim, in0=sim, in1=c1, op=ALU.add), ["SS", "TC1"], ["SS"])
        ss3 = SSv.rearrange("p (g d c) -> p g d c", g=8, d=2)
        hp3 = HPv.rearrange("p (g d c) -> p g d c", g=8, d=2)
        h03 = Hc0.rearrange("p (g d) -> p g d", d=2)
        self.A(lambda e: e.activation(out=hp3[:, :, 0, 1:ncg], in_=ss3[:, :, 0, 0:ncg - 1], func=AF.Copy), ["SS"], ["HP"])
        self.A(lambda e: e.activation(out=hp3[:, :, 1, 0:ncg - 1], in_=ss3[:, :, 1, 1:ncg], func=AF.Copy), ["SS"], ["HP"])
        E(seng, lambda e: e.tensor_copy(out=hp3[:, :, 0, 0:1], in_=h03[:, :, 0:1]), ["Hc0"], ["HP"])
        E(seng, lambda e: e.tensor_copy(out=hp3[:, :, 1, ncg - 1:ncg], in_=h03[:, :, 1:2]), ["Hc0"], ["HP"])

    def s5_mixer(self, js, grp):
        if grp == 1:
            self.arena_reset()
            self.s5_alloc()
            self.s5_layer_prep(js)
        else:
            self.V(lambda e: e.memset(self.LM, 0.0), [], ["LM"])
        t0, n = self.trange(grp)
        nseq = 1 if grp == 1 else NPS
        ncs = (n // nseq) // 8
        ncg = n // 8
        V, A, T, E = self.V, self.A, self.T, self.E
        H0, H1 = slice(0, 64), slice(64, 128)
        def ld_wu(k_):
            self.ldc(self.wgs[k_ % 2], self.s5_w_in[js][:, k_ * 128:(k_ + 1) * 128].rearrange("(k p) n -> p k n", p=128), w=[f"wgs{k_ % 2}"])
        ld_wu(0)
        for k in range(8):
            wu, wuk = self.wgs[k % 2], f"wgs{k % 2}"
            if k + 1 < 8:
                ld_wu(k + 1)
            eng = "vector"
            seng = "vector"
            for tb in range(n // 512):
                pt = self.ps[0]
                for kk in range(8):
                    T(lambda e: e.matmul(pt, lhsT=wu[:, kk, :],
                                         rhs=self.hT[:, kk, tb * 512:(tb + 1) * 512], start=(kk == 0), stop=(kk == 7)),
                      [wuk, "hT"], ["ps0"])
                A(lambda e: e.activation(out=self.u8[:, :, tb * 64:(tb + 1) * 64].rearrange("p j c -> p c j"),
                                         in_=pt.rearrange("p (c j) -> p c j", j=8), func=AF.Copy), ["ps0"], ["u_k"])
            if grp == 1:
                for half in range(2):
                    hs = slice(half * 64, half * 64 + 64)
                    for d in range(2):
                        self.ld(self.BRk[hs, d], self.s5_b_re[js, d, 8 * k:8 * k + 8].rearrange("g p h -> p g h"), w=["BRk"])
                        self.ld(self.BIk[hs, d], self.s5_b_im[js, d, 8 * k:8 * k + 8].rearrange("g p h -> p g h"), w=["BIk"])
                for dup in range(2):
                    self.ld(self.CRn[:, :, dup, :], self.s5_c_re[js][:, 8 * k:8 * k + 8].rearrange("d g h p -> (g h) d p"), w=["CRn"])
                    self.ld(self.CIn[:, :, dup, :], self.s5_c_im[js][:, 8 * k:8 * k + 8].rearrange("d g h p -> (g h) d p"), w=["CIn"])
                for (cn, cnk, ck, ckk) in ((self.CRn, "CRn", self.CRk, "CRk"), (self.CIn, "CIn", self.CIk, "CIk")):
                    for d in range(2):
                        pt = self.ps[1][:, 0:128]
                        src = cn[:, d].rearrange("p a c -> p (a c)")
                        T(lambda e: e.transpose(pt, src, self.identf), [cnk, "identf"], ["ps1"])
                        A(lambda e: e.activation(out=ck[:, d].rearrange("p g h -> p (g h)"), in_=pt, func=AF.Copy), ["ps1"], [ckk])
                self.cmul(eng, self.bbr, self.bbi, self.bcf(self.FR, k), self.bcf(self.FI, k), self.BRk, self.BIk,
                          ["FR", "FI", "BRk", "BIk"], ["bb"])
                BAv = self.BA.rearrange("p m d (g h) -> p m d g h", g=8)
                for m in range(8):
                    self.cmul(eng, BAv[:, m], BAv[:, m], self.bcg(self.PR, m, k), self.bcg(self.PI, m, k), self.bbr, self.bbi,
                              ["PR", "PI", "bb"], ["BA"], hs_r=H0, hs_i=H1)
                CCv = self.CC.rearrange("p d (g h) -> p d g h", g=8)
                V(lambda e: e.tensor_copy(out=CCv[H0], in_=self.CRk[H0]), ["CRk"], ["CC"])
                V(lambda e: e.tensor_scalar(out=CCv[H1], in0=self.CIk[H1], scalar1=-1.0, scalar2=None, op0=ALU.mult), ["CIk"], ["CC"])
                CArv = self.CAr.rearrange("p m d (g h) -> p m d g h", g=8)
                CAiv = self.CAi.rearrange("p m d (g h) -> p m d g h", g=8)
                for m in range(1, 9):
                    self.cmul(eng, CArv[:, m], CAiv[:, m], self.bcg(self.PR, m, k), self.bcg(self.PI, m, k), self.CRk, self.CIk,
                              ["PR", "PI", "CRk", "CIk"], ["CA"], negi=True)
                for tau in range(8):
                    for d in range(2):
                        pt = self.ps[1][:, 0:128]
                        if tau == 0:
                            T(lambda e: e.matmul(pt, lhsT=self.BA[:, 0, d], rhs=self.CC[:, d], start=(d == 0), stop=(d == 1)),
                              ["BA", "CC"], ["ps1"])
                            if d == 0:
                                continue
                            tt = self.t3[:, 0:128]
                            V(lambda e: e.tensor_tensor(out=tt, in0=pt, in1=self.bdmask, op=ALU.mult), ["ps1", "bdmask"], ["t3"])
                            V(lambda e: e.scalar_tensor_tensor(out=self.Kc[:, 7], in0=self.identf, scalar=self.dcol[:, k:k + 1],
                                                               in1=tt, op0=ALU.mult, op1=ALU.add), ["t3", "identf", "dcol"], ["Kc"])
                        else:
                            T(lambda e: e.matmul(pt, lhsT=self.BA[:, tau, d], rhs=self.CC[:, d], start=True, stop=True),
                              ["BA", "CC"], ["ps1"])
                            idx = 7 + tau if d == 0 else 7 - tau
                            V(lambda e: e.tensor_tensor(out=self.Kc[:, idx], in0=pt, in1=self.bdmask, op=ALU.mult),
                              ["ps1", "bdmask"], ["Kc"])
                for g4 in range(4):
                    bank, bk = (self.ps[1], "ps1") if g4 % 2 == 0 else (self.ps[0], "ps0")
                    for ii in range(4):
                        idx = g4 * 4 + ii
                        T(lambda e: e.transpose(bank[:, ii * 128:(ii + 1) * 128], self.BA[:, idx // 2, idx % 2], self.identf),
                          ["BA", "identf"], [bk])
                    A(lambda e: e.activation(out=self.Tsb[:, g4 * 4:(g4 + 1) * 4].rearrange("p a b -> p (a b)"), in_=bank, func=AF.Copy),
                      [bk], ["Tsb"])
                self.ld(self.Tsb_scr[js, k], self.Tsb.rearrange("p a b -> p (a b)"), r=["Tsb"], w=[("s5c", k)])
                self.ld(self.Kc_scr[js, k], self.Kc.rearrange("p a b -> p (a b)"), r=["Kc"], w=[("s5c", k)])
                self.ld(self.CA_scr[js, k, :, 0], self.CAr.rearrange("p m d c -> p (m d c)"), r=["CA"], w=[("s5c", k)])
                self.ld(self.CA_scr[js, k, :, 1], self.CAi.rearrange("p m d c -> p (m d c)"), r=["CA"], w=[("s5c", k)])
            else:
                self.ld(self.Tsb.rearrange("p a b -> p (a b)"), self.Tsb_scr[js, k], r=[("s5c", k)], w=["Tsb"])
                self.ld(self.Kc.rearrange("p a b -> p (a b)"), self.Kc_scr[js, k], r=[("s5c", k)], w=["Kc"])
                self.ld(self.CAr.rearrange("p m d c -> p (m d c)"), self.CA_scr[js, k, :, 0], r=[("s5c", k)], w=["CA"])
                self.ld(self.CAi.rearrange("p m d c -> p (m d c)"), self.CA_scr[js, k, :, 1], r=[("s5c", k)], w=["CA"])
            HHv = lambda t: t.rearrange("p (x q d s) -> p x q d s", x=2, q=4, d=2)
            A1s, A2s = self.A1[:, 0:16 * nseq], self.A2[:, 0:16 * nseq]
            A1v, A2v = HHv(A1s), HHv(A2s)
            for hf, hsl in ((0, H0), (1, H1)):
                pr8 = self.PR[hsl, 8, :].rearrange("p (d g) -> p d g", d=2)[:, :, 8 * k + hf:8 * k + 8:2]
                pi8 = self.PI[hsl, 8, :].rearrange("p (d g) -> p d g", d=2)[:, :, 8 * k + hf:8 * k + 8:2]
                for s in range(nseq):
                    for x in range(2):
                        V(lambda e: e.tensor_copy(out=A1v[hsl, x, :, :, s].rearrange("p q d -> p d q"), in_=pr8), ["PR"], ["A1"])
                    V(lambda e: e.tensor_scalar(out=A2v[hsl, 0, :, :, s].rearrange("p q d -> p d q"), in0=pi8, scalar1=-1.0,
                                                scalar2=None, op0=ALU.mult), ["PI"], ["A2"])
                    V(lambda e: e.tensor_copy(out=A2v[hsl, 1, :, :, s].rearrange("p q d -> p d q"), in_=pi8), ["PI"], ["A2"])
            SQs = 2 * ncg
            SSv = self.SS[:, 0:16 * ncg]
            HPv = self.HP[:, 0:16 * ncg]
            for q in range(4):
                for x in range(2):
                    in0 = self.Tsb[:, :, x * 64:(x + 1) * 64].unsqueeze(2).to_broadcast([128, 16, 2, 64])
                    in1 = self.pmask[:, 2 * q:2 * q + 2].unsqueeze(1).unsqueeze(3).to_broadcast([128, 16, 2, 64])
                    outv = self.LW[:, :, x, :].rearrange("p m (a c) -> p m a c", a=2)
                    V(lambda e: e.tensor_tensor(out=outv, in0=in0, in1=in1, op=ALU.mult), ["Tsb", "pmask"], ["LW"])
                for d in range(2):
                    for x in range(2):
                        pt = self.ps[2 + x][:, 0:ncg]
                        for j in range(8):
                            m = 7 - j if d == 0 else j
                            T(lambda e: e.matmul(pt, lhsT=self.LW[:, m * 2 + d, x, :], rhs=self.u8[:, j, 0:ncg],
                                                 start=(j == 0), stop=(j == 7)), ["LW", "u_k"], [f"ps{2 + x}"])
                        off = (x * 4 + q) * SQs + d * ncg
                        A(lambda e: e.activation(out=SSv[:, off:off + ncg], in_=pt, func=AF.Copy), [f"ps{2 + x}"], ["SS"])
            if grp == 1:
                self.s5_scan2(k, seng, SSv, HPv, SQs, ncg, A1s, A2s)
            else:
                self.s5_scan1(k, seng, SSv, HPv, SQs, ncg, ncs, nseq, A1s, A2s, grp)
            for jh in range(4):
                for d in range(2):
                    for x, ca in ((0, self.CAr), (1, self.CAi)):
                        for hf, hsl in ((0, H0), (1, H1)):
                            lm = self.LM[hsl, :, d, x, :, :]
                            outv = AP(lm, lm.offset + 16 * hf, [[lm.ap[0][0], 64], [lm.ap[1][0] + 32, 4], [lm.ap[2][0], 2], [1, 16]])
                            if d == 0:
                                m0, ms = 2 * jh + 1, 1
                            else:
                                m0, ms = 8 - 2 * jh, -1
                            cam = ca[hsl, m0, d, :]
                            mstride = ca.ap[1][0]
                            inv = AP(cam, cam.offset + 16 * hf, [[cam.ap[0][0], 64], [32, 4], [ms * mstride, 2], [1, 16]])
                            V(lambda e: e.tensor_copy(out=outv, in_=inv), ["CA"], ["LM"])
                for jj in range(2):
                    j = jh * 2 + jj
                    pt = self.ps[4 + jj][:, 0:ncg]
                    pk = f"ps{4 + jj}"
                    for j2 in range(8):
                        T(lambda e: e.matmul(pt, lhsT=self.Kc[:, j - j2 + 7, :], rhs=self.u8[:, j2, 0:ncg],
                                             start=(j2 == 0), stop=False), ["Kc", "u_k"], [pk])
                    cnt = 0
                    for q in range(4):
                        for d in range(2):
                            for x in range(2):
                                off = (x * 4 + q) * SQs + d * ncg
                                cnt += 1
                                T(lambda e: e.matmul(pt, lhsT=self.LM[:, q, d, x, jj, :], rhs=HPv[:, off:off + ncg],
                                                     start=False, stop=(cnt == 16)), ["LM", "HP"], [pk])
                    A(lambda e: e.activation(out=self.g_k[:, j:n:8], in_=pt, func=AF.Gelu_apprx_tanh), [pk], ["g_k"])
            self.ld(self.gscr[k, :, t0:t0 + n], self.g_k[:, 0:n], r=["g_k"], w=[("gscr", grp)])
        if grp == 0:
            for s in range(nseq):
                pt = self.ps[1][:, 0:128]
                T(lambda e: e.transpose(pt, self.FS[:, s].rearrange("p d x g -> p (d x g)"), self.identf), ["FS", "identf"], ["ps1"])
                V(lambda e: e.tensor_copy(out=self.FSo, in_=pt), ["ps1"], ["FSo"])
                self.ld(self.ns5[s, js], self.FSo, r=["FSo"], w=["ns5"])
        lwv = self.LW.rearrange("p a b c -> p (a b c)")
        wglu = AP(lwv, lwv.offset, [[lwv.ap[0][0], 128], [1024, 8], [1, 1024]])
        self.ldc(wglu, self.s5_w_glu[js].rearrange("(k p) n -> p k n", p=128), w=["LW", "LM"])
        steps = [(tb, nn) for tb in range(n // 512) for nn in range(8)]
        def load_gate(i):
            nn_ = steps[i][1]
            self.ldc(self.wgs[i % 2], self.s5_w_in[js][:, D + nn_ * 128:D + (nn_ + 1) * 128].rearrange("(k p) n -> p k n", p=128),
                     w=[f"wgs{i % 2}"])
        load_gate(0)
        for si, (tb, nn) in enumerate(steps):
            ts = slice(tb * 512, (tb + 1) * 512)
            if nn == 0:
                self.ld(self.gblk, self.gscr[0:8, :, t0 + tb * 512:t0 + (tb + 1) * 512].rearrange("k p t -> p k t"),
                        r=[("gscr", grp)], w=["wst0"])
            if si + 1 < len(steps):
                load_gate(si + 1)
            pz, pg = self.ps[0 + 2 * (nn % 2)], self.ps[1 + 2 * (nn % 2)]
            pzk, pgk = f"ps{0 + 2 * (nn % 2)}", f"ps{1 + 2 * (nn % 2)}"
            for kk in range(8):
                T(lambda e: e.matmul(pz, lhsT=wglu[:, kk, nn * 128:(nn + 1) * 128], rhs=self.gblk[:, kk, :],
                                     start=(kk == 0), stop=(kk == 7)), ["LW", "LM", "wst0"], [pzk])
            A(lambda e: e.activation(out=self.sgm, in_=pz, func=AF.Sigmoid, bias=self.bgT[:, nn:nn + 1]), [pzk, "bgT"], ["SS"])
            wg, wgk = self.wgs[si % 2], f"wgs{si % 2}"
            for kk in range(8):
                T(lambda e: e.matmul(pg, lhsT=wg[:, kk, :], rhs=self.hT[:, kk, ts],
                                     start=(kk == 0), stop=(kk == 7)), [wgk, "hT"], [pgk])
            A(lambda e: e.activation(out=self.slu, in_=pg, func=AF.Silu), [pgk], ["SS2"])
            yb = self.yb[nn % 2]
            ybk = f"yb{nn % 2}"
            V(lambda e: e.tensor_tensor(out=self.sgm, in0=self.sgm, in1=self.gblk[:, nn, :], op=ALU.mult), ["SS", "wst0"], ["SS"])
            V(lambda e: e.tensor_tensor(out=yb, in0=self.sgm, in1=self.slu, op=ALU.mult), ["SS", "SS2"], [ybk, "HP"])
            self.ld(self.g2scr[nn, :, t0 + tb * 512:t0 + (tb + 1) * 512], yb, r=[ybk], w=[("g2scr", grp)])
        self.ysrc = (self.g2scr, ("g2scr", grp))
        return D


def host_consts():
    r = np.arange(128)
    pm = np.zeros((128, 8), np.float32)
    for q in range(4):
        for qq in range(2):
            pm[:, 2 * q + qq] = ((r // 16) == 2 * q + qq)
    bd = ((r[:, None] // 16) == (r[None, :] // 16)).astype(np.float32)
    t = np.arange(2048)
    row, col = t // 64, t % 64
    inv = (10000.0 ** (-np.arange(32, dtype=np.float32) / 32)).astype(np.float32)
    ang = np.concatenate([row[:, None].astype(np.float32) * inv[None], col[:, None].astype(np.float32) * inv[None]], 1)
    jj = r[:, None].astype(np.float32)
    ii = r[None, :].astype(np.float32)
    retE = np.stack([np.maximum(ii - jj, 0.0), np.maximum(jj - ii, 0.0)]).astype(np.float32)
    retM = np.stack([(ii >= jj), (jj > ii)]).astype(np.float32)
    retqe = np.stack([r + 1.0, 128.0 - r]).astype(np.float32)
    retke = np.stack([127.0 - r, r * 1.0], 1).astype(np.float32)
    extra = {
        "c_ropeC": np.cos(ang).astype(np.float32).reshape(16, 128, 64),
        "c_ropeS": np.sin(ang).astype(np.float32).reshape(16, 128, 64),
        "c_retE": retE, "c_retM": retM, "c_retqe": retqe, "c_retke": retke,
    }
    extra.update(hy_consts())
    return extra | {
        "c_identb": np.eye(128, dtype=np.float32).astype(ml_dtypes.bfloat16),
        "c_identf": np.eye(128, dtype=np.float32),
        "c_pmask": pm,
        "c_bdmask": bd,
    }


_CONSTS = None


def make_in_maps(prog, inputs):
    global _CONSTS
    if _CONSTS is None:
        _CONSTS = host_consts()
    consts = _CONSTS
    maps = []
    for c in range(8):
        m = {}
        for name in prog.inputs:
            if name in consts:
                m[name] = consts[name]
            elif name == "xs":
                m[name] = np.ascontiguousarray(inputs["x_sample"][c])
            elif name == "xp":
                m[name] = np.ascontiguousarray(inputs["x_prompt"][4 * c:4 * c + 4].reshape(NPS * LP, D))
            elif name == "cvec":
                m[name] = np.ascontiguousarray(np.stack([inputs["c_ctx"], inputs["c"][c]], 0))
            elif name == "st5":
                m[name] = np.ascontiguousarray(inputs["state_s5"][c].reshape(2, 128, 128))
            elif name == "stret":
                m[name] = np.ascontiguousarray(inputs["state_ret"][c, 0])
            else:
                a = np.asarray(inputs[name])
                shp = prog.inputs[name][0]
                m[name] = np.ascontiguousarray(a.reshape(shp))
        maps.append(m)
    return maps


_PROG = None


def kernel(**inputs):
    global _PROG
    inputs = {k: np.asarray(v) for k, v in inputs.items()}
    if _PROG is None:
        _PROG = K()
    prog = _PROG
    res = run_bass_kernel_spmd(prog.nc, make_in_maps(prog, inputs), core_ids=list(range(8)))
    rs = res.results
    y_prompt = np.concatenate([r["yp"].reshape(NPS, LP, D) for r in rs], 0)
    y_sample = np.stack([r["ys"] for r in rs], 0)
    ns5 = np.concatenate([r["ns5"].reshape(NPS, 2, 2, 2, 64, 64) for r in rs], 0)
    nret = np.concatenate([r["nret"][:, None].reshape(NPS, 1, 2, 8, 128, 256) for r in rs], 0)
    return (y_prompt.astype(np.float32), y_sample.astype(np.float32), ns5.astype(np.float32), nret.astype(np.float32))


def _ret_decl(self):
    self.ret_w_in = self.din("ret_w_in", [1, D, 6 * D])
    self.ret_decay_logit = self.din("ret_decay_logit", [1, 2, 8])
    self.ret_w_out = self.din("ret_w_out", [1, 2 * D, D])
    self.c_ropeC = self.din("c_ropeC", [16, 128, 64])
    self.c_ropeS = self.din("c_ropeS", [16, 128, 64])
    self.c_retE = self.din("c_retE", [2, 128, 128])
    self.c_retM = self.din("c_retM", [2, 128, 128])
    self.c_retqe = self.din("c_retqe", [2, 128])
    self.c_retke = self.din("c_retke", [128, 2])
    self.qT_scr = self.dscr("qT_scr", [8, 128, NT], BF16)
    self.kT_scr = self.dscr("kT_scr", [8, 128, NT], BF16)
    self.ktok_scr = self.dscr("ktok_scr", [NT, D], BF16)
    self.v_scr = self.dscr("v_scr", [NT, 2 * D], BF16)
    self.gate_scr = self.dscr("gate_scr", [NT, 2 * D], BF16)
    self.of_scr = self.dscr("of_scr", [NT, 2 * D])


def _ret_mixer(self, grp):
    V, A, T, G = self.V, self.A, self.T, self.G
    t0, n = self.trange(grp)
    nseq = 1 if grp == 1 else NPS
    L = n // nseq
    nch = L // 128
    self.arena_reset()
    sb = self.asb
    wblk = [sb(f"rwb{i}", [128, 8, 512], BF16) for i in range(2)]
    lgt = sb("lgt", [128, 16]); kdt = sb("kdt", [128, 16]); cdt = sb("cdt", [128, 16]); ke = sb("ke", [128, 2])
    Et = sb("Et", [128, 2, 128]); Mt = sb("Mt", [128, 2, 128]); qe = sb("qe", [128, 2, 128])
    Dtab = sb("Dtab", [128, 16, 128]); qdtab = sb("qdtab", [128, 16, 128])
    rc = sb("rc", [128, 64]); rs = sb("rs", [128, 64])
    pq = sb("pq", [128, 512]); pq2 = sb("pq2", [128, 512]); pt1 = sb("pt1", [128, 512])
    pbf = [sb(f"pbf{i}", [128, 512], BF16) for i in range(2)]
    trb = sb("trb", [128, 4, 128], BF16)
    S = sb("S", [128, 8, 256]); Sb = sb("Sb", [128, 8, 256], BF16)
    qTc = [sb(f"qTc{i}", [128, 8, 128], BF16) for i in range(2)]
    kTc = [sb(f"kTc{i}", [128, 8, 128], BF16) for i in range(2)]
    ktc = [sb(f"ktc{i}", [128, 1024], BF16) for i in range(2)]
    vc = [sb(f"vc{i}", [128, 2048], BF16) for i in range(2)]
    gc = sb("gc", [128, 2048], BF16)
    ofc = sb("ofc", [128, 2048])
    ot = sb("ot", [128, 2048])
    ybf = sb("ybf", [128, 2048], BF16)
    yTt = sb("yTt", [128, 16, 128], BF16)
    attb2 = [sb(f"attb{i}", [128, 128], BF16) for i in range(2)]
    qd2 = [sb(f"qd{i}", [128, 128], BF16) for i in range(2)]
    kd2 = [sb(f"kd{i}", [128, 128], BF16) for i in range(2)]
    rst = sb("rst", [128, 24])
    self.ld(lgt, self.ret_decay_logit[0].rearrange("d h -> (d h)").partition_broadcast(128), w=["lgt"])
    self.ld(ke, self.c_retke, w=["ke"])
    for d in range(2):
        self.ld(Et[:, d], self.c_retE[d], w=["Et"])
        self.ld(Mt[:, d], self.c_retM[d], w=["Mt"])
        self.ld(qe[:, d], self.c_retqe[d].partition_broadcast(128), w=["qe"])
    A(lambda e: e.activation(out=lgt, in_=lgt, func=AF.Exp, scale=-1.0), ["lgt"], ["lgt"])
    V(lambda e: e.tensor_scalar(out=lgt, in0=lgt, scalar1=1.0, scalar2=None, op0=ALU.add), ["lgt"], ["lgt"])
    A(lambda e: e.activation(out=lgt, in_=lgt, func=AF.Ln), ["lgt"], ["lgt"])
    V(lambda e: e.tensor_scalar(out=lgt, in0=lgt, scalar1=-1.0, scalar2=None, op0=ALU.mult), ["lgt"], ["lgt"])
    for d in range(2):
        for h in range(8):
            c = d * 8 + h
            A(lambda e: e.activation(out=Dtab[:, c], in_=Et[:, d], func=AF.Exp, scale=lgt[:, c:c + 1]), ["Et", "lgt"], ["Dtab"])
            V(lambda e: e.tensor_tensor(out=Dtab[:, c], in0=Dtab[:, c], in1=Mt[:, d], op=ALU.mult), ["Dtab", "Mt"], ["Dtab"])
            A(lambda e: e.activation(out=qdtab[:, c], in_=qe[:, d], func=AF.Exp, scale=lgt[:, c:c + 1]), ["qe", "lgt"], ["qdtab"])
            A(lambda e: e.activation(out=kdt[:, c:c + 1], in_=ke[:, d:d + 1], func=AF.Exp, scale=lgt[:, c:c + 1]), ["ke", "lgt"], ["kdt"])
    A(lambda e: e.activation(out=cdt, in_=lgt, func=AF.Exp, scale=128.0), ["lgt"], ["cdt"])
    gk = lambda nm: (nm, grp)
    def ld_wb(cb_):
        self.ldc(wblk[cb_ % 2], self.ret_w_in[0][:, cb_ * 512:(cb_ + 1) * 512].rearrange("(k p) n -> p k n", p=128), w=[f"rwb{cb_ % 2}"])
    ld_wb(0)
    for cb in range(12):
        wb, wk = wblk[cb % 2], f"rwb{cb % 2}"
        if cb + 1 < 12:
            ld_wb(cb + 1)
        for tt in range(n // 128):
            ts = slice(tt * 128, (tt + 1) * 128)
            gts = slice(t0 + tt * 128, t0 + (tt + 1) * 128)
            pp = self.ps[tt % 2]
            pk = f"ps{tt % 2}"
            for kk in range(8):
                T(lambda e: e.matmul(pp, lhsT=self.hT[:, kk, ts], rhs=wb[:, kk, :], start=(kk == 0), stop=(kk == 7)),
                  ["hT", wk], [pk])
            ob = pbf[tt % 2]
            obk = f"pbf{tt % 2}"
            if cb < 4:
                isk = cb >= 2
                sc = (128.0 ** -0.5) if isk else 1.0
                if grp == 1:
                    if True:
                        self.ld(rc, self.c_ropeC[tt], w=["rc"])
                        self.ld(rs, self.c_ropeS[tt], w=["rs"])
                    A(lambda e: e.activation(out=pq, in_=pp, func=AF.Copy, scale=sc), [pk], ["pq"])
                    v5 = lambda t: t.rearrange("p (h a b f) -> p h a b f", h=4, a=2, b=2)
                    x1 = v5(pq)[:, :, :, 0, :]
                    x2 = v5(pq)[:, :, :, 1, :]
                    cosb = rc.rearrange("p (a f) -> p a f", a=2).unsqueeze(1).to_broadcast([128, 4, 2, 32])
                    sinb = rs.rearrange("p (a f) -> p a f", a=2).unsqueeze(1).to_broadcast([128, 4, 2, 32])
                    o1 = v5(pq2)[:, :, :, 0, :]
                    o2 = v5(pq2)[:, :, :, 1, :]
                    u1 = v5(pt1)[:, :, :, 0, :]
                    u2 = v5(pt1)[:, :, :, 1, :]
                    V(lambda e: e.tensor_tensor(out=o1, in0=x1, in1=cosb, op=ALU.mult), ["pq", "rc"], ["pq2"])
                    V(lambda e: e.tensor_tensor(out=u1, in0=x2, in1=sinb, op=ALU.mult), ["pq", "rs"], ["pt1"])
                    G(lambda e: e.tensor_tensor(out=o2, in0=x1, in1=sinb, op=ALU.mult), ["pq", "rs"], ["pq2b"])
                    G(lambda e: e.tensor_tensor(out=u2, in0=x2, in1=cosb, op=ALU.mult), ["pq", "rc"], ["pt1b"])
                    V(lambda e: e.tensor_tensor(out=v5(ob)[:, :, :, 0, :], in0=o1, in1=u1, op=ALU.subtract), ["pq2", "pt1"], [obk])
                    V(lambda e: e.tensor_tensor(out=v5(ob)[:, :, :, 1, :], in0=o2, in1=u2, op=ALU.add), ["pq2b", "pt1b"], [obk])
                else:
                    A(lambda e: e.activation(out=ob, in_=pp, func=AF.Copy, scale=sc), [pk], [obk])
                if isk:
                    self.ld(self.ktok_scr[gts, (cb - 2) * 512:(cb - 1) * 512], ob, r=[obk], w=[gk("ktok")])
                for hh in range(4):
                    ptr = self.ps[2].bitcast(BF16)[:, hh * 128:(hh + 1) * 128]
                    T(lambda e: e.transpose(ptr, ob[:, hh * 128:(hh + 1) * 128], self.identb), [obk, "identb"], ["ps2"])
                V(lambda e: e.tensor_copy(out=trb.rearrange("p a b -> p (a b)"), in_=self.ps[2].bitcast(BF16)[:, 0:512]), ["ps2"], ["trb"])
                dst = self.kT_scr if isk else self.qT_scr
                h0 = (cb % 2) * 4
                self.ld(dst[h0:h0 + 4, :, gts].rearrange("h p t -> p h t"), trb, r=["trb"], w=[gk("kT" if isk else "qT")])
            elif cb < 8:
                A(lambda e: e.activation(out=ob, in_=pp, func=AF.Copy), [pk], [obk])
                self.ld(self.v_scr[gts, (cb - 4) * 512:(cb - 3) * 512], ob, r=[obk], w=[gk("v")])
            else:
                A(lambda e: e.activation(out=ob, in_=pp, func=AF.Silu), [pk], [obk])
                self.ld(self.gate_scr[gts, (cb - 8) * 512:(cb - 7) * 512], ob, r=[obk], w=[gk("gate")])
    for s in range(nseq):
        for d in range(2):
            if grp == 1:
                self.ld(S, self.stret[d].rearrange("h p e -> p h e"), w=["S"])
            else:
                V(lambda e: e.memset(S, 0.0), [], ["S"])
            V(lambda e: e.tensor_copy(out=Sb, in_=S), ["S"], ["Sb"])
            order = range(nch) if d == 0 else range(nch - 1, -1, -1)
            for ci, c in enumerate(order):
                b = ci % 2
                ts = slice(t0 + s * L + c * 128, t0 + s * L + (c + 1) * 128)
                self.ld(qTc[b], self.qT_scr[:, :, ts].rearrange("h p t -> p h t"), r=[gk("qT")], w=[f"qTc{b}"])
                self.ld(kTc[b], self.kT_scr[:, :, ts].rearrange("h p t -> p h t"), r=[gk("kT")], w=[f"kTc{b}"])
                self.ld(ktc[b], self.ktok_scr[ts, :], r=[gk("ktok")], w=[f"ktc{b}"])
                self.ld(vc[b], self.v_scr[ts, :], r=[gk("v")], w=[f"vc{b}"])
                if d == 1:
                    self.ld(ofc, self.of_scr[ts, :], r=[gk("of")], w=["ofc"])
                    self.ld(gc, self.gate_scr[ts, :], r=[gk("gate")], w=["gc"])
                for h in range(8):
                    cI = d * 8 + h
                    hb = h % 2
                    attb, qd, kd = attb2[hb], qd2[hb], kd2[hb]
                    attk, qdk, kdk = f"attb{hb}", f"qd{hb}", f"kd{hb}"
                    pak = "ps3" if hb == 0 else "ps0"
                    pa = (self.ps[3] if hb == 0 else self.ps[0])[:, 0:128]
                    T(lambda e: e.matmul(pa, lhsT=kTc[b][:, h, :], rhs=qTc[b][:, h, :], start=True, stop=True),
                      [f"kTc{b}", f"qTc{b}"], [pak])
                    V(lambda e: e.tensor_tensor(out=attb, in0=pa, in1=Dtab[:, cI], op=ALU.mult), [pak, "Dtab"], [attk])
                    G(lambda e: e.tensor_tensor(out=qd, in0=qTc[b][:, h, :], in1=qdtab[:, cI], op=ALU.mult), [f"qTc{b}", "qdtab"], [qdk])
                    A(lambda e: e.activation(out=kd, in_=ktc[b][:, h * 128:(h + 1) * 128], func=AF.Copy, scale=kdt[:, cI:cI + 1]),
                      [f"ktc{b}", "kdt"], [kdk])
                    po = self.ps[4 + (h % 2)][:, 0:256]
                    pok = f"ps{4 + (h % 2)}"
                    T(lambda e: e.matmul(po, lhsT=attb, rhs=vc[b][:, h * 256:(h + 1) * 256], start=True, stop=False),
                      [attk, f"vc{b}"], [pok])
                    T(lambda e: e.matmul(po, lhsT=qd, rhs=Sb[:, h, :], start=False, stop=True), [qdk, "Sb"], [pok])
                    psu = self.ps[6 + (h % 2)][:, 0:256]
                    psk = f"ps{6 + (h % 2)}"
                    T(lambda e: e.matmul(psu, lhsT=kd, rhs=vc[b][:, h * 256:(h + 1) * 256], start=True, stop=True),
                      [kdk, f"vc{b}"], [psk])
                    if d == 0:
                        A(lambda e: e.activation(out=ot[:, h * 256:(h + 1) * 256], in_=po, func=AF.Copy), [pok], ["ot"])
                    else:
                        V(lambda e: e.tensor_tensor(out=ot[:, h * 256:(h + 1) * 256], in0=po, in1=ofc[:, h * 256:(h + 1) * 256],
                                                    op=ALU.add), [pok, "ofc"], ["ot"])
                    V(lambda e: e.scalar_tensor_tensor(out=S[:, h, :], in0=S[:, h, :], scalar=cdt[:, cI:cI + 1], in1=psu,
                                                       op0=ALU.mult, op1=ALU.add), ["S", "cdt", psk], ["S"])
                    A(lambda e: e.activation(out=Sb[:, h, :], in_=S[:, h, :], func=AF.Copy), ["S"], ["Sb"])
                if d == 0:
                    self.ld(self.of_scr[ts, :], ot, r=["ot"], w=[gk("of")])
                else:
                    for h in range(8):
                        A(lambda e: e.activation(out=ofc[:, h * 256:(h + 1) * 256], in_=ot[:, h * 256:(h + 1) * 256], func=AF.Square,
                                                 accum_out=rst[:, h:h + 1]), ["ot"], ["ofc", "rst"])
                    V(lambda e: e.tensor_scalar(out=rst[:, 8:16], in0=rst[:, 0:8], scalar1=1.0 / 256, scalar2=EPS,
                                                op0=ALU.mult, op1=ALU.add), ["rst"], ["rst"])
                    A(lambda e: e.activation(out=rst[:, 8:16], in_=rst[:, 8:16], func=AF.Sqrt), ["rst"], ["rst"])
                    V(lambda e: e.reciprocal(out=rst[:, 16:24], in_=rst[:, 8:16]), ["rst"], ["rst"])
                    for h in range(8):
                        V(lambda e: e.scalar_tensor_tensor(out=ybf[:, h * 256:(h + 1) * 256], in0=ot[:, h * 256:(h + 1) * 256],
                                                           scalar=rst[:, 16 + h:17 + h], in1=gc[:, h * 256:(h + 1) * 256],
                                                           op0=ALU.mult, op1=ALU.mult), ["ot", "rst", "gc"], ["ybf"])
                    for k4 in range(4):
                        for kk in range(4):
                            k = k4 * 4 + kk
                            ptr = self.ps[2].bitcast(BF16)[:, kk * 128:(kk + 1) * 128]
                            T(lambda e: e.transpose(ptr, ybf[:, k * 128:(k + 1) * 128], self.identb), ["ybf", "identb"], ["ps2"])
                        A(lambda e: e.activation(out=yTt[:, k4 * 4:(k4 + 1) * 4, :].rearrange("p a b -> p (a b)"),
                                                 in_=self.ps[2].bitcast(BF16)[:, 0:512], func=AF.Copy), ["ps2"], ["yTt"])
                    self.ld(self.gscr[0:16, :, ts].rearrange("k p t -> p k t"), yTt, r=["yTt"], w=[("gscr", grp)])
            if grp == 0:
                self.ld(self.nret[s, d].rearrange("h p e -> p h e"), S, r=["S"], w=["nret"])
    self.ysrc = (self.gscr, ("gscr", grp))
    return 2 * D


K.ret_decl = _ret_decl
K.ret_mixer = _ret_mixer


HY_FT = {2048: 17, 256: 3}


def _hy_decl(self):
    self.hy_w_in = self.din("hy_w_in", [1, D, 8 * D])
    self.hy_conv_w = self.din("hy_conv_w", [1, 3, 6 * D])
    self.hy_conv_b = self.din("hy_conv_b", [1, 6 * D])
    self.hy_f_w1 = self.din("hy_f_w1", [1, 33, 64])
    self.hy_f_b1 = self.din("hy_f_b1", [1, 64])
    self.hy_f_w2 = self.din("hy_f_w2", [1, 64, 64])
    self.hy_f_b2 = self.din("hy_f_b2", [1, 64])
    self.hy_f_w3 = self.din("hy_f_w3", [1, 64, 8 * D])
    self.hy_skip = self.din("hy_skip", [1, 2, 2 * D])
    self.hy_w_out = self.din("hy_w_out", [1, 2 * D, D])
    self.c_absd = self.din("c_absd", [2 * D])
    self.c_ones = self.din("c_ones", [128, 128])
    self.hyc = {}
    for L in (2048, 256):
        FT = HY_FT[L]
        self.hyc[L] = dict(
            feat=self.din(f"c_feat{L}", [33, L]),
            tneg=self.din(f"c_tneg{L}", [128, L // 128]),
            C=self.din(f"c_C{L}", [FT, 128, L // 128, 128], BF16), S=self.din(f"c_S{L}", [FT, 128, L // 128, 128], BF16),
            IC=self.din(f"c_IC{L}", [L // min(512, L), 128, FT, min(512, L)], BF16),
            IS=self.din(f"c_IS{L}", [L // min(512, L), 128, FT, min(512, L)], BF16))
    self.vT_scr = self.dscr("vT_scr", [16, 128, NT])
    self.x1T_scr = self.dscr("x1T_scr", [16, 128, NT])
    self.x2T_scr = self.dscr("x2T_scr", [16, 128, NT])
    self.z1T_scr = self.dscr("z1T_scr", [16, 128, NT])
    self.sgT_scr = self.dscr("sgT_scr", [16, 128, NT], BF16)
    self.ztok_scr = self.dscr("ztok_scr", [2, NT, 2 * D], BF16)
    self.Eo_scr = self.dscr("Eo_scr", [2, 2, 2048, 2 * D], BF16)
    self.KH_scr = self.dscr("KH_scr", [2, 2, 17 * 128, 2 * D])


def _sin_any(self, out, x, rk, wk, tmpf, tmpi, tmpk, biasp):
    V, A = self.V, self.A
    V(lambda e: e.tensor_scalar(out=out, in0=x, scalar1=biasp, scalar2=1.0 / TWO_PI, op0=ALU.add, op1=ALU.mult), rk, wk)
    V(lambda e: e.tensor_copy(out=tmpi, in_=out), wk, [tmpk + "i"])
    V(lambda e: e.tensor_copy(out=tmpf, in_=tmpi), [tmpk + "i"], [tmpk])
    V(lambda e: e.tensor_tensor(out=out, in0=out, in1=tmpf, op=ALU.subtract), wk + [tmpk], wk)
    V(lambda e: e.tensor_scalar(out=tmpf, in0=out, scalar1=0.5, scalar2=None, op0=ALU.is_gt), wk, [tmpk])
    V(lambda e: e.tensor_tensor(out=out, in0=out, in1=tmpf, op=ALU.subtract), wk + [tmpk], wk)
    V(lambda e: e.tensor_scalar(out=tmpf, in0=out, scalar1=-0.5, scalar2=None, op0=ALU.is_lt), wk, [tmpk])
    V(lambda e: e.tensor_tensor(out=out, in0=out, in1=tmpf, op=ALU.add), wk + [tmpk], wk)
    A(lambda e: e.activation(out=out, in_=out, func=AF.Sin, scale=6.283185), wk, wk)


def _hy_filters(self, grp):
    V, A, T, G = self.V, self.A, self.T, self.G
    L = LS if grp == 1 else LP
    FT, LT = HY_FT[L], L // 128
    hc = self.hyc[L]
    self.arena_reset()
    sb = self.asb
    w1 = sb("hw1", [33, 64]); w2 = sb("hw2", [64, 64]); b1 = sb("hb1", [64, 1]); b2 = sb("hb2", [64, 1])
    feat = sb("hfeat", [33, L]); z1 = sb("hz1", [64, L]); z2 = sb("hz2", [64, L])
    tf = sb("htf", [64, 512]); ti = sb("hti", [64, 512], I32)
    w3b = [sb(f"hw3{i}", [64, 512]) for i in range(2)]
    absd = sb("habsd", [128, 2 * D]); tneg = sb("htneg", [128, LT]); ones = sb("hones", [128, 128], BF16)
    wins = [sb(f"hwin{i}", [128, 512]) for i in range(2)]
    fds = [[sb(f"hfd{j}{i}", [128, 512]) for i in range(2)] for j in range(2)]
    fabs = [sb(f"hfab{i}", [128, 512], BF16) for i in range(2)]
    ebs = [[sb(f"heb{j}{i}", [128, 512], BF16) for i in range(2)] for j in range(2)]
    rn = sb("hrn", [128, 2, 2 * D])
    Eb = sb("hE", [128, LT, 512], BF16); Ob = sb("hO", [128, LT, 512], BF16)
    Cs = [sb(f"hCs{i}", [128, LT, 128], BF16) for i in range(2)]
    Ss = [sb(f"hSs{i}", [128, LT, 128], BF16) for i in range(2)]
    ko = [fds[0][0], fds[0][1]]
    self.ld(w1, self.hy_f_w1[0], w=["hw1"]); self.ld(w2, self.hy_f_w2[0], w=["hw2"])
    self.ld(b1, self.hy_f_b1[0].rearrange("(p o) -> p o", o=1), w=["hb1"])
    self.ld(b2, self.hy_f_b2[0].rearrange("(p o) -> p o", o=1), w=["hb2"])
    self.ld(feat, hc["feat"], w=["hfeat"])
    self.ld(absd, self.c_absd.partition_broadcast(128), w=["habsd"])
    self.ld(tneg, hc["tneg"], w=["htneg"])
    self.ldc(ones, self.c_ones, w=["hones"])
    V(lambda e: e.tensor_scalar(out=b1, in0=b1, scalar1=16.0 * math.pi, scalar2=None, op0=ALU.add), ["hb1"], ["hb1"])
    V(lambda e: e.tensor_scalar(out=b2, in0=b2, scalar1=16.0 * math.pi, scalar2=None, op0=ALU.add), ["hb2"], ["hb2"])
    BW = min(512, L)
    for (src, srck, wt, wtk, bb, bbk, dst, dstk, kdim) in ((feat, "hfeat", w1, "hw1", b1, "hb1", z1, "hz1", 33),
                                                          (z1, "hz1", w2, "hw2", b2, "hb2", z2, "hz2", 64)):
        for tb in range(L // BW):
            pp = self.ps[0][0:64, 0:BW]
            T(lambda e: e.matmul(pp, lhsT=wt[0:kdim, :], rhs=src[0:kdim, tb * BW:(tb + 1) * BW], start=True, stop=True),
              [srck, wtk], ["ps0"])
            self.sin_any(dst[:, tb * BW:(tb + 1) * BW], pp, ["ps0", bbk], [dstk], tf[:, 0:BW], ti[:, 0:BW], "htf", bb[:, 0:1])
    for o in range(2):
        for cb in range(4):
            cs = slice(cb * 512, (cb + 1) * 512)
            for dr in range(2):
                col0 = dr * 4096 + o * 2048 + cb * 512
                self.ld(w3b[dr], self.hy_f_w3[0][:, col0:col0 + 512], w=[f"hw3{dr}"])
            pacc = self.ps[3]
            for lt in range(LT):
                pb_ = lt % 2
                win, fd, eb = wins[pb_], fds[pb_], ebs[pb_]
                wink = f"hwin{pb_}"
                A(lambda e: e.activation(out=win, in_=absd[:, cs], func=AF.Exp, scale=tneg[:, lt:lt + 1]), ["habsd", "htneg"], [wink])
                for dr in range(2):
                    fab, fabk = fabs[dr], f"hfab{dr}"
                    pf = self.ps[(1 + dr) if pb_ == 0 else (5 + dr)]
                    pfk = f"ps{(1 + dr) if pb_ == 0 else (5 + dr)}"
                    T(lambda e: e.matmul(pf, lhsT=z2[:, lt * 128:(lt + 1) * 128], rhs=w3b[dr], start=True, stop=True),
                      ["hz2", f"hw3{dr}"], [pfk])
                    V(lambda e: e.tensor_tensor(out=fd[dr], in0=pf, in1=win, op=ALU.mult), [pfk, wink], [f"hfd{pb_}{dr}"])
                    A(lambda e: e.activation(out=fab, in_=fd[dr], func=AF.Abs), [f"hfd{pb_}{dr}"], [fabk])
                    T(lambda e: e.matmul(pacc, lhsT=ones, rhs=fab, start=(lt == 0 and dr == 0), stop=(lt == LT - 1 and dr == 1)),
                      ["hones", fabk], ["ps3"])
                if lt == 0:
                    V(lambda e: e.memset(fd[1][0:1, :], 0.0), [], [f"hfd{pb_}1"])
                V(lambda e: e.tensor_tensor(out=eb[0], in0=fd[0], in1=fd[1], op=ALU.add), [f"hfd{pb_}0", f"hfd{pb_}1"], [f"heb{pb_}0"])
                V(lambda e: e.tensor_tensor(out=eb[1], in0=fd[1], in1=fd[0], op=ALU.subtract), [f"hfd{pb_}0", f"hfd{pb_}1"], [f"heb{pb_}1"])
                for eo in range(2):
                    self.ld(self.Eo_scr[o, eo, lt * 128:(lt + 1) * 128, cs], eb[eo], r=[f"heb{pb_}{eo}"], w=[("Eo", grp)])
            V(lambda e: e.tensor_scalar(out=rn[:, o, cs], in0=pacc, scalar1=EPS, scalar2=None, op0=ALU.add), ["ps3"], ["hrn"])
            V(lambda e: e.reciprocal(out=rn[:, o, cs], in_=rn[:, o, cs]), ["hrn"], ["hrn"])
    for o in range(2):
        for cb in range(4):
            cs = slice(cb * 512, (cb + 1) * 512)
            self.ld(Eb, self.Eo_scr[o, 0, 0:L, cs].rearrange("(lt p) c -> p lt c", p=128), r=[("Eo", grp)], w=["hE"])
            self.ld(Ob, self.Eo_scr[o, 1, 0:L, cs].rearrange("(lt p) c -> p lt c", p=128), r=[("Eo", grp)], w=["hO"])
            for ft in range(FT):
                b = ft % 2
                fs = slice(ft * 128, (ft + 1) * 128)
                self.ld(Cs[b], hc["C"][ft], w=[f"hCs{b}"])
                self.ld(Ss[b], hc["S"][ft], w=[f"hSs{b}"])
                for ri, (tab, tabk, dat, datk) in enumerate(((Cs[b], f"hCs{b}", Eb, "hE"), (Ss[b], f"hSs{b}", Ob, "hO"))):
                    pk_ = self.ps[4 + ri]
                    for lt in range(LT):
                        T(lambda e: e.matmul(pk_, lhsT=tab[:, lt, :], rhs=dat[:, lt, :], start=(lt == 0), stop=(lt == LT - 1)),
                          [tabk, datk], [f"ps{4 + ri}"])
                    V(lambda e: e.tensor_tensor(out=ko[ri], in0=pk_, in1=rn[:, o, cs], op=ALU.mult), [f"ps{4 + ri}", "hrn"], [f"hfd0{ri}"])
                    self.ld(self.KH_scr[o, ri, fs, cs], ko[ri], r=[f"hfd0{ri}"], w=[("KH", grp)])


def _hy_mixer(self, grp):
    V, A, T, G = self.V, self.A, self.T, self.G
    t0, n = self.trange(grp)
    nseq = 1 if grp == 1 else NPS
    L = n // nseq
    FT, LT = HY_FT[L], L // 128
    hc = self.hyc[L]
    self.hy_filters(grp)
    self.arena_reset()
    sb = self.asb
    wch = [sb(f"ywc{i}", [128, 8, 128], BF16) for i in range(2)]
    PBs = [sb(f"yPB{i}", [128, nseq, L + 2]) for i in range(2)]
    cvs = [sb(f"ycv{i}", [128, nseq, L]) for i in range(2)]
    cvbs = [sb(f"ycvb{i}", [128, n], BF16) for i in range(2)]
    cwT = sb("ycwT", [128, 3, 48]); cbT = sb("ycbT", [128, 48]); skT = sb("yskT", [128, 2, 16])
    ztt = sb("yztt", [128, n // 128, 128], BF16)
    for j in range(3):
        self.ld(cwT[:, j, :], self.hy_conv_w[0, j].rearrange("(c p) -> p c", p=128), w=["ycwT"], allow_slow_non_contiguous=True)
    self.ld(cbT, self.hy_conv_b[0].rearrange("(c p) -> p c", p=128), w=["ycbT"], allow_slow_non_contiguous=True)
    for o in range(2):
        self.ld(skT[:, o, :], self.hy_skip[0, o].rearrange("(c p) -> p c", p=128), w=["yskT"], allow_slow_non_contiguous=True)
    for i in range(2):
        V(lambda e: e.memset(PBs[i], 0.0), [], [f"yPB{i}"])
    def ld_wch(c_):
        self.ldc(wch[c_ % 2], self.hy_w_in[0][:, c_ * 128:(c_ + 1) * 128].rearrange("(k p) n -> p k n", p=128), w=[f"ywc{c_ % 2}"])
    ld_wch(0)
    for c in range(64):
        PB, cv, cvb = PBs[c % 2], cvs[c % 2], cvbs[c % 2]
        PBk, cvk, cvbk = f"yPB{c % 2}", f"ycv{c % 2}", f"ycvb{c % 2}"
        wc, wck = wch[c % 2], f"ywc{c % 2}"
        if c + 1 < 64:
            ld_wch(c + 1)
        for tb in range(n // 512):
            pp = self.ps[tb % 2]
            pk = f"ps{tb % 2}"
            for kk in range(8):
                T(lambda e: e.matmul(pp, lhsT=wc[:, kk, :], rhs=self.hT[:, kk, tb * 512:(tb + 1) * 512], start=(kk == 0), stop=(kk == 7)),
                  [wck, "hT"], [pk])
            if c < 48:
                if grp == 1:
                    dstp = PB[:, 0, 1 + tb * 512:1 + (tb + 1) * 512]
                    srcp = pp
                else:
                    dstp = PB[:, 2 * tb:2 * tb + 2, 1:L + 1]
                    srcp = pp.rearrange("p (s t) -> p s t", s=2)
                A(lambda e: e.activation(out=dstp, in_=srcp, func=AF.Copy), [pk], [PBk])
            else:
                A(lambda e: e.activation(out=cvb[:, tb * 512:(tb + 1) * 512], in_=pp, func=AF.Silu), [pk], [cvbk])
        if c >= 48:
            self.ld(self.sgT_scr[c - 48, :, t0:t0 + n], cvb, r=[cvbk], w=[("sgT", grp)])
            continue
        A(lambda e: e.activation(out=cv, in_=PB[:, :, 1:L + 1], func=AF.Identity, scale=cwT[:, 1, c:c + 1], bias=cbT[:, c:c + 1]),
          [PBk, "ycwT", "ycbT"], [cvk])
        V(lambda e: e.scalar_tensor_tensor(out=cv, in0=PB[:, :, 0:L], scalar=cwT[:, 0, c:c + 1], in1=cv, op0=ALU.mult, op1=ALU.add),
          [PBk, "ycwT", cvk], [cvk])
        V(lambda e: e.scalar_tensor_tensor(out=cv, in0=PB[:, :, 2:L + 2], scalar=cwT[:, 2, c:c + 1], in1=cv, op0=ALU.mult, op1=ALU.add),
          [PBk, "ycwT", cvk], [cvk])
        cvf = cv.rearrange("p s t -> p (s t)")
        dst = (self.vT_scr, self.x1T_scr, self.x2T_scr)[c // 16]
        self.ld(dst[c % 16, :, t0:t0 + n], cvf, r=[cvk], w=[(("vT", "x1T", "x2T")[c // 16], grp)])
        if c < 16:
            V(lambda e: e.tensor_copy(out=cvb, in_=cvf), [cvk], [cvbk])
            for t4 in range(n // 512):
                for kk in range(4):
                    tt = t4 * 4 + kk
                    ptr = self.ps[2].bitcast(BF16)[:, kk * 128:(kk + 1) * 128]
                    T(lambda e: e.transpose(ptr, cvb[:, tt * 128:(tt + 1) * 128], self.identb), [cvbk, "identb"], ["ps2"])
                A(lambda e: e.activation(out=ztt[:, t4 * 4:(t4 + 1) * 4, :].rearrange("p a b -> p (a b)"),
                                         in_=self.ps[2].bitcast(BF16)[:, 0:512], func=AF.Copy), ["ps2"], ["yztt"])
            self.ld(self.ztok_scr[0, t0:t0 + n, c * 128:(c + 1) * 128].rearrange("(tt p) c -> p tt c", p=128), ztt,
                    r=["yztt"], w=[("ztok0", grp)])
    self.arena_reset()
    sb = self.asb
    TBW = min(512, L)
    NTB = L // TBW
    zt = sb("yzt", [128, LT, 512], BF16)
    Cs = [sb(f"yCs{i}", [128, LT, 128], BF16) for i in range(2)]
    Ss = [sb(f"ySs{i}", [128, LT, 128], BF16) for i in range(2)]
    Yh = sb("yYh", [128, FT, 2, 512], BF16)
    ICs = sb("yIC", [128, FT, TBW], BF16); ISs = sb("yIS", [128, FT, TBW], BF16)
    kre = [sb(f"ykre{i}", [128, 512]) for i in range(2)]; kim = [sb(f"ykim{i}", [128, 512]) for i in range(2)]
    u1 = sb("yu1", [128, 512]); u2 = sb("yu2", [128, 512])
    NB2 = 2 if L == LP else 1
    tas = [sb(f"yta{i}", [128, TBW]) for i in range(NB2)]; txs = [sb(f"ytx{i}", [128, TBW]) for i in range(NB2)]
    tgs = [sb(f"ytg{i}", [128, TBW], BF16) for i in range(NB2)]; tos = [sb(f"yto{i}", [128, TBW]) for i in range(NB2)]
    tobs = [sb(f"ytob{i}", [128, TBW], BF16) for i in range(NB2)]
    ztt2s = [sb(f"yztt2{i}", [128, TBW // 128, 128], BF16) for i in range(NB2)]
    skT = sb("yskT2", [128, 2, 16])
    for o in range(2):
        self.ld(skT[:, o, :], self.hy_skip[0, o].rearrange("(c p) -> p c", p=128), w=["yskT2"], allow_slow_non_contiguous=True)
    small = (L == LP)
    if small:
        Call = sb("yCall", [128, FT, LT, 128], BF16); Sall = sb("ySall", [128, FT, LT, 128], BF16)
        kra = sb("ykra", [128, FT, 512]); kia = sb("ykia", [128, FT, 512])
        for ft in range(FT):
            self.ld(Call[:, ft], hc["C"][ft], w=["yCall"])
            self.ld(Sall[:, ft], hc["S"][ft], w=["ySall"])
        self.ld(ICs, hc["IC"][0], w=["yIC"])
        self.ld(ISs, hc["IS"][0], w=["yIS"])
    for o in range(2):
        zprev = (self.vT_scr, ("vT", grp)) if o == 0 else (self.z1T_scr, ("z1T", grp))
        xg = (self.x1T_scr, ("x1T", grp)) if o == 0 else (self.x2T_scr, ("x2T", grp))
        for cb in range(4):
          cs = slice(cb * 512, (cb + 1) * 512)
          if small:
              self.ld(kra, self.KH_scr[o, 0, 0:FT * 128, cs].rearrange("(ft p) c -> p ft c", p=128), r=[("KH", grp)], w=["ykra"])
              self.ld(kia, self.KH_scr[o, 1, 0:FT * 128, cs].rearrange("(ft p) c -> p ft c", p=128), r=[("KH", grp)], w=["ykia"])
          for s in range(nseq):
                tq = t0 + s * L
                self.ld(zt, self.ztok_scr[o, tq:tq + L, cs].rearrange("(lt p) c -> p lt c", p=128), r=[(f"ztok{o}", grp)], w=["yzt"])
                for ft in range(FT):
                    b = ft % 2
                    fs = slice(ft * 128, (ft + 1) * 128)
                    if small:
                        Csb, Ssb, krb, kib = Call[:, ft], Sall[:, ft], kra[:, ft], kia[:, ft]
                        Ck, Sk, krk, kik = "yCall", "ySall", "ykra", "ykia"
                    else:
                        Csb, Ssb, krb, kib = Cs[b], Ss[b], kre[b], kim[b]
                        Ck, Sk, krk, kik = f"yCs{b}", f"ySs{b}", f"ykre{b}", f"ykim{b}"
                        self.ld(Cs[b], hc["C"][ft], w=[Ck])
                        self.ld(Ss[b], hc["S"][ft], w=[Sk])
                        self.ld(kre[b], self.KH_scr[o, 0, fs, cs], r=[("KH", grp)], w=[krk])
                        self.ld(kim[b], self.KH_scr[o, 1, fs, cs], r=[("KH", grp)], w=[kik])
                    pA, pB = self.ps[0 + 2 * b], self.ps[1 + 2 * b]
                    pAk, pBk = f"ps{0 + 2 * b}", f"ps{1 + 2 * b}"
                    for lt in range(LT):
                        T(lambda e: e.matmul(pA, lhsT=Csb[:, lt, :], rhs=zt[:, lt, :], start=(lt == 0), stop=(lt == LT - 1)),
                          [Ck, "yzt"], [pAk])
                    for lt in range(LT):
                        T(lambda e: e.matmul(pB, lhsT=Ssb[:, lt, :], rhs=zt[:, lt, :], start=(lt == 0), stop=(lt == LT - 1)),
                          [Sk, "yzt"], [pBk])
                    V(lambda e: e.tensor_tensor(out=u1, in0=pA, in1=krb, op=ALU.mult), [pAk, krk], ["yu1"])
                    V(lambda e: e.tensor_tensor(out=u2, in0=pB, in1=kib, op=ALU.mult), [pBk, kik], ["yu2"])
                    G(lambda e: e.tensor_tensor(out=Yh[:, ft, 0, :], in0=u1, in1=u2, op=ALU.add), ["yu1", "yu2"], ["yYh"])
                    V(lambda e: e.tensor_tensor(out=u1, in0=pA, in1=kib, op=ALU.mult), [pAk, kik], ["yu1"])
                    V(lambda e: e.tensor_tensor(out=u2, in0=pB, in1=krb, op=ALU.mult), [pBk, krk], ["yu2"])
                    V(lambda e: e.tensor_tensor(out=Yh[:, ft, 1, :], in0=u1, in1=u2, op=ALU.subtract), ["yu1", "yu2"], ["yYh"])
                for tb in range(NTB):
                    tsl = slice(tb * TBW, (tb + 1) * TBW)
                    gsl = slice(tq + tb * TBW, tq + (tb + 1) * TBW)
                    if not small:
                        self.ld(ICs, hc["IC"][tb], w=["yIC"])
                        self.ld(ISs, hc["IS"][tb], w=["yIS"])
                    for cc in range(4):
                        ch = cb * 4 + cc
                        bi_ = cc % NB2
                        ta, tx, tg, to, tob, ztt2 = tas[bi_], txs[bi_], tgs[bi_], tos[bi_], tobs[bi_], ztt2s[bi_]
                        tak, txk, tgk, tok, tobk, zt2k = f"yta{bi_}", f"ytx{bi_}", f"ytg{bi_}", f"yto{bi_}", f"ytob{bi_}", f"yztt2{bi_}"
                        pz = self.ps[4 + (cc % 2)][:, 0:TBW]
                        pzk = f"ps{4 + (cc % 2)}"
                        self.ld(ta, zprev[0][ch, :, gsl], r=[zprev[1]], w=[tak])
                        self.ld(tx, xg[0][ch, :, gsl], r=[xg[1]], w=[txk])
                        if o == 1:
                            self.ld(tg, self.sgT_scr[ch, :, gsl], r=[("sgT", grp)], w=[tgk])
                        for ft in range(FT):
                            T(lambda e: e.matmul(pz, lhsT=Yh[:, ft, 0, cc * 128:(cc + 1) * 128], rhs=ICs[:, ft, :], start=(ft == 0), stop=False),
                              ["yYh", "yIC"], [pzk])
                            T(lambda e: e.matmul(pz, lhsT=Yh[:, ft, 1, cc * 128:(cc + 1) * 128], rhs=ISs[:, ft, :], start=False, stop=(ft == FT - 1)),
                              ["yYh", "yIS"], [pzk])
                        V(lambda e: e.scalar_tensor_tensor(out=ta, in0=ta, scalar=skT[:, o, ch:ch + 1], in1=pz, op0=ALU.mult, op1=ALU.add),
                          [tak, "yskT2", pzk], [tak])
                        if o == 0:
                            G(lambda e: e.tensor_tensor(out=to, in0=ta, in1=tx, op=ALU.mult), [tak, txk], [tok])
                            self.ld(self.z1T_scr[ch, :, gsl], to, r=[tok], w=[("z1T", grp)])
                            A(lambda e: e.activation(out=tob, in_=to, func=AF.Copy), [tok], [tobk])
                            for kk in range(TBW // 128):
                                ptr = self.ps[6 + bi_].bitcast(BF16)[:, kk * 128:(kk + 1) * 128]
                                T(lambda e: e.transpose(ptr, tob[:, kk * 128:(kk + 1) * 128], self.identb), [tobk, "identb"], ["ps6" if bi_ == 0 else "ps7"])
                            A(lambda e: e.activation(out=ztt2.rearrange("p a b -> p (a b)"), in_=self.ps[6 + bi_].bitcast(BF16)[:, 0:TBW], func=AF.Copy),
                              ["ps6" if bi_ == 0 else "ps7"], [zt2k])
                            self.ld(self.ztok_scr[1, gsl, ch * 128:(ch + 1) * 128].rearrange("(tt p) c -> p tt c", p=128), ztt2,
                                    r=[zt2k], w=[("ztok1", grp)])
                        else:
                            G(lambda e: e.tensor_tensor(out=to, in0=ta, in1=tx, op=ALU.mult), [tak, txk], [tok])
                            G(lambda e: e.tensor_tensor(out=tob, in0=to, in1=tg, op=ALU.mult), [tok, tgk], [tobk])
                            self.ld(self.gscr[ch, :, gsl], tob, r=[tobk], w=[("gscr", grp)])
    self.ysrc = (self.gscr, ("gscr", grp))
    return 2 * D


K.hy_decl = _hy_decl
K.sin_any = _sin_any
K.hy_filters = _hy_filters
K.hy_mixer = _hy_mixer


def hy_consts():
    out = {}
    HY_BANDS = 16
    min_decay = math.log(1e-2) / 1.5
    max_decay = math.log(1e-2) / 0.3
    out["c_absd"] = np.abs(np.linspace(min_decay, max_decay, 2048, dtype=np.float32)).astype(np.float32)
    out["c_ones"] = np.ones((128, 128), np.float32)
    for L in (2048, 256):
        FT = HY_FT[L]
        N = 2 * L
        t = np.linspace(0.0, 1.0, L, dtype=np.float32)[:, None]
        w = (2.0 * np.float32(math.pi) * np.arange(L, dtype=np.float32)[:, None] / np.float32(L)).astype(np.float32)
        f = np.linspace(1e-4, HY_BANDS - 1.0, HY_BANDS, dtype=np.float32)[None, :]
        fw_ = (f * w).astype(np.float32)
        feat = np.concatenate([t, np.cos(fw_), -np.sin(fw_)], -1).astype(np.float32)
        out[f"c_feat{L}"] = np.ascontiguousarray(feat.T)
        out[f"c_tneg{L}"] = np.ascontiguousarray((-t[:, 0]).reshape(L // 128, 128).T)
        tt = np.arange(L, dtype=np.int64)[:, None]
        ff = np.arange(FT * 128, dtype=np.int64)[None, :]
        ang = 2.0 * np.pi * ((tt * ff) % N).astype(np.float64) / N
        valid = (ff <= L).astype(np.float64)
        C = np.cos(ang) * valid
        S = np.sin(ang) * valid
        wf = np.where((ff == 0) | (ff == L), 1.0, 2.0) * valid / N
        LT = L // 128
        TBW = min(512, L)
        def fwd_tile(M):
            return np.ascontiguousarray(M.reshape(LT, 128, FT, 128).transpose(2, 1, 0, 3)).astype(np.float32).astype(ml_dtypes.bfloat16)
        def inv_tile(M):
            return np.ascontiguousarray(M.reshape(FT, 128, L // TBW, TBW).transpose(2, 1, 0, 3)).astype(np.float32).astype(ml_dtypes.bfloat16)
        out[f"c_C{L}"] = fwd_tile(C)
        out[f"c_S{L}"] = fwd_tile(S)
        out[f"c_IC{L}"] = inv_tile((C * wf).T)
        out[f"c_IS{L}"] = inv_tile((-S * wf).T)
    return out
```

```python
import math
import numpy as np
import ml_dtypes
import concourse.bass as bass
import concourse.mybir as mybir
from concourse.bass_utils import run_bass_kernel_spmd

F32 = mybir.dt.float32
BF16 = mybir.dt.bfloat16
I32 = mybir.dt.int32
ALU = mybir.AluOpType
AF = mybir.ActivationFunctionType
AX = mybir.AxisListType

D = 1024
LS = 2048
LP = 256
NPS = 4
NT = LS + NPS * LP
EPS = 1e-6
TWO_PI = 2.0 * math.pi
ARENA_W = 31500

SAME_ENG_SYNC = True


class _PEProxy:
    def __init__(self, eng):
        self.eng = eng
        self.stop = True

    def matmul(self, *a, **kw):
        self.stop = bool(kw.get("stop", True))
        return self.eng.matmul(*a, **kw)

    def transpose(self, *a, **kw):
        self.stop = True
        return self.eng.transpose(*a, **kw)


class Fw:
    def __init__(self, nc, n_dma_sems=20):
        self.nc = nc
        self.engs = {}
        for name in ("tensor", "vector", "scalar", "gpsimd", "sync"):
            e = getattr(nc, name)
            self.engs[name] = dict(eng=e, sem=nc.alloc_semaphore("s_" + name), count=0, seen={})
        self.dma_pool = {}
        for q in ("sync", "gpsimd", "scalar"):
            self.dma_pool[q] = dict(
                sems=[nc.alloc_semaphore(f"d_{q}_{i}") for i in range(n_dma_sems)],
                vals=[0] * n_dma_sems, nxt=0)
        self.bufs = {}
        self.sem_owner = {id(E["sem"]): name for name, E in self.engs.items()}

    def _st(self, key):
        s = self.bufs.get(key)
        if s is None:
            s = dict(w=None, r=[])
            self.bufs[key] = s
        return s

    def _deps(self, reads, writes):
        deps = []
        for k in reads:
            s = self._st(k)
            if s["w"] is not None:
                deps.append(s["w"])
        for k in writes:
            s = self._st(k)
            if s["w"] is not None:
                deps.append(s["w"])
            deps.extend(s["r"])
        return deps

    def _wait(self, E, deps):
        best = {}
        for (sem, val) in deps:
            if sem is E["sem"] and not SAME_ENG_SYNC:
                continue
            k = id(sem)
            if k not in best or best[k][1] < val:
                best[k] = (sem, val)
        for k, (sem, val) in best.items():
            if E["seen"].get(k, 0) < val:
                E["eng"].wait_ge(sem, val)
                E["seen"][k] = val

    def _mark(self, tok, reads, writes):
        for k in reads:
            r = self._st(k)["r"]
            r.append(tok)
            if len(r) > 12:
                best = {}
                for (sem, val) in r:
                    if id(sem) not in best or best[id(sem)][1] < val:
                        best[id(sem)] = (sem, val)
                r[:] = list(best.values())
        for k in writes:
            s = self._st(k)
            s["w"] = tok
            s["r"] = []

    def op(self, eng, fn, reads=(), writes=()):
        E = self.engs[eng]
        self._wait(E, self._deps(reads, writes))
        if eng == "tensor":
            px = _PEProxy(E["eng"])
            ins = fn(px)
            if not px.stop:
                E.setdefault("pend", []).append((tuple(reads), tuple(writes)))
                return ins
            pend = E.get("pend", [])
            E["pend"] = []
            E["count"] += 1
            ins.then_inc(E["sem"], 1)
            tok = (E["sem"], E["count"])
            for (r, w) in pend:
                self._mark(tok, r, w)
            self._mark(tok, reads, writes)
            return ins
        ins = fn(E["eng"])
        E["count"] += 1
        ins.then_inc(E["sem"], 1)
        self._mark((E["sem"], E["count"]), reads, writes)
        return ins

    def dma(self, q, out, in_, reads=(), writes=(), **kw):
        E = self.engs[q]
        P = self.dma_pool[q]
        i = P["nxt"]
        P["nxt"] = (i + 1) % len(P["sems"])
        sem = P["sems"][i]
        deps = self._deps(reads, writes)
        if P["vals"][i] > 0:
            deps.append((sem, P["vals"][i]))
        self._wait(E, deps)
        ins = E["eng"].dma_start(out=out, in_=in_, **kw)
        P["vals"][i] += 16
        ins.then_inc(sem, 16)
        tok = (sem, P["vals"][i])
        self._mark(tok, reads, writes)
        return tok

    def barrier(self):
        toks = [(E["sem"], E["count"]) for E in self.engs.values() if E["count"] > 0]
        for P in self.dma_pool.values():
            for sem, val in zip(P["sems"], P["vals"]):
                if val > 0:
                    toks.append((sem, val))
        for E in self.engs.values():
            self._wait(E, toks)

    def finish(self, out_keys):
        E = self.engs["sync"]
        deps = []
        for k in out_keys:
            s = self._st(k)
            if s["w"] is not None:
                deps.append(s["w"])
        self._wait(E, deps)


def AP(t, off, dims):
    return bass.AP(t.tensor if hasattr(t, "tensor") else t, off, [list(d) for d in dims])


class K:
    def __init__(self, layers=(0, 1, 2, 3), final=True, dbg=False):
        self.dbg = dbg
        self.layers = layers
        self.final = final
        nc = self.nc = bass.Bass("TRN2", target_bir_lowering=False)
        self.fw = Fw(nc)
        self.inputs = {}
        self.build()

    def din(self, name, shape, dt=F32):
        t = self.nc.dram_tensor(name, list(shape), dt, kind="ExternalInput").ap()
        self.inputs[name] = (tuple(shape), dt)
        return t

    def dout(self, name, shape, dt=F32):
        return self.nc.dram_tensor(name, list(shape), dt, kind="ExternalOutput").ap()

    def dscr(self, name, shape, dt=F32):
        return self.nc.dram_tensor(name, list(shape), dt, kind="Internal").ap()

    def sb(self, name, shape, dt=F32):
        return self.nc.alloc_sbuf_tensor(name, list(shape), dt).ap()

    def arena_reset(self):
        self.fw.barrier()
        self.aoff = 0

    def asb(self, name, shape, dt=F32):
        n = 1
        for x in shape[1:]:
            n *= x
        words = n if dt in (F32, I32) else (n + 1) // 2
        words = (words + 7) // 8 * 8
        assert self.aoff + words <= ARENA_W, (name, self.aoff, words)
        v = self.arena[0:shape[0], self.aoff:self.aoff + words]
        self.aoff += words
        if dt not in (F32,):
            v = v.bitcast(dt)
        v = v[:, 0:n]
        if len(shape) > 2:
            names = " ".join(f"a{i}" for i in range(len(shape) - 1))
            kw = {f"a{i}": shape[i + 1] for i in range(len(shape) - 2)}
            v = v.rearrange(f"p ({names}) -> p {names}", **kw)
        return v

    def V(self, fn, r=(), w=()):
        return self.fw.op("vector", fn, r, w)

    def G(self, fn, r=(), w=()):
        return self.fw.op("gpsimd", fn, r, w)

    def A(self, fn, r=(), w=()):
        return self.fw.op("scalar", fn, r, w)

    def T(self, fn, r=(), w=()):
        return self.fw.op("tensor", fn, r, w)

    def ld(self, out, in_, r=(), w=(), q="sync", **kw):
        if q == "sync" and r and "DRam" in type(out.tensor).__name__ and "DRam" not in type(in_.tensor).__name__:
            engs = set()
            for k in r:
                st = self.fw.bufs.get(k)
                if st is None or st["w"] is None:
                    continue
                engs.add(self.fw.sem_owner.get(id(st["w"][0]), "dma"))
            if len(engs) == 1:
                e = engs.pop()
                if e in ("scalar", "gpsimd"):
                    q = e
                elif e == "vector":
                    q = "scalar"
        return self.fw.dma(q, out, in_, r, w, **kw)

    def ldc(self, out, in_, r=(), w=()):
        return self.fw.dma("gpsimd", out, in_, r, w)

    def build(self):
        nc = self.nc
        self.xs = self.din("xs", [LS, D])
        self.xp = self.din("xp", [NPS * LP, D])
        self.cvec = self.din("cvec", [2, D])
        self.st5 = self.din("st5", [2, 128, 128])
        self.stret = self.din("stret", [2, 8, 128, 256])
        self.norm_g = self.din("norm_g", [4, D])
        self.mod_w = self.din("mod_w", [4, D, 3 * D])
        self.mod_b = self.din("mod_b", [4, 3 * D])
        self.s5_w_in = self.din("s5_w_in", [2, D, 2 * D])
        self.s5_lam_re = self.din("s5_lam_re", [2, 2, 64, 64])
        self.s5_lam_im = self.din("s5_lam_im", [2, 2, 64, 64])
        self.s5_log_step = self.din("s5_log_step", [2, 2, 64])
        self.s5_b_re = self.din("s5_b_re", [2, 2, 64, 64, 16])
        self.s5_b_im = self.din("s5_b_im", [2, 2, 64, 64, 16])
        self.s5_c_re = self.din("s5_c_re", [2, 2, 64, 16, 64])
        self.s5_c_im = self.din("s5_c_im", [2, 2, 64, 16, 64])
        self.s5_d = self.din("s5_d", [2, D])
        self.s5_w_glu = self.din("s5_w_glu", [2, D, D])
        self.s5_b_glu = self.din("s5_b_glu", [2, D])
        self.s5_w_out = self.din("s5_w_out", [2, D, D])
        self.final_g = self.din("final_g", [D])
        self.ret_decl()
        self.hy_decl()
        self.c_identb = self.din("c_identb", [128, 128], BF16)
        self.c_identf = self.din("c_identf", [128, 128])
        self.c_pmask = self.din("c_pmask", [128, 8])
        self.c_bdmask = self.din("c_bdmask", [128, 128])
        self.ys = self.dout("ys", [LS, D])
        self.yp = self.dout("yp", [NPS * LP, D])
        self.ns5 = self.dout("ns5", [NPS, 2, 128, 128])
        self.nret = self.dout("nret", [NPS, 2, 8, 128, 256])
        self.xres = (self.dout if self.dbg else self.dscr)("xres", [NT, D])
        self.gscr = self.dscr("gscr", [16, 128, NT], BF16)
        self.g2scr = self.dscr("g2scr", [8, 128, NT], BF16)
        self.Tsb_scr = self.dscr("Tsb_scr", [2, 8, 128, 2048], BF16)
        self.Kc_scr = self.dscr("Kc_scr", [2, 8, 128, 1920], BF16)
        self.CA_scr = self.dscr("CA_scr", [2, 8, 128, 2, 2304], BF16)
        self.identb = self.sb("identb", [128, 128], BF16)
        self.identf = self.sb("identf", [128, 128])
        self.pmask = self.sb("pmask", [128, 8])
        self.bdmask = self.sb("bdmask", [128, 128])
        self.hT = self.sb("hT", [128, 8, LS], BF16)
        self.arena = self.sb("arena", [128, ARENA_W])
        self.aoff = 0
        self.wgs = [self.sb(f"wgs{i}", [128, 8, 128], BF16) for i in range(2)]
        self.wst = [self.sb("wst0", [128, 8, 512], BF16)] * 2
        self.xt = [self.sb(f"xt{i}", [128, D]) for i in range(2)]
        self.xn = [self.sb(f"xn{i}", [128, D], BF16) for i in range(2)]
        self.sq = self.sb("sq", [128, D])
        self.stat = self.sb("stat", [128, 8])
        self.modT = self.sb("modT", [128, 4, 24, 2])
        self.gsc = self.sb("gsc", [128, 4, 8, 2])
        self.gt_bc = self.sb("gt_bc", [128, 2, D])
        self.cT = self.sb("cT", [128, 8, 2])
        self.cTb = self.sb("cTb", [128, 8, 2], BF16)
        self.cTrep = self.sb("cTrep", [128, 2, 8, 128], BF16)
        self.ngT = self.sb("ngT", [128, 4, 8])
        self.mbT = self.sb("mbT", [128, 4, 24])
        self.mbg = None
        self.fgb = self.sb("fgb", [128, D])
        self.ylt = [self.sb("ylt", [128, 16, 128], BF16)] * 2
        self.ps = [nc.alloc_psum_tensor(f"ps{i}", [128, 512], F32).ap() for i in range(8)]

        f = self.fw
        self.ld(self.identb, self.c_identb, w=["identb"])
        self.ld(self.identf, self.c_identf, w=["identf"])
        self.ld(self.pmask, self.c_pmask, w=["pmask"])
        self.ld(self.bdmask, self.c_bdmask, w=["bdmask"])
        self.ld(self.fgb, self.final_g.partition_broadcast(128), w=["fgb"])

        self.mod_stage()
        out_keys = []
        nl = len(self.layers)
        for li, i in enumerate(self.layers):
            last = (li == nl - 1) and self.final
            first = (li == 0)
            self.gate_table(i)
            for grp in (1, 0):
                kind = i % 3
                self.prologue(i, grp, first)
                if kind == 0:
                    kdim = self.s5_mixer(i // 3, grp)
                    w_out = self.s5_w_out[i // 3]
                elif kind == 1:
                    kdim = self.ret_mixer(grp)
                    w_out = self.ret_w_out[0]
                else:
                    kdim = self.hy_mixer(grp)
                    w_out = self.hy_w_out[0]
                self.epilogue(i, grp, kdim, w_out, last)
        out_keys = ["ys", "yp", "ns5", "nret", "xres"]
        f.finish(out_keys)

    def trange(self, grp):
        return (0, LS) if grp == 1 else (LS, NPS * LP)

    def mod_stage(self):
        for r in range(2):
            self.ld(self.cT[:, :, r], self.cvec[r].rearrange("(k p) -> p k", p=128), w=["cT"], allow_slow_non_contiguous=True)
        self.A(lambda e: e.activation(out=self.cTb, in_=self.cT, func=AF.Silu), ["cT"], ["cTb"])
        for r in range(2):
            self.V(lambda e: e.tensor_copy(out=self.cTrep[:, r], in_=self.cTb[:, :, r:r + 1].to_broadcast([128, 8, 128])),
                   ["cTb"], ["cTrep"])
        for l in range(4):
            self.ld(self.ngT[:, l, :], self.norm_g[l].rearrange("(k p) -> p k", p=128), w=["ngT"], allow_slow_non_contiguous=True)
            self.ld(self.mbT[:, l, :], self.mod_b[l].rearrange("(k p) -> p k", p=128), w=["mbT"], allow_slow_non_contiguous=True)
        for i in self.layers:
            for half in range(4):
                wt = self.wst[half % 2]
                wk = "wst0"
                self.ldc(wt, self.mod_w[i, :, half * 512:(half + 1) * 512].rearrange("(k p) n -> p k n", p=128), w=[wk])
                for cc in range(4):
                    ch = half * 4 + cc
                    pt = self.ps[0][:, 0:2]
                    for k in range(8):
                        self.T(lambda e: e.matmul(pt, lhsT=wt[:, k, cc * 128:(cc + 1) * 128], rhs=self.cTb[:, k, :],
                                                  start=(k == 0), stop=(k == 7)), [wk, "cTb"], ["ps0"])
                    self.V(lambda e: e.tensor_tensor(out=self.modT[:, i, ch, :], in0=pt,
                                                     in1=self.mbT[:, i, ch:ch + 1].to_broadcast([128, 2]), op=ALU.add),
                           ["ps0", "mbT"], ["modT"])
            self.V(lambda e: e.tensor_scalar(out=self.gsc[:, i], in0=self.modT[:, i, 8:16, :], scalar1=1.0, scalar2=None,
                                             op0=ALU.add), ["modT"], ["gsc"])
            self.V(lambda e: e.tensor_tensor(out=self.gsc[:, i], in0=self.gsc[:, i],
                                             in1=self.ngT[:, i, :].unsqueeze(2).to_broadcast([128, 8, 2]), op=ALU.mult),
                   ["gsc", "ngT"], ["gsc"])

    def gate_table(self, i):
        self.mbg = self.sq
        self.ld(self.mbg, self.mod_b[i, 2 * D:3 * D].partition_broadcast(128), w=["sq"])
        for half in range(2):
            wt = self.wst[half % 2]
            wk = "wst0"
            self.ldc(wt, self.mod_w[i, :, 2 * D + half * 512: 2 * D + (half + 1) * 512].rearrange("(k p) n -> p k n", p=128),
                     w=[wk])
            for r in range(2):
                pt = self.ps[1]
                for k in range(8):
                    self.T(lambda e: e.matmul(pt, lhsT=self.cTrep[:, r, k, :], rhs=wt[:, k, :],
                                              start=(k == 0), stop=(k == 7)), [wk, "cTrep"], ["ps1"])
                self.V(lambda e: e.tensor_tensor(out=self.gt_bc[:, r, half * 512:(half + 1) * 512], in0=pt,
                                                 in1=self.mbg[:, half * 512:(half + 1) * 512], op=ALU.add),
                       ["ps1", "sq"], ["gt_bc"])

    def prologue(self, i, grp, first):
        t0, n = self.trange(grp)
        for tt in range(n // 128):
            b = tt % 2
            xt, xn = self.xt[b], self.xn[b]
            if first:
                src = self.xs[tt * 128:(tt + 1) * 128, :] if grp == 1 else self.xp[tt * 128:(tt + 1) * 128, :]
                rk = []
            else:
                src = self.xres[t0 + tt * 128: t0 + (tt + 1) * 128, :]
                rk = ["xres"]
            self.ld(xt, src, r=rk, w=[f"xt{b}"])
            self.rms_scale(xt, f"xt{b}", xn, f"xn{b}")
            for k in range(8):
                pt = self.ps[2 + (k % 2)].bitcast(BF16)[:, 0:128]
                pk = f"ps{2 + (k % 2)}"
                self.T(lambda e: e.transpose(pt, xn[:, k * 128:(k + 1) * 128], self.identb), [f"xn{b}", "identb"], [pk])
                self.A(lambda e: e.activation(out=self.hT[:, k, tt * 128:(tt + 1) * 128], in_=pt, func=AF.Identity,
                                              scale=self.gsc[:, i, k, grp:grp + 1], bias=self.modT[:, i, k, grp:grp + 1]),
                       [pk, "gsc", "modT"], ["hT"])

    def rms_scale(self, xt, xk, out, ok, gtab=None):
        self.A(lambda e: e.activation(out=self.sq, in_=xt, func=AF.Square, accum_out=self.stat[:, 0:1]), [xk], ["sq", "stat"])
        self.V(lambda e: e.tensor_scalar(out=self.stat[:, 1:2], in0=self.stat[:, 0:1], scalar1=1.0 / D, scalar2=EPS,
                                         op0=ALU.mult, op1=ALU.add), ["stat"], ["stat"])
        self.A(lambda e: e.activation(out=self.stat[:, 3:4], in_=self.stat[:, 1:2], func=AF.Sqrt), ["stat"], ["stat"])
        self.V(lambda e: e.reciprocal(out=self.stat[:, 2:3], in_=self.stat[:, 3:4]), ["stat"], ["stat"])
        if gtab is None:
            self.V(lambda e: e.tensor_scalar(out=out, in0=xt, scalar1=self.stat[:, 2:3], scalar2=None, op0=ALU.mult),
                   [xk, "stat"], [ok])
        else:
            self.V(lambda e: e.scalar_tensor_tensor(out=out, in0=xt, scalar=self.stat[:, 2:3], in1=gtab,
                                                    op0=ALU.mult, op1=ALU.mult), [xk, "stat", "fgb"], [ok])

    def epilogue(self, i, grp, kdim, w_out, last):
        t0, n = self.trange(grp)
        kc = kdim // 128
        if kdim > D:
            self.arena_reset()
            self.wres = self.asb("wres", [128, 16, 1024], BF16)
            wrk = ["wres"]
        else:
            self.wres = self.SS.bitcast(BF16).rearrange("p (k n) -> p k n", k=8)
            wrk = ["SS", "SS2"]
        self.ldc(self.wres[:, 0:kc, :], w_out.rearrange("(k p) n -> p k n", p=128), w=wrk)
        yb = self.xn
        for tt in range(n // 128):
            b = tt % 2
            xt = self.xt[b]
            src = self.xres[t0 + tt * 128: t0 + (tt + 1) * 128, :]
            if i == self.layers[0]:
                src = self.xs[tt * 128:(tt + 1) * 128, :] if grp == 1 else self.xp[tt * 128:(tt + 1) * 128, :]
                rk = []
            else:
                rk = ["xres"]
            self.ld(xt, src, r=rk, w=[f"xt{b}"])
            yt = self.ylt[b]
            ysrc, ykey = self.ysrc
            self.ld(yt[:, 0:kc, :], ysrc[0:kc, :, t0 + tt * 128: t0 + (tt + 1) * 128].rearrange("k p t -> p k t"),
                    r=[ykey], w=["ylt"], q="sync")
            for h in range(2):
                pt = self.ps[4 + h]
                for k in range(kc):
                    self.T(lambda e: e.matmul(pt, lhsT=yt[:, k, :], rhs=self.wres[:, k, h * 512:(h + 1) * 512],
                                              start=(k == 0), stop=(k == kc - 1)), ["ylt"] + wrk, [f"ps{4 + h}"])
                self.V(lambda e: e.tensor_tensor(out=self.sq[:, h * 512:(h + 1) * 512], in0=pt,
                                                 in1=self.gt_bc[:, grp, h * 512:(h + 1) * 512], op=ALU.mult),
                       [f"ps{4 + h}", "gt_bc"], ["sq"])
            self.V(lambda e: e.tensor_tensor(out=xt, in0=xt, in1=self.sq, op=ALU.add), [f"xt{b}", "sq"], [f"xt{b}"])
            if not last:
                self.ld(self.xres[t0 + tt * 128: t0 + (tt + 1) * 128, :], xt, r=[f"xt{b}"], w=["xres"])
            else:
                ot = self.xt[1 - b]
                self.rms_scale(xt, f"xt{b}", ot, f"xt{1 - b}", gtab=self.fgb)
                dst = self.ys if grp == 1 else self.yp
                self.ld(dst[tt * 128:(tt + 1) * 128, :], ot, r=[f"xt{1 - b}"], w=["ys" if grp == 1 else "yp"])

    def E(self, eng, fn, r=(), w=()):
        return self.fw.op(eng, fn, r, w)

    def sin_of(self, out, x, shift, eng="vector"):
        y, yi, yf = self.tr_y, self.tr_yi, self.tr_yf
        self.E(eng, lambda e: e.tensor_scalar(out=y, in0=x, scalar1=1.0 / TWO_PI, scalar2=shift / TWO_PI + 8.0,
                                              op0=ALU.mult, op1=ALU.add), ["trx"], ["try"])
        self.E(eng, lambda e: e.tensor_copy(out=yi, in_=y), ["try"], ["tryi"])
        self.E(eng, lambda e: e.tensor_copy(out=yf, in_=yi), ["tryi"], ["tryf"])
        self.E(eng, lambda e: e.tensor_tensor(out=y, in0=y, in1=yf, op=ALU.subtract), ["try", "tryf"], ["try"])
        self.E(eng, lambda e: e.tensor_scalar(out=yf, in0=y, scalar1=0.5, scalar2=None, op0=ALU.is_gt), ["try"], ["tryf"])
        self.E(eng, lambda e: e.tensor_tensor(out=y, in0=y, in1=yf, op=ALU.subtract), ["try", "tryf"], ["try"])
        self.E(eng, lambda e: e.tensor_scalar(out=yf, in0=y, scalar1=-0.5, scalar2=None, op0=ALU.is_lt), ["try"], ["tryf"])
        self.E(eng, lambda e: e.tensor_tensor(out=y, in0=y, in1=yf, op=ALU.add), ["try", "tryf"], ["try"])
        self.A(lambda e: e.activation(out=out, in_=y, func=AF.Sin, scale=6.283185), ["try"], ["trx"])

    def s5_alloc(self):
        sb = self.asb
        self.wgl = [sb(f"wgl{i}", [128, 8, 128], BF16) for i in range(2)]
        self.LR = sb("LR", [128, 128]); self.LI = sb("LI", [128, 128]); self.DT = sb("DT", [128, 128])
        self.ANG = sb("ANG", [128, 128]); self.AR = sb("AR", [128, 128])
        self.SN = sb("SN", [128, 128]); self.CS = sb("CS", [128, 128])
        self.tr_y = sb("tr_y", [128, 128]); self.tr_yi = sb("tr_yi", [128, 128], I32); self.tr_yf = sb("tr_yf", [128, 128])
        self.PR = sb("PR", [128, 9, 128]); self.PI = sb("PI", [128, 9, 128])
        self.FR = sb("FR", [128, 128]); self.FI = sb("FI", [128, 128])
        self.t1 = sb("t1", [128, 256]); self.t2 = sb("t2", [128, 256]); self.t3 = sb("t3", [128, 256])
        self.BRk = sb("BRk", [128, 2, 8, 16]); self.BIk = sb("BIk", [128, 2, 8, 16])
        self.bbr = sb("bbr", [128, 2, 8, 16]); self.bbi = sb("bbi", [128, 2, 8, 16])
        self.CRn = sb("CRn", [128, 2, 2, 64]); self.CIn = sb("CIn", [128, 2, 2, 64])
        self.CRk = sb("CRk", [128, 2, 8, 16]); self.CIk = sb("CIk", [128, 2, 8, 16])
        self.BA = sb("BA", [128, 8, 2, 128]); self.CC = sb("CC", [128, 2, 128])
        self.CAr = sb("CAr", [128, 9, 2, 128], BF16); self.CAi = sb("CAi", [128, 9, 2, 128], BF16)
        self.Tsb = sb("Tsb", [128, 16, 128], BF16)
        self.LW = sb("LW", [128, 16, 2, 128], BF16)
        self.LM = sb("LM", [128, 4, 2, 2, 2, 128], BF16)
        self.Kc = sb("Kc", [128, 15, 128], BF16)
        self.SS = sb("SS", [128, 8 * 2 * 256]); self.HP = sb("HP", [128, 8 * 2 * 256], BF16)
        self.HH = sb("HH", [128, 64]); self.HT1 = sb("HT1", [128, 64]); self.HU1 = sb("HU1", [128, 64])
        self.A1 = sb("A1", [128, 64]); self.A2 = sb("A2", [128, 64])
        self.HL = sb("HL", [128, 256]); self.HLT1 = sb("HLT1", [128, 256]); self.HLU1 = sb("HLU1", [128, 256])
        self.A1f = sb("A1f", [128, 256]); self.A2f = sb("A2f", [128, 256])
        self.PWp = sb("PWp", [128, 256]); self.PW = sb("PW", [128, 256]); self.HST = sb("HST", [128, 256])
        self.Hc = sb("Hc", [128, 16]); self.Hc0 = sb("Hc0", [128, 16]); self.HcT = sb("HcT", [128, 16]); self.HcU = sb("HcU", [128, 16])
        self.B1 = sb("B1", [128, 16]); self.B2 = sb("B2", [128, 16])
        self.TC1 = sb("TC1", [128, 512]); self.TC2 = sb("TC2", [128, 512])
        self.h0T = sb("h0T", [128, 2, 2, 32]); self.FS = sb("FS", [128, 4, 2, 2, 32]); self.FSo = sb("FSo", [128, 128])
        self.dcol = sb("dcol", [128, 8]); self.bgT = sb("bgT", [128, 8])
        self.u8 = sb("u8", [128, 8, LS // 8], BF16); self.g_k = sb("g_k", [128, LS], BF16)
        self.t1g = sb("t1g", [128, 256]); self.t2g = sb("t2g", [128, 256])
        self.gblk = self.wst[0]
        self.sgm = self.SS[:, 0:512]; self.slu = self.SS[:, 512:1024]; self.yb = [self.HP[:, i * 512:(i + 1) * 512] for i in range(2)]
        self.V(lambda e: e.memset(self.LM, 0.0), [], ["LM"])

    def s5_layer_prep(self, js):
        V, A = self.V, self.A
        for half in range(2):
            hs = slice(half * 64, half * 64 + 64)
            self.ld(self.LR[hs, :], self.s5_lam_re[js].rearrange("d g p -> p (d g)"), w=["LR"], allow_slow_non_contiguous=True)
            self.ld(self.LI[hs, :], self.s5_lam_im[js].rearrange("d g p -> p (d g)"), w=["LI"], allow_slow_non_contiguous=True)
        self.ld(self.DT, self.s5_log_step[js].rearrange("d g -> (d g)").partition_broadcast(128), w=["DT"])
        self.ld(self.dcol, self.s5_d[js].rearrange("(k p) -> p k", p=128), w=["dcol"], allow_slow_non_contiguous=True)
        self.ld(self.bgT, self.s5_b_glu[js].rearrange("(k p) -> p k", p=128), w=["bgT"], allow_slow_non_contiguous=True)
        A(lambda e: e.activation(out=self.DT, in_=self.DT, func=AF.Exp), ["DT"], ["DT"])
        V(lambda e: e.tensor_tensor(out=self.ANG, in0=self.LI, in1=self.DT, op=ALU.mult), ["LI", "DT"], ["trx", "ANG"])
        V(lambda e: e.tensor_tensor(out=self.AR, in0=self.LR, in1=self.DT, op=ALU.mult), ["LR", "DT"], ["AR"])
        A(lambda e: e.activation(out=self.AR, in_=self.AR, func=AF.Exp), ["AR"], ["AR"])
        self.sin_of(self.SN, self.ANG, 0.0)
        self.sin_of(self.CS, self.ANG, math.pi / 2)
        PR, PI = self.PR, self.PI
        V(lambda e: e.memset(PR[:, 0], 1.0), [], ["PR"])
        V(lambda e: e.memset(PI[:, 0], 0.0), [], ["PI"])
        V(lambda e: e.tensor_tensor(out=PR[:, 1], in0=self.AR, in1=self.CS, op=ALU.mult), ["AR", "trx"], ["PR"])
        V(lambda e: e.tensor_tensor(out=PI[:, 1], in0=self.AR, in1=self.SN, op=ALU.mult), ["AR", "trx"], ["PI"])
        t1, t2 = self.t1[:, 0:128], self.t2[:, 0:128]
        for m in range(2, 9):
            V(lambda e: e.tensor_tensor(out=t1, in0=PR[:, m - 1], in1=PR[:, 1], op=ALU.mult), ["PR"], ["t1"])
            V(lambda e: e.tensor_tensor(out=t2, in0=PI[:, m - 1], in1=PI[:, 1], op=ALU.mult), ["PI"], ["t2"])
            V(lambda e: e.tensor_tensor(out=PR[:, m], in0=t1, in1=t2, op=ALU.subtract), ["t1", "t2"], ["PR"])
            V(lambda e: e.tensor_tensor(out=t1, in0=PR[:, m - 1], in1=PI[:, 1], op=ALU.mult), ["PR", "PI"], ["t1"])
            V(lambda e: e.tensor_tensor(out=t2, in0=PI[:, m - 1], in1=PR[:, 1], op=ALU.mult), ["PR", "PI"], ["t2"])
            V(lambda e: e.tensor_tensor(out=PI[:, m], in0=t1, in1=t2, op=ALU.add), ["t1", "t2"], ["PI"])
        nr, den = self.SN, self.CS
        V(lambda e: e.tensor_scalar(out=nr, in0=PR[:, 1], scalar1=-1.0, scalar2=None, op0=ALU.add), ["PR"], ["trx"])
        V(lambda e: e.tensor_tensor(out=t1, in0=self.LR, in1=self.LR, op=ALU.mult), ["LR"], ["t1"])
        V(lambda e: e.tensor_tensor(out=t2, in0=self.LI, in1=self.LI, op=ALU.mult), ["LI"], ["t2"])
        V(lambda e: e.tensor_tensor(out=den, in0=t1, in1=t2, op=ALU.add), ["t1", "t2"], ["trx"])
        V(lambda e: e.reciprocal(out=den, in_=den), ["trx"], ["trx"])
        V(lambda e: e.tensor_tensor(out=t1, in0=nr, in1=self.LR, op=ALU.mult), ["trx", "LR"], ["t1"])
        V(lambda e: e.tensor_tensor(out=t2, in0=PI[:, 1], in1=self.LI, op=ALU.mult), ["PI", "LI"], ["t2"])
        V(lambda e: e.tensor_tensor(out=t1, in0=t1, in1=t2, op=ALU.add), ["t1", "t2"], ["t1"])
        V(lambda e: e.tensor_tensor(out=self.FR, in0=t1, in1=den, op=ALU.mult), ["t1", "trx"], ["FR"])
        V(lambda e: e.tensor_tensor(out=t1, in0=PI[:, 1], in1=self.LR, op=ALU.mult), ["PI", "LR"], ["t1"])
        V(lambda e: e.tensor_tensor(out=t2, in0=nr, in1=self.LI, op=ALU.mult), ["trx", "LI"], ["t2"])
        V(lambda e: e.tensor_tensor(out=t1, in0=t1, in1=t2, op=ALU.subtract), ["t1", "t2"], ["t1"])
        V(lambda e: e.tensor_tensor(out=self.FI, in0=t1, in1=den, op=ALU.mult), ["t1", "trx"], ["FI"])
        self.ld(self.FSo, self.st5[js], w=["FSo"])
        pt = self.ps[1][:, 0:128]
        self.T(lambda e: e.transpose(pt, self.FSo, self.identf), ["FSo", "identf"], ["ps1"])
        V(lambda e: e.tensor_copy(out=self.h0T.rearrange("p d x g -> p (d x g)"), in_=pt), ["ps1"], ["h0T"])

    def bcg(self, tab, m, k):
        a = tab[:, m, :].rearrange("p (d g) -> p d g", d=2)[:, :, 8 * k:8 * k + 8]
        return a.unsqueeze(3).to_broadcast([128, 2, 8, 16])

    def bcf(self, tab, k):
        a = tab.rearrange("p (d g) -> p d g", d=2)[:, :, 8 * k:8 * k + 8]
        return a.unsqueeze(3).to_broadcast([128, 2, 8, 16])

    def cmul(self, eng, outr, outi, ar, ai, br, bi, rk, wk, hs_r=slice(0, 128), hs_i=slice(0, 128), negi=False):
        if eng == "gpsimd":
            t1 = self.t1g.rearrange("p (d g h) -> p d g h", d=2, g=8)
            t2 = self.t2g.rearrange("p (d g h) -> p d g h", d=2, g=8)
            k1, k2 = "t1g", "t2g"
        else:
            t1 = self.t1.rearrange("p (d g h) -> p d g h", d=2, g=8)
            t2 = self.t2.rearrange("p (d g h) -> p d g h", d=2, g=8)
            k1, k2 = "t1", "t2"
        E = self.E
        s = hs_r
        E(eng, lambda e: e.tensor_tensor(out=t1[s], in0=ar[s], in1=br[s], op=ALU.mult), rk, [k1])
        E(eng, lambda e: e.tensor_tensor(out=t2[s], in0=ai[s], in1=bi[s], op=ALU.mult), rk, [k2])
        E(eng, lambda e: e.tensor_tensor(out=outr[s], in0=t1[s], in1=t2[s], op=ALU.subtract), [k1, k2], wk)
        s = hs_i
        E(eng, lambda e: e.tensor_tensor(out=t1[s], in0=ar[s], in1=bi[s], op=ALU.mult), rk, [k1])
        E(eng, lambda e: e.tensor_tensor(out=t2[s], in0=ai[s], in1=br[s], op=ALU.mult), rk, [k2])
        if negi:
            E(eng, lambda e: e.tensor_tensor(out=t1[s], in0=t1[s], in1=t2[s], op=ALU.add), [k1, k2], [k1])
            E(eng, lambda e: e.tensor_scalar(out=outi[s], in0=t1[s], scalar1=-1.0, scalar2=None, op0=ALU.mult), [k1], wk)
        else:
            E(eng, lambda e: e.tensor_tensor(out=outi[s], in0=t1[s], in1=t2[s], op=ALU.add), [k1, k2], wk)

    def s5_scan1(self, k, seng, SSv, HPv, SQs, ncg, ncs, nseq, A1s, A2s, grp):
        E = self.E
        HHv = lambda t: t.rearrange("p (x q d s) -> p x q d s", x=2, q=4, d=2)
        HHs, T1s, U1s = self.HH[:, 0:16 * nseq], self.HT1[:, 0:16 * nseq], self.HU1[:, 0:16 * nseq]
        if grp == 1:
            E(seng, lambda e: e.tensor_copy(out=HHv(HHs)[:, :, :, :, 0].rearrange("p x q d -> p d x q"),
                                            in_=self.h0T[:, :, :, 4 * k:4 * k + 4]), ["h0T"], ["HH"])
        else:
            E(seng, lambda e: e.memset(HHs, 0.0), [], ["HH"])
        hsw = AP(HHs, HHs.offset + 8 * nseq, [[HHs.ap[0][0], 128], [-8 * nseq, 2], [1, 8 * nseq]])
        hfl = HHs.rearrange("p (x r) -> p x r", x=2)
        a2f = A2s.rearrange("p (x r) -> p x r", x=2)
        u1f = U1s.rearrange("p (x r) -> p x r", x=2)
        hh4 = HHs.rearrange("p (xq d s) -> p xq d s", d=2, s=nseq)
        for i in range(ncs):
            def colap(t):
                return AP(t, t.offset + i, [[t.ap[0][0], 128], [SQs, 8], [ncg + ncs - 1 - 2 * i, 2], [ncs, nseq]])
            E(seng, lambda e: e.tensor_copy(out=colap(HPv), in_=hh4), ["HH"], ["HP"])
            E(seng, lambda e: e.tensor_tensor(out=T1s, in0=A1s, in1=HHs, op=ALU.mult), ["A1", "HH"], ["HT1"])
            E(seng, lambda e: e.tensor_tensor(out=u1f, in0=a2f, in1=hsw, op=ALU.mult), ["A2", "HH"], ["HU1"])
            E(seng, lambda e: e.tensor_tensor(out=T1s, in0=T1s, in1=U1s, op=ALU.add), ["HT1", "HU1"], ["HT1"])
            E(seng, lambda e: e.tensor_tensor(out=hh4, in0=T1s.rearrange("p (xq d s) -> p xq d s", d=2, s=nseq),
                                              in1=colap(SSv), op=ALU.add), ["HT1", "SS"], ["HH"])
        if grp == 0:
            for s in range(nseq):
                E(seng, lambda e: e.tensor_copy(out=self.FS[:, s, :, :, 4 * k:4 * k + 4],
                                                in_=HHv(HHs)[:, :, :, :, s].rearrange("p x q d -> p d x q")), ["HH"], ["FS"])

    def s5_scan2(self, k, seng, SSv, HPv, SQs, ncg, A1s, A2s):
        E = self.E
        MB = 16
        ps_ = SSv.ap[0][0]
        HL, T1, U1, A1f, A2f = self.HL, self.HLT1, self.HLU1, self.A1f, self.A2f
        PWp, PW, HST = self.PWp, self.PW, self.HST
        Hc, Hc0, HcT, HcU, B1, B2 = self.Hc, self.Hc0, self.HcT, self.HcU, self.B1, self.B2
        TC1, TC2 = self.TC1, self.TC2
        E(seng, lambda e: e.tensor_copy(out=A1f.rearrange("p (a b) -> p a b", b=MB), in_=A1s.unsqueeze(2).to_broadcast([128, 16, MB])), ["A1"], ["A1f"])
        E(seng, lambda e: e.tensor_copy(out=A2f.rearrange("p (a b) -> p a b", b=MB), in_=A2s.unsqueeze(2).to_broadcast([128, 16, MB])), ["A2"], ["A2f"])
        PWv = PWp.rearrange("p (x g i) -> p x g i", x=2, g=8)
        E(seng, lambda e: e.tensor_copy(out=PWv[:, 0, :, 0], in_=A1s[:, 0:8]), ["A1"], ["PWp"])
        E(seng, lambda e: e.tensor_copy(out=PWv[:, 1, :, 0], in_=A2s[:, 8:16]), ["A2"], ["PWp"])
        ln = 1
        t1 = TC1[:, 0:64].rearrange("p (g i) -> p g i", g=8)
        t2 = TC2[:, 0:64].rearrange("p (g i) -> p g i", g=8)
        while ln < MB:
            mr = PWv[:, 0, :, ln - 1:ln].to_broadcast([128, 8, ln])
            mi = PWv[:, 1, :, ln - 1:ln].to_broadcast([128, 8, ln])
            ar, ai = PWv[:, 0, :, 0:ln], PWv[:, 1, :, 0:ln]
            E(seng, lambda e: e.tensor_tensor(out=t1[:, :, 0:ln], in0=ar, in1=mr, op=ALU.mult), ["PWp"], ["TC1"])
            E(seng, lambda e: e.tensor_tensor(out=t2[:, :, 0:ln], in0=ai, in1=mi, op=ALU.mult), ["PWp"], ["TC2"])
            E(seng, lambda e: e.tensor_tensor(out=PWv[:, 0, :, ln:2 * ln], in0=t1[:, :, 0:ln], in1=t2[:, :, 0:ln], op=ALU.subtract), ["TC1", "TC2"], ["PWp"])
            E(seng, lambda e: e.tensor_tensor(out=t1[:, :, 0:ln], in0=ar, in1=mi, op=ALU.mult), ["PWp"], ["TC1"])
            E(seng, lambda e: e.tensor_tensor(out=t2[:, :, 0:ln], in0=ai, in1=mr, op=ALU.mult), ["PWp"], ["TC2"])
            E(seng, lambda e: e.tensor_tensor(out=PWv[:, 1, :, ln:2 * ln], in0=t1[:, :, 0:ln], in1=t2[:, :, 0:ln], op=ALU.add), ["TC1", "TC2"], ["PWp"])
            ln *= 2
        PW5 = PW.rearrange("p (x q d i) -> p x q d i", x=2, q=4, d=2)
        PWp5 = PWp.rearrange("p (x q d i) -> p x q d i", x=2, q=4, d=2)
        for x in range(2):
            E(seng, lambda e: e.tensor_copy(out=PW5[:, x, :, 0, :], in_=PWp5[:, x, :, 0, :]), ["PWp"], ["PW"])
            E(seng, lambda e: e.tensor_copy(out=PW5[:, x, :, 1, :], in_=PWp5[:, x, :, 1, ::-1]), ["PWp"], ["PW"])
        B1v = B1.rearrange("p (x g) -> p x g", x=2)
        B2v = B2.rearrange("p (x g) -> p x g", x=2)
        for x in range(2):
            E(seng, lambda e: e.tensor_copy(out=B1v[:, x, :], in_=PWv[:, 0, :, MB - 1]), ["PWp"], ["B1"])
        E(seng, lambda e: e.tensor_scalar(out=B2v[:, 0, :], in0=PWv[:, 1, :, MB - 1], scalar1=-1.0, scalar2=None, op0=ALU.mult), ["PWp"], ["B2"])
        E(seng, lambda e: e.tensor_copy(out=B2v[:, 1, :], in_=PWv[:, 1, :, MB - 1]), ["PWp"], ["B2"])
        E(seng, lambda e: e.memset(HL, 0.0), [], ["HL"])
        hsw = AP(HL, HL.offset + 128, [[HL.ap[0][0], 128], [-128, 2], [1, 128]])
        a2f = A2f.rearrange("p (x r) -> p x r", x=2)
        u1f = U1.rearrange("p (x r) -> p x r", x=2)
        hl4 = HL.rearrange("p (g d b) -> p g d b", g=8, d=2)
        t14 = T1.rearrange("p (g d b) -> p g d b", g=8, d=2)
        for i in range(MB):
            col = AP(SSv, SSv.offset + i, [[ps_, 128], [SQs, 8], [ncg + MB - 1 - 2 * i, 2], [MB, MB]])
            E(seng, lambda e: e.tensor_tensor(out=T1, in0=A1f, in1=HL, op=ALU.mult), ["A1f", "HL"], ["HLT1"])
            E(seng, lambda e: e.tensor_tensor(out=u1f, in0=a2f, in1=hsw, op=ALU.mult), ["A2f", "HL"], ["HLU1"])
            E(seng, lambda e: e.tensor_tensor(out=T1, in0=T1, in1=U1, op=ALU.add), ["HLT1", "HLU1"], ["HLT1"])
            E(seng, lambda e: e.tensor_tensor(out=hl4, in0=t14, in1=col, op=ALU.add), ["HLT1", "SS"], ["HL"])
            E(seng, lambda e: e.tensor_copy(out=col, in_=hl4), ["HL"], ["SS"])
        E(seng, lambda e: e.tensor_copy(out=Hc.rearrange("p (x q d) -> p d x q", x=2, q=4), in_=self.h0T[:, :, :, 4 * k:4 * k + 4]), ["h0T"], ["Hc"])
        E(seng, lambda e: e.tensor_copy(out=Hc0, in_=Hc), ["Hc"], ["Hc0"])
        hcsw = AP(Hc, Hc.offset + 8, [[Hc.ap[0][0], 128], [-8, 2], [1, 8]])
        b2f = B2.rearrange("p (x r) -> p x r", x=2)
        hcuf = HcU.rearrange("p (x r) -> p x r", x=2)
        hc2 = Hc.rearrange("p (g d) -> p g d", d=2)
        hct2 = HcT.rearrange("p (g d) -> p g d", d=2)
        for b in range(MB):
            hpos = AP(HST, HST.offset + b, [[HST.ap[0][0], 128], [2 * MB, 8], [MB + MB - 1 - 2 * b, 2]])
            E(seng, lambda e: e.tensor_copy(out=hpos, in_=hc2), ["Hc"], ["HST"])
            if b == MB - 1:
                break
            send = AP(SSv, SSv.offset + MB * b + MB - 1, [[ps_, 128], [SQs, 8], [ncg + MB * (MB - 1 - b) - (MB * b + MB - 1), 2]])
            E(seng, lambda e: e.tensor_tensor(out=HcT, in0=B1, in1=Hc, op=ALU.mult), ["B1", "Hc"], ["HcT"])
            E(seng, lambda e: e.tensor_tensor(out=hcuf, in0=b2f, in1=hcsw, op=ALU.mult), ["B2", "Hc"], ["HcU"])
            E(seng, lambda e: e.tensor_tensor(out=HcT, in0=HcT, in1=HcU, op=ALU.add), ["HcT", "HcU"], ["HcT"])
            E(seng, lambda e: e.tensor_tensor(out=hc2, in0=hct2, in1=send, op=ALU.add), ["HcT", "SS"], ["Hc"])
        HST4 = HST.rearrange("p (x q d b) -> p x q d b", x=2, q=4, d=2)
        c1 = TC1.rearrange("p (q b i) -> p q b i", q=2, b=MB)
        c2 = TC2.rearrange("p (q b i) -> p q b i", q=2, b=MB)
        for d in range(2):
          for qh in range(2):
            qs = slice(2 * qh, 2 * qh + 2)
            pr = PW5[:, 0, qs, d, :].unsqueeze(2).to_broadcast([128, 2, MB, MB])
            pi = PW5[:, 1, qs, d, :].unsqueeze(2).to_broadcast([128, 2, MB, MB])
            hr = HST4[:, 0, qs, d, :].unsqueeze(3).to_broadcast([128, 2, MB, MB])
            hi = HST4[:, 1, qs, d, :].unsqueeze(3).to_broadcast([128, 2, MB, MB])
            sre = AP(SSv, SSv.offset + 2 * qh * SQs + d * ncg, [[ps_, 128], [SQs, 2], [MB, MB], [1, MB]])
            sim = AP(SSv, SSv.offset + (4 + 2 * qh) * SQs + d * ncg, [[ps_, 128], [SQs, 2], [MB, MB], [1, MB]])
            E(seng, lambda e: e.tensor_tensor(out=c1, in0=pr, in1=hr, op=ALU.mult), ["PW", "HST"], ["TC1"])
            E(seng, lambda e: e.tensor_tensor(out=c2, in0=pi, in1=hi, op=ALU.mult), ["PW", "HST"], ["TC2"])
            E(seng, lambda e: e.tensor_tensor(out=c1, in0=c1, in1=c2, op=ALU.subtract), ["TC1", "TC2"], ["TC1"])
            E(seng, lambda e: e.tensor_tensor(out=sre, in0=sre, in1=c1, op=ALU.add), ["SS", "TC1"], ["SS"])
            E(seng, lambda e: e.tensor_tensor(out=c1, in0=pr, in1=hi, op=ALU.mult), ["PW", "HST"], ["TC1"])
            E(seng, lambda e: e.tensor_tensor(out=c2, in0=pi, in1=hr, op=ALU.mult), ["PW", "HST"], ["TC2"])
            E(seng, lambda e: e.tensor_tensor(out=c1, in0=c1, in1=c2, op=ALU.add), ["TC1", "TC2"], ["TC1"])
            E(seng, lambda e: e.tensor_tensor(out=sim, in0=sim, in1=c1, op=ALU.add), ["SS", "TC1"], ["SS"])
        ss3 = SSv.rearrange("p (g d c) -> p g d c", g=8, d=2)
        hp3 = HPv.rearrange("p (g d c) -> p g d c", g=8, d=2)
        h03 = Hc0.rearrange("p (g d) -> p g d", d=2)
        self.A(lambda e: e.activation(out=hp3[:, :, 0, 1:ncg], in_=ss3[:, :, 0, 0:ncg - 1], func=AF.Copy), ["SS"], ["HP"])
        self.A(lambda e: e.activation(out=hp3[:, :, 1, 0:ncg - 1], in_=ss3[:, :, 1, 1:ncg], func=AF.Copy), ["SS"], ["HP"])
        E(seng, lambda e: e.tensor_copy(out=hp3[:, :, 0, 0:1], in_=h03[:, :, 0:1]), ["Hc0"], ["HP"])
        E(seng, lambda e: e.tensor_copy(out=hp3[:, :, 1, ncg - 1:ncg], in_=h03[:, :, 1:2]), ["Hc0"], ["HP"])

    def s5_mixer(self, js, grp):
        if grp == 1:
            self.arena_reset()
            self.s5_alloc()
            self.s5_layer_prep(js)
        else:
            self.V(lambda e: e.memset(self.LM, 0.0), [], ["LM"])
        t0, n = self.trange(grp)
        nseq = 1 if grp == 1 else NPS
        ncs = (n // nseq) // 8
        ncg = n // 8
        V, A, T, E = self.V, self.A, self.T, self.E
        H0, H1 = slice(0, 64), slice(64, 128)
        def ld_wu(k_):
            self.ldc(self.wgs[k_ % 2], self.s5_w_in[js][:, k_ * 128:(k_ + 1) * 128].rearrange("(k p) n -> p k n", p=128), w=[f"wgs{k_ % 2}"])
        ld_wu(0)
        for k in range(8):
            wu, wuk = self.wgs[k % 2], f"wgs{k % 2}"
            if k + 1 < 8:
                ld_wu(k + 1)
            eng = "vector"
            seng = "vector"
            for tb in range(n // 512):
                pt = self.ps[0]
                for kk in range(8):
                    T(lambda e: e.matmul(pt, lhsT=wu[:, kk, :],
                                         rhs=self.hT[:, kk, tb * 512:(tb + 1) * 512], start=(kk == 0), stop=(kk == 7)),
                      [wuk, "hT"], ["ps0"])
                A(lambda e: e.activation(out=self.u8[:, :, tb * 64:(tb + 1) * 64].rearrange("p j c -> p c j"),
                                         in_=pt.rearrange("p (c j) -> p c j", j=8), func=AF.Copy), ["ps0"], ["u_k"])
            if grp == 1:
                for half in range(2):
                    hs = slice(half * 64, half * 64 + 64)
                    for d in range(2):
                        self.ld(self.BRk[hs, d], self.s5_b_re[js, d, 8 * k:8 * k + 8].rearrange("g p h -> p g h"), w=["BRk"])
                        self.ld(self.BIk[hs, d], self.s5_b_im[js, d, 8 * k:8 * k + 8].rearrange("g p h -> p g h"), w=["BIk"])
                for dup in range(2):
                    self.ld(self.CRn[:, :, dup, :], self.s5_c_re[js][:, 8 * k:8 * k + 8].rearrange("d g h p -> (g h) d p"), w=["CRn"])
                    self.ld(self.CIn[:, :, dup, :], self.s5_c_im[js][:, 8 * k:8 * k + 8].rearrange("d g h p -> (g h) d p"), w=["CIn"])
                for (cn, cnk, ck, ckk) in ((self.CRn, "CRn", self.CRk, "CRk"), (self.CIn, "CIn", self.CIk, "CIk")):
                    for d in range(2):
                        pt = self.ps[1][:, 0:128]
                        src = cn[:, d].rearrange("p a c -> p (a c)")
                        T(lambda e: e.transpose(pt, src, self.identf), [cnk, "identf"], ["ps1"])
                        A(lambda e: e.activation(out=ck[:, d].rearrange("p g h -> p (g h)"), in_=pt, func=AF.Copy), ["ps1"], [ckk])
                self.cmul(eng, self.bbr, self.bbi, self.bcf(self.FR, k), self.bcf(self.FI, k), self.BRk, self.BIk,
                          ["FR", "FI", "BRk", "BIk"], ["bb"])
                BAv = self.BA.rearrange("p m d (g h) -> p m d g h", g=8)
                for m in range(8):
                    self.cmul(eng, BAv[:, m], BAv[:, m], self.bcg(self.PR, m, k), self.bcg(self.PI, m, k), self.bbr, self.bbi,
                              ["PR", "PI", "bb"], ["BA"], hs_r=H0, hs_i=H1)
                CCv = self.CC.rearrange("p d (g h) -> p d g h", g=8)
                V(lambda e: e.tensor_copy(out=CCv[H0], in_=self.CRk[H0]), ["CRk"], ["CC"])
                V(lambda e: e.tensor_scalar(out=CCv[H1], in0=self.CIk[H1], scalar1=-1.0, scalar2=None, op0=ALU.mult), ["CIk"], ["CC"])
                CArv = self.CAr.rearrange("p m d (g h) -> p m d g h", g=8)
                CAiv = self.CAi.rearrange("p m d (g h) -> p m d g h", g=8)
                for m in range(1, 9):
                    self.cmul(eng, CArv[:, m], CAiv[:, m], self.bcg(self.PR, m, k), self.bcg(self.PI, m, k), self.CRk, self.CIk,
                              ["PR", "PI", "CRk", "CIk"], ["CA"], negi=True)
                for tau in range(8):
                    for d in range(2):
                        pt = self.ps[1][:, 0:128]
                        if tau == 0:
                            T(lambda e: e.matmul(pt, lhsT=self.BA[:, 0, d], rhs=self.CC[:, d], start=(d == 0), stop=(d == 1)),
                              ["BA", "CC"], ["ps1"])
                            if d == 0:
                                continue
                            tt = self.t3[:, 0:128]
                            V(lambda e: e.tensor_tensor(out=tt, in0=pt, in1=self.bdmask, op=ALU.mult), ["ps1", "bdmask"], ["t3"])
                            V(lambda e: e.scalar_tensor_tensor(out=self.Kc[:, 7], in0=self.identf, scalar=self.dcol[:, k:k + 1],
                                                               in1=tt, op0=ALU.mult, op1=ALU.add), ["t3", "identf", "dcol"], ["Kc"])
                        else:
                            T(lambda e: e.matmul(pt, lhsT=self.BA[:, tau, d], rhs=self.CC[:, d], start=True, stop=True),
                              ["BA", "CC"], ["ps1"])
                            idx = 7 + tau if d == 0 else 7 - tau
                            V(lambda e: e.tensor_tensor(out=self.Kc[:, idx], in0=pt, in1=self.bdmask, op=ALU.mult),
                              ["ps1", "bdmask"], ["Kc"])
                for g4 in range(4):
                    bank, bk = (self.ps[1], "ps1") if g4 % 2 == 0 else (self.ps[0], "ps0")
                    for ii in range(4):
                        idx = g4 * 4 + ii
                        T(lambda e: e.transpose(bank[:, ii * 128:(ii + 1) * 128], self.BA[:, idx // 2, idx % 2], self.identf),
                          ["BA", "identf"], [bk])
                    A(lambda e: e.activation(out=self.Tsb[:, g4 * 4:(g4 + 1) * 4].rearrange("p a b -> p (a b)"), in_=bank, func=AF.Copy),
                      [bk], ["Tsb"])
                self.ld(self.Tsb_scr[js, k], self.Tsb.rearrange("p a b -> p (a b)"), r=["Tsb"], w=[("s5c", k)])
                self.ld(self.Kc_scr[js, k], self.Kc.rearrange("p a b -> p (a b)"), r=["Kc"], w=[("s5c", k)])
                self.ld(self.CA_scr[js, k, :, 0], self.CAr.rearrange("p m d c -> p (m d c)"), r=["CA"], w=[("s5c", k)])
                self.ld(self.CA_scr[js, k, :, 1], self.CAi.rearrange("p m d c -> p (m d c)"), r=["CA"], w=[("s5c", k)])
            else:
                self.ld(self.Tsb.rearrange("p a b -> p (a b)"), self.Tsb_scr[js, k], r=[("s5c", k)], w=["Tsb"])
                self.ld(self.Kc.rearrange("p a b -> p (a b)"), self.Kc_scr[js, k], r=[("s5c", k)], w=["Kc"])
                self.ld(self.CAr.rearrange("p m d c -> p (m d c)"), self.CA_scr[js, k, :, 0], r=[("s5c", k)], w=["CA"])
                self.ld(self.CAi.rearrange("p m d c -> p (m d c)"), self.CA_scr[js, k, :, 1], r=[("s5c", k)], w=["CA"])
            HHv = lambda t: t.rearrange("p (x q d s) -> p x q d s", x=2, q=4, d=2)
            A1s, A2s = self.A1[:, 0:16 * nseq], self.A2[:, 0:16 * nseq]
            A1v, A2v = HHv(A1s), HHv(A2s)
            for hf, hsl in ((0, H0), (1, H1)):
                pr8 = self.PR[hsl, 8, :].rearrange("p (d g) -> p d g", d=2)[:, :, 8 * k + hf:8 * k + 8:2]
                pi8 = self.PI[hsl, 8, :].rearrange("p (d g) -> p d g", d=2)[:, :, 8 * k + hf:8 * k + 8:2]
                for s in range(nseq):
                    for x in range(2):
                        V(lambda e: e.tensor_copy(out=A1v[hsl, x, :, :, s].rearrange("p q d -> p d q"), in_=pr8), ["PR"], ["A1"])
                    V(lambda e: e.tensor_scalar(out=A2v[hsl, 0, :, :, s].rearrange("p q d -> p d q"), in0=pi8, scalar1=-1.0,
                                                scalar2=None, op0=ALU.mult), ["PI"], ["A2"])
                    V(lambda e: e.tensor_copy(out=A2v[hsl, 1, :, :, s].rearrange("p q d -> p d q"), in_=pi8), ["PI"], ["A2"])
            SQs = 2 * ncg
            SSv = self.SS[:, 0:16 * ncg]
            HPv = self.HP[:, 0:16 * ncg]
            for q in range(4):
                for x in range(2):
                    in0 = self.Tsb[:, :, x * 64:(x + 1) * 64].unsqueeze(2).to_broadcast([128, 16, 2, 64])
                    in1 = self.pmask[:, 2 * q:2 * q + 2].unsqueeze(1).unsqueeze(3).to_broadcast([128, 16, 2, 64])
                    outv = self.LW[:, :, x, :].rearrange("p m (a c) -> p m a c", a=2)
                    V(lambda e: e.tensor_tensor(out=outv, in0=in0, in1=in1, op=ALU.mult), ["Tsb", "pmask"], ["LW"])
                for d in range(2):
                    for x in range(2):
                        pt = self.ps[2 + x][:, 0:ncg]
                        for j in range(8):
                            m = 7 - j if d == 0 else j
                            T(lambda e: e.matmul(pt, lhsT=self.LW[:, m * 2 + d, x, :], rhs=self.u8[:, j, 0:ncg],
                                                 start=(j == 0), stop=(j == 7)), ["LW", "u_k"], [f"ps{2 + x}"])
                        off = (x * 4 + q) * SQs + d * ncg
                        A(lambda e: e.activation(out=SSv[:, off:off + ncg], in_=pt, func=AF.Copy), [f"ps{2 + x}"], ["SS"])
            if grp == 1:
                self.s5_scan2(k, seng, SSv, HPv, SQs, ncg, A1s, A2s)
            else:
                self.s5_scan1(k, seng, SSv, HPv, SQs, ncg, ncs, nseq, A1s, A2s, grp)
            for jh in range(4):
                for d in range(2):
                    for x, ca in ((0, self.CAr), (1, self.CAi)):
                        for hf, hsl in ((0, H0), (1, H1)):
                            lm = self.LM[hsl, :, d, x, :, :]
                            outv = AP(lm, lm.offset + 16 * hf, [[lm.ap[0][0], 64], [lm.ap[1][0] + 32, 4], [lm.ap[2][0], 2], [1, 16]])
                            if d == 0:
                                m0, ms = 2 * jh + 1, 1
                            else:
                                m0, ms = 8 - 2 * jh, -1
                            cam = ca[hsl, m0, d, :]
                            mstride = ca.ap[1][0]
                            inv = AP(cam, cam.offset + 16 * hf, [[cam.ap[0][0], 64], [32, 4], [ms * mstride, 2], [1, 16]])
                            V(lambda e: e.tensor_copy(out=outv, in_=inv), ["CA"], ["LM"])
                for jj in range(2):
                    j = jh * 2 + jj
                    pt = self.ps[4 + jj][:, 0:ncg]
                    pk = f"ps{4 + jj}"
                    for j2 in range(8):
                        T(lambda e: e.matmul(pt, lhsT=self.Kc[:, j - j2 + 7, :], rhs=self.u8[:, j2, 0:ncg],
                                             start=(j2 == 0), stop=False), ["Kc", "u_k"], [pk])
                    cnt = 0
                    for q in range(4):
                        for d in range(2):
                            for x in range(2):
                                off = (x * 4 + q) * SQs + d * ncg
                                cnt += 1
                                T(lambda e: e.matmul(pt, lhsT=self.LM[:, q, d, x, jj, :], rhs=HPv[:, off:off + ncg],
                                                     start=False, stop=(cnt == 16)), ["LM", "HP"], [pk])
                    A(lambda e: e.activation(out=self.g_k[:, j:n:8], in_=pt, func=AF.Gelu_apprx_tanh), [pk], ["g_k"])
            self.ld(self.gscr[k, :, t0:t0 + n], self.g_k[:, 0:n], r=["g_k"], w=[("gscr", grp)])
        if grp == 0:
            for s in range(nseq):
                pt = self.ps[1][:, 0:128]
                T(lambda e: e.transpose(pt, self.FS[:, s].rearrange("p d x g -> p (d x g)"), self.identf), ["FS", "identf"], ["ps1"])
                V(lambda e: e.tensor_copy(out=self.FSo, in_=pt), ["ps1"], ["FSo"])
                self.ld(self.ns5[s, js], self.FSo, r=["FSo"], w=["ns5"])
        lwv = self.LW.rearrange("p a b c -> p (a b c)")
        wglu = AP(lwv, lwv.offset, [[lwv.ap[0][0], 128], [1024, 8], [1, 1024]])
        self.ldc(wglu, self.s5_w_glu[js].rearrange("(k p) n -> p k n", p=128), w=["LW", "LM"])
        steps = [(tb, nn) for tb in range(n // 512) for nn in range(8)]
        def load_gate(i):
            nn_ = steps[i][1]
            self.ldc(self.wgs[i % 2], self.s5_w_in[js][:, D + nn_ * 128:D + (nn_ + 1) * 128].rearrange("(k p) n -> p k n", p=128),
                     w=[f"wgs{i % 2}"])
        load_gate(0)
        for si, (tb, nn) in enumerate(steps):
            ts = slice(tb * 512, (tb + 1) * 512)
            if nn == 0:
                self.ld(self.gblk, self.gscr[0:8, :, t0 + tb * 512:t0 + (tb + 1) * 512].rearrange("k p t -> p k t"),
                        r=[("gscr", grp)], w=["wst0"])
            if si + 1 < len(steps):
                load_gate(si + 1)
            pz, pg = self.ps[0 + 2 * (nn % 2)], self.ps[1 + 2 * (nn % 2)]
            pzk, pgk = f"ps{0 + 2 * (nn % 2)}", f"ps{1 + 2 * (nn % 2)}"
            for kk in range(8):
                T(lambda e: e.matmul(pz, lhsT=wglu[:, kk, nn * 128:(nn + 1) * 128], rhs=self.gblk[:, kk, :],
                                     start=(kk == 0), stop=(kk == 7)), ["LW", "LM", "wst0"], [pzk])
            A(lambda e: e.activation(out=self.sgm, in_=pz, func=AF.Sigmoid, bias=self.bgT[:, nn:nn + 1]), [pzk, "bgT"], ["SS"])
            wg, wgk = self.wgs[si % 2], f"wgs{si % 2}"
            for kk in range(8):
                T(lambda e: e.matmul(pg, lhsT=wg[:, kk, :], rhs=self.hT[:, kk, ts],
                                     start=(kk == 0), stop=(kk == 7)), [wgk, "hT"], [pgk])
            A(lambda e: e.activation(out=self.slu, in_=pg, func=AF.Silu), [pgk], ["SS2"])
            yb = self.yb[nn % 2]
            ybk = f"yb{nn % 2}"
            V(lambda e: e.tensor_tensor(out=self.sgm, in0=self.sgm, in1=self.gblk[:, nn, :], op=ALU.mult), ["SS", "wst0"], ["SS"])
            V(lambda e: e.tensor_tensor(out=yb, in0=self.sgm, in1=self.slu, op=ALU.mult), ["SS", "SS2"], [ybk, "HP"])
            self.ld(self.g2scr[nn, :, t0 + tb * 512:t0 + (tb + 1) * 512], yb, r=[ybk], w=[("g2scr", grp)])
        self.ysrc = (self.g2scr, ("g2scr", grp))
        return D


def host_consts():
    r = np.arange(128)
    pm = np.zeros((128, 8), np.float32)
    for q in range(4):
        for qq in range(2):
            pm[:, 2 * q + qq] = ((r // 16) == 2 * q + qq)
    bd = ((r[:, None] // 16) == (r[None, :] // 16)).astype(np.float32)
    t = np.arange(2048)
    row, col = t // 64, t % 64
    inv = (10000.0 ** (-np.arange(32, dtype=np.float32) / 32)).astype(np.float32)
    ang = np.concatenate([row[:, None].astype(np.float32) * inv[None], col[:, None].astype(np.float32) * inv[None]], 1)
    jj = r[:, None].astype(np.float32)
    ii = r[None, :].astype(np.float32)
    retE = np.stack([np.maximum(ii - jj, 0.0), np.maximum(jj - ii, 0.0)]).astype(np.float32)
    retM = np.stack([(ii >= jj), (jj > ii)]).astype(np.float32)
    retqe = np.stack([r + 1.0, 128.0 - r]).astype(np.float32)
    retke = np.stack([127.0 - r, r * 1.0], 1).astype(np.float32)
    extra = {
        "c_ropeC": np.cos(ang).astype(np.float32).reshape(16, 128, 64),
        "c_ropeS": np.sin(ang).astype(np.float32).reshape(16, 128, 64),
        "c_retE": retE, "c_retM": retM, "c_retqe": retqe, "c_retke": retke,
    }
    extra.update(hy_consts())
    return extra | {
        "c_identb": np.eye(128, dtype=np.float32).astype(ml_dtypes.bfloat16),
        "c_identf": np.eye(128, dtype=np.float32),
        "c_pmask": pm,
        "c_bdmask": bd,
    }


_CONSTS = None


def make_in_maps(prog, inputs):
    global _CONSTS
    if _CONSTS is None:
        _CONSTS = host_consts()
    consts = _CONSTS
    maps = []
    for c in range(8):
        m = {}
        for name in prog.inputs:
            if name in consts:
                m[name] = consts[name]
            elif name == "xs":
                m[name] = np.ascontiguousarray(inputs["x_sample"][c])
            elif name == "xp":
                m[name] = np.ascontiguousarray(inputs["x_prompt"][4 * c:4 * c + 4].reshape(NPS * LP, D))
            elif name == "cvec":
                m[name] = np.ascontiguousarray(np.stack([inputs["c_ctx"], inputs["c"][c]], 0))
            elif name == "st5":
                m[name] = np.ascontiguousarray(inputs["state_s5"][c].reshape(2, 128, 128))
            elif name == "stret":
                m[name] = np.ascontiguousarray(inputs["state_ret"][c, 0])
            else:
                a = np.asarray(inputs[name])
                shp = prog.inputs[name][0]
                m[name] = np.ascontiguousarray(a.reshape(shp))
        maps.append(m)
    return maps


_PROG = None


def kernel(**inputs):
    global _PROG
    inputs = {k: np.asarray(v) for k, v in inputs.items()}
    if _PROG is None:
        _PROG = K()
    prog = _PROG
    res = run_bass_kernel_spmd(prog.nc, make_in_maps(prog, inputs), core_ids=list(range(8)))
    rs = res.results
    y_prompt = np.concatenate([r["yp"].reshape(NPS, LP, D) for r in rs], 0)
    y_sample = np.stack([r["ys"] for r in rs], 0)
    ns5 = np.concatenate([r["ns5"].reshape(NPS, 2, 2, 2, 64, 64) for r in rs], 0)
    nret = np.concatenate([r["nret"][:, None].reshape(NPS, 1, 2, 8, 128, 256) for r in rs], 0)
    return (y_prompt.astype(np.float32), y_sample.astype(np.float32), ns5.astype(np.float32), nret.astype(np.float32))


def _ret_decl(self):
    self.ret_w_in = self.din("ret_w_in", [1, D, 6 * D])
    self.ret_decay_logit = self.din("ret_decay_logit", [1, 2, 8])
    self.ret_w_out = self.din("ret_w_out", [1, 2 * D, D])
    self.c_ropeC = self.din("c_ropeC", [16, 128, 64])
    self.c_ropeS = self.din("c_ropeS", [16, 128, 64])
    self.c_retE = self.din("c_retE", [2, 128, 128])
    self.c_retM = self.din("c_retM", [2, 128, 128])
    self.c_retqe = self.din("c_retqe", [2, 128])
    self.c_retke = self.din("c_retke", [128, 2])
    self.qT_scr = self.dscr("qT_scr", [8, 128, NT], BF16)
    self.kT_scr = self.dscr("kT_scr", [8, 128, NT], BF16)
    self.ktok_scr = self.dscr("ktok_scr", [NT, D], BF16)
    self.v_scr = self.dscr("v_scr", [NT, 2 * D], BF16)
    self.gate_scr = self.dscr("gate_scr", [NT, 2 * D], BF16)
    self.of_scr = self.dscr("of_scr", [NT, 2 * D])


def _ret_mixer(self, grp):
    V, A, T, G = self.V, self.A, self.T, self.G
    t0, n = self.trange(grp)
    nseq = 1 if grp == 1 else NPS
    L = n // nseq
    nch = L // 128
    self.arena_reset()
    sb = self.asb
    wblk = [sb(f"rwb{i}", [128, 8, 512], BF16) for i in range(2)]
    lgt = sb("lgt", [128, 16]); kdt = sb("kdt", [128, 16]); cdt = sb("cdt", [128, 16]); ke = sb("ke", [128, 2])
    Et = sb("Et", [128, 2, 128]); Mt = sb("Mt", [128, 2, 128]); qe = sb("qe", [128, 2, 128])
    Dtab = sb("Dtab", [128, 16, 128]); qdtab = sb("qdtab", [128, 16, 128])
    rc = sb("rc", [128, 64]); rs = sb("rs", [128, 64])
    pq = sb("pq", [128, 512]); pq2 = sb("pq2", [128, 512]); pt1 = sb("pt1", [128, 512])
    pbf = [sb(f"pbf{i}", [128, 512], BF16) for i in range(2)]
    trb = sb("trb", [128, 4, 128], BF16)
    S = sb("S", [128, 8, 256]); Sb = sb("Sb", [128, 8, 256], BF16)
    qTc = [sb(f"qTc{i}", [128, 8, 128], BF16) for i in range(2)]
    kTc = [sb(f"kTc{i}", [128, 8, 128], BF16) for i in range(2)]
    ktc = [sb(f"ktc{i}", [128, 1024], BF16) for i in range(2)]
    vc = [sb(f"vc{i}", [128, 2048], BF16) for i in range(2)]
    gc = sb("gc", [128, 2048], BF16)
    ofc = sb("ofc", [128, 2048])
    ot = sb("ot", [128, 2048])
    ybf = sb("ybf", [128, 2048], BF16)
    yTt = sb("yTt", [128, 16, 128], BF16)
    attb2 = [sb(f"attb{i}", [128, 128], BF16) for i in range(2)]
    qd2 = [sb(f"qd{i}", [128, 128], BF16) for i in range(2)]
    kd2 = [sb(f"kd{i}", [128, 128], BF16) for i in range(2)]
    rst = sb("rst", [128, 24])
    self.ld(lgt, self.ret_decay_logit[0].rearrange("d h -> (d h)").partition_broadcast(128), w=["lgt"])
    self.ld(ke, self.c_retke, w=["ke"])
    for d in range(2):
        self.ld(Et[:, d], self.c_retE[d], w=["Et"])
        self.ld(Mt[:, d], self.c_retM[d], w=["Mt"])
        self.ld(qe[:, d], self.c_retqe[d].partition_broadcast(128), w=["qe"])
    A(lambda e: e.activation(out=lgt, in_=lgt, func=AF.Exp, scale=-1.0), ["lgt"], ["lgt"])
    V(lambda e: e.tensor_scalar(out=lgt, in0=lgt, scalar1=1.0, scalar2=None, op0=ALU.add), ["lgt"], ["lgt"])
    A(lambda e: e.activation(out=lgt, in_=lgt, func=AF.Ln), ["lgt"], ["lgt"])
    V(lambda e: e.tensor_scalar(out=lgt, in0=lgt, scalar1=-1.0, scalar2=None, op0=ALU.mult), ["lgt"], ["lgt"])
    for d in range(2):
        for h in range(8):
            c = d * 8 + h
            A(lambda e: e.activation(out=Dtab[:, c], in_=Et[:, d], func=AF.Exp, scale=lgt[:, c:c + 1]), ["Et", "lgt"], ["Dtab"])
            V(lambda e: e.tensor_tensor(out=Dtab[:, c], in0=Dtab[:, c], in1=Mt[:, d], op=ALU.mult), ["Dtab", "Mt"], ["Dtab"])
            A(lambda e: e.activation(out=qdtab[:, c], in_=qe[:, d], func=AF.Exp, scale=lgt[:, c:c + 1]), ["qe", "lgt"], ["qdtab"])
            A(lambda e: e.activation(out=kdt[:, c:c + 1], in_=ke[:, d:d + 1], func=AF.Exp, scale=lgt[:, c:c + 1]), ["ke", "lgt"], ["kdt"])
    A(lambda e: e.activation(out=cdt, in_=lgt, func=AF.Exp, scale=128.0), ["lgt"], ["cdt"])
    gk = lambda nm: (nm, grp)
    def ld_wb(cb_):
        self.ldc(wblk[cb_ % 2], self.ret_w_in[0][:, cb_ * 512:(cb_ + 1) * 512].rearrange("(k p) n -> p k n", p=128), w=[f"rwb{cb_ % 2}"])
    ld_wb(0)
    for cb in range(12):
        wb, wk = wblk[cb % 2], f"rwb{cb % 2}"
        if cb + 1 < 12:
            ld_wb(cb + 1)
        for tt in range(n // 128):
            ts = slice(tt * 128, (tt + 1) * 128)
            gts = slice(t0 + tt * 128, t0 + (tt + 1) * 128)
            pp = self.ps[tt % 2]
            pk = f"ps{tt % 2}"
            for kk in range(8):
                T(lambda e: e.matmul(pp, lhsT=self.hT[:, kk, ts], rhs=wb[:, kk, :], start=(kk == 0), stop=(kk == 7)),
                  ["hT", wk], [pk])
            ob = pbf[tt % 2]
            obk = f"pbf{tt % 2}"
            if cb < 4:
                isk = cb >= 2
                sc = (128.0 ** -0.5) if isk else 1.0
                if grp == 1:
                    if True:
                        self.ld(rc, self.c_ropeC[tt], w=["rc"])
                        self.ld(rs, self.c_ropeS[tt], w=["rs"])
                    A(lambda e: e.activation(out=pq, in_=pp, func=AF.Copy, scale=sc), [pk], ["pq"])
                    v5 = lambda t: t.rearrange("p (h a b f) -> p h a b f", h=4, a=2, b=2)
                    x1 = v5(pq)[:, :, :, 0, :]
                    x2 = v5(pq)[:, :, :, 1, :]
                    cosb = rc.rearrange("p (a f) -> p a f", a=2).unsqueeze(1).to_broadcast([128, 4, 2, 32])
                    sinb = rs.rearrange("p (a f) -> p a f", a=2).unsqueeze(1).to_broadcast([128, 4, 2, 32])
                    o1 = v5(pq2)[:, :, :, 0, :]
                    o2 = v5(pq2)[:, :, :, 1, :]
                    u1 = v5(pt1)[:, :, :, 0, :]
                    u2 = v5(pt1)[:, :, :, 1, :]
                    V(lambda e: e.tensor_tensor(out=o1, in0=x1, in1=cosb, op=ALU.mult), ["pq", "rc"], ["pq2"])
                    V(lambda e: e.tensor_tensor(out=u1, in0=x2, in1=sinb, op=ALU.mult), ["pq", "rs"], ["pt1"])
                    G(lambda e: e.tensor_tensor(out=o2, in0=x1, in1=sinb, op=ALU.mult), ["pq", "rs"], ["pq2b"])
                    G(lambda e: e.tensor_tensor(out=u2, in0=x2, in1=cosb, op=ALU.mult), ["pq", "rc"], ["pt1b"])
                    V(lambda e: e.tensor_tensor(out=v5(ob)[:, :, :, 0, :], in0=o1, in1=u1, op=ALU.subtract), ["pq2", "pt1"], [obk])
                    V(lambda e: e.tensor_tensor(out=v5(ob)[:, :, :, 1, :], in0=o2, in1=u2, op=ALU.add), ["pq2b", "pt1b"], [obk])
                else:
                    A(lambda e: e.activation(out=ob, in_=pp, func=AF.Copy, scale=sc), [pk], [obk])
                if isk:
                    self.ld(self.ktok_scr[gts, (cb - 2) * 512:(cb - 1) * 512], ob, r=[obk], w=[gk("ktok")])
                for hh in range(4):
                    ptr = self.ps[2].bitcast(BF16)[:, hh * 128:(hh + 1) * 128]
                    T(lambda e: e.transpose(ptr, ob[:, hh * 128:(hh + 1) * 128], self.identb), [obk, "identb"], ["ps2"])
                V(lambda e: e.tensor_copy(out=trb.rearrange("p a b -> p (a b)"), in_=self.ps[2].bitcast(BF16)[:, 0:512]), ["ps2"], ["trb"])
                dst = self.kT_scr if isk else self.qT_scr
                h0 = (cb % 2) * 4
                self.ld(dst[h0:h0 + 4, :, gts].rearrange("h p t -> p h t"), trb, r=["trb"], w=[gk("kT" if isk else "qT")])
            elif cb < 8:
                A(lambda e: e.activation(out=ob, in_=pp, func=AF.Copy), [pk], [obk])
                self.ld(self.v_scr[gts, (cb - 4) * 512:(cb - 3) * 512], ob, r=[obk], w=[gk("v")])
            else:
                A(lambda e: e.activation(out=ob, in_=pp, func=AF.Silu), [pk], [obk])
                self.ld(self.gate_scr[gts, (cb - 8) * 512:(cb - 7) * 512], ob, r=[obk], w=[gk("gate")])
    for s in range(nseq):
        for d in range(2):
            if grp == 1:
                self.ld(S, self.stret[d].rearrange("h p e -> p h e"), w=["S"])
            else:
                V(lambda e: e.memset(S, 0.0), [], ["S"])
            V(lambda e: e.tensor_copy(out=Sb, in_=S), ["S"], ["Sb"])
            order = range(nch) if d == 0 else range(nch - 1, -1, -1)
            for ci, c in enumerate(order):
                b = ci % 2
                ts = slice(t0 + s * L + c * 128, t0 + s * L + (c + 1) * 128)
                self.ld(qTc[b], self.qT_scr[:, :, ts].rearrange("h p t -> p h t"), r=[gk("qT")], w=[f"qTc{b}"])
                self.ld(kTc[b], self.kT_scr[:, :, ts].rearrange("h p t -> p h t"), r=[gk("kT")], w=[f"kTc{b}"])
                self.ld(ktc[b], self.ktok_scr[ts, :], r=[gk("ktok")], w=[f"ktc{b}"])
                self.ld(vc[b], self.v_scr[ts, :], r=[gk("v")], w=[f"vc{b}"])
                if d == 1:
                    self.ld(ofc, self.of_scr[ts, :], r=[gk("of")], w=["ofc"])
                    self.ld(gc, self.gate_scr[ts, :], r=[gk("gate")], w=["gc"])
                for h in range(8):
                    cI = d * 8 + h
                    hb = h % 2
                    attb, qd, kd = attb2[hb], qd2[hb], kd2[hb]
                    attk, qdk, kdk = f"attb{hb}", f"qd{hb}", f"kd{hb}"
                    pak = "ps3" if hb == 0 else "ps0"
                    pa = (self.ps[3] if hb == 0 else self.ps[0])[:, 0:128]
                    T(lambda e: e.matmul(pa, lhsT=kTc[b][:, h, :], rhs=qTc[b][:, h, :], start=True, stop=True),
                      [f"kTc{b}", f"qTc{b}"], [pak])
                    V(lambda e: e.tensor_tensor(out=attb, in0=pa, in1=Dtab[:, cI], op=ALU.mult), [pak, "Dtab"], [attk])
                    G(lambda e: e.tensor_tensor(out=qd, in0=qTc[b][:, h, :], in1=qdtab[:, cI], op=ALU.mult), [f"qTc{b}", "qdtab"], [qdk])
                    A(lambda e: e.activation(out=kd, in_=ktc[b][:, h * 128:(h + 1) * 128], func=AF.Copy, scale=kdt[:, cI:cI + 1]),
                      [f"ktc{b}", "kdt"], [kdk])
                    po = self.ps[4 + (h % 2)][:, 0:256]
                    pok = f"ps{4 + (h % 2)}"
                    T(lambda e: e.matmul(po, lhsT=attb, rhs=vc[b][:, h * 256:(h + 1) * 256], start=True, stop=False),
                      [attk, f"vc{b}"], [pok])
                    T(lambda e: e.matmul(po, lhsT=qd, rhs=Sb[:, h, :], start=False, stop=True), [qdk, "Sb"], [pok])
                    psu = self.ps[6 + (h % 2)][:, 0:256]
                    psk = f"ps{6 + (h % 2)}"
                    T(lambda e: e.matmul(psu, lhsT=kd, rhs=vc[b][:, h * 256:(h + 1) * 256], start=True, stop=True),
                      [kdk, f"vc{b}"], [psk])
                    if d == 0:
                        A(lambda e: e.activation(out=ot[:, h * 256:(h + 1) * 256], in_=po, func=AF.Copy), [pok], ["ot"])
                    else:
                        V(lambda e: e.tensor_tensor(out=ot[:, h * 256:(h + 1) * 256], in0=po, in1=ofc[:, h * 256:(h + 1) * 256],
                                                    op=ALU.add), [pok, "ofc"], ["ot"])
                    V(lambda e: e.scalar_tensor_tensor(out=S[:, h, :], in0=S[:, h, :], scalar=cdt[:, cI:cI + 1], in1=psu,
                                                       op0=ALU.mult, op1=ALU.add), ["S", "cdt", psk], ["S"])
                    A(lambda e: e.activation(out=Sb[:, h, :], in_=S[:, h, :], func=AF.Copy), ["S"], ["Sb"])
                if d == 0:
                    self.ld(self.of_scr[ts, :], ot, r=["ot"], w=[gk("of")])
                else:
                    for h in range(8):
                        A(lambda e: e.activation(out=ofc[:, h * 256:(h + 1) * 256], in_=ot[:, h * 256:(h + 1) * 256], func=AF.Square,
                                                 accum_out=rst[:, h:h + 1]), ["ot"], ["ofc", "rst"])
                    V(lambda e: e.tensor_scalar(out=rst[:, 8:16], in0=rst[:, 0:8], scalar1=1.0 / 256, scalar2=EPS,
                                                op0=ALU.mult, op1=ALU.add), ["rst"], ["rst"])
                    A(lambda e: e.activation(out=rst[:, 8:16], in_=rst[:, 8:16], func=AF.Sqrt), ["rst"], ["rst"])
                    V(lambda e: e.reciprocal(out=rst[:, 16:24], in_=rst[:, 8:16]), ["rst"], ["rst"])
                    for h in range(8):
                        V(lambda e: e.scalar_tensor_tensor(out=ybf[:, h * 256:(h + 1) * 256], in0=ot[:, h * 256:(h + 1) * 256],
                                                           scalar=rst[:, 16 + h:17 + h], in1=gc[:, h * 256:(h + 1) * 256],
                                                           op0=ALU.mult, op1=ALU.mult), ["ot", "rst", "gc"], ["ybf"])
                    for k4 in range(4):
                        for kk in range(4):
                            k = k4 * 4 + kk
                            ptr = self.ps[2].bitcast(BF16)[:, kk * 128:(kk + 1) * 128]
                            T(lambda e: e.transpose(ptr, ybf[:, k * 128:(k + 1) * 128], self.identb), ["ybf", "identb"], ["ps2"])
                        A(lambda e: e.activation(out=yTt[:, k4 * 4:(k4 + 1) * 4, :].rearrange("p a b -> p (a b)"),
                                                 in_=self.ps[2].bitcast(BF16)[:, 0:512], func=AF.Copy), ["ps2"], ["yTt"])
                    self.ld(self.gscr[0:16, :, ts].rearrange("k p t -> p k t"), yTt, r=["yTt"], w=[("gscr", grp)])
            if grp == 0:
                self.ld(self.nret[s, d].rearrange("h p e -> p h e"), S, r=["S"], w=["nret"])
    self.ysrc = (self.gscr, ("gscr", grp))
    return 2 * D


K.ret_decl = _ret_decl
K.ret_mixer = _ret_mixer


HY_FT = {2048: 17, 256: 3}


def _hy_decl(self):
    self.hy_w_in = self.din("hy_w_in", [1, D, 8 * D])
    self.hy_conv_w = self.din("hy_conv_w", [1, 3, 6 * D])
    self.hy_conv_b = self.din("hy_conv_b", [1, 6 * D])
    self.hy_f_w1 = self.din("hy_f_w1", [1, 33, 64])
    self.hy_f_b1 = self.din("hy_f_b1", [1, 64])
    self.hy_f_w2 = self.din("hy_f_w2", [1, 64, 64])
    self.hy_f_b2 = self.din("hy_f_b2", [1, 64])
    self.hy_f_w3 = self.din("hy_f_w3", [1, 64, 8 * D])
    self.hy_skip = self.din("hy_skip", [1, 2, 2 * D])
    self.hy_w_out = self.din("hy_w_out", [1, 2 * D, D])
    self.c_absd = self.din("c_absd", [2 * D])
    self.c_ones = self.din("c_ones", [128, 128])
    self.hyc = {}
    for L in (2048, 256):
        FT = HY_FT[L]
        self.hyc[L] = dict(
            feat=self.din(f"c_feat{L}", [33, L]),
            tneg=self.din(f"c_tneg{L}", [128, L // 128]),
            C=self.din(f"c_C{L}", [FT, 128, L // 128, 128], BF16), S=self.din(f"c_S{L}", [FT, 128, L // 128, 128], BF16),
            IC=self.din(f"c_IC{L}", [L // min(512, L), 128, FT, min(512, L)], BF16),
            IS=self.din(f"c_IS{L}", [L // min(512, L), 128, FT, min(512, L)], BF16))
    self.vT_scr = self.dscr("vT_scr", [16, 128, NT])
    self.x1T_scr = self.dscr("x1T_scr", [16, 128, NT])
    self.x2T_scr = self.dscr("x2T_scr", [16, 128, NT])
    self.z1T_scr = self.dscr("z1T_scr", [16, 128, NT])
    self.sgT_scr = self.dscr("sgT_scr", [16, 128, NT], BF16)
    self.ztok_scr = self.dscr("ztok_scr", [2, NT, 2 * D], BF16)
    self.Eo_scr = self.dscr("Eo_scr", [2, 2, 2048, 2 * D], BF16)
    self.KH_scr = self.dscr("KH_scr", [2, 2, 17 * 128, 2 * D])


def _sin_any(self, out, x, rk, wk, tmpf, tmpi, tmpk, biasp):
    V, A = self.V, self.A
    V(lambda e: e.tensor_scalar(out=out, in0=x, scalar1=biasp, scalar2=1.0 / TWO_PI, op0=ALU.add, op1=ALU.mult), rk, wk)
    V(lambda e: e.tensor_copy(out=tmpi, in_=out), wk, [tmpk + "i"])
    V(lambda e: e.tensor_copy(out=tmpf, in_=tmpi), [tmpk + "i"], [tmpk])
    V(lambda e: e.tensor_tensor(out=out, in0=out, in1=tmpf, op=ALU.subtract), wk + [tmpk], wk)
    V(lambda e: e.tensor_scalar(out=tmpf, in0=out, scalar1=0.5, scalar2=None, op0=ALU.is_gt), wk, [tmpk])
    V(lambda e: e.tensor_tensor(out=out, in0=out, in1=tmpf, op=ALU.subtract), wk + [tmpk], wk)
    V(lambda e: e.tensor_scalar(out=tmpf, in0=out, scalar1=-0.5, scalar2=None, op0=ALU.is_lt), wk, [tmpk])
    V(lambda e: e.tensor_tensor(out=out, in0=out, in1=tmpf, op=ALU.add), wk + [tmpk], wk)
    A(lambda e: e.activation(out=out, in_=out, func=AF.Sin, scale=6.283185), wk, wk)


def _hy_filters(self, grp):
    V, A, T, G = self.V, self.A, self.T, self.G
    L = LS if grp == 1 else LP
    FT, LT = HY_FT[L], L // 128
    hc = self.hyc[L]
    self.arena_reset()
    sb = self.asb
    w1 = sb("hw1", [33, 64]); w2 = sb("hw2", [64, 64]); b1 = sb("hb1", [64, 1]); b2 = sb("hb2", [64, 1])
    feat = sb("hfeat", [33, L]); z1 = sb("hz1", [64, L]); z2 = sb("hz2", [64, L])
    tf = sb("htf", [64, 512]); ti = sb("hti", [64, 512], I32)
    w3b = [sb(f"hw3{i}", [64, 512]) for i in range(2)]
    absd = sb("habsd", [128, 2 * D]); tneg = sb("htneg", [128, LT]); ones = sb("hones", [128, 128], BF16)
    wins = [sb(f"hwin{i}", [128, 512]) for i in range(2)]
    fds = [[sb(f"hfd{j}{i}", [128, 512]) for i in range(2)] for j in range(2)]
    fabs = [sb(f"hfab{i}", [128, 512], BF16) for i in range(2)]
    ebs = [[sb(f"heb{j}{i}", [128, 512], BF16) for i in range(2)] for j in range(2)]
    rn = sb("hrn", [128, 2, 2 * D])
    Eb = sb("hE", [128, LT, 512], BF16); Ob = sb("hO", [128, LT, 512], BF16)
    Cs = [sb(f"hCs{i}", [128, LT, 128], BF16) for i in range(2)]
    Ss = [sb(f"hSs{i}", [128, LT, 128], BF16) for i in range(2)]
    ko = [fds[0][0], fds[0][1]]
    self.ld(w1, self.hy_f_w1[0], w=["hw1"]); self.ld(w2, self.hy_f_w2[0], w=["hw2"])
    self.ld(b1, self.hy_f_b1[0].rearrange("(p o) -> p o", o=1), w=["hb1"])
    self.ld(b2, self.hy_f_b2[0].rearrange("(p o) -> p o", o=1), w=["hb2"])
    self.ld(feat, hc["feat"], w=["hfeat"])
    self.ld(absd, self.c_absd.partition_broadcast(128), w=["habsd"])
    self.ld(tneg, hc["tneg"], w=["htneg"])
    self.ldc(ones, self.c_ones, w=["hones"])
    V(lambda e: e.tensor_scalar(out=b1, in0=b1, scalar1=16.0 * math.pi, scalar2=None, op0=ALU.add), ["hb1"], ["hb1"])
    V(lambda e: e.tensor_scalar(out=b2, in0=b2, scalar1=16.0 * math.pi, scalar2=None, op0=ALU.add), ["hb2"], ["hb2"])
    BW = min(512, L)
    for (src, srck, wt, wtk, bb, bbk, dst, dstk, kdim) in ((feat, "hfeat", w1, "hw1", b1, "hb1", z1, "hz1", 33),
                                                          (z1, "hz1", w2, "hw2", b2, "hb2", z2, "hz2", 64)):
        for tb in range(L // BW):
            pp = self.ps[0][0:64, 0:BW]
            T(lambda e: e.matmul(pp, lhsT=wt[0:kdim, :], rhs=src[0:kdim, tb * BW:(tb + 1) * BW], start=True, stop=True),
              [srck, wtk], ["ps0"])
            self.sin_any(dst[:, tb * BW:(tb + 1) * BW], pp, ["ps0", bbk], [dstk], tf[:, 0:BW], ti[:, 0:BW], "htf", bb[:, 0:1])
    for o in range(2):
        for cb in range(4):
            cs = slice(cb * 512, (cb + 1) * 512)
            for dr in range(2):
                col0 = dr * 4096 + o * 2048 + cb * 512
                self.ld(w3b[dr], self.hy_f_w3[0][:, col0:col0 + 512], w=[f"hw3{dr}"])
            pacc = self.ps[3]
            for lt in range(LT):
                pb_ = lt % 2
                win, fd, eb = wins[pb_], fds[pb_], ebs[pb_]
                wink = f"hwin{pb_}"
                A(lambda e: e.activation(out=win, in_=absd[:, cs], func=AF.Exp, scale=tneg[:, lt:lt + 1]), ["habsd", "htneg"], [wink])
                for dr in range(2):
                    fab, fabk = fabs[dr], f"hfab{dr}"
                    pf = self.ps[(1 + dr) if pb_ == 0 else (5 + dr)]
                    pfk = f"ps{(1 + dr) if pb_ == 0 else (5 + dr)}"
                    T(lambda e: e.matmul(pf, lhsT=z2[:, lt * 128:(lt + 1) * 128], rhs=w3b[dr], start=True, stop=True),
                      ["hz2", f"hw3{dr}"], [pfk])
                    V(lambda e: e.tensor_tensor(out=fd[dr], in0=pf, in1=win, op=ALU.mult), [pfk, wink], [f"hfd{pb_}{dr}"])
                    A(lambda e: e.activation(out=fab, in_=fd[dr], func=AF.Abs), [f"hfd{pb_}{dr}"], [fabk])
                    T(lambda e: e.matmul(pacc, lhsT=ones, rhs=fab, start=(lt == 0 and dr == 0), stop=(lt == LT - 1 and dr == 1)),
                      ["hones", fabk], ["ps3"])
                if lt == 0:
                    V(lambda e: e.memset(fd[1][0:1, :], 0.0), [], [f"hfd{pb_}1"])
                V(lambda e: e.tensor_tensor(out=eb[0], in0=fd[0], in1=fd[1], op=ALU.add), [f"hfd{pb_}0", f"hfd{pb_}1"], [f"heb{pb_}0"])
                V(lambda e: e.tensor_tensor(out=eb[1], in0=fd[1], in1=fd[0], op=ALU.subtract), [f"hfd{pb_}0", f"hfd{pb_}1"], [f"heb{pb_}1"])
                for eo in range(2):
                    self.ld(self.Eo_scr[o, eo, lt * 128:(lt + 1) * 128, cs], eb[eo], r=[f"heb{pb_}{eo}"], w=[("Eo", grp)])
            V(lambda e: e.tensor_scalar(out=rn[:, o, cs], in0=pacc, scalar1=EPS, scalar2=None, op0=ALU.add), ["ps3"], ["hrn"])
            V(lambda e: e.reciprocal(out=rn[:, o, cs], in_=rn[:, o, cs]), ["hrn"], ["hrn"])
    for o in range(2):
        for cb in range(4):
            cs = slice(cb * 512, (cb + 1) * 512)
            self.ld(Eb, self.Eo_scr[o, 0, 0:L, cs].rearrange("(lt p) c -> p lt c", p=128), r=[("Eo", grp)], w=["hE"])
            self.ld(Ob, self.Eo_scr[o, 1, 0:L, cs].rearrange("(lt p) c -> p lt c", p=128), r=[("Eo", grp)], w=["hO"])
            for ft in range(FT):
                b = ft % 2
                fs = slice(ft * 128, (ft + 1) * 128)
                self.ld(Cs[b], hc["C"][ft], w=[f"hCs{b}"])
                self.ld(Ss[b], hc["S"][ft], w=[f"hSs{b}"])
                for ri, (tab, tabk, dat, datk) in enumerate(((Cs[b], f"hCs{b}", Eb, "hE"), (Ss[b], f"hSs{b}", Ob, "hO"))):
                    pk_ = self.ps[4 + ri]
                    for lt in range(LT):
                        T(lambda e: e.matmul(pk_, lhsT=tab[:, lt, :], rhs=dat[:, lt, :], start=(lt == 0), stop=(lt == LT - 1)),
                          [tabk, datk], [f"ps{4 + ri}"])
                    V(lambda e: e.tensor_tensor(out=ko[ri], in0=pk_, in1=rn[:, o, cs], op=ALU.mult), [f"ps{4 + ri}", "hrn"], [f"hfd0{ri}"])
                    self.ld(self.KH_scr[o, ri, fs, cs], ko[ri], r=[f"hfd0{ri}"], w=[("KH", grp)])


def _hy_mixer(self, grp):
    V, A, T, G = self.V, self.A, self.T, self.G
    t0, n = self.trange(grp)
    nseq = 1 if grp == 1 else NPS
    L = n // nseq
    FT, LT = HY_FT[L], L // 128
    hc = self.hyc[L]
    self.hy_filters(grp)
    self.arena_reset()
    sb = self.asb
    wch = [sb(f"ywc{i}", [128, 8, 128], BF16) for i in range(2)]
    PBs = [sb(f"yPB{i}", [128, nseq, L + 2]) for i in range(2)]
    cvs = [sb(f"ycv{i}", [128, nseq, L]) for i in range(2)]
    cvbs = [sb(f"ycvb{i}", [128, n], BF16) for i in range(2)]
    cwT = sb("ycwT", [128, 3, 48]); cbT = sb("ycbT", [128, 48]); skT = sb("yskT", [128, 2, 16])
    ztt = sb("yztt", [128, n // 128, 128], BF16)
    for j in range(3):
        self.ld(cwT[:, j, :], self.hy_conv_w[0, j].rearrange("(c p) -> p c", p=128), w=["ycwT"], allow_slow_non_contiguous=True)
    self.ld(cbT, self.hy_conv_b[0].rearrange("(c p) -> p c", p=128), w=["ycbT"], allow_slow_non_contiguous=True)
    for o in range(2):
        self.ld(skT[:, o, :], self.hy_skip[0, o].rearrange("(c p) -> p c", p=128), w=["yskT"], allow_slow_non_contiguous=True)
    for i in range(2):
        V(lambda e: e.memset(PBs[i], 0.0), [], [f"yPB{i}"])
    def ld_wch(c_):
        self.ldc(wch[c_ % 2], self.hy_w_in[0][:, c_ * 128:(c_ + 1) * 128].rearrange("(k p) n -> p k n", p=128), w=[f"ywc{c_ % 2}"])
    ld_wch(0)
    pend_tr = []
    for c in range(64):
        PB, cv, cvb = PBs[c % 2], cvs[c % 2], cvbs[c % 2]
        PBk, cvk, cvbk = f"yPB{c % 2}", f"ycv{c % 2}", f"ycvb{c % 2}"
        wc, wck = wch[c % 2], f"ywc{c % 2}"
        if c + 1 < 64:
            ld_wch(c + 1)
        for tb in range(n // 512):
            pp = self.ps[tb % 2]
            pk = f"ps{tb % 2}"
            for kk in range(8):
                T(lambda e: e.matmul(pp, lhsT=wc[:, kk, :], rhs=self.hT[:, kk, tb * 512:(tb + 1) * 512], start=(kk == 0), stop=(kk == 7)),
                  [wck, "hT"], [pk])
            if c < 48:
                if grp == 1:
                    dstp = PB[:, 0, 1 + tb * 512:1 + (tb + 1) * 512]
                    srcp = pp
                else:
                    dstp = PB[:, 2 * tb:2 * tb + 2, 1:L + 1]
                    srcp = pp.rearrange("p (s t) -> p s t", s=2)
                A(lambda e: e.activation(out=dstp, in_=srcp, func=AF.Copy), [pk], [PBk])
            else:
                A(lambda e: e.activation(out=cvb[:, tb * 512:(tb + 1) * 512], in_=pp, func=AF.Silu), [pk], [cvbk])
        while pend_tr:
            pend_tr.pop(0)()
        if c >= 48:
            self.ld(self.sgT_scr[c - 48, :, t0:t0 + n], cvb, r=[cvbk], w=[("sgT", grp)])
            continue
        A(lambda e: e.activation(out=cv, in_=PB[:, :, 1:L + 1], func=AF.Identity, scale=cwT[:, 1, c:c + 1], bias=cbT[:, c:c + 1]),
          [PBk, "ycwT", "ycbT"], [cvk])
        V(lambda e: e.scalar_tensor_tensor(out=cv, in0=PB[:, :, 0:L], scalar=cwT[:, 0, c:c + 1], in1=cv, op0=ALU.mult, op1=ALU.add),
          [PBk, "ycwT", cvk], [cvk])
        V(lambda e: e.scalar_tensor_tensor(out=cv, in0=PB[:, :, 2:L + 2], scalar=cwT[:, 2, c:c + 1], in1=cv, op0=ALU.mult, op1=ALU.add),
          [PBk, "ycwT", cvk], [cvk])
        cvf = cv.rearrange("p s t -> p (s t)")
        dst = (self.vT_scr, self.x1T_scr, self.x2T_scr)[c // 16]
        self.ld(dst[c % 16, :, t0:t0 + n], cvf, r=[cvk], w=[(("vT", "x1T", "x2T")[c // 16], grp)])
        if c < 16:
            V(lambda e: e.tensor_copy(out=cvb, in_=cvf), [cvk], [cvbk])

            def do_tr(c=c, cvb=cvb, cvbk=cvbk):
                for t4 in range(n // 512):
                    for kk in range(4):
                        tt = t4 * 4 + kk
                        ptr = self.ps[2].bitcast(BF16)[:, kk * 128:(kk + 1) * 128]
                        T(lambda e: e.transpose(ptr, cvb[:, tt * 128:(tt + 1) * 128], self.identb), [cvbk, "identb"], ["ps2"])
                    A(lambda e: e.activation(out=ztt[:, t4 * 4:(t4 + 1) * 4, :].rearrange("p a b -> p (a b)"),
                                             in_=self.ps[2].bitcast(BF16)[:, 0:512], func=AF.Copy), ["ps2"], ["yztt"])
                self.ld(self.ztok_scr[0, t0:t0 + n, c * 128:(c + 1) * 128].rearrange("(tt p) c -> p tt c", p=128), ztt,
                        r=["yztt"], w=[("ztok0", grp)])
            pend_tr.append(do_tr)
    while pend_tr:
        pend_tr.pop(0)()
    self.arena_reset()
    sb = self.asb
    TBW = min(512, L)
    NTB = L // TBW
    zt = sb("yzt", [128, LT, 512], BF16)
    Cs = [sb(f"yCs{i}", [128, LT, 128], BF16) for i in range(2)]
    Ss = [sb(f"ySs{i}", [128, LT, 128], BF16) for i in range(2)]
    Yh = sb("yYh", [128, FT, 2, 512], BF16)
    ICs = sb("yIC", [128, FT, TBW], BF16); ISs = sb("yIS", [128, FT, TBW], BF16)
    kre = [sb(f"ykre{i}", [128, 512]) for i in range(2)]; kim = [sb(f"ykim{i}", [128, 512]) for i in range(2)]
    u1 = sb("yu1", [128, 512]); u2 = sb("yu2", [128, 512])
    NB2 = 2 if L == LP else 1
    tas = [sb(f"yta{i}", [128, TBW]) for i in range(NB2)]; txs = [sb(f"ytx{i}", [128, TBW]) for i in range(NB2)]
    tgs = [sb(f"ytg{i}", [128, TBW], BF16) for i in range(NB2)]; tos = [sb(f"yto{i}", [128, TBW]) for i in range(NB2)]
    tobs = [sb(f"ytob{i}", [128, TBW], BF16) for i in range(NB2)]
    ztt2s = [sb(f"yztt2{i}", [128, TBW // 128, 128], BF16) for i in range(NB2)]
    skT = sb("yskT2", [128, 2, 16])
    for o in range(2):
        self.ld(skT[:, o, :], self.hy_skip[0, o].rearrange("(c p) -> p c", p=128), w=["yskT2"], allow_slow_non_contiguous=True)
    small = (L == LP)
    if small:
        Call = sb("yCall", [128, FT, LT, 128], BF16); Sall = sb("ySall", [128, FT, LT, 128], BF16)
        kra = sb("ykra", [128, FT, 512]); kia = sb("ykia", [128, FT, 512])
        for ft in range(FT):
            self.ld(Call[:, ft], hc["C"][ft], w=["yCall"])
            self.ld(Sall[:, ft], hc["S"][ft], w=["ySall"])
        self.ld(ICs, hc["IC"][0], w=["yIC"])
        self.ld(ISs, hc["IS"][0], w=["yIS"])
    for o in range(2):
        zprev = (self.vT_scr, ("vT", grp)) if o == 0 else (self.z1T_scr, ("z1T", grp))
        xg = (self.x1T_scr, ("x1T", grp)) if o == 0 else (self.x2T_scr, ("x2T", grp))
        for cb in range(4):
          cs = slice(cb * 512, (cb + 1) * 512)
          if small:
              self.ld(kra, self.KH_scr[o, 0, 0:FT * 128, cs].rearrange("(ft p) c -> p ft c", p=128), r=[("KH", grp)], w=["ykra"])
              self.ld(kia, self.KH_scr[o, 1, 0:FT * 128, cs].rearrange("(ft p) c -> p ft c", p=128), r=[("KH", grp)], w=["ykia"])
          for s in range(nseq):
                tq = t0 + s * L
                self.ld(zt, self.ztok_scr[o, tq:tq + L, cs].rearrange("(lt p) c -> p lt c", p=128), r=[(f"ztok{o}", grp)], w=["yzt"])
                for ft in range(FT):
                    b = ft % 2
                    fs = slice(ft * 128, (ft + 1) * 128)
                    if small:
                        Csb, Ssb, krb, kib = Call[:, ft], Sall[:, ft], kra[:, ft], kia[:, ft]
                        Ck, Sk, krk, kik = "yCall", "ySall", "ykra", "ykia"
                    else:
                        Csb, Ssb, krb, kib = Cs[b], Ss[b], kre[b], kim[b]
                        Ck, Sk, krk, kik = f"yCs{b}", f"ySs{b}", f"ykre{b}", f"ykim{b}"
                        self.ld(Cs[b], hc["C"][ft], w=[Ck])
                        self.ld(Ss[b], hc["S"][ft], w=[Sk])
                        self.ld(kre[b], self.KH_scr[o, 0, fs, cs], r=[("KH", grp)], w=[krk])
                        self.ld(kim[b], self.KH_scr[o, 1, fs, cs], r=[("KH", grp)], w=[kik])
                    pA, pB = self.ps[0 + 2 * b], self.ps[1 + 2 * b]
                    pAk, pBk = f"ps{0 + 2 * b}", f"ps{1 + 2 * b}"
                    for lt in range(LT):
                        T(lambda e: e.matmul(pA, lhsT=Csb[:, lt, :], rhs=zt[:, lt, :], start=(lt == 0), stop=(lt == LT - 1)),
                          [Ck, "yzt"], [pAk])
                    for lt in range(LT):
                        T(lambda e: e.matmul(pB, lhsT=Ssb[:, lt, :], rhs=zt[:, lt, :], start=(lt == 0), stop=(lt == LT - 1)),
                          [Sk, "yzt"], [pBk])
                    V(lambda e: e.tensor_tensor(out=u1, in0=pA, in1=krb, op=ALU.mult), [pAk, krk], ["yu1"])
                    V(lambda e: e.tensor_tensor(out=u2, in0=pB, in1=kib, op=ALU.mult), [pBk, kik], ["yu2"])
                    G(lambda e: e.tensor_tensor(out=Yh[:, ft, 0, :], in0=u1, in1=u2, op=ALU.add), ["yu1", "yu2"], ["yYh"])
                    V(lambda e: e.tensor_tensor(out=u1, in0=pA, in1=kib, op=ALU.mult), [pAk, kik], ["yu1"])
                    V(lambda e: e.tensor_tensor(out=u2, in0=pB, in1=krb, op=ALU.mult), [pBk, krk], ["yu2"])
                    V(lambda e: e.tensor_tensor(out=Yh[:, ft, 1, :], in0=u1, in1=u2, op=ALU.subtract), ["yu1", "yu2"], ["yYh"])
                for tb in range(NTB):
                    tsl = slice(tb * TBW, (tb + 1) * TBW)
                    gsl = slice(tq + tb * TBW, tq + (tb + 1) * TBW)
                    if not small:
                        self.ld(ICs, hc["IC"][tb], w=["yIC"])
                        self.ld(ISs, hc["IS"][tb], w=["yIS"])
                    for cc in range(4):
                        ch = cb * 4 + cc
                        bi_ = cc % NB2
                        ta, tx, tg, to, tob, ztt2 = tas[bi_], txs[bi_], tgs[bi_], tos[bi_], tobs[bi_], ztt2s[bi_]
                        tak, txk, tgk, tok, tobk, zt2k = f"yta{bi_}", f"ytx{bi_}", f"ytg{bi_}", f"yto{bi_}", f"ytob{bi_}", f"yztt2{bi_}"
                        pz = self.ps[4 + (cc % 2)][:, 0:TBW]
                        pzk = f"ps{4 + (cc % 2)}"
                        self.ld(ta, zprev[0][ch, :, gsl], r=[zprev[1]], w=[tak])
                        self.ld(tx, xg[0][ch, :, gsl], r=[xg[1]], w=[txk])
                        if o == 1:
                            self.ld(tg, self.sgT_scr[ch, :, gsl], r=[("sgT", grp)], w=[tgk])
                        for ft in range(FT):
                            T(lambda e: e.matmul(pz, lhsT=Yh[:, ft, 0, cc * 128:(cc + 1) * 128], rhs=ICs[:, ft, :], start=(ft == 0), stop=False),
                              ["yYh", "yIC"], [pzk])
                            T(lambda e: e.matmul(pz, lhsT=Yh[:, ft, 1, cc * 128:(cc + 1) * 128], rhs=ISs[:, ft, :], start=False, stop=(ft == FT - 1)),
                              ["yYh", "yIS"], [pzk])
                        V(lambda e: e.scalar_tensor_tensor(out=ta, in0=ta, scalar=skT[:, o, ch:ch + 1], in1=pz, op0=ALU.mult, op1=ALU.add),
                          [tak, "yskT2", pzk], [tak])
                        if o == 0:
                            G(lambda e: e.tensor_tensor(out=to, in0=ta, in1=tx, op=ALU.mult), [tak, txk], [tok])
                            self.ld(self.z1T_scr[ch, :, gsl], to, r=[tok], w=[("z1T", grp)])
                            A(lambda e: e.activation(out=tob, in_=to, func=AF.Copy), [tok], [tobk])
                            for kk in range(TBW // 128):
                                ptr = self.ps[6 + bi_].bitcast(BF16)[:, kk * 128:(kk + 1) * 128]
                                T(lambda e: e.transpose(ptr, tob[:, kk * 128:(kk + 1) * 128], self.identb), [tobk, "identb"], ["ps6" if bi_ == 0 else "ps7"])
                            A(lambda e: e.activation(out=ztt2.rearrange("p a b -> p (a b)"), in_=self.ps[6 + bi_].bitcast(BF16)[:, 0:TBW], func=AF.Copy),
                              ["ps6" if bi_ == 0 else "ps7"], [zt2k])
                            self.ld(self.ztok_scr[1, gsl, ch * 128:(ch + 1) * 128].rearrange("(tt p) c -> p tt c", p=128), ztt2,
                                    r=[zt2k], w=[("ztok1", grp)])
                        else:
                            G(lambda e: e.tensor_tensor(out=to, in0=ta, in1=tx, op=ALU.mult), [tak, txk], [tok])
                            G(lambda e: e.tensor_tensor(out=tob, in0=to, in1=tg, op=ALU.mult), [tok, tgk], [tobk])
                            self.ld(self.gscr[ch, :, gsl], tob, r=[tobk], w=[("gscr", grp)])
    self.ysrc = (self.gscr, ("gscr", grp))
    return 2 * D


K.hy_decl = _hy_decl
K.sin_any = _sin_any
K.hy_filters = _hy_filters
K.hy_mixer = _hy_mixer


def hy_consts():
    out = {}
    HY_BANDS = 16
    min_decay = math.log(1e-2) / 1.5
    max_decay = math.log(1e-2) / 0.3
    out["c_absd"] = np.abs(np.linspace(min_decay, max_decay, 2048, dtype=np.float32)).astype(np.float32)
    out["c_ones"] = np.ones((128, 128), np.float32)
    for L in (2048, 256):
        FT = HY_FT[L]
        N = 2 * L
        t = np.linspace(0.0, 1.0, L, dtype=np.float32)[:, None]
        w = (2.0 * np.float32(math.pi) * np.arange(L, dtype=np.float32)[:, None] / np.float32(L)).astype(np.float32)
        f = np.linspace(1e-4, HY_BANDS - 1.0, HY_BANDS, dtype=np.float32)[None, :]
        fw_ = (f * w).astype(np.float32)
        feat = np.concatenate([t, np.cos(fw_), -np.sin(fw_)], -1).astype(np.float32)
        out[f"c_feat{L}"] = np.ascontiguousarray(feat.T)
        out[f"c_tneg{L}"] = np.ascontiguousarray((-t[:, 0]).reshape(L // 128, 128).T)
        tt = np.arange(L, dtype=np.int64)[:, None]
        ff = np.arange(FT * 128, dtype=np.int64)[None, :]
        ang = 2.0 * np.pi * ((tt * ff) % N).astype(np.float64) / N
        valid = (ff <= L).astype(np.float64)
        C = np.cos(ang) * valid
        S = np.sin(ang) * valid
        wf = np.where((ff == 0) | (ff == L), 1.0, 2.0) * valid / N
        LT = L // 128
        TBW = min(512, L)
        def fwd_tile(M):
            return np.ascontiguousarray(M.reshape(LT, 128, FT, 128).transpose(2, 1, 0, 3)).astype(np.float32).astype(ml_dtypes.bfloat16)
        def inv_tile(M):
            return np.ascontiguousarray(M.reshape(FT, 128, L // TBW, TBW).transpose(2, 1, 0, 3)).astype(np.float32).astype(ml_dtypes.bfloat16)
        out[f"c_C{L}"] = fwd_tile(C)
        out[f"c_S{L}"] = fwd_tile(S)
        out[f"c_IC{L}"] = inv_tile((C * wf).T)
        out[f"c_IS{L}"] = inv_tile((-S * wf).T)
    return out
```

```python
import math
import numpy as np
import ml_dtypes
import concourse.bass as bass
import concourse.mybir as mybir
from concourse.bass_utils import run_bass_kernel_spmd

F32 = mybir.dt.float32
BF16 = mybir.dt.bfloat16
I32 = mybir.dt.int32
ALU = mybir.AluOpType
AF = mybir.ActivationFunctionType
AX = mybir.AxisListType

D = 1024
LS = 2048
LP = 256
NPS = 4
NT = LS + NPS * LP
EPS = 1e-6
TWO_PI = 2.0 * math.pi
ARENA_W = 31500

SAME_ENG_SYNC = True


class _PEProxy:
    def __init__(self, eng):
        self.eng = eng
        self.stop = True

    def matmul(self, *a, **kw):
        self.stop = bool(kw.get("stop", True))
        return self.eng.matmul(*a, **kw)

    def transpose(self, *a, **kw):
        self.stop = True
        return self.eng.transpose(*a, **kw)


class Fw:
    def __init__(self, nc, n_dma_sems=20):
        self.nc = nc
        self.engs = {}
        for name in ("tensor", "vector", "scalar", "gpsimd", "sync"):
            e = getattr(nc, name)
            self.engs[name] = dict(eng=e, sem=nc.alloc_semaphore("s_" + name), count=0, seen={})
        self.dma_pool = {}
        for q in ("sync", "gpsimd", "scalar"):
            self.dma_pool[q] = dict(
                sems=[nc.alloc_semaphore(f"d_{q}_{i}") for i in range(n_dma_sems)],
                vals=[0] * n_dma_sems, nxt=0)
        self.bufs = {}
        self.sem_owner = {id(E["sem"]): name for name, E in self.engs.items()}

    def _st(self, key):
        s = self.bufs.get(key)
        if s is None:
            s = dict(w=None, r=[])
            self.bufs[key] = s
        return s

    def _deps(self, reads, writes):
        deps = []
        for k in reads:
            s = self._st(k)
            if s["w"] is not None:
                deps.append(s["w"])
        for k in writes:
            s = self._st(k)
            if s["w"] is not None:
                deps.append(s["w"])
            deps.extend(s["r"])
        return deps

    def _wait(self, E, deps):
        best = {}
        for (sem, val) in deps:
            if sem is E["sem"] and not SAME_ENG_SYNC:
                continue
            k = id(sem)
            if k not in best or best[k][1] < val:
                best[k] = (sem, val)
        for k, (sem, val) in best.items():
            if E["seen"].get(k, 0) < val:
                E["eng"].wait_ge(sem, val)
                E["seen"][k] = val

    def _mark(self, tok, reads, writes):
        for k in reads:
            r = self._st(k)["r"]
            r.append(tok)
            if len(r) > 12:
                best = {}
                for (sem, val) in r:
                    if id(sem) not in best or best[id(sem)][1] < val:
                        best[id(sem)] = (sem, val)
                r[:] = list(best.values())
        for k in writes:
            s = self._st(k)
            s["w"] = tok
            s["r"] = []

    def op(self, eng, fn, reads=(), writes=()):
        E = self.engs[eng]
        self._wait(E, self._deps(reads, writes))
        if eng == "tensor":
            px = _PEProxy(E["eng"])
            ins = fn(px)
            if not px.stop:
                E.setdefault("pend", []).append((tuple(reads), tuple(writes)))
                return ins
            pend = E.get("pend", [])
            E["pend"] = []
            E["count"] += 1
            ins.then_inc(E["sem"], 1)
            tok = (E["sem"], E["count"])
            for (r, w) in pend:
                self._mark(tok, r, w)
            self._mark(tok, reads, writes)
            return ins
        ins = fn(E["eng"])
        E["count"] += 1
        ins.then_inc(E["sem"], 1)
        self._mark((E["sem"], E["count"]), reads, writes)
        return ins

    def dma(self, q, out, in_, reads=(), writes=(), **kw):
        E = self.engs[q]
        P = self.dma_pool[q]
        i = P["nxt"]
        P["nxt"] = (i + 1) % len(P["sems"])
        sem = P["sems"][i]
        deps = self._deps(reads, writes)
        if P["vals"][i] > 0:
            deps.append((sem, P["vals"][i]))
        self._wait(E, deps)
        ins = E["eng"].dma_start(out=out, in_=in_, **kw)
        P["vals"][i] += 16
        ins.then_inc(sem, 16)
        tok = (sem, P["vals"][i])
        self._mark(tok, reads, writes)
        return tok

    def barrier(self):
        toks = [(E["sem"], E["count"]) for E in self.engs.values() if E["count"] > 0]
        for P in self.dma_pool.values():
            for sem, val in zip(P["sems"], P["vals"]):
                if val > 0:
                    toks.append((sem, val))
        for E in self.engs.values():
            self._wait(E, toks)

    def finish(self, out_keys):
        E = self.engs["sync"]
        deps = []
        for k in out_keys:
            s = self._st(k)
            if s["w"] is not None:
                deps.append(s["w"])
        self._wait(E, deps)


def AP(t, off, dims):
    return bass.AP(t.tensor if hasattr(t, "tensor") else t, off, [list(d) for d in dims])


class K:
    def __init__(self, layers=(0, 1, 2, 3), final=True, dbg=False):
        self.dbg = dbg
        self.layers = layers
        self.final = final
        nc = self.nc = bass.Bass("TRN2", target_bir_lowering=False)
        self.fw = Fw(nc)
        self.inputs = {}
        self.build()

    def din(self, name, shape, dt=F32):
        t = self.nc.dram_tensor(name, list(shape), dt, kind="ExternalInput").ap()
        self.inputs[name] = (tuple(shape), dt)
        return t

    def dout(self, name, shape, dt=F32):
        return self.nc.dram_tensor(name, list(shape), dt, kind="ExternalOutput").ap()

    def dscr(self, name, shape, dt=F32):
        return self.nc.dram_tensor(name, list(shape), dt, kind="Internal").ap()

    def sb(self, name, shape, dt=F32):
        return self.nc.alloc_sbuf_tensor(name, list(shape), dt).ap()

    def arena_reset(self):
        self.fw.barrier()
        self.aoff = 0

    def asb(self, name, shape, dt=F32):
        n = 1
        for x in shape[1:]:
            n *= x
        words = n if dt in (F32, I32) else (n + 1) // 2
        words = (words + 7) // 8 * 8
        assert self.aoff + words <= ARENA_W, (name, self.aoff, words)
        v = self.arena[0:shape[0], self.aoff:self.aoff + words]
        self.aoff += words
        if dt not in (F32,):
            v = v.bitcast(dt)
        v = v[:, 0:n]
        if len(shape) > 2:
            names = " ".join(f"a{i}" for i in range(len(shape) - 1))
            kw = {f"a{i}": shape[i + 1] for i in range(len(shape) - 2)}
            v = v.rearrange(f"p ({names}) -> p {names}", **kw)
        return v

    def V(self, fn, r=(), w=()):
        return self.fw.op("vector", fn, r, w)

    def G(self, fn, r=(), w=()):
        return self.fw.op("gpsimd", fn, r, w)

    def A(self, fn, r=(), w=()):
        return self.fw.op("scalar", fn, r, w)

    def T(self, fn, r=(), w=()):
        return self.fw.op("tensor", fn, r, w)

    def ld(self, out, in_, r=(), w=(), q="sync", **kw):
        if q == "sync" and r and "DRam" in type(out.tensor).__name__ and "DRam" not in type(in_.tensor).__name__:
            engs = set()
            for k in r:
                st = self.fw.bufs.get(k)
                if st is None or st["w"] is None:
                    continue
                engs.add(self.fw.sem_owner.get(id(st["w"][0]), "dma"))
            if len(engs) == 1:
                e = engs.pop()
                if e in ("scalar", "gpsimd"):
                    q = e
                elif e == "vector":
                    q = "scalar"
        return self.fw.dma(q, out, in_, r, w, **kw)

    def ldc(self, out, in_, r=(), w=()):
        return self.fw.dma("gpsimd", out, in_, r, w)

    def build(self):
        nc = self.nc
        self.xs = self.din("xs", [LS, D])
        self.xp = self.din("xp", [NPS * LP, D])
        self.cvec = self.din("cvec", [2, D])
        self.st5 = self.din("st5", [2, 128, 128])
        self.stret = self.din("stret", [2, 8, 128, 256])
        self.norm_g = self.din("norm_g", [4, D])
        self.mod_w = self.din("mod_w", [4, D, 3 * D])
        self.mod_b = self.din("mod_b", [4, 3 * D])
        self.s5_w_in = self.din("s5_w_in", [2, D, 2 * D])
        self.s5_lam_re = self.din("s5_lam_re", [2, 2, 64, 64])
        self.s5_lam_im = self.din("s5_lam_im", [2, 2, 64, 64])
        self.s5_log_step = self.din("s5_log_step", [2, 2, 64])
        self.s5_b_re = self.din("s5_b_re", [2, 2, 64, 64, 16])
        self.s5_b_im = self.din("s5_b_im", [2, 2, 64, 64, 16])
        self.s5_c_re = self.din("s5_c_re", [2, 2, 64, 16, 64])
        self.s5_c_im = self.din("s5_c_im", [2, 2, 64, 16, 64])
        self.s5_d = self.din("s5_d", [2, D])
        self.s5_w_glu = self.din("s5_w_glu", [2, D, D])
        self.s5_b_glu = self.din("s5_b_glu", [2, D])
        self.s5_w_out = self.din("s5_w_out", [2, D, D])
        self.final_g = self.din("final_g", [D])
        self.ret_decl()
        self.hy_decl()
        self.c_identb = self.din("c_identb", [128, 128], BF16)
        self.c_identf = self.din("c_identf", [128, 128])
        self.c_pmask = self.din("c_pmask", [128, 8])
        self.c_bdmask = self.din("c_bdmask", [128, 128])
        self.ys = self.dout("ys", [LS, D])
        self.yp = self.dout("yp", [NPS * LP, D])
        self.ns5 = self.dout("ns5", [NPS, 2, 128, 128])
        self.nret = self.dout("nret", [NPS, 2, 8, 128, 256])
        self.xres = (self.dout if self.dbg else self.dscr)("xres", [NT, D])
        self.gscr = self.dscr("gscr", [16, 128, NT], BF16)
        self.g2scr = self.dscr("g2scr", [8, 128, NT], BF16)
        self.Tsb_scr = self.dscr("Tsb_scr", [2, 8, 128, 2048], BF16)
        self.Kc_scr = self.dscr("Kc_scr", [2, 8, 128, 1920], BF16)
        self.CA_scr = self.dscr("CA_scr", [2, 8, 128, 2, 2304], BF16)
        self.identb = self.sb("identb", [128, 128], BF16)
        self.identf = self.sb("identf", [128, 128])
        self.pmask = self.sb("pmask", [128, 8])
        self.bdmask = self.sb("bdmask", [128, 128])
        self.hT = self.sb("hT", [128, 8, LS], BF16)
        self.arena = self.sb("arena", [128, ARENA_W])
        self.aoff = 0
        self.wgs = [self.sb(f"wgs{i}", [128, 8, 128], BF16) for i in range(2)]
        self.wst = [self.sb("wst0", [128, 8, 512], BF16)] * 2
        self.xt = [self.sb(f"xt{i}", [128, D]) for i in range(2)]
        self.xn = [self.sb(f"xn{i}", [128, D], BF16) for i in range(2)]
        self.sq = self.sb("sq", [128, D])
        self.stat = self.sb("stat", [128, 8])
        self.modT = self.sb("modT", [128, 4, 24, 2])
        self.gsc = self.sb("gsc", [128, 4, 8, 2])
        self.gt_bc = self.sb("gt_bc", [128, 2, D])
        self.cT = self.sb("cT", [128, 8, 2])
        self.cTb = self.sb("cTb", [128, 8, 2], BF16)
        self.cTrep = self.sb("cTrep", [128, 2, 8, 128], BF16)
        self.ngT = self.sb("ngT", [128, 4, 8])
        self.mbT = self.sb("mbT", [128, 4, 24])
        self.mbg = None
        self.fgb = self.sb("fgb", [128, D])
        self.ylt = [self.sb("ylt", [128, 16, 128], BF16)] * 2
        self.ps = [nc.alloc_psum_tensor(f"ps{i}", [128, 512], F32).ap() for i in range(8)]

        f = self.fw
        self.ld(self.identb, self.c_identb, w=["identb"])
        self.ld(self.identf, self.c_identf, w=["identf"])
        self.ld(self.pmask, self.c_pmask, w=["pmask"])
        self.ld(self.bdmask, self.c_bdmask, w=["bdmask"])
        self.ld(self.fgb, self.final_g.partition_broadcast(128), w=["fgb"])

        self.mod_stage()
        out_keys = []
        nl = len(self.layers)
        for li, i in enumerate(self.layers):
            last = (li == nl - 1) and self.final
            first = (li == 0)
            self.gate_table(i)
            for grp in (1, 0):
                kind = i % 3
                self.prologue(i, grp, first)
                if kind == 0:
                    kdim = self.s5_mixer(i // 3, grp)
                    w_out = self.s5_w_out[i // 3]
                elif kind == 1:
                    kdim = self.ret_mixer(grp)
                    w_out = self.ret_w_out[0]
                else:
                    kdim = self.hy_mixer(grp)
                    w_out = self.hy_w_out[0]
                self.epilogue(i, grp, kdim, w_out, last)
        out_keys = ["ys", "yp", "ns5", "nret", "xres"]
        f.finish(out_keys)

    def trange(self, grp):
        return (0, LS) if grp == 1 else (LS, NPS * LP)

    def mod_stage(self):
        for r in range(2):
            self.ld(self.cT[:, :, r], self.cvec[r].rearrange("(k p) -> p k", p=128), w=["cT"], allow_slow_non_contiguous=True)
        self.A(lambda e: e.activation(out=self.cTb, in_=self.cT, func=AF.Silu), ["cT"], ["cTb"])
        for r in range(2):
            self.V(lambda e: e.tensor_copy(out=self.cTrep[:, r], in_=self.cTb[:, :, r:r + 1].to_broadcast([128, 8, 128])),
                   ["cTb"], ["cTrep"])
        for l in range(4):
            self.ld(self.ngT[:, l, :], self.norm_g[l].rearrange("(k p) -> p k", p=128), w=["ngT"], allow_slow_non_contiguous=True)
            self.ld(self.mbT[:, l, :], self.mod_b[l].rearrange("(k p) -> p k", p=128), w=["mbT"], allow_slow_non_contiguous=True)
        for i in self.layers:
            for half in range(4):
                wt = self.wst[half % 2]
                wk = "wst0"
                self.ldc(wt, self.mod_w[i, :, half * 512:(half + 1) * 512].rearrange("(k p) n -> p k n", p=128), w=[wk])
                for cc in range(4):
                    ch = half * 4 + cc
                    pt = self.ps[0][:, 0:2]
                    for k in range(8):
                        self.T(lambda e: e.matmul(pt, lhsT=wt[:, k, cc * 128:(cc + 1) * 128], rhs=self.cTb[:, k, :],
                                                  start=(k == 0), stop=(k == 7)), [wk, "cTb"], ["ps0"])
                    self.V(lambda e: e.tensor_tensor(out=self.modT[:, i, ch, :], in0=pt,
                                                     in1=self.mbT[:, i, ch:ch + 1].to_broadcast([128, 2]), op=ALU.add),
                           ["ps0", "mbT"], ["modT"])
            self.V(lambda e: e.tensor_scalar(out=self.gsc[:, i], in0=self.modT[:, i, 8:16, :], scalar1=1.0, scalar2=None,
                                             op0=ALU.add), ["modT"], ["gsc"])
            self.V(lambda e: e.tensor_tensor(out=self.gsc[:, i], in0=self.gsc[:, i],
                                             in1=self.ngT[:, i, :].unsqueeze(2).to_broadcast([128, 8, 2]), op=ALU.mult),
                   ["gsc", "ngT"], ["gsc"])

    def gate_table(self, i):
        self.mbg = self.sq
        self.ld(self.mbg, self.mod_b[i, 2 * D:3 * D].partition_broadcast(128), w=["sq"])
        for half in range(2):
            wt = self.wst[half % 2]
            wk = "wst0"
            self.ldc(wt, self.mod_w[i, :, 2 * D + half * 512: 2 * D + (half + 1) * 512].rearrange("(k p) n -> p k n", p=128),
                     w=[wk])
            for r in range(2):
                pt = self.ps[1]
                for k in range(8):
                    self.T(lambda e: e.matmul(pt, lhsT=self.cTrep[:, r, k, :], rhs=wt[:, k, :],
                                              start=(k == 0), stop=(k == 7)), [wk, "cTrep"], ["ps1"])
                self.V(lambda e: e.tensor_tensor(out=self.gt_bc[:, r, half * 512:(half + 1) * 512], in0=pt,
                                                 in1=self.mbg[:, half * 512:(half + 1) * 512], op=ALU.add),
                       ["ps1", "sq"], ["gt_bc"])

    def prologue(self, i, grp, first):
        t0, n = self.trange(grp)
        for tt in range(n // 128):
            b = tt % 2
            xt, xn = self.xt[b], self.xn[b]
            if first:
                src = self.xs[tt * 128:(tt + 1) * 128, :] if grp == 1 else self.xp[tt * 128:(tt + 1) * 128, :]
                rk = []
            else:
                src = self.xres[t0 + tt * 128: t0 + (tt + 1) * 128, :]
                rk = ["xres"]
            self.ld(xt, src, r=rk, w=[f"xt{b}"])
            self.rms_scale(xt, f"xt{b}", xn, f"xn{b}")
            for k in range(8):
                pt = self.ps[2 + (k % 2)].bitcast(BF16)[:, 0:128]
                pk = f"ps{2 + (k % 2)}"
                self.T(lambda e: e.transpose(pt, xn[:, k * 128:(k + 1) * 128], self.identb), [f"xn{b}", "identb"], [pk])
                self.A(lambda e: e.activation(out=self.hT[:, k, tt * 128:(tt + 1) * 128], in_=pt, func=AF.Identity,
                                              scale=self.gsc[:, i, k, grp:grp + 1], bias=self.modT[:, i, k, grp:grp + 1]),
                       [pk, "gsc", "modT"], ["hT"])

    def rms_scale(self, xt, xk, out, ok, gtab=None):
        self.A(lambda e: e.activation(out=self.sq, in_=xt, func=AF.Square, accum_out=self.stat[:, 0:1]), [xk], ["sq", "stat"])
        self.V(lambda e: e.tensor_scalar(out=self.stat[:, 1:2], in0=self.stat[:, 0:1], scalar1=1.0 / D, scalar2=EPS,
                                         op0=ALU.mult, op1=ALU.add), ["stat"], ["stat"])
        self.A(lambda e: e.activation(out=self.stat[:, 3:4], in_=self.stat[:, 1:2], func=AF.Sqrt), ["stat"], ["stat"])
        self.V(lambda e: e.reciprocal(out=self.stat[:, 2:3], in_=self.stat[:, 3:4]), ["stat"], ["stat"])
        if gtab is None:
            self.V(lambda e: e.tensor_scalar(out=out, in0=xt, scalar1=self.stat[:, 2:3], scalar2=None, op0=ALU.mult),
                   [xk, "stat"], [ok])
        else:
            self.V(lambda e: e.scalar_tensor_tensor(out=out, in0=xt, scalar=self.stat[:, 2:3], in1=gtab,
                                                    op0=ALU.mult, op1=ALU.mult), [xk, "stat", "fgb"], [ok])

    def epilogue(self, i, grp, kdim, w_out, last):
        t0, n = self.trange(grp)
        kc = kdim // 128
        if kdim > D:
            self.arena_reset()
            self.wres = self.asb("wres", [128, 16, 1024], BF16)
            wrk = ["wres"]
        else:
            self.wres = self.SS.bitcast(BF16).rearrange("p (k n) -> p k n", k=8)
            wrk = ["SS", "SS2"]
        self.ldc(self.wres[:, 0:kc, :], w_out.rearrange("(k p) n -> p k n", p=128), w=wrk)
        yb = self.xn
        for tt in range(n // 128):
            b = tt % 2
            xt = self.xt[b]
            src = self.xres[t0 + tt * 128: t0 + (tt + 1) * 128, :]
            if i == self.layers[0]:
                src = self.xs[tt * 128:(tt + 1) * 128, :] if grp == 1 else self.xp[tt * 128:(tt + 1) * 128, :]
                rk = []
            else:
                rk = ["xres"]
            self.ld(xt, src, r=rk, w=[f"xt{b}"])
            yt = self.ylt[b]
            ysrc, ykey = self.ysrc
            self.ld(yt[:, 0:kc, :], ysrc[0:kc, :, t0 + tt * 128: t0 + (tt + 1) * 128].rearrange("k p t -> p k t"),
                    r=[ykey], w=["ylt"], q="sync")
            for h in range(2):
                pt = self.ps[4 + h]
                for k in range(kc):
                    self.T(lambda e: e.matmul(pt, lhsT=yt[:, k, :], rhs=self.wres[:, k, h * 512:(h + 1) * 512],
                                              start=(k == 0), stop=(k == kc - 1)), ["ylt"] + wrk, [f"ps{4 + h}"])
                self.V(lambda e: e.tensor_tensor(out=self.sq[:, h * 512:(h + 1) * 512], in0=pt,
                                                 in1=self.gt_bc[:, grp, h * 512:(h + 1) * 512], op=ALU.mult),
                       [f"ps{4 + h}", "gt_bc"], ["sq"])
            self.V(lambda e: e.tensor_tensor(out=xt, in0=xt, in1=self.sq, op=ALU.add), [f"xt{b}", "sq"], [f"xt{b}"])
            if not last:
                self.ld(self.xres[t0 + tt * 128: t0 + (tt + 1) * 128, :], xt, r=[f"xt{b}"], w=["xres"])
            else:
                ot = self.xt[1 - b]
                self.rms_scale(xt, f"xt{b}", ot, f"xt{1 - b}", gtab=self.fgb)
                dst = self.ys if grp == 1 else self.yp
                self.ld(dst[tt * 128:(tt + 1) * 128, :], ot, r=[f"xt{1 - b}"], w=["ys" if grp == 1 else "yp"])

    def E(self, eng, fn, r=(), w=()):
        return self.fw.op(eng, fn, r, w)

    def sin_of(self, out, x, shift, eng="vector"):
        y, yi, yf = self.tr_y, self.tr_yi, self.tr_yf
        self.E(eng, lambda e: e.tensor_scalar(out=y, in0=x, scalar1=1.0 / TWO_PI, scalar2=shift / TWO_PI + 8.0,
                                              op0=ALU.mult, op1=ALU.add), ["trx"], ["try"])
        self.E(eng, lambda e: e.tensor_copy(out=yi, in_=y), ["try"], ["tryi"])
        self.E(eng, lambda e: e.tensor_copy(out=yf, in_=yi), ["tryi"], ["tryf"])
        self.E(eng, lambda e: e.tensor_tensor(out=y, in0=y, in1=yf, op=ALU.subtract), ["try", "tryf"], ["try"])
        self.E(eng, lambda e: e.tensor_scalar(out=yf, in0=y, scalar1=0.5, scalar2=None, op0=ALU.is_gt), ["try"], ["tryf"])
        self.E(eng, lambda e: e.tensor_tensor(out=y, in0=y, in1=yf, op=ALU.subtract), ["try", "tryf"], ["try"])
        self.E(eng, lambda e: e.tensor_scalar(out=yf, in0=y, scalar1=-0.5, scalar2=None, op0=ALU.is_lt), ["try"], ["tryf"])
        self.E(eng, lambda e: e.tensor_tensor(out=y, in0=y, in1=yf, op=ALU.add), ["try", "tryf"], ["try"])
        self.A(lambda e: e.activation(out=out, in_=y, func=AF.Sin, scale=6.283185), ["try"], ["trx"])

    def s5_alloc(self):
        sb = self.asb
        self.wgl = [sb(f"wgl{i}", [128, 8, 128], BF16) for i in range(2)]
        self.LR = sb("LR", [128, 128]); self.LI = sb("LI", [128, 128]); self.DT = sb("DT", [128, 128])
        self.ANG = sb("ANG", [128, 128]); self.AR = sb("AR", [128, 128])
        self.SN = sb("SN", [128, 128]); self.CS = sb("CS", [128, 128])
        self.tr_y = sb("tr_y", [128, 128]); self.tr_yi = sb("tr_yi", [128, 128], I32); self.tr_yf = sb("tr_yf", [128, 128])
        self.PR = sb("PR", [128, 9, 128]); self.PI = sb("PI", [128, 9, 128])
        self.FR = sb("FR", [128, 128]); self.FI = sb("FI", [128, 128])
        self.t1 = sb("t1", [128, 256]); self.t2 = sb("t2", [128, 256]); self.t3 = sb("t3", [128, 256])
        self.BRk = sb("BRk", [128, 2, 8, 16]); self.BIk = sb("BIk", [128, 2, 8, 16])
        self.bbr = sb("bbr", [128, 2, 8, 16]); self.bbi = sb("bbi", [128, 2, 8, 16])
        self.CRn = sb("CRn", [128, 2, 2, 64]); self.CIn = sb("CIn", [128, 2, 2, 64])
        self.CRk = sb("CRk", [128, 2, 8, 16]); self.CIk = sb("CIk", [128, 2, 8, 16])
        self.BA = sb("BA", [128, 8, 2, 128]); self.CC = sb("CC", [128, 2, 128])
        self.CAr = sb("CAr", [128, 9, 2, 128], BF16); self.CAi = sb("CAi", [128, 9, 2, 128], BF16)
        self.Tsb = sb("Tsb", [128, 16, 128], BF16)
        self.LW = sb("LW", [128, 16, 2, 128], BF16)
        self.LM = sb("LM", [128, 4, 2, 2, 2, 128], BF16)
        self.Kc = sb("Kc", [128, 15, 128], BF16)
        self.SS = sb("SS", [128, 8 * 2 * 256]); self.HP = sb("HP", [128, 8 * 2 * 256], BF16)
        self.HH = sb("HH", [128, 64]); self.HT1 = sb("HT1", [128, 64]); self.HU1 = sb("HU1", [128, 64])
        self.A1 = sb("A1", [128, 64]); self.A2 = sb("A2", [128, 64])
        self.HL = sb("HL", [128, 256]); self.HLT1 = sb("HLT1", [128, 256]); self.HLU1 = sb("HLU1", [128, 256])
        self.A1f = sb("A1f", [128, 256]); self.A2f = sb("A2f", [128, 256])
        self.PWp = sb("PWp", [128, 256]); self.PW = sb("PW", [128, 256]); self.HST = sb("HST", [128, 256])
        self.Hc = sb("Hc", [128, 16]); self.Hc0 = sb("Hc0", [128, 16]); self.HcT = sb("HcT", [128, 16]); self.HcU = sb("HcU", [128, 16])
        self.B1 = sb("B1", [128, 16]); self.B2 = sb("B2", [128, 16])
        self.TC1 = sb("TC1", [128, 512]); self.TC2 = sb("TC2", [128, 512])
        self.h0T = sb("h0T", [128, 2, 2, 32]); self.FS = sb("FS", [128, 4, 2, 2, 32]); self.FSo = sb("FSo", [128, 128])
        self.dcol = sb("dcol", [128, 8]); self.bgT = sb("bgT", [128, 8])
        self.u8 = sb("u8", [128, 8, LS // 8], BF16); self.g_k = sb("g_k", [128, LS], BF16)
        self.t1g = sb("t1g", [128, 256]); self.t2g = sb("t2g", [128, 256])
        self.gblk = self.wst[0]
        self.sgm = self.SS[:, 0:512]; self.slu = self.SS[:, 512:1024]; self.yb = [self.HP[:, i * 512:(i + 1) * 512] for i in range(2)]
        self.V(lambda e: e.memset(self.LM, 0.0), [], ["LM"])

    def s5_layer_prep(self, js):
        V, A = self.V, self.A
        for half in range(2):
            hs = slice(half * 64, half * 64 + 64)
            self.ld(self.LR[hs, :], self.s5_lam_re[js].rearrange("d g p -> p (d g)"), w=["LR"], allow_slow_non_contiguous=True)
            self.ld(self.LI[hs, :], self.s5_lam_im[js].rearrange("d g p -> p (d g)"), w=["LI"], allow_slow_non_contiguous=True)
        self.ld(self.DT, self.s5_log_step[js].rearrange("d g -> (d g)").partition_broadcast(128), w=["DT"])
        self.ld(self.dcol, self.s5_d[js].rearrange("(k p) -> p k", p=128), w=["dcol"], allow_slow_non_contiguous=True)
        self.ld(self.bgT, self.s5_b_glu[js].rearrange("(k p) -> p k", p=128), w=["bgT"], allow_slow_non_contiguous=True)
        A(lambda e: e.activation(out=self.DT, in_=self.DT, func=AF.Exp), ["DT"], ["DT"])
        V(lambda e: e.tensor_tensor(out=self.ANG, in0=self.LI, in1=self.DT, op=ALU.mult), ["LI", "DT"], ["trx", "ANG"])
        V(lambda e: e.tensor_tensor(out=self.AR, in0=self.LR, in1=self.DT, op=ALU.mult), ["LR", "DT"], ["AR"])
        A(lambda e: e.activation(out=self.AR, in_=self.AR, func=AF.Exp), ["AR"], ["AR"])
        self.sin_of(self.SN, self.ANG, 0.0)
        self.sin_of(self.CS, self.ANG, math.pi / 2)
        PR, PI = self.PR, self.PI
        V(lambda e: e.memset(PR[:, 0], 1.0), [], ["PR"])
        V(lambda e: e.memset(PI[:, 0], 0.0), [], ["PI"])
        V(lambda e: e.tensor_tensor(out=PR[:, 1], in0=self.AR, in1=self.CS, op=ALU.mult), ["AR", "trx"], ["PR"])
        V(lambda e: e.tensor_tensor(out=PI[:, 1], in0=self.AR, in1=self.SN, op=ALU.mult), ["AR", "trx"], ["PI"])
        t1, t2 = self.t1[:, 0:128], self.t2[:, 0:128]
        for m in range(2, 9):
            V(lambda e: e.tensor_tensor(out=t1, in0=PR[:, m - 1], in1=PR[:, 1], op=ALU.mult), ["PR"], ["t1"])
            V(lambda e: e.tensor_tensor(out=t2, in0=PI[:, m - 1], in1=PI[:, 1], op=ALU.mult), ["PI"], ["t2"])
            V(lambda e: e.tensor_tensor(out=PR[:, m], in0=t1, in1=t2, op=ALU.subtract), ["t1", "t2"], ["PR"])
            V(lambda e: e.tensor_tensor(out=t1, in0=PR[:, m - 1], in1=PI[:, 1], op=ALU.mult), ["PR", "PI"], ["t1"])
            V(lambda e: e.tensor_tensor(out=t2, in0=PI[:, m - 1], in1=PR[:, 1], op=ALU.mult), ["PR", "PI"], ["t2"])
            V(lambda e: e.tensor_tensor(out=PI[:, m], in0=t1, in1=t2, op=ALU.add), ["t1", "t2"], ["PI"])
        nr, den = self.SN, self.CS
        V(lambda e: e.tensor_scalar(out=nr, in0=PR[:, 1], scalar1=-1.0, scalar2=None, op0=ALU.add), ["PR"], ["trx"])
        V(lambda e: e.tensor_tensor(out=t1, in0=self.LR, in1=self.LR, op=ALU.mult), ["LR"], ["t1"])
        V(lambda e: e.tensor_tensor(out=t2, in0=self.LI, in1=self.LI, op=ALU.mult), ["LI"], ["t2"])
        V(lambda e: e.tensor_tensor(out=den, in0=t1, in1=t2, op=ALU.add), ["t1", "t2"], ["trx"])
        V(lambda e: e.reciprocal(out=den, in_=den), ["trx"], ["trx"])
        V(lambda e: e.tensor_tensor(out=t1, in0=nr, in1=self.LR, op=ALU.mult), ["trx", "LR"], ["t1"])
        V(lambda e: e.tensor_tensor(out=t2, in0=PI[:, 1], in1=self.LI, op=ALU.mult), ["PI", "LI"], ["t2"])
        V(lambda e: e.tensor_tensor(out=t1, in0=t1, in1=t2, op=ALU.add), ["t1", "t2"], ["t1"])
        V(lambda e: e.tensor_tensor(out=self.FR, in0=t1, in1=den, op=ALU.mult), ["t1", "trx"], ["FR"])
        V(lambda e: e.tensor_tensor(out=t1, in0=PI[:, 1], in1=self.LR, op=ALU.mult), ["PI", "LR"], ["t1"])
        V(lambda e: e.tensor_tensor(out=t2, in0=nr, in1=self.LI, op=ALU.mult), ["trx", "LI"], ["t2"])
        V(lambda e: e.tensor_tensor(out=t1, in0=t1, in1=t2, op=ALU.subtract), ["t1", "t2"], ["t1"])
        V(lambda e: e.tensor_tensor(out=self.FI, in0=t1, in1=den, op=ALU.mult), ["t1", "trx"], ["FI"])
        self.ld(self.FSo, self.st5[js], w=["FSo"])
        pt = self.ps[1][:, 0:128]
        self.T(lambda e: e.transpose(pt, self.FSo, self.identf), ["FSo", "identf"], ["ps1"])
        V(lambda e: e.tensor_copy(out=self.h0T.rearrange("p d x g -> p (d x g)"), in_=pt), ["ps1"], ["h0T"])

    def bcg(self, tab, m, k):
        a = tab[:, m, :].rearrange("p (d g) -> p d g", d=2)[:, :, 8 * k:8 * k + 8]
        return a.unsqueeze(3).to_broadcast([128, 2, 8, 16])

    def bcf(self, tab, k):
        a = tab.rearrange("p (d g) -> p d g", d=2)[:, :, 8 * k:8 * k + 8]
        return a.unsqueeze(3).to_broadcast([128, 2, 8, 16])

    def cmul(self, eng, outr, outi, ar, ai, br, bi, rk, wk, hs_r=slice(0, 128), hs_i=slice(0, 128), negi=False):
        if eng == "gpsimd":
            t1 = self.t1g.rearrange("p (d g h) -> p d g h", d=2, g=8)
            t2 = self.t2g.rearrange("p (d g h) -> p d g h", d=2, g=8)
            k1, k2 = "t1g", "t2g"
        else:
            t1 = self.t1.rearrange("p (d g h) -> p d g h", d=2, g=8)
            t2 = self.t2.rearrange("p (d g h) -> p d g h", d=2, g=8)
            k1, k2 = "t1", "t2"
        E = self.E
        s = hs_r
        E(eng, lambda e: e.tensor_tensor(out=t1[s], in0=ar[s], in1=br[s], op=ALU.mult), rk, [k1])
        E(eng, lambda e: e.tensor_tensor(out=t2[s], in0=ai[s], in1=bi[s], op=ALU.mult), rk, [k2])
        E(eng, lambda e: e.tensor_tensor(out=outr[s], in0=t1[s], in1=t2[s], op=ALU.subtract), [k1, k2], wk)
        s = hs_i
        E(eng, lambda e: e.tensor_tensor(out=t1[s], in0=ar[s], in1=bi[s], op=ALU.mult), rk, [k1])
        E(eng, lambda e: e.tensor_tensor(out=t2[s], in0=ai[s], in1=br[s], op=ALU.mult), rk, [k2])
        if negi:
            E(eng, lambda e: e.tensor_tensor(out=t1[s], in0=t1[s], in1=t2[s], op=ALU.add), [k1, k2], [k1])
            E(eng, lambda e: e.tensor_scalar(out=outi[s], in0=t1[s], scalar1=-1.0, scalar2=None, op0=ALU.mult), [k1], wk)
        else:
            E(eng, lambda e: e.tensor_tensor(out=outi[s], in0=t1[s], in1=t2[s], op=ALU.add), [k1, k2], wk)

    def s5_scan1(self, k, seng, SSv, HPv, SQs, ncg, ncs, nseq, A1s, A2s, grp):
        E = self.E
        HHv = lambda t: t.rearrange("p (x q d s) -> p x q d s", x=2, q=4, d=2)
        HHs, T1s, U1s = self.HH[:, 0:16 * nseq], self.HT1[:, 0:16 * nseq], self.HU1[:, 0:16 * nseq]
        if grp == 1:
            E(seng, lambda e: e.tensor_copy(out=HHv(HHs)[:, :, :, :, 0].rearrange("p x q d -> p d x q"),
                                            in_=self.h0T[:, :, :, 4 * k:4 * k + 4]), ["h0T"], ["HH"])
        else:
            E(seng, lambda e: e.memset(HHs, 0.0), [], ["HH"])
        hsw = AP(HHs, HHs.offset + 8 * nseq, [[HHs.ap[0][0], 128], [-8 * nseq, 2], [1, 8 * nseq]])
        hfl = HHs.rearrange("p (x r) -> p x r", x=2)
        a2f = A2s.rearrange("p (x r) -> p x r", x=2)
        u1f = U1s.rearrange("p (x r) -> p x r", x=2)
        hh4 = HHs.rearrange("p (xq d s) -> p xq d s", d=2, s=nseq)
        for i in range(ncs):
            def colap(t):
                return AP(t, t.offset + i, [[t.ap[0][0], 128], [SQs, 8], [ncg + ncs - 1 - 2 * i, 2], [ncs, nseq]])
            E(seng, lambda e: e.tensor_copy(out=colap(HPv), in_=hh4), ["HH"], ["HP"])
            E(seng, lambda e: e.tensor_tensor(out=T1s, in0=A1s, in1=HHs, op=ALU.mult), ["A1", "HH"], ["HT1"])
            E(seng, lambda e: e.tensor_tensor(out=u1f, in0=a2f, in1=hsw, op=ALU.mult), ["A2", "HH"], ["HU1"])
            E(seng, lambda e: e.tensor_tensor(out=T1s, in0=T1s, in1=U1s, op=ALU.add), ["HT1", "HU1"], ["HT1"])
            E(seng, lambda e: e.tensor_tensor(out=hh4, in0=T1s.rearrange("p (xq d s) -> p xq d s", d=2, s=nseq),
                                              in1=colap(SSv), op=ALU.add), ["HT1", "SS"], ["HH"])
        if grp == 0:
            for s in range(nseq):
                E(seng, lambda e: e.tensor_copy(out=self.FS[:, s, :, :, 4 * k:4 * k + 4],
                                                in_=HHv(HHs)[:, :, :, :, s].rearrange("p x q d -> p d x q")), ["HH"], ["FS"])

    def s5_scan2(self, k, seng, SSv, HPv, SQs, ncg, A1s, A2s):
        E = self.E
        MB = 16
        ps_ = SSv.ap[0][0]
        HL, T1, U1, A1f, A2f = self.HL, self.HLT1, self.HLU1, self.A1f, self.A2f
        PWp, PW, HST = self.PWp, self.PW, self.HST
        Hc, Hc0, HcT, HcU, B1, B2 = self.Hc, self.Hc0, self.HcT, self.HcU, self.B1, self.B2
        TC1, TC2 = self.TC1, self.TC2
        E(seng, lambda e: e.tensor_copy(out=A1f.rearrange("p (a b) -> p a b", b=MB), in_=A1s.unsqueeze(2).to_broadcast([128, 16, MB])), ["A1"], ["A1f"])
        E(seng, lambda e: e.tensor_copy(out=A2f.rearrange("p (a b) -> p a b", b=MB), in_=A2s.unsqueeze(2).to_broadcast([128, 16, MB])), ["A2"], ["A2f"])
        PWv = PWp.rearrange("p (x g i) -> p x g i", x=2, g=8)
        E(seng, lambda e: e.tensor_copy(out=PWv[:, 0, :, 0], in_=A1s[:, 0:8]), ["A1"], ["PWp"])
        E(seng, lambda e: e.tensor_copy(out=PWv[:, 1, :, 0], in_=A2s[:, 8:16]), ["A2"], ["PWp"])
        ln = 1
        t1 = TC1[:, 0:64].rearrange("p (g i) -> p g i", g=8)
        t2 = TC2[:, 0:64].rearrange("p (g i) -> p g i", g=8)
        while ln < MB:
            mr = PWv[:, 0, :, ln - 1:ln].to_broadcast([128, 8, ln])
            mi = PWv[:, 1, :, ln - 1:ln].to_broadcast([128, 8, ln])
            ar, ai = PWv[:, 0, :, 0:ln], PWv[:, 1, :, 0:ln]
            E(seng, lambda e: e.tensor_tensor(out=t1[:, :, 0:ln], in0=ar, in1=mr, op=ALU.mult), ["PWp"], ["TC1"])
            E(seng, lambda e: e.tensor_tensor(out=t2[:, :, 0:ln], in0=ai, in1=mi, op=ALU.mult), ["PWp"], ["TC2"])
            E(seng, lambda e: e.tensor_tensor(out=PWv[:, 0, :, ln:2 * ln], in0=t1[:, :, 0:ln], in1=t2[:, :, 0:ln], op=ALU.subtract), ["TC1", "TC2"], ["PWp"])
            E(seng, lambda e: e.tensor_tensor(out=t1[:, :, 0:ln], in0=ar, in1=mi, op=ALU.mult), ["PWp"], ["TC1"])
            E(seng, lambda e: e.tensor_tensor(out=t2[:, :, 0:ln], in0=ai, in1=mr, op=ALU.mult), ["PWp"], ["TC2"])
            E(seng, lambda e: e.tensor_tensor(out=PWv[:, 1, :, ln:2 * ln], in0=t1[:, :, 0:ln], in1=t2[:, :, 0:ln], op=ALU.add), ["TC1", "TC2"], ["PWp"])
            ln *= 2
        PW5 = PW.rearrange("p (x q d i) -> p x q d i", x=2, q=4, d=2)
        PWp5 = PWp.rearrange("p (x q d i) -> p x q d i", x=2, q=4, d=2)
        for x in range(2):
            E(seng, lambda e: e.tensor_copy(out=PW5[:, x, :, 0, :], in_=PWp5[:, x, :, 0, :]), ["PWp"], ["PW"])
            E(seng, lambda e: e.tensor_copy(out=PW5[:, x, :, 1, :], in_=PWp5[:, x, :, 1, ::-1]), ["PWp"], ["PW"])
        B1v = B1.rearrange("p (x g) -> p x g", x=2)
        B2v = B2.rearrange("p (x g) -> p x g", x=2)
        for x in range(2):
            E(seng, lambda e: e.tensor_copy(out=B1v[:, x, :], in_=PWv[:, 0, :, MB - 1]), ["PWp"], ["B1"])
        E(seng, lambda e: e.tensor_scalar(out=B2v[:, 0, :], in0=PWv[:, 1, :, MB - 1], scalar1=-1.0, scalar2=None, op0=ALU.mult), ["PWp"], ["B2"])
        E(seng, lambda e: e.tensor_copy(out=B2v[:, 1, :], in_=PWv[:, 1, :, MB - 1]), ["PWp"], ["B2"])
        E(seng, lambda e: e.memset(HL, 0.0), [], ["HL"])
        hsw = AP(HL, HL.offset + 128, [[HL.ap[0][0], 128], [-128, 2], [1, 128]])
        a2f = A2f.rearrange("p (x r) -> p x r", x=2)
        u1f = U1.rearrange("p (x r) -> p x r", x=2)
        hl4 = HL.rearrange("p (g d b) -> p g d b", g=8, d=2)
        t14 = T1.rearrange("p (g d b) -> p g d b", g=8, d=2)
        for i in range(MB):
            col = AP(SSv, SSv.offset + i, [[ps_, 128], [SQs, 8], [ncg + MB - 1 - 2 * i, 2], [MB, MB]])
            E(seng, lambda e: e.tensor_tensor(out=T1, in0=A1f, in1=HL, op=ALU.mult), ["A1f", "HL"], ["HLT1"])
            E(seng, lambda e: e.tensor_tensor(out=u1f, in0=a2f, in1=hsw, op=ALU.mult), ["A2f", "HL"], ["HLU1"])
            E(seng, lambda e: e.tensor_tensor(out=T1, in0=T1, in1=U1, op=ALU.add), ["HLT1", "HLU1"], ["HLT1"])
            E(seng, lambda e: e.tensor_tensor(out=hl4, in0=t14, in1=col, op=ALU.add), ["HLT1", "SS"], ["HL"])
            E(seng, lambda e: e.tensor_copy(out=col, in_=hl4), ["HL"], ["SS"])
        E(seng, lambda e: e.tensor_copy(out=Hc.rearrange("p (x q d) -> p d x q", x=2, q=4), in_=self.h0T[:, :, :, 4 * k:4 * k + 4]), ["h0T"], ["Hc"])
        E(seng, lambda e: e.tensor_copy(out=Hc0, in_=Hc), ["Hc"], ["Hc0"])
        hcsw = AP(Hc, Hc.offset + 8, [[Hc.ap[0][0], 128], [-8, 2], [1, 8]])
        b2f = B2.rearrange("p (x r) -> p x r", x=2)
        hcuf = HcU.rearrange("p (x r) -> p x r", x=2)
        hc2 = Hc.rearrange("p (g d) -> p g d", d=2)
        hct2 = HcT.rearrange("p (g d) -> p g d", d=2)
        for b in range(MB):
            hpos = AP(HST, HST.offset + b, [[HST.ap[0][0], 128], [2 * MB, 8], [MB + MB - 1 - 2 * b, 2]])
            E(seng, lambda e: e.tensor_copy(out=hpos, in_=hc2), ["Hc"], ["HST"])
            if b == MB - 1:
                break
            send = AP(SSv, SSv.offset + MB * b + MB - 1, [[ps_, 128], [SQs, 8], [ncg + MB * (MB - 1 - b) - (MB * b + MB - 1), 2]])
            E(seng, lambda e: e.tensor_tensor(out=HcT, in0=B1, in1=Hc, op=ALU.mult), ["B1", "Hc"], ["HcT"])
            E(seng, lambda e: e.tensor_tensor(out=hcuf, in0=b2f, in1=hcsw, op=ALU.mult), ["B2", "Hc"], ["HcU"])
            E(seng, lambda e: e.tensor_tensor(out=HcT, in0=HcT, in1=HcU, op=ALU.add), ["HcT", "HcU"], ["HcT"])
            E(seng, lambda e: e.tensor_tensor(out=hc2, in0=hct2, in1=send, op=ALU.add), ["HcT", "SS"], ["Hc"])
        HST4 = HST.rearrange("p (x q d b) -> p x q d b", x=2, q=4, d=2)
        c1 = TC1.rearrange("p (q b i) -> p q b i", q=2, b=MB)
        c2 = TC2.rearrange("p (q b i) -> p q b i", q=2, b=MB)
        for d in range(2):
          for qh in range(2):
            qs = slice(2 * qh, 2 * qh + 2)
            pr = PW5[:, 0, qs, d, :].unsqueeze(2).to_broadcast([128, 2, MB, MB])
            pi = PW5[:, 1, qs, d, :].unsqueeze(2).to_broadcast([128, 2, MB, MB])
            hr = HST4[:, 0, qs, d, :].unsqueeze(3).to_broadcast([128, 2, MB, MB])
            hi = HST4[:, 1, qs, d, :].unsqueeze(3).to_broadcast([128, 2, MB, MB])
            sre = AP(SSv, SSv.offset + 2 * qh * SQs + d * ncg, [[ps_, 128], [SQs, 2], [MB, MB], [1, MB]])
            sim = AP(SSv, SSv.offset + (4 + 2 * qh) * SQs + d * ncg, [[ps_, 128], [SQs, 2], [MB, MB], [1, MB]])
            E(seng, lambda e: e.tensor_tensor(out=c1, in0=pr, in1=hr, op=ALU.mult), ["PW", "HST"], ["TC1"])
            E(seng, lambda e: e.tensor_tensor(out=c2, in0=pi, in1=hi, op=ALU.mult), ["PW", "HST"], ["TC2"])
            E(seng, lambda e: e.tensor_tensor(out=c1, in0=c1, in1=c2, op=ALU.subtract), ["TC1", "TC2"], ["TC1"])
            E(seng, lambda e: e.tensor_tensor(out=sre, in0=sre, in1=c1, op=ALU.add), ["SS", "TC1"], ["SS"])
            E(seng, lambda e: e.tensor_tensor(out=c1, in0=pr, in1=hi, op=ALU.mult), ["PW", "HST"], ["TC1"])
            E(seng, lambda e: e.tensor_tensor(out=c2, in0=pi, in1=hr, op=ALU.mult), ["PW", "HST"], ["TC2"])
            E(seng, lambda e: e.tensor_tensor(out=c1, in0=c1, in1=c2, op=ALU.add), ["TC1", "TC2"], ["TC1"])
            E(seng, lambda e: e.tensor_tensor(out=sim, in0=sim, in1=c1, op=ALU.add), ["SS", "TC1"], ["SS"])
        ss3 = SSv.rearrange("p (g d c) -> p g d c", g=8, d=2)
        hp3 = HPv.rearrange("p (g d c) -> p g d c", g=8, d=2)
        h03 = Hc0.rearrange("p (g d) -> p g d", d=2)
        self.A(lambda e: e.activation(out=hp3[:, :, 0, 1:ncg], in_=ss3[:, :, 0, 0:ncg - 1], func=AF.Copy), ["SS"], ["HP"])
        self.A(lambda e: e.activation(out=hp3[:, :, 1, 0:ncg - 1], in_=ss3[:, :, 1, 1:ncg], func=AF.Copy), ["SS"], ["HP"])
        E(seng, lambda e: e.tensor_copy(out=hp3[:, :, 0, 0:1], in_=h03[:, :, 0:1]), ["Hc0"], ["HP"])
        E(seng, lambda e: e.tensor_copy(out=hp3[:, :, 1, ncg - 1:ncg], in_=h03[:, :, 1:2]), ["Hc0"], ["HP"])

    def s5_mixer(self, js, grp):
        if grp == 1:
            self.arena_reset()
            self.s5_alloc()
            self.s5_layer_prep(js)
        else:
            self.V(lambda e: e.memset(self.LM, 0.0), [], ["LM"])
        t0, n = self.trange(grp)
        nseq = 1 if grp == 1 else NPS
        ncs = (n // nseq) // 8
        ncg = n // 8
        V, A, T, E = self.V, self.A, self.T, self.E
        H0, H1 = slice(0, 64), slice(64, 128)
        def ld_wu(k_):
            self.ldc(self.wgs[k_ % 2], self.s5_w_in[js][:, k_ * 128:(k_ + 1) * 128].rearrange("(k p) n -> p k n", p=128), w=[f"wgs{k_ % 2}"])
        ld_wu(0)
        for k in range(8):
            wu, wuk = self.wgs[k % 2], f"wgs{k % 2}"
            if k + 1 < 8:
                ld_wu(k + 1)
            eng = "vector"
            seng = "vector"
            for tb in range(n // 512):
                pt = self.ps[0]
                for kk in range(8):
                    T(lambda e: e.matmul(pt, lhsT=wu[:, kk, :],
                                         rhs=self.hT[:, kk, tb * 512:(tb + 1) * 512], start=(kk == 0), stop=(kk == 7)),
                      [wuk, "hT"], ["ps0"])
                A(lambda e: e.activation(out=self.u8[:, :, tb * 64:(tb + 1) * 64].rearrange("p j c -> p c j"),
                                         in_=pt.rearrange("p (c j) -> p c j", j=8), func=AF.Copy), ["ps0"], ["u_k"])
            if grp == 1:
                for half in range(2):
                    hs = slice(half * 64, half * 64 + 64)
                    for d in range(2):
                        self.ld(self.BRk[hs, d], self.s5_b_re[js, d, 8 * k:8 * k + 8].rearrange("g p h -> p g h"), w=["BRk"])
                        self.ld(self.BIk[hs, d], self.s5_b_im[js, d, 8 * k:8 * k + 8].rearrange("g p h -> p g h"), w=["BIk"])
                for dup in range(2):
                    self.ld(self.CRn[:, :, dup, :], self.s5_c_re[js][:, 8 * k:8 * k + 8].rearrange("d g h p -> (g h) d p"), w=["CRn"])
                    self.ld(self.CIn[:, :, dup, :], self.s5_c_im[js][:, 8 * k:8 * k + 8].rearrange("d g h p -> (g h) d p"), w=["CIn"])
                for (cn, cnk, ck, ckk) in ((self.CRn, "CRn", self.CRk, "CRk"), (self.CIn, "CIn", self.CIk, "CIk")):
                    for d in range(2):
                        pt = self.ps[1][:, 0:128]
                        src = cn[:, d].rearrange("p a c -> p (a c)")
                        T(lambda e: e.transpose(pt, src, self.identf), [cnk, "identf"], ["ps1"])
                        A(lambda e: e.activation(out=ck[:, d].rearrange("p g h -> p (g h)"), in_=pt, func=AF.Copy), ["ps1"], [ckk])
                self.cmul(eng, self.bbr, self.bbi, self.bcf(self.FR, k), self.bcf(self.FI, k), self.BRk, self.BIk,
                          ["FR", "FI", "BRk", "BIk"], ["bb"])
                BAv = self.BA.rearrange("p m d (g h) -> p m d g h", g=8)
                for m in range(8):
                    self.cmul(eng, BAv[:, m], BAv[:, m], self.bcg(self.PR, m, k), self.bcg(self.PI, m, k), self.bbr, self.bbi,
                              ["PR", "PI", "bb"], ["BA"], hs_r=H0, hs_i=H1)
                CCv = self.CC.rearrange("p d (g h) -> p d g h", g=8)
                V(lambda e: e.tensor_copy(out=CCv[H0], in_=self.CRk[H0]), ["CRk"], ["CC"])
                V(lambda e: e.tensor_scalar(out=CCv[H1], in0=self.CIk[H1], scalar1=-1.0, scalar2=None, op0=ALU.mult), ["CIk"], ["CC"])
                CArv = self.CAr.rearrange("p m d (g h) -> p m d g h", g=8)
                CAiv = self.CAi.rearrange("p m d (g h) -> p m d g h", g=8)
                for m in range(1, 9):
                    self.cmul(eng, CArv[:, m], CAiv[:, m], self.bcg(self.PR, m, k), self.bcg(self.PI, m, k), self.CRk, self.CIk,
                              ["PR", "PI", "CRk", "CIk"], ["CA"], negi=True)
                for tau in range(8):
                    for d in range(2):
                        pt = self.ps[1][:, 0:128]
                        if tau == 0:
                            T(lambda e: e.matmul(pt, lhsT=self.BA[:, 0, d], rhs=self.CC[:, d], start=(d == 0), stop=(d == 1)),
                              ["BA", "CC"], ["ps1"])
                            if d == 0:
                                continue
                            tt = self.t3[:, 0:128]
                            V(lambda e: e.tensor_tensor(out=tt, in0=pt, in1=self.bdmask, op=ALU.mult), ["ps1", "bdmask"], ["t3"])
                            V(lambda e: e.scalar_tensor_tensor(out=self.Kc[:, 7], in0=self.identf, scalar=self.dcol[:, k:k + 1],
                                                               in1=tt, op0=ALU.mult, op1=ALU.add), ["t3", "identf", "dcol"], ["Kc"])
                        else:
                            T(lambda e: e.matmul(pt, lhsT=self.BA[:, tau, d], rhs=self.CC[:, d], start=True, stop=True),
                              ["BA", "CC"], ["ps1"])
                            idx = 7 + tau if d == 0 else 7 - tau
                            V(lambda e: e.tensor_tensor(out=self.Kc[:, idx], in0=pt, in1=self.bdmask, op=ALU.mult),
                              ["ps1", "bdmask"], ["Kc"])
                for g4 in range(4):
                    bank, bk = (self.ps[1], "ps1") if g4 % 2 == 0 else (self.ps[0], "ps0")
                    for ii in range(4):
                        idx = g4 * 4 + ii
                        T(lambda e: e.transpose(bank[:, ii * 128:(ii + 1) * 128], self.BA[:, idx // 2, idx % 2], self.identf),
                          ["BA", "identf"], [bk])
                    A(lambda e: e.activation(out=self.Tsb[:, g4 * 4:(g4 + 1) * 4].rearrange("p a b -> p (a b)"), in_=bank, func=AF.Copy),
                      [bk], ["Tsb"])
                self.ld(self.Tsb_scr[js, k], self.Tsb.rearrange("p a b -> p (a b)"), r=["Tsb"], w=[("s5c", k)])
                self.ld(self.Kc_scr[js, k], self.Kc.rearrange("p a b -> p (a b)"), r=["Kc"], w=[("s5c", k)])
                self.ld(self.CA_scr[js, k, :, 0], self.CAr.rearrange("p m d c -> p (m d c)"), r=["CA"], w=[("s5c", k)])
                self.ld(self.CA_scr[js, k, :, 1], self.CAi.rearrange("p m d c -> p (m d c)"), r=["CA"], w=[("s5c", k)])
            else:
                self.ld(self.Tsb.rearrange("p a b -> p (a b)"), self.Tsb_scr[js, k], r=[("s5c", k)], w=["Tsb"])
                self.ld(self.Kc.rearrange("p a b -> p (a b)"), self.Kc_scr[js, k], r=[("s5c", k)], w=["Kc"])
                self.ld(self.CAr.rearrange("p m d c -> p (m d c)"), self.CA_scr[js, k, :, 0], r=[("s5c", k)], w=["CA"])
                self.ld(self.CAi.rearrange("p m d c -> p (m d c)"), self.CA_scr[js, k, :, 1], r=[("s5c", k)], w=["CA"])
            HHv = lambda t: t.rearrange("p (x q d s) -> p x q d s", x=2, q=4, d=2)
            A1s, A2s = self.A1[:, 0:16 * nseq], self.A2[:, 0:16 * nseq]
            A1v, A2v = HHv(A1s), HHv(A2s)
            for hf, hsl in ((0, H0), (1, H1)):
                pr8 = self.PR[hsl, 8, :].rearrange("p (d g) -> p d g", d=2)[:, :, 8 * k + hf:8 * k + 8:2]
                pi8 = self.PI[hsl, 8, :].rearrange("p (d g) -> p d g", d=2)[:, :, 8 * k + hf:8 * k + 8:2]
                for s in range(nseq):
                    for x in range(2):
                        V(lambda e: e.tensor_copy(out=A1v[hsl, x, :, :, s].rearrange("p q d -> p d q"), in_=pr8), ["PR"], ["A1"])
                    V(lambda e: e.tensor_scalar(out=A2v[hsl, 0, :, :, s].rearrange("p q d -> p d q"), in0=pi8, scalar1=-1.0,
                                                scalar2=None, op0=ALU.mult), ["PI"], ["A2"])
                    V(lambda e: e.tensor_copy(out=A2v[hsl, 1, :, :, s].rearrange("p q d -> p d q"), in_=pi8), ["PI"], ["A2"])
            SQs = 2 * ncg
            SSv = self.SS[:, 0:16 * ncg]
            HPv = self.HP[:, 0:16 * ncg]
            for q in range(4):
                for x in range(2):
                    in0 = self.Tsb[:, :, x * 64:(x + 1) * 64].unsqueeze(2).to_broadcast([128, 16, 2, 64])
                    in1 = self.pmask[:, 2 * q:2 * q + 2].unsqueeze(1).unsqueeze(3).to_broadcast([128, 16, 2, 64])
                    outv = self.LW[:, :, x, :].rearrange("p m (a c) -> p m a c", a=2)
                    V(lambda e: e.tensor_tensor(out=outv, in0=in0, in1=in1, op=ALU.mult), ["Tsb", "pmask"], ["LW"])
                for d in range(2):
                    for x in range(2):
                        pt = self.ps[2 + x][:, 0:ncg]
                        for j in range(8):
                            m = 7 - j if d == 0 else j
                            T(lambda e: e.matmul(pt, lhsT=self.LW[:, m * 2 + d, x, :], rhs=self.u8[:, j, 0:ncg],
                                                 start=(j == 0), stop=(j == 7)), ["LW", "u_k"], [f"ps{2 + x}"])
                        off = (x * 4 + q) * SQs + d * ncg
                        A(lambda e: e.activation(out=SSv[:, off:off + ncg], in_=pt, func=AF.Copy), [f"ps{2 + x}"], ["SS"])
            if grp == 1:
                self.s5_scan2(k, seng, SSv, HPv, SQs, ncg, A1s, A2s)
            else:
                self.s5_scan1(k, seng, SSv, HPv, SQs, ncg, ncs, nseq, A1s, A2s, grp)
            for jh in range(4):
                for d in range(2):
                    for x, ca in ((0, self.CAr), (1, self.CAi)):
                        for hf, hsl in ((0, H0), (1, H1)):
                            lm = self.LM[hsl, :, d, x, :, :]
                            outv = AP(lm, lm.offset + 16 * hf, [[lm.ap[0][0], 64], [lm.ap[1][0] + 32, 4], [lm.ap[2][0], 2], [1, 16]])
                            if d == 0:
                                m0, ms = 2 * jh + 1, 1
                            else:
                                m0, ms = 8 - 2 * jh, -1
                            cam = ca[hsl, m0, d, :]
                            mstride = ca.ap[1][0]
                            inv = AP(cam, cam.offset + 16 * hf, [[cam.ap[0][0], 64], [32, 4], [ms * mstride, 2], [1, 16]])
                            V(lambda e: e.tensor_copy(out=outv, in_=inv), ["CA"], ["LM"])
                for jj in range(2):
                    j = jh * 2 + jj
                    pt = self.ps[4 + jj][:, 0:ncg]
                    pk = f"ps{4 + jj}"
                    for j2 in range(8):
                        T(lambda e: e.matmul(pt, lhsT=self.Kc[:, j - j2 + 7, :], rhs=self.u8[:, j2, 0:ncg],
                                             start=(j2 == 0), stop=False), ["Kc", "u_k"], [pk])
                    cnt = 0
                    for q in range(4):
                        for d in range(2):
                            for x in range(2):
                                off = (x * 4 + q) * SQs + d * ncg
                                cnt += 1
                                T(lambda e: e.matmul(pt, lhsT=self.LM[:, q, d, x, jj, :], rhs=HPv[:, off:off + ncg],
                                                     start=False, stop=(cnt == 16)), ["LM", "HP"], [pk])
                    A(lambda e: e.activation(out=self.g_k[:, j:n:8], in_=pt, func=AF.Gelu_apprx_tanh), [pk], ["g_k"])
            self.ld(self.gscr[k, :, t0:t0 + n], self.g_k[:, 0:n], r=["g_k"], w=[("gscr", grp)])
        if grp == 0:
            for s in range(nseq):
                pt = self.ps[1][:, 0:128]
                T(lambda e: e.transpose(pt, self.FS[:, s].rearrange("p d x g -> p (d x g)"), self.identf), ["FS", "identf"], ["ps1"])
                V(lambda e: e.tensor_copy(out=self.FSo, in_=pt), ["ps1"], ["FSo"])
                self.ld(self.ns5[s, js], self.FSo, r=["FSo"], w=["ns5"])
        lwv = self.LW.rearrange("p a b c -> p (a b c)")
        wglu = AP(lwv, lwv.offset, [[lwv.ap[0][0], 128], [1024, 8], [1, 1024]])
        self.ldc(wglu, self.s5_w_glu[js].rearrange("(k p) n -> p k n", p=128), w=["LW", "LM"])
        steps = [(tb, nn) for tb in range(n // 512) for nn in range(8)]
        def load_gate(i):
            nn_ = steps[i][1]
            self.ldc(self.wgs[i % 2], self.s5_w_in[js][:, D + nn_ * 128:D + (nn_ + 1) * 128].rearrange("(k p) n -> p k n", p=128),
                     w=[f"wgs{i % 2}"])
        load_gate(0)
        for si, (tb, nn) in enumerate(steps):
            ts = slice(tb * 512, (tb + 1) * 512)
            if nn == 0:
                self.ld(self.gblk, self.gscr[0:8, :, t0 + tb * 512:t0 + (tb + 1) * 512].rearrange("k p t -> p k t"),
                        r=[("gscr", grp)], w=["wst0"])
            if si + 1 < len(steps):
                load_gate(si + 1)
            pz, pg = self.ps[0 + 2 * (nn % 2)], self.ps[1 + 2 * (nn % 2)]
            pzk, pgk = f"ps{0 + 2 * (nn % 2)}", f"ps{1 + 2 * (nn % 2)}"
            for kk in range(8):
                T(lambda e: e.matmul(pz, lhsT=wglu[:, kk, nn * 128:(nn + 1) * 128], rhs=self.gblk[:, kk, :],
                                     start=(kk == 0), stop=(kk == 7)), ["LW", "LM", "wst0"], [pzk])
            A(lambda e: e.activation(out=self.sgm, in_=pz, func=AF.Sigmoid, bias=self.bgT[:, nn:nn + 1]), [pzk, "bgT"], ["SS"])
            wg, wgk = self.wgs[si % 2], f"wgs{si % 2}"
            for kk in range(8):
                T(lambda e: e.matmul(pg, lhsT=wg[:, kk, :], rhs=self.hT[:, kk, ts],
                                     start=(kk == 0), stop=(kk == 7)), [wgk, "hT"], [pgk])
            A(lambda e: e.activation(out=self.slu, in_=pg, func=AF.Silu), [pgk], ["SS2"])
            yb = self.yb[nn % 2]
            ybk = f"yb{nn % 2}"
            V(lambda e: e.tensor_tensor(out=self.sgm, in0=self.sgm, in1=self.gblk[:, nn, :], op=ALU.mult), ["SS", "wst0"], ["SS"])
            V(lambda e: e.tensor_tensor(out=yb, in0=self.sgm, in1=self.slu, op=ALU.mult), ["SS", "SS2"], [ybk, "HP"])
            self.ld(self.g2scr[nn, :, t0 + tb * 512:t0 + (tb + 1) * 512], yb, r=[ybk], w=[("g2scr", grp)])
        self.ysrc = (self.g2scr, ("g2scr", grp))
        return D


def host_consts():
    r = np.arange(128)
    pm = np.zeros((128, 8), np.float32)
    for q in range(4):
        for qq in range(2):
            pm[:, 2 * q + qq] = ((r // 16) == 2 * q + qq)
    bd = ((r[:, None] // 16) == (r[None, :] // 16)).astype(np.float32)
    t = np.arange(2048)
    row, col = t // 64, t % 64
    inv = (10000.0 ** (-np.arange(32, dtype=np.float32) / 32)).astype(np.float32)
    ang = np.concatenate([row[:, None].astype(np.float32) * inv[None], col[:, None].astype(np.float32) * inv[None]], 1)
    jj = r[:, None].astype(np.float32)
    ii = r[None, :].astype(np.float32)
    retE = np.stack([np.maximum(ii - jj, 0.0), np.maximum(jj - ii, 0.0)]).astype(np.float32)
    retM = np.stack([(ii >= jj), (jj > ii)]).astype(np.float32)
    retqe = np.stack([r + 1.0, 128.0 - r]).astype(np.float32)
    retke = np.stack([127.0 - r, r * 1.0], 1).astype(np.float32)
    extra = {
        "c_ropeC": np.cos(ang).astype(np.float32).reshape(16, 128, 64),
        "c_ropeS": np.sin(ang).astype(np.float32).reshape(16, 128, 64),
        "c_retE": retE, "c_retM": retM, "c_retqe": retqe, "c_retke": retke,
    }
    extra.update(hy_consts())
    return extra | {
        "c_identb": np.eye(128, dtype=np.float32).astype(ml_dtypes.bfloat16),
        "c_identf": np.eye(128, dtype=np.float32),
        "c_pmask": pm,
        "c_bdmask": bd,
    }


_CONSTS = None


def make_in_maps(prog, inputs):
    global _CONSTS
    if _CONSTS is None:
        _CONSTS = host_consts()
    consts = _CONSTS
    maps = []
    for c in range(8):
        m = {}
        for name in prog.inputs:
            if name in consts:
                m[name] = consts[name]
            elif name == "xs":
                m[name] = np.ascontiguousarray(inputs["x_sample"][c])
            elif name == "xp":
                m[name] = np.ascontiguousarray(inputs["x_prompt"][4 * c:4 * c + 4].reshape(NPS * LP, D))
            elif name == "cvec":
                m[name] = np.ascontiguousarray(np.stack([inputs["c_ctx"], inputs["c"][c]], 0))
            elif name == "st5":
                m[name] = np.ascontiguousarray(inputs["state_s5"][c].reshape(2, 128, 128))
            elif name == "stret":
                m[name] = np.ascontiguousarray(inputs["state_ret"][c, 0])
            else:
                a = np.asarray(inputs[name])
                shp = prog.inputs[name][0]
                m[name] = np.ascontiguousarray(a.reshape(shp))
        maps.append(m)
    return maps


_PROG = None


def kernel(**inputs):
    global _PROG
    inputs = {k: np.asarray(v) for k, v in inputs.items()}
    if _PROG is None:
        _PROG = K()
    prog = _PROG
    res = run_bass_kernel_spmd(prog.nc, make_in_maps(prog, inputs), core_ids=list(range(8)))
    rs = res.results
    y_prompt = np.concatenate([r["yp"].reshape(NPS, LP, D) for r in rs], 0)
    y_sample = np.stack([r["ys"] for r in rs], 0)
    ns5 = np.concatenate([r["ns5"].reshape(NPS, 2, 2, 2, 64, 64) for r in rs], 0)
    nret = np.concatenate([r["nret"][:, None].reshape(NPS, 1, 2, 8, 128, 256) for r in rs], 0)
    return (y_prompt.astype(np.float32), y_sample.astype(np.float32), ns5.astype(np.float32), nret.astype(np.float32))


def _ret_decl(self):
    self.ret_w_in = self.din("ret_w_in", [1, D, 6 * D])
    self.ret_decay_logit = self.din("ret_decay_logit", [1, 2, 8])
    self.ret_w_out = self.din("ret_w_out", [1, 2 * D, D])
    self.c_ropeC = self.din("c_ropeC", [16, 128, 64])
    self.c_ropeS = self.din("c_ropeS", [16, 128, 64])
    self.c_retE = self.din("c_retE", [2, 128, 128])
    self.c_retM = self.din("c_retM", [2, 128, 128])
    self.c_retqe = self.din("c_retqe", [2, 128])
    self.c_retke = self.din("c_retke", [128, 2])
    self.qT_scr = self.dscr("qT_scr", [8, 128, NT], BF16)
    self.kT_scr = self.dscr("kT_scr", [8, 128, NT], BF16)
    self.ktok_scr = self.dscr("ktok_scr", [NT, D], BF16)
    self.v_scr = self.dscr("v_scr", [NT, 2 * D], BF16)
    self.gate_scr = self.dscr("gate_scr", [NT, 2 * D], BF16)
    self.of_scr = self.dscr("of_scr", [NT, 2 * D])


def _ret_mixer(self, grp):
    V, A, T, G = self.V, self.A, self.T, self.G
    t0, n = self.trange(grp)
    nseq = 1 if grp == 1 else NPS
    L = n // nseq
    nch = L // 128
    self.arena_reset()
    sb = self.asb
    wblk = [sb(f"rwb{i}", [128, 8, 512], BF16) for i in range(2)]
    lgt = sb("lgt", [128, 16]); kdt = sb("kdt", [128, 16]); cdt = sb("cdt", [128, 16]); ke = sb("ke", [128, 2])
    Et = sb("Et", [128, 2, 128]); Mt = sb("Mt", [128, 2, 128]); qe = sb("qe", [128, 2, 128])
    Dtab = sb("Dtab", [128, 16, 128]); qdtab = sb("qdtab", [128, 16, 128])
    rc = sb("rc", [128, 64]); rs = sb("rs", [128, 64])
    pq = sb("pq", [128, 512]); pq2 = sb("pq2", [128, 512]); pt1 = sb("pt1", [128, 512])
    pbf = [sb(f"pbf{i}", [128, 512], BF16) for i in range(2)]
    trb = sb("trb", [128, 4, 128], BF16)
    S = sb("S", [128, 8, 256]); Sb = sb("Sb", [128, 8, 256], BF16)
    qTc = [sb(f"qTc{i}", [128, 8, 128], BF16) for i in range(2)]
    kTc = [sb(f"kTc{i}", [128, 8, 128], BF16) for i in range(2)]
    ktc = [sb(f"ktc{i}", [128, 1024], BF16) for i in range(2)]
    vc = [sb(f"vc{i}", [128, 2048], BF16) for i in range(2)]
    gc = sb("gc", [128, 2048], BF16)
    ofc = sb("ofc", [128, 2048])
    ot = sb("ot", [128, 2048])
    ybf = sb("ybf", [128, 2048], BF16)
    yTt = sb("yTt", [128, 16, 128], BF16)
    attb2 = [sb(f"attb{i}", [128, 128], BF16) for i in range(2)]
    qd2 = [sb(f"qd{i}", [128, 128], BF16) for i in range(2)]
    kd2 = [sb(f"kd{i}", [128, 128], BF16) for i in range(2)]
    rst = sb("rst", [128, 24])
    self.ld(lgt, self.ret_decay_logit[0].rearrange("d h -> (d h)").partition_broadcast(128), w=["lgt"])
    self.ld(ke, self.c_retke, w=["ke"])
    for d in range(2):
        self.ld(Et[:, d], self.c_retE[d], w=["Et"])
        self.ld(Mt[:, d], self.c_retM[d], w=["Mt"])
        self.ld(qe[:, d], self.c_retqe[d].partition_broadcast(128), w=["qe"])
    A(lambda e: e.activation(out=lgt, in_=lgt, func=AF.Exp, scale=-1.0), ["lgt"], ["lgt"])
    V(lambda e: e.tensor_scalar(out=lgt, in0=lgt, scalar1=1.0, scalar2=None, op0=ALU.add), ["lgt"], ["lgt"])
    A(lambda e: e.activation(out=lgt, in_=lgt, func=AF.Ln), ["lgt"], ["lgt"])
    V(lambda e: e.tensor_scalar(out=lgt, in0=lgt, scalar1=-1.0, scalar2=None, op0=ALU.mult), ["lgt"], ["lgt"])
    for d in range(2):
        for h in range(8):
            c = d * 8 + h
            A(lambda e: e.activation(out=Dtab[:, c], in_=Et[:, d], func=AF.Exp, scale=lgt[:, c:c + 1]), ["Et", "lgt"], ["Dtab"])
            V(lambda e: e.tensor_tensor(out=Dtab[:, c], in0=Dtab[:, c], in1=Mt[:, d], op=ALU.mult), ["Dtab", "Mt"], ["Dtab"])
            A(lambda e: e.activation(out=qdtab[:, c], in_=qe[:, d], func=AF.Exp, scale=lgt[:, c:c + 1]), ["qe", "lgt"], ["qdtab"])
            A(lambda e: e.activation(out=kdt[:, c:c + 1], in_=ke[:, d:d + 1], func=AF.Exp, scale=lgt[:, c:c + 1]), ["ke", "lgt"], ["kdt"])
    A(lambda e: e.activation(out=cdt, in_=lgt, func=AF.Exp, scale=128.0), ["lgt"], ["cdt"])
    gk = lambda nm: (nm, grp)
    def ld_wb(cb_):
        self.ldc(wblk[cb_ % 2], self.ret_w_in[0][:, cb_ * 512:(cb_ + 1) * 512].rearrange("(k p) n -> p k n", p=128), w=[f"rwb{cb_ % 2}"])
    ld_wb(0)
    for cb in range(12):
        wb, wk = wblk[cb % 2], f"rwb{cb % 2}"
        if cb + 1 < 12:
            ld_wb(cb + 1)
        for tt in range(n // 128):
            ts = slice(tt * 128, (tt + 1) * 128)
            gts = slice(t0 + tt * 128, t0 + (tt + 1) * 128)
            pp = self.ps[tt % 2]
            pk = f"ps{tt % 2}"
            for kk in range(8):
                T(lambda e: e.matmul(pp, lhsT=self.hT[:, kk, ts], rhs=wb[:, kk, :], start=(kk == 0), stop=(kk == 7)),
                  ["hT", wk], [pk])
            ob = pbf[tt % 2]
            obk = f"pbf{tt % 2}"
            if cb < 4:
                isk = cb >= 2
                sc = (128.0 ** -0.5) if isk else 1.0
                if grp == 1:
                    if True:
                        self.ld(rc, self.c_ropeC[tt], w=["rc"])
                        self.ld(rs, self.c_ropeS[tt], w=["rs"])
                    A(lambda e: e.activation(out=pq, in_=pp, func=AF.Copy, scale=sc), [pk], ["pq"])
                    v5 = lambda t: t.rearrange("p (h a b f) -> p h a b f", h=4, a=2, b=2)
                    x1 = v5(pq)[:, :, :, 0, :]
                    x2 = v5(pq)[:, :, :, 1, :]
                    cosb = rc.rearrange("p (a f) -> p a f", a=2).unsqueeze(1).to_broadcast([128, 4, 2, 32])
                    sinb = rs.rearrange("p (a f) -> p a f", a=2).unsqueeze(1).to_broadcast([128, 4, 2, 32])
                    o1 = v5(pq2)[:, :, :, 0, :]
                    o2 = v5(pq2)[:, :, :, 1, :]
                    u1 = v5(pt1)[:, :, :, 0, :]
                    u2 = v5(pt1)[:, :, :, 1, :]
                    V(lambda e: e.tensor_tensor(out=o1, in0=x1, in1=cosb, op=ALU.mult), ["pq", "rc"], ["pq2"])
                    V(lambda e: e.tensor_tensor(out=u1, in0=x2, in1=sinb, op=ALU.mult), ["pq", "rs"], ["pt1"])
                    G(lambda e: e.tensor_tensor(out=o2, in0=x1, in1=sinb, op=ALU.mult), ["pq", "rs"], ["pq2b"])
                    G(lambda e: e.tensor_tensor(out=u2, in0=x2, in1=cosb, op=ALU.mult), ["pq", "rc"], ["pt1b"])
                    V(lambda e: e.tensor_tensor(out=v5(ob)[:, :, :, 0, :], in0=o1, in1=u1, op=ALU.subtract), ["pq2", "pt1"], [obk])
                    V(lambda e: e.tensor_tensor(out=v5(ob)[:, :, :, 1, :], in0=o2, in1=u2, op=ALU.add), ["pq2b", "pt1b"], [obk])
                else:
                    A(lambda e: e.activation(out=ob, in_=pp, func=AF.Copy, scale=sc), [pk], [obk])
                if isk:
                    self.ld(self.ktok_scr[gts, (cb - 2) * 512:(cb - 1) * 512], ob, r=[obk], w=[gk("ktok")])
                for hh in range(4):
                    ptr = self.ps[2].bitcast(BF16)[:, hh * 128:(hh + 1) * 128]
                    T(lambda e: e.transpose(ptr, ob[:, hh * 128:(hh + 1) * 128], self.identb), [obk, "identb"], ["ps2"])
                V(lambda e: e.tensor_copy(out=trb.rearrange("p a b -> p (a b)"), in_=self.ps[2].bitcast(BF16)[:, 0:512]), ["ps2"], ["trb"])
                dst = self.kT_scr if isk else self.qT_scr
                h0 = (cb % 2) * 4
                self.ld(dst[h0:h0 + 4, :, gts].rearrange("h p t -> p h t"), trb, r=["trb"], w=[gk("kT" if isk else "qT")])
            elif cb < 8:
                A(lambda e: e.activation(out=ob, in_=pp, func=AF.Copy), [pk], [obk])
                self.ld(self.v_scr[gts, (cb - 4) * 512:(cb - 3) * 512], ob, r=[obk], w=[gk("v")])
            else:
                A(lambda e: e.activation(out=ob, in_=pp, func=AF.Silu), [pk], [obk])
                self.ld(self.gate_scr[gts, (cb - 8) * 512:(cb - 7) * 512], ob, r=[obk], w=[gk("gate")])
    for s in range(nseq):
        for d in range(2):
            if grp == 1:
                self.ld(S, self.stret[d].rearrange("h p e -> p h e"), w=["S"])
            else:
                V(lambda e: e.memset(S, 0.0), [], ["S"])
            V(lambda e: e.tensor_copy(out=Sb, in_=S), ["S"], ["Sb"])
            order = range(nch) if d == 0 else range(nch - 1, -1, -1)
            for ci, c in enumerate(order):
                b = ci % 2
                ts = slice(t0 + s * L + c * 128, t0 + s * L + (c + 1) * 128)
                self.ld(qTc[b], self.qT_scr[:, :, ts].rearrange("h p t -> p h t"), r=[gk("qT")], w=[f"qTc{b}"])
                self.ld(kTc[b], self.kT_scr[:, :, ts].rearrange("h p t -> p h t"), r=[gk("kT")], w=[f"kTc{b}"])
                self.ld(ktc[b], self.ktok_scr[ts, :], r=[gk("ktok")], w=[f"ktc{b}"])
                self.ld(vc[b], self.v_scr[ts, :], r=[gk("v")], w=[f"vc{b}"])
                if d == 1:
                    self.ld(ofc, self.of_scr[ts, :], r=[gk("of")], w=["ofc"])
                    self.ld(gc, self.gate_scr[ts, :], r=[gk("gate")], w=["gc"])
                for h in range(8):
                    cI = d * 8 + h
                    hb = h % 2
                    attb, qd, kd = attb2[hb], qd2[hb], kd2[hb]
                    attk, qdk, kdk = f"attb{hb}", f"qd{hb}", f"kd{hb}"
                    pak = "ps3" if hb == 0 else "ps0"
                    pa = (self.ps[3] if hb == 0 else self.ps[0])[:, 0:128]
                    T(lambda e: e.matmul(pa, lhsT=kTc[b][:, h, :], rhs=qTc[b][:, h, :], start=True, stop=True),
                      [f"kTc{b}", f"qTc{b}"], [pak])
                    V(lambda e: e.tensor_tensor(out=attb, in0=pa, in1=Dtab[:, cI], op=ALU.mult), [pak, "Dtab"], [attk])
                    G(lambda e: e.tensor_tensor(out=qd, in0=qTc[b][:, h, :], in1=qdtab[:, cI], op=ALU.mult), [f"qTc{b}", "qdtab"], [qdk])
                    A(lambda e: e.activation(out=kd, in_=ktc[b][:, h * 128:(h + 1) * 128], func=AF.Copy, scale=kdt[:, cI:cI + 1]),
                      [f"ktc{b}", "kdt"], [kdk])
                    po = self.ps[4 + (h % 2)][:, 0:256]
                    pok = f"ps{4 + (h % 2)}"
                    T(lambda e: e.matmul(po, lhsT=attb, rhs=vc[b][:, h * 256:(h + 1) * 256], start=True, stop=False),
                      [attk, f"vc{b}"], [pok])
                    T(lambda e: e.matmul(po, lhsT=qd, rhs=Sb[:, h, :], start=False, stop=True), [qdk, "Sb"], [pok])
                    psu = self.ps[6 + (h % 2)][:, 0:256]
                    psk = f"ps{6 + (h % 2)}"
                    T(lambda e: e.matmul(psu, lhsT=kd, rhs=vc[b][:, h * 256:(h + 1) * 256], start=True, stop=True),
                      [kdk, f"vc{b}"], [psk])
                    if d == 0:
                        A(lambda e: e.activation(out=ot[:, h * 256:(h + 1) * 256], in_=po, func=AF.Copy), [pok], ["ot"])
                    else:
                        V(lambda e: e.tensor_tensor(out=ot[:, h * 256:(h + 1) * 256], in0=po, in1=ofc[:, h * 256:(h + 1) * 256],
                                                    op=ALU.add), [pok, "ofc"], ["ot"])
                    V(lambda e: e.scalar_tensor_tensor(out=S[:, h, :], in0=S[:, h, :], scalar=cdt[:, cI:cI + 1], in1=psu,
                                                       op0=ALU.mult, op1=ALU.add), ["S", "cdt", psk], ["S"])
                    A(lambda e: e.activation(out=Sb[:, h, :], in_=S[:, h, :], func=AF.Copy), ["S"], ["Sb"])
                if d == 0:
                    self.ld(self.of_scr[ts, :], ot, r=["ot"], w=[gk("of")])
                else:
                    for h in range(8):
                        A(lambda e: e.activation(out=ofc[:, h * 256:(h + 1) * 256], in_=ot[:, h * 256:(h + 1) * 256], func=AF.Square,
                                                 accum_out=rst[:, h:h + 1]), ["ot"], ["ofc", "rst"])
                    V(lambda e: e.tensor_scalar(out=rst[:, 8:16], in0=rst[:, 0:8], scalar1=1.0 / 256, scalar2=EPS,
                                                op0=ALU.mult, op1=ALU.add), ["rst"], ["rst"])
                    A(lambda e: e.activation(out=rst[:, 8:16], in_=rst[:, 8:16], func=AF.Sqrt), ["rst"], ["rst"])
                    V(lambda e: e.reciprocal(out=rst[:, 16:24], in_=rst[:, 8:16]), ["rst"], ["rst"])
                    for h in range(8):
                        V(lambda e: e.scalar_tensor_tensor(out=ybf[:, h * 256:(h + 1) * 256], in0=ot[:, h * 256:(h + 1) * 256],
                                                           scalar=rst[:, 16 + h:17 + h], in1=gc[:, h * 256:(h + 1) * 256],
                                                           op0=ALU.mult, op1=ALU.mult), ["ot", "rst", "gc"], ["ybf"])
                    for k4 in range(4):
                        for kk in range(4):
                            k = k4 * 4 + kk
                            ptr = self.ps[2].bitcast(BF16)[:, kk * 128:(kk + 1) * 128]
                            T(lambda e: e.transpose(ptr, ybf[:, k * 128:(k + 1) * 128], self.identb), ["ybf", "identb"], ["ps2"])
                        A(lambda e: e.activation(out=yTt[:, k4 * 4:(k4 + 1) * 4, :].rearrange("p a b -> p (a b)"),
                                                 in_=self.ps[2].bitcast(BF16)[:, 0:512], func=AF.Copy), ["ps2"], ["yTt"])
                    self.ld(self.gscr[0:16, :, ts].rearrange("k p t -> p k t"), yTt, r=["yTt"], w=[("gscr", grp)])
            if grp == 0:
                self.ld(self.nret[s, d].rearrange("h p e -> p h e"), S, r=["S"], w=["nret"])
    self.ysrc = (self.gscr, ("gscr", grp))
    return 2 * D


K.ret_decl = _ret_decl
K.ret_mixer = _ret_mixer


HY_FT = {2048: 17, 256: 3}


def _hy_decl(self):
    self.hy_w_in = self.din("hy_w_in", [1, D, 8 * D])
    self.hy_conv_w = self.din("hy_conv_w", [1, 3, 6 * D])
    self.hy_conv_b = self.din("hy_conv_b", [1, 6 * D])
    self.hy_f_w1 = self.din("hy_f_w1", [1, 33, 64])
    self.hy_f_b1 = self.din("hy_f_b1", [1, 64])
    self.hy_f_w2 = self.din("hy_f_w2", [1, 64, 64])
    self.hy_f_b2 = self.din("hy_f_b2", [1, 64])
    self.hy_f_w3 = self.din("hy_f_w3", [1, 64, 8 * D])
    self.hy_skip = self.din("hy_skip", [1, 2, 2 * D])
    self.hy_w_out = self.din("hy_w_out", [1, 2 * D, D])
    self.c_absd = self.din("c_absd", [2 * D])
    self.c_ones = self.din("c_ones", [128, 128])
    self.hyc = {}
    for L in (2048, 256):
        FT = HY_FT[L]
        self.hyc[L] = dict(
            feat=self.din(f"c_feat{L}", [33, L]),
            tneg=self.din(f"c_tneg{L}", [128, L // 128]),
            C=self.din(f"c_C{L}", [FT, 128, L // 128, 128], BF16), S=self.din(f"c_S{L}", [FT, 128, L // 128, 128], BF16),
            IC=self.din(f"c_IC{L}", [L // min(512, L), 128, FT, min(512, L)], BF16),
            IS=self.din(f"c_IS{L}", [L // min(512, L), 128, FT, min(512, L)], BF16))
    self.vT_scr = self.dscr("vT_scr", [16, 128, NT])
    self.x1T_scr = self.dscr("x1T_scr", [16, 128, NT])
    self.x2T_scr = self.dscr("x2T_scr", [16, 128, NT])
    self.z1T_scr = self.dscr("z1T_scr", [16, 128, NT])
    self.sgT_scr = self.dscr("sgT_scr", [16, 128, NT], BF16)
    self.ztok_scr = self.dscr("ztok_scr", [2, NT, 2 * D], BF16)
    self.Eo_scr = self.dscr("Eo_scr", [2, 2, 2048, 2 * D], BF16)
    self.KH_scr = self.dscr("KH_scr", [2, 2, 17 * 128, 2 * D])


def _sin_any(self, out, x, rk, wk, tmpf, tmpi, tmpk, biasp):
    V, A = self.V, self.A
    V(lambda e: e.tensor_scalar(out=out, in0=x, scalar1=biasp, scalar2=1.0 / TWO_PI, op0=ALU.add, op1=ALU.mult), rk, wk)
    V(lambda e: e.tensor_copy(out=tmpi, in_=out), wk, [tmpk + "i"])
    V(lambda e: e.tensor_copy(out=tmpf, in_=tmpi), [tmpk + "i"], [tmpk])
    V(lambda e: e.tensor_tensor(out=out, in0=out, in1=tmpf, op=ALU.subtract), wk + [tmpk], wk)
    V(lambda e: e.tensor_scalar(out=tmpf, in0=out, scalar1=0.5, scalar2=None, op0=ALU.is_gt), wk, [tmpk])
    V(lambda e: e.tensor_tensor(out=out, in0=out, in1=tmpf, op=ALU.subtract), wk + [tmpk], wk)
    V(lambda e: e.tensor_scalar(out=tmpf, in0=out, scalar1=-0.5, scalar2=None, op0=ALU.is_lt), wk, [tmpk])
    V(lambda e: e.tensor_tensor(out=out, in0=out, in1=tmpf, op=ALU.add), wk + [tmpk], wk)
    A(lambda e: e.activation(out=out, in_=out, func=AF.Sin, scale=6.283185), wk, wk)


def _hy_filters(self, grp):
    V, A, T, G = self.V, self.A, self.T, self.G
    L = LS if grp == 1 else LP
    FT, LT = HY_FT[L], L // 128
    hc = self.hyc[L]
    self.arena_reset()
    sb = self.asb
    w1 = sb("hw1", [33, 64]); w2 = sb("hw2", [64, 64]); b1 = sb("hb1", [64, 1]); b2 = sb("hb2", [64, 1])
    feat = sb("hfeat", [33, L]); z1 = sb("hz1", [64, L]); z2 = sb("hz2", [64, L])
    tf = sb("htf", [64, 512]); ti = sb("hti", [64, 512], I32)
    w3b = [sb(f"hw3{i}", [64, 512], BF16) for i in range(2)]
    z2b = None
    absd = sb("habsd", [128, 2 * D]); tneg = sb("htneg", [128, LT]); ones = sb("hones", [128, 128], BF16)
    wins = [sb(f"hwin{i}", [128, 512]) for i in range(2)]
    fds = [[sb(f"hfd{j}{i}", [128, 512]) for i in range(2)] for j in range(2)]
    fabs = [sb(f"hfab{i}", [128, 512], BF16) for i in range(2)]
    ebs = [[sb(f"heb{j}{i}", [128, 512], BF16) for i in range(2)] for j in range(2)]
    rn = sb("hrn", [128, 2, 2 * D])
    Eb = sb("hE", [128, LT, 512], BF16); Ob = sb("hO", [128, LT, 512], BF16)
    Cs = [sb(f"hCs{i}", [128, LT, 128], BF16) for i in range(2)]
    Ss = [sb(f"hSs{i}", [128, LT, 128], BF16) for i in range(2)]
    ko = [fds[0][0], fds[0][1]]
    self.ld(w1, self.hy_f_w1[0], w=["hw1"]); self.ld(w2, self.hy_f_w2[0], w=["hw2"])
    self.ld(b1, self.hy_f_b1[0].rearrange("(p o) -> p o", o=1), w=["hb1"])
    self.ld(b2, self.hy_f_b2[0].rearrange("(p o) -> p o", o=1), w=["hb2"])
    self.ld(feat, hc["feat"], w=["hfeat"])
    self.ld(absd, self.c_absd.partition_broadcast(128), w=["habsd"])
    self.ld(tneg, hc["tneg"], w=["htneg"])
    self.ldc(ones, self.c_ones, w=["hones"])
    V(lambda e: e.tensor_scalar(out=b1, in0=b1, scalar1=16.0 * math.pi, scalar2=None, op0=ALU.add), ["hb1"], ["hb1"])
    V(lambda e: e.tensor_scalar(out=b2, in0=b2, scalar1=16.0 * math.pi, scalar2=None, op0=ALU.add), ["hb2"], ["hb2"])
    BW = min(512, L)
    for (src, srck, wt, wtk, bb, bbk, dst, dstk, kdim) in ((feat, "hfeat", w1, "hw1", b1, "hb1", z1, "hz1", 33),
                                                          (z1, "hz1", w2, "hw2", b2, "hb2", z2, "hz2", 64)):
        for tb in range(L // BW):
            pp = self.ps[0][0:64, 0:BW]
            T(lambda e: e.matmul(pp, lhsT=wt[0:kdim, :], rhs=src[0:kdim, tb * BW:(tb + 1) * BW], start=True, stop=True),
              [srck, wtk], ["ps0"])
            self.sin_any(dst[:, tb * BW:(tb + 1) * BW], pp, ["ps0", bbk], [dstk], tf[:, 0:BW], ti[:, 0:BW], "htf", bb[:, 0:1])
    z2b = z1.bitcast(BF16)[:, 0:L]
    A(lambda e: e.activation(out=z2b, in_=z2, func=AF.Copy), ["hz2", "hz1"], ["hz1"])
    for o in range(2):
        for cb in range(4):
            cs = slice(cb * 512, (cb + 1) * 512)
            for dr in range(2):
                col0 = dr * 4096 + o * 2048 + cb * 512
                self.ldc(w3b[dr], self.hy_f_w3[0][:, col0:col0 + 512], w=[f"hw3{dr}"])
            pacc = self.ps[3]
            for lt in range(LT):
                pb_ = lt % 2
                win, fd, eb = wins[pb_], fds[pb_], ebs[pb_]
                wink = f"hwin{pb_}"
                A(lambda e: e.activation(out=win, in_=absd[:, cs], func=AF.Exp, scale=tneg[:, lt:lt + 1]), ["habsd", "htneg"], [wink])
                for dr in range(2):
                    fab, fabk = fabs[dr], f"hfab{dr}"
                    pf = self.ps[(1 + dr) if pb_ == 0 else (5 + dr)]
                    pfk = f"ps{(1 + dr) if pb_ == 0 else (5 + dr)}"
                    T(lambda e: e.matmul(pf, lhsT=z2b[:, lt * 128:(lt + 1) * 128], rhs=w3b[dr], start=True, stop=True),
                      ["hz1", f"hw3{dr}"], [pfk])
                    V(lambda e: e.tensor_tensor(out=fd[dr], in0=pf, in1=win, op=ALU.mult), [pfk, wink], [f"hfd{pb_}{dr}"])
                    A(lambda e: e.activation(out=fab, in_=fd[dr], func=AF.Abs), [f"hfd{pb_}{dr}"], [fabk])
                    T(lambda e: e.matmul(pacc, lhsT=ones, rhs=fab, start=(lt == 0 and dr == 0), stop=(lt == LT - 1 and dr == 1)),
                      ["hones", fabk], ["ps3"])
                if lt == 0:
                    V(lambda e: e.memset(fd[1][0:1, :], 0.0), [], [f"hfd{pb_}1"])
                V(lambda e: e.tensor_tensor(out=eb[0], in0=fd[0], in1=fd[1], op=ALU.add), [f"hfd{pb_}0", f"hfd{pb_}1"], [f"heb{pb_}0"])
                V(lambda e: e.tensor_tensor(out=eb[1], in0=fd[1], in1=fd[0], op=ALU.subtract), [f"hfd{pb_}0", f"hfd{pb_}1"], [f"heb{pb_}1"])
                for eo in range(2):
                    self.ld(self.Eo_scr[o, eo, lt * 128:(lt + 1) * 128, cs], eb[eo], r=[f"heb{pb_}{eo}"], w=[("Eo", grp)])
            V(lambda e: e.tensor_scalar(out=rn[:, o, cs], in0=pacc, scalar1=EPS, scalar2=None, op0=ALU.add), ["ps3"], ["hrn"])
            V(lambda e: e.reciprocal(out=rn[:, o, cs], in_=rn[:, o, cs]), ["hrn"], ["hrn"])
    for o in range(2):
        for cb in range(4):
            cs = slice(cb * 512, (cb + 1) * 512)
            self.ld(Eb, self.Eo_scr[o, 0, 0:L, cs].rearrange("(lt p) c -> p lt c", p=128), r=[("Eo", grp)], w=["hE"])
            self.ld(Ob, self.Eo_scr[o, 1, 0:L, cs].rearrange("(lt p) c -> p lt c", p=128), r=[("Eo", grp)], w=["hO"])
            for ft in range(FT):
                b = ft % 2
                fs = slice(ft * 128, (ft + 1) * 128)
                self.ld(Cs[b], hc["C"][ft], w=[f"hCs{b}"])
                self.ld(Ss[b], hc["S"][ft], w=[f"hSs{b}"])
                for ri, (tab, tabk, dat, datk) in enumerate(((Cs[b], f"hCs{b}", Eb, "hE"), (Ss[b], f"hSs{b}", Ob, "hO"))):
                    pk_ = self.ps[4 + ri]
                    for lt in range(LT):
                        T(lambda e: e.matmul(pk_, lhsT=tab[:, lt, :], rhs=dat[:, lt, :], start=(lt == 0), stop=(lt == LT - 1)),
                          [tabk, datk], [f"ps{4 + ri}"])
                    V(lambda e: e.tensor_tensor(out=ko[ri], in0=pk_, in1=rn[:, o, cs], op=ALU.mult), [f"ps{4 + ri}", "hrn"], [f"hfd0{ri}"])
                    self.ld(self.KH_scr[o, ri, fs, cs], ko[ri], r=[f"hfd0{ri}"], w=[("KH", grp)])


def _hy_mixer(self, grp):
    V, A, T, G = self.V, self.A, self.T, self.G
    t0, n = self.trange(grp)
    nseq = 1 if grp == 1 else NPS
    L = n // nseq
    FT, LT = HY_FT[L], L // 128
    hc = self.hyc[L]
    self.hy_filters(grp)
    self.arena_reset()
    sb = self.asb
    wch = [sb(f"ywc{i}", [128, 8, 128], BF16) for i in range(2)]
    PBs = [sb(f"yPB{i}", [128, nseq, L + 2]) for i in range(2)]
    cvs = [sb(f"ycv{i}", [128, nseq, L]) for i in range(2)]
    cvbs = [sb(f"ycvb{i}", [128, n], BF16) for i in range(2)]
    cwT = sb("ycwT", [128, 3, 48]); cbT = sb("ycbT", [128, 48]); skT = sb("yskT", [128, 2, 16])
    ztt = sb("yztt", [128, n // 128, 128], BF16)
    for j in range(3):
        self.ld(cwT[:, j, :], self.hy_conv_w[0, j].rearrange("(c p) -> p c", p=128), w=["ycwT"], allow_slow_non_contiguous=True)
    self.ld(cbT, self.hy_conv_b[0].rearrange("(c p) -> p c", p=128), w=["ycbT"], allow_slow_non_contiguous=True)
    for o in range(2):
        self.ld(skT[:, o, :], self.hy_skip[0, o].rearrange("(c p) -> p c", p=128), w=["yskT"], allow_slow_non_contiguous=True)
    for i in range(2):
        V(lambda e: e.memset(PBs[i], 0.0), [], [f"yPB{i}"])
    def ld_wch(c_):
        self.ldc(wch[c_ % 2], self.hy_w_in[0][:, c_ * 128:(c_ + 1) * 128].rearrange("(k p) n -> p k n", p=128), w=[f"ywc{c_ % 2}"])
    ld_wch(0)
    pend_tr = []
    for c in range(64):
        PB, cv, cvb = PBs[c % 2], cvs[c % 2], cvbs[c % 2]
        PBk, cvk, cvbk = f"yPB{c % 2}", f"ycv{c % 2}", f"ycvb{c % 2}"
        wc, wck = wch[c % 2], f"ywc{c % 2}"
        if c + 1 < 64:
            ld_wch(c + 1)
        for tb in range(n // 512):
            pp = self.ps[tb % 2]
            pk = f"ps{tb % 2}"
            for kk in range(8):
                T(lambda e: e.matmul(pp, lhsT=wc[:, kk, :], rhs=self.hT[:, kk, tb * 512:(tb + 1) * 512], start=(kk == 0), stop=(kk == 7)),
                  [wck, "hT"], [pk])
            if c < 48:
                if grp == 1:
                    dstp = PB[:, 0, 1 + tb * 512:1 + (tb + 1) * 512]
                    srcp = pp
                else:
                    dstp = PB[:, 2 * tb:2 * tb + 2, 1:L + 1]
                    srcp = pp.rearrange("p (s t) -> p s t", s=2)
                A(lambda e: e.activation(out=dstp, in_=srcp, func=AF.Copy), [pk], [PBk])
            else:
                A(lambda e: e.activation(out=cvb[:, tb * 512:(tb + 1) * 512], in_=pp, func=AF.Silu), [pk], [cvbk])
        while pend_tr:
            pend_tr.pop(0)()
        if c >= 48:
            self.ld(self.sgT_scr[c - 48, :, t0:t0 + n], cvb, r=[cvbk], w=[("sgT", grp)])
            continue
        A(lambda e: e.activation(out=cv, in_=PB[:, :, 1:L + 1], func=AF.Identity, scale=cwT[:, 1, c:c + 1], bias=cbT[:, c:c + 1]),
          [PBk, "ycwT", "ycbT"], [cvk])
        V(lambda e: e.scalar_tensor_tensor(out=cv, in0=PB[:, :, 0:L], scalar=cwT[:, 0, c:c + 1], in1=cv, op0=ALU.mult, op1=ALU.add),
          [PBk, "ycwT", cvk], [cvk])
        V(lambda e: e.scalar_tensor_tensor(out=cv, in0=PB[:, :, 2:L + 2], scalar=cwT[:, 2, c:c + 1], in1=cv, op0=ALU.mult, op1=ALU.add),
          [PBk, "ycwT", cvk], [cvk])
        cvf = cv.rearrange("p s t -> p (s t)")
        dst = (self.vT_scr, self.x1T_scr, self.x2T_scr)[c // 16]
        self.ld(dst[c % 16, :, t0:t0 + n], cvf, r=[cvk], w=[(("vT", "x1T", "x2T")[c // 16], grp)])
        if c < 16:
            V(lambda e: e.tensor_copy(out=cvb, in_=cvf), [cvk], [cvbk])

            def do_tr(c=c, cvb=cvb, cvbk=cvbk):
                for t4 in range(n // 512):
                    for kk in range(4):
                        tt = t4 * 4 + kk
                        ptr = self.ps[2].bitcast(BF16)[:, kk * 128:(kk + 1) * 128]
                        T(lambda e: e.transpose(ptr, cvb[:, tt * 128:(tt + 1) * 128], self.identb), [cvbk, "identb"], ["ps2"])
                    A(lambda e: e.activation(out=ztt[:, t4 * 4:(t4 + 1) * 4, :].rearrange("p a b -> p (a b)"),
                                             in_=self.ps[2].bitcast(BF16)[:, 0:512], func=AF.Copy), ["ps2"], ["yztt"])
                self.ld(self.ztok_scr[0, t0:t0 + n, c * 128:(c + 1) * 128].rearrange("(tt p) c -> p tt c", p=128), ztt,
                        r=["yztt"], w=[("ztok0", grp)])
            pend_tr.append(do_tr)
    while pend_tr:
        pend_tr.pop(0)()
    self.arena_reset()
    sb = self.asb
    TBW = min(512, L)
    NTB = L // TBW
    zt = sb("yzt", [128, LT, 512], BF16)
    Cs = [sb(f"yCs{i}", [128, LT, 128], BF16) for i in range(2)]
    Ss = [sb(f"ySs{i}", [128, LT, 128], BF16) for i in range(2)]
    Yh = sb("yYh", [128, FT, 2, 512], BF16)
    ICs = sb("yIC", [128, FT, TBW], BF16); ISs = sb("yIS", [128, FT, TBW], BF16)
    kre = [sb(f"ykre{i}", [128, 512]) for i in range(2)]; kim = [sb(f"ykim{i}", [128, 512]) for i in range(2)]
    u1 = sb("yu1", [128, 512]); u2 = sb("yu2", [128, 512])
    NB2 = 2 if L == LP else 1
    tas = [sb(f"yta{i}", [128, TBW]) for i in range(NB2)]; txs = [sb(f"ytx{i}", [128, TBW]) for i in range(NB2)]
    tgs = [sb(f"ytg{i}", [128, TBW], BF16) for i in range(NB2)]; tos = [sb(f"yto{i}", [128, TBW]) for i in range(NB2)]
    tobs = [sb(f"ytob{i}", [128, TBW], BF16) for i in range(NB2)]
    ztt2s = [sb(f"yztt2{i}", [128, TBW // 128, 128], BF16) for i in range(NB2)]
    skT = sb("yskT2", [128, 2, 16])
    for o in range(2):
        self.ld(skT[:, o, :], self.hy_skip[0, o].rearrange("(c p) -> p c", p=128), w=["yskT2"], allow_slow_non_contiguous=True)
    small = (L == LP)
    if small:
        Call = sb("yCall", [128, FT, LT, 128], BF16); Sall = sb("ySall", [128, FT, LT, 128], BF16)
        kra = sb("ykra", [128, FT, 512]); kia = sb("ykia", [128, FT, 512])
        for ft in range(FT):
            self.ld(Call[:, ft], hc["C"][ft], w=["yCall"])
            self.ld(Sall[:, ft], hc["S"][ft], w=["ySall"])
        self.ld(ICs, hc["IC"][0], w=["yIC"])
        self.ld(ISs, hc["IS"][0], w=["yIS"])
    for o in range(2):
        zprev = (self.vT_scr, ("vT", grp)) if o == 0 else (self.z1T_scr, ("z1T", grp))
        xg = (self.x1T_scr, ("x1T", grp)) if o == 0 else (self.x2T_scr, ("x2T", grp))
        for cb in range(4):
          cs = slice(cb * 512, (cb + 1) * 512)
          if small:
              self.ld(kra, self.KH_scr[o, 0, 0:FT * 128, cs].rearrange("(ft p) c -> p ft c", p=128), r=[("KH", grp)], w=["ykra"])
              self.ld(kia, self.KH_scr[o, 1, 0:FT * 128, cs].rearrange("(ft p) c -> p ft c", p=128), r=[("KH", grp)], w=["ykia"])
          for s in range(nseq):
                tq = t0 + s * L
                self.ld(zt, self.ztok_scr[o, tq:tq + L, cs].rearrange("(lt p) c -> p lt c", p=128), r=[(f"ztok{o}", grp)], w=["yzt"])
                for ft in range(FT):
                    b = ft % 2
                    fs = slice(ft * 128, (ft + 1) * 128)
                    if small:
                        Csb, Ssb, krb, kib = Call[:, ft], Sall[:, ft], kra[:, ft], kia[:, ft]
                        Ck, Sk, krk, kik = "yCall", "ySall", "ykra", "ykia"
                    else:
                        Csb, Ssb, krb, kib = Cs[b], Ss[b], kre[b], kim[b]
                        Ck, Sk, krk, kik = f"yCs{b}", f"ySs{b}", f"ykre{b}", f"ykim{b}"
                        self.ld(Cs[b], hc["C"][ft], w=[Ck])
                        self.ld(Ss[b], hc["S"][ft], w=[Sk])
                        self.ld(kre[b], self.KH_scr[o, 0, fs, cs], r=[("KH", grp)], w=[krk])
                        self.ld(kim[b], self.KH_scr[o, 1, fs, cs], r=[("KH", grp)], w=[kik])
                    pA, pB = self.ps[0 + 2 * b], self.ps[1 + 2 * b]
                    pAk, pBk = f"ps{0 + 2 * b}", f"ps{1 + 2 * b}"
                    for lt in range(LT):
                        T(lambda e: e.matmul(pA, lhsT=Csb[:, lt, :], rhs=zt[:, lt, :], start=(lt == 0), stop=(lt == LT - 1)),
                          [Ck, "yzt"], [pAk])
                    for lt in range(LT):
                        T(lambda e: e.matmul(pB, lhsT=Ssb[:, lt, :], rhs=zt[:, lt, :], start=(lt == 0), stop=(lt == LT - 1)),
                          [Sk, "yzt"], [pBk])
                    V(lambda e: e.tensor_tensor(out=u1, in0=pA, in1=krb, op=ALU.mult), [pAk, krk], ["yu1"])
                    V(lambda e: e.tensor_tensor(out=u2, in0=pB, in1=kib, op=ALU.mult), [pBk, kik], ["yu2"])
                    G(lambda e: e.tensor_tensor(out=Yh[:, ft, 0, :], in0=u1, in1=u2, op=ALU.add), ["yu1", "yu2"], ["yYh"])
                    V(lambda e: e.tensor_tensor(out=u1, in0=pA, in1=kib, op=ALU.mult), [pAk, kik], ["yu1"])
                    V(lambda e: e.tensor_tensor(out=u2, in0=pB, in1=krb, op=ALU.mult), [pBk, krk], ["yu2"])
                    V(lambda e: e.tensor_tensor(out=Yh[:, ft, 1, :], in0=u1, in1=u2, op=ALU.subtract), ["yu1", "yu2"], ["yYh"])
                for tb in range(NTB):
                    tsl = slice(tb * TBW, (tb + 1) * TBW)
                    gsl = slice(tq + tb * TBW, tq + (tb + 1) * TBW)
                    if not small:
                        self.ld(ICs, hc["IC"][tb], w=["yIC"])
                        self.ld(ISs, hc["IS"][tb], w=["yIS"])
                    for cc in range(4):
                        ch = cb * 4 + cc
                        bi_ = cc % NB2
                        ta, tx, tg, to, tob, ztt2 = tas[bi_], txs[bi_], tgs[bi_], tos[bi_], tobs[bi_], ztt2s[bi_]
                        tak, txk, tgk, tok, tobk, zt2k = f"yta{bi_}", f"ytx{bi_}", f"ytg{bi_}", f"yto{bi_}", f"ytob{bi_}", f"yztt2{bi_}"
                        pz = self.ps[4 + (cc % 2)][:, 0:TBW]
                        pzk = f"ps{4 + (cc % 2)}"
                        self.ld(ta, zprev[0][ch, :, gsl], r=[zprev[1]], w=[tak])
                        self.ld(tx, xg[0][ch, :, gsl], r=[xg[1]], w=[txk])
                        if o == 1:
                            self.ld(tg, self.sgT_scr[ch, :, gsl], r=[("sgT", grp)], w=[tgk])
                        for ft in range(FT):
                            T(lambda e: e.matmul(pz, lhsT=Yh[:, ft, 0, cc * 128:(cc + 1) * 128], rhs=ICs[:, ft, :], start=(ft == 0), stop=False),
                              ["yYh", "yIC"], [pzk])
                            T(lambda e: e.matmul(pz, lhsT=Yh[:, ft, 1, cc * 128:(cc + 1) * 128], rhs=ISs[:, ft, :], start=False, stop=(ft == FT - 1)),
                              ["yYh", "yIS"], [pzk])
                        V(lambda e: e.scalar_tensor_tensor(out=ta, in0=ta, scalar=skT[:, o, ch:ch + 1], in1=pz, op0=ALU.mult, op1=ALU.add),
                          [tak, "yskT2", pzk], [tak])
                        if o == 0:
                            G(lambda e: e.tensor_tensor(out=to, in0=ta, in1=tx, op=ALU.mult), [tak, txk], [tok])
                            self.ld(self.z1T_scr[ch, :, gsl], to, r=[tok], w=[("z1T", grp)])
                            A(lambda e: e.activation(out=tob, in_=to, func=AF.Copy), [tok], [tobk])
                            for kk in range(TBW // 128):
                                ptr = self.ps[6 + bi_].bitcast(BF16)[:, kk * 128:(kk + 1) * 128]
                                T(lambda e: e.transpose(ptr, tob[:, kk * 128:(kk + 1) * 128], self.identb), [tobk, "identb"], ["ps6" if bi_ == 0 else "ps7"])
                            A(lambda e: e.activation(out=ztt2.rearrange("p a b -> p (a b)"), in_=self.ps[6 + bi_].bitcast(BF16)[:, 0:TBW], func=AF.Copy),
                              ["ps6" if bi_ == 0 else "ps7"], [zt2k])
                            self.ld(self.ztok_scr[1, gsl, ch * 128:(ch + 1) * 128].rearrange("(tt p) c -> p tt c", p=128), ztt2,
                                    r=[zt2k], w=[("ztok1", grp)])
                        else:
                            G(lambda e: e.tensor_tensor(out=to, in0=ta, in1=tx, op=ALU.mult), [tak, txk], [tok])
                            G(lambda e: e.tensor_tensor(out=tob, in0=to, in1=tg, op=ALU.mult), [tok, tgk], [tobk])
                            self.ld(self.gscr[ch, :, gsl], tob, r=[tobk], w=[("gscr", grp)])
    self.ysrc = (self.gscr, ("gscr", grp))
    return 2 * D


K.hy_decl = _hy_decl
K.sin_any = _sin_any
K.hy_filters = _hy_filters
K.hy_mixer = _hy_mixer


def hy_consts():
    out = {}
    HY_BANDS = 16
    min_decay = math.log(1e-2) / 1.5
    max_decay = math.log(1e-2) / 0.3
    out["c_absd"] = np.abs(np.linspace(min_decay, max_decay, 2048, dtype=np.float32)).astype(np.float32)
    out["c_ones"] = np.ones((128, 128), np.float32)
    for L in (2048, 256):
        FT = HY_FT[L]
        N = 2 * L
        t = np.linspace(0.0, 1.0, L, dtype=np.float32)[:, None]
        w = (2.0 * np.float32(math.pi) * np.arange(L, dtype=np.float32)[:, None] / np.float32(L)).astype(np.float32)
        f = np.linspace(1e-4, HY_BANDS - 1.0, HY_BANDS, dtype=np.float32)[None, :]
        fw_ = (f * w).astype(np.float32)
        feat = np.concatenate([t, np.cos(fw_), -np.sin(fw_)], -1).astype(np.float32)
        out[f"c_feat{L}"] = np.ascontiguousarray(feat.T)
        out[f"c_tneg{L}"] = np.ascontiguousarray((-t[:, 0]).reshape(L // 128, 128).T)
        tt = np.arange(L, dtype=np.int64)[:, None]
        ff = np.arange(FT * 128, dtype=np.int64)[None, :]
        ang = 2.0 * np.pi * ((tt * ff) % N).astype(np.float64) / N
        valid = (ff <= L).astype(np.float64)
        C = np.cos(ang) * valid
        S = np.sin(ang) * valid
        wf = np.where((ff == 0) | (ff == L), 1.0, 2.0) * valid / N
        LT = L // 128
        TBW = min(512, L)
        def fwd_tile(M):
            return np.ascontiguousarray(M.reshape(LT, 128, FT, 128).transpose(2, 1, 0, 3)).astype(np.float32).astype(ml_dtypes.bfloat16)
        def inv_tile(M):
            return np.ascontiguousarray(M.reshape(FT, 128, L // TBW, TBW).transpose(2, 1, 0, 3)).astype(np.float32).astype(ml_dtypes.bfloat16)
        out[f"c_C{L}"] = fwd_tile(C)
        out[f"c_S{L}"] = fwd_tile(S)
        out[f"c_IC{L}"] = inv_tile((C * wf).T)
        out[f"c_IS{L}"] = inv_tile((-S * wf).T)
    return out
```

```python
import math
import numpy as np
import ml_dtypes
import concourse.bass as bass
import concourse.mybir as mybir
from concourse.bass_utils import run_bass_kernel_spmd

F32 = mybir.dt.float32
BF16 = mybir.dt.bfloat16
I32 = mybir.dt.int32
ALU = mybir.AluOpType
AF = mybir.ActivationFunctionType
AX = mybir.AxisListType

D = 1024
LS = 2048
LP = 256
NPS = 4
NT = LS + NPS * LP
EPS = 1e-6
TWO_PI = 2.0 * math.pi
ARENA_W = 31500

SAME_ENG_SYNC = True


class _PEProxy:
    def __init__(self, eng):
        self.eng = eng
        self.stop = True

    def matmul(self, *a, **kw):
        self.stop = bool(kw.get("stop", True))
        return self.eng.matmul(*a, **kw)

    def transpose(self, *a, **kw):
        self.stop = True
        return self.eng.transpose(*a, **kw)


class Fw:
    def __init__(self, nc, n_dma_sems=20):
        self.nc = nc
        self.engs = {}
        for name in ("tensor", "vector", "scalar", "gpsimd", "sync"):
            e = getattr(nc, name)
            self.engs[name] = dict(eng=e, sem=nc.alloc_semaphore("s_" + name), count=0, seen={})
        self.dma_pool = {}
        for q in ("sync", "gpsimd", "scalar"):
            self.dma_pool[q] = dict(
                sems=[nc.alloc_semaphore(f"d_{q}_{i}") for i in range(n_dma_sems)],
                vals=[0] * n_dma_sems, nxt=0)
        self.bufs = {}
        self.sem_owner = {id(E["sem"]): name for name, E in self.engs.items()}

    def _st(self, key):
        s = self.bufs.get(key)
        if s is None:
            s = dict(w=None, r=[])
            self.bufs[key] = s
        return s

    def _deps(self, reads, writes):
        deps = []
        for k in reads:
            s = self._st(k)
            if s["w"] is not None:
                deps.append(s["w"])
        for k in writes:
            s = self._st(k)
            if s["w"] is not None:
                deps.append(s["w"])
            deps.extend(s["r"])
        return deps

    def _wait(self, E, deps):
        best = {}
        for (sem, val) in deps:
            if sem is E["sem"] and not SAME_ENG_SYNC:
                continue
            k = id(sem)
            if k not in best or best[k][1] < val:
                best[k] = (sem, val)
        for k, (sem, val) in best.items():
            if E["seen"].get(k, 0) < val:
                E["eng"].wait_ge(sem, val)
                E["seen"][k] = val

    def _mark(self, tok, reads, writes):
        for k in reads:
            r = self._st(k)["r"]
            r.append(tok)
            if len(r) > 12:
                best = {}
                for (sem, val) in r:
                    if id(sem) not in best or best[id(sem)][1] < val:
                        best[id(sem)] = (sem, val)
                r[:] = list(best.values())
        for k in writes:
            s = self._st(k)
            s["w"] = tok
            s["r"] = []

    def op(self, eng, fn, reads=(), writes=()):
        E = self.engs[eng]
        self._wait(E, self._deps(reads, writes))
        if eng == "tensor":
            px = _PEProxy(E["eng"])
            ins = fn(px)
            if not px.stop:
                E.setdefault("pend", []).append((tuple(reads), tuple(writes)))
                return ins
            pend = E.get("pend", [])
            E["pend"] = []
            E["count"] += 1
            ins.then_inc(E["sem"], 1)
            tok = (E["sem"], E["count"])
            for (r, w) in pend:
                self._mark(tok, r, w)
            self._mark(tok, reads, writes)
            return ins
        ins = fn(E["eng"])
        E["count"] += 1
        ins.then_inc(E["sem"], 1)
        self._mark((E["sem"], E["count"]), reads, writes)
        return ins

    def dma(self, q, out, in_, reads=(), writes=(), **kw):
        E = self.engs[q]
        P = self.dma_pool[q]
        i = P["nxt"]
        P["nxt"] = (i + 1) % len(P["sems"])
        sem = P["sems"][i]
        deps = self._deps(reads, writes)
        if P["vals"][i] > 0:
            deps.append((sem, P["vals"][i]))
        self._wait(E, deps)
        ins = E["eng"].dma_start(out=out, in_=in_, **kw)
        P["vals"][i] += 16
        ins.then_inc(sem, 16)
        tok = (sem, P["vals"][i])
        self._mark(tok, reads, writes)
        return tok

    def barrier(self):
        toks = [(E["sem"], E["count"]) for E in self.engs.values() if E["count"] > 0]
        for P in self.dma_pool.values():
            for sem, val in zip(P["sems"], P["vals"]):
                if val > 0:
                    toks.append((sem, val))
        for E in self.engs.values():
            self._wait(E, toks)

    def finish(self, out_keys):
        E = self.engs["sync"]
        deps = []
        for k in out_keys:
            s = self._st(k)
            if s["w"] is not None:
                deps.append(s["w"])
        self._wait(E, deps)


def AP(t, off, dims):
    return bass.AP(t.tensor if hasattr(t, "tensor") else t, off, [list(d) for d in dims])


class K:
    def __init__(self, layers=(0, 1, 2, 3), final=True, dbg=False):
        self.dbg = dbg
        self.layers = layers
        self.final = final
        nc = self.nc = bass.Bass("TRN2", target_bir_lowering=False)
        self.fw = Fw(nc)
        self.inputs = {}
        self.build()

    def din(self, name, shape, dt=F32):
        t = self.nc.dram_tensor(name, list(shape), dt, kind="ExternalInput").ap()
        self.inputs[name] = (tuple(shape), dt)
        return t

    def dout(self, name, shape, dt=F32):
        return self.nc.dram_tensor(name, list(shape), dt, kind="ExternalOutput").ap()

    def dscr(self, name, shape, dt=F32):
        return self.nc.dram_tensor(name, list(shape), dt, kind="Internal").ap()

    def sb(self, name, shape, dt=F32):
        return self.nc.alloc_sbuf_tensor(name, list(shape), dt).ap()

    def arena_reset(self):
        self.fw.barrier()
        self.aoff = 0

    def asb(self, name, shape, dt=F32):
        n = 1
        for x in shape[1:]:
            n *= x
        words = n if dt in (F32, I32) else (n + 1) // 2
        words = (words + 7) // 8 * 8
        assert self.aoff + words <= ARENA_W, (name, self.aoff, words)
        v = self.arena[0:shape[0], self.aoff:self.aoff + words]
        self.aoff += words
        if dt not in (F32,):
            v = v.bitcast(dt)
        v = v[:, 0:n]
        if len(shape) > 2:
            names = " ".join(f"a{i}" for i in range(len(shape) - 1))
            kw = {f"a{i}": shape[i + 1] for i in range(len(shape) - 2)}
            v = v.rearrange(f"p ({names}) -> p {names}", **kw)
        return v

    def V(self, fn, r=(), w=()):
        return self.fw.op("vector", fn, r, w)

    def G(self, fn, r=(), w=()):
        return self.fw.op("gpsimd", fn, r, w)

    def A(self, fn, r=(), w=()):
        return self.fw.op("scalar", fn, r, w)

    def T(self, fn, r=(), w=()):
        return self.fw.op("tensor", fn, r, w)

    def ld(self, out, in_, r=(), w=(), q="sync", **kw):
        if q == "sync" and r and "DRam" in type(out.tensor).__name__ and "DRam" not in type(in_.tensor).__name__:
            engs = set()
            for k in r:
                st = self.fw.bufs.get(k)
                if st is None or st["w"] is None:
                    continue
                engs.add(self.fw.sem_owner.get(id(st["w"][0]), "dma"))
            if len(engs) == 1:
                e = engs.pop()
                if e in ("scalar", "gpsimd"):
                    q = e
                elif e == "vector":
                    q = "scalar"
        return self.fw.dma(q, out, in_, r, w, **kw)

    def ldc(self, out, in_, r=(), w=()):
        return self.fw.dma("gpsimd", out, in_, r, w)

    def build(self):
        nc = self.nc
        self.xs = self.din("xs", [LS, D])
        self.xp = self.din("xp", [NPS * LP, D])
        self.cvec = self.din("cvec", [2, D])
        self.st5 = self.din("st5", [2, 128, 128])
        self.stret = self.din("stret", [2, 8, 128, 256])
        self.norm_g = self.din("norm_g", [4, D])
        self.mod_w = self.din("mod_w", [4, D, 3 * D])
        self.mod_b = self.din("mod_b", [4, 3 * D])
        self.s5_w_in = self.din("s5_w_in", [2, D, 2 * D])
        self.s5_lam_re = self.din("s5_lam_re", [2, 2, 64, 64])
        self.s5_lam_im = self.din("s5_lam_im", [2, 2, 64, 64])
        self.s5_log_step = self.din("s5_log_step", [2, 2, 64])
        self.s5_b_re = self.din("s5_b_re", [2, 2, 64, 64, 16])
        self.s5_b_im = self.din("s5_b_im", [2, 2, 64, 64, 16])
        self.s5_c_re = self.din("s5_c_re", [2, 2, 64, 16, 64])
        self.s5_c_im = self.din("s5_c_im", [2, 2, 64, 16, 64])
        self.s5_d = self.din("s5_d", [2, D])
        self.s5_w_glu = self.din("s5_w_glu", [2, D, D])
        self.s5_b_glu = self.din("s5_b_glu", [2, D])
        self.s5_w_out = self.din("s5_w_out", [2, D, D])
        self.final_g = self.din("final_g", [D])
        self.ret_decl()
        self.hy_decl()
        self.c_identb = self.din("c_identb", [128, 128], BF16)
        self.c_identf = self.din("c_identf", [128, 128])
        self.c_pmask = self.din("c_pmask", [128, 8])
        self.c_bdmask = self.din("c_bdmask", [128, 128])
        self.ys = self.dout("ys", [LS, D])
        self.yp = self.dout("yp", [NPS * LP, D])
        self.ns5 = self.dout("ns5", [NPS, 2, 128, 128])
        self.nret = self.dout("nret", [NPS, 2, 8, 128, 256])
        self.xres = (self.dout if self.dbg else self.dscr)("xres", [NT, D])
        self.gscr = self.dscr("gscr", [16, 128, NT], BF16)
        self.g2scr = self.dscr("g2scr", [8, 128, NT], BF16)
        self.Tsb_scr = self.dscr("Tsb_scr", [2, 8, 128, 2048], BF16)
        self.Kc_scr = self.dscr("Kc_scr", [2, 8, 128, 1920], BF16)
        self.CA_scr = self.dscr("CA_scr", [2, 8, 128, 2, 2304], BF16)
        self.identb = self.sb("identb", [128, 128], BF16)
        self.identf = self.sb("identf", [128, 128])
        self.pmask = self.sb("pmask", [128, 8])
        self.bdmask = self.sb("bdmask", [128, 128])
        self.hT = self.sb("hT", [128, 8, LS], BF16)
        self.arena = self.sb("arena", [128, ARENA_W])
        self.aoff = 0
        self.wgs = [self.sb(f"wgs{i}", [128, 8, 128], BF16) for i in range(2)]
        self.wst = [self.sb("wst0", [128, 8, 512], BF16)] * 2
        self.xt = [self.sb(f"xt{i}", [128, D]) for i in range(2)]
        self.xn = [self.sb(f"xn{i}", [128, D], BF16) for i in range(2)]
        self.sq = self.sb("sq", [128, D])
        self.stat = self.sb("stat", [128, 8])
        self.modT = self.sb("modT", [128, 4, 24, 2])
        self.gsc = self.sb("gsc", [128, 4, 8, 2])
        self.gt_bc = self.sb("gt_bc", [128, 2, D])
        self.cT = self.sb("cT", [128, 8, 2])
        self.cTb = self.sb("cTb", [128, 8, 2], BF16)
        self.cTrep = self.sb("cTrep", [128, 2, 8, 128], BF16)
        self.ngT = self.sb("ngT", [128, 4, 8])
        self.mbT = self.sb("mbT", [128, 4, 24])
        self.mbg = None
        self.fgb = self.sb("fgb", [128, D])
        self.ylt = [self.sb("ylt", [128, 16, 128], BF16)] * 2
        self.ps = [nc.alloc_psum_tensor(f"ps{i}", [128, 512], F32).ap() for i in range(8)]

        f = self.fw
        self.ld(self.identb, self.c_identb, w=["identb"])
        self.ld(self.identf, self.c_identf, w=["identf"])
        self.ld(self.pmask, self.c_pmask, w=["pmask"])
        self.ld(self.bdmask, self.c_bdmask, w=["bdmask"])
        self.ld(self.fgb, self.final_g.partition_broadcast(128), w=["fgb"])

        self.mod_stage()
        out_keys = []
        nl = len(self.layers)
        for li, i in enumerate(self.layers):
            last = (li == nl - 1) and self.final
            first = (li == 0)
            self.gate_table(i)
            for grp in (1, 0):
                kind = i % 3
                self.prologue(i, grp, first)
                if kind == 0:
                    kdim = self.s5_mixer(i // 3, grp)
                    w_out = self.s5_w_out[i // 3]
                elif kind == 1:
                    kdim = self.ret_mixer(grp)
                    w_out = self.ret_w_out[0]
                else:
                    kdim = self.hy_mixer(grp)
                    w_out = self.hy_w_out[0]
                self.epilogue(i, grp, kdim, w_out, last)
        out_keys = ["ys", "yp", "ns5", "nret", "xres"]
        f.finish(out_keys)

    def trange(self, grp):
        return (0, LS) if grp == 1 else (LS, NPS * LP)

    def mod_stage(self):
        for r in range(2):
            self.ld(self.cT[:, :, r], self.cvec[r].rearrange("(k p) -> p k", p=128), w=["cT"], allow_slow_non_contiguous=True)
        self.A(lambda e: e.activation(out=self.cTb, in_=self.cT, func=AF.Silu), ["cT"], ["cTb"])
        for r in range(2):
            self.V(lambda e: e.tensor_copy(out=self.cTrep[:, r], in_=self.cTb[:, :, r:r + 1].to_broadcast([128, 8, 128])),
                   ["cTb"], ["cTrep"])
        for l in range(4):
            self.ld(self.ngT[:, l, :], self.norm_g[l].rearrange("(k p) -> p k", p=128), w=["ngT"], allow_slow_non_contiguous=True)
            self.ld(self.mbT[:, l, :], self.mod_b[l].rearrange("(k p) -> p k", p=128), w=["mbT"], allow_slow_non_contiguous=True)
        for i in self.layers:
            for half in range(4):
                wt = self.wst[half % 2]
                wk = "wst0"
                self.ldc(wt, self.mod_w[i, :, half * 512:(half + 1) * 512].rearrange("(k p) n -> p k n", p=128), w=[wk])
                for cc in range(4):
                    ch = half * 4 + cc
                    pt = self.ps[0][:, 0:2]
                    for k in range(8):
                        self.T(lambda e: e.matmul(pt, lhsT=wt[:, k, cc * 128:(cc + 1) * 128], rhs=self.cTb[:, k, :],
                                                  start=(k == 0), stop=(k == 7)), [wk, "cTb"], ["ps0"])
                    self.V(lambda e: e.tensor_tensor(out=self.modT[:, i, ch, :], in0=pt,
                                                     in1=self.mbT[:, i, ch:ch + 1].to_broadcast([128, 2]), op=ALU.add),
                           ["ps0", "mbT"], ["modT"])
            self.V(lambda e: e.tensor_scalar(out=self.gsc[:, i], in0=self.modT[:, i, 8:16, :], scalar1=1.0, scalar2=None,
                                             op0=ALU.add), ["modT"], ["gsc"])
            self.V(lambda e: e.tensor_tensor(out=self.gsc[:, i], in0=self.gsc[:, i],
                                             in1=self.ngT[:, i, :].unsqueeze(2).to_broadcast([128, 8, 2]), op=ALU.mult),
                   ["gsc", "ngT"], ["gsc"])

    def gate_table(self, i):
        self.mbg = self.sq
        self.ld(self.mbg, self.mod_b[i, 2 * D:3 * D].partition_broadcast(128), w=["sq"])
        for half in range(2):
            wt = self.wst[half % 2]
            wk = "wst0"
            self.ldc(wt, self.mod_w[i, :, 2 * D + half * 512: 2 * D + (half + 1) * 512].rearrange("(k p) n -> p k n", p=128),
                     w=[wk])
            for r in range(2):
                pt = self.ps[1]
                for k in range(8):
                    self.T(lambda e: e.matmul(pt, lhsT=self.cTrep[:, r, k, :], rhs=wt[:, k, :],
                                              start=(k == 0), stop=(k == 7)), [wk, "cTrep"], ["ps1"])
                self.V(lambda e: e.tensor_tensor(out=self.gt_bc[:, r, half * 512:(half + 1) * 512], in0=pt,
                                                 in1=self.mbg[:, half * 512:(half + 1) * 512], op=ALU.add),
                       ["ps1", "sq"], ["gt_bc"])

    def prologue(self, i, grp, first):
        t0, n = self.trange(grp)
        for tt in range(n // 128):
            b = tt % 2
            xt, xn = self.xt[b], self.xn[b]
            if first:
                src = self.xs[tt * 128:(tt + 1) * 128, :] if grp == 1 else self.xp[tt * 128:(tt + 1) * 128, :]
                rk = []
            else:
                src = self.xres[t0 + tt * 128: t0 + (tt + 1) * 128, :]
                rk = ["xres"]
            self.ld(xt, src, r=rk, w=[f"xt{b}"])
            self.rms_scale(xt, f"xt{b}", xn, f"xn{b}")
            for k in range(8):
                pt = self.ps[2 + (k % 2)].bitcast(BF16)[:, 0:128]
                pk = f"ps{2 + (k % 2)}"
                self.T(lambda e: e.transpose(pt, xn[:, k * 128:(k + 1) * 128], self.identb), [f"xn{b}", "identb"], [pk])
                self.A(lambda e: e.activation(out=self.hT[:, k, tt * 128:(tt + 1) * 128], in_=pt, func=AF.Identity,
                                              scale=self.gsc[:, i, k, grp:grp + 1], bias=self.modT[:, i, k, grp:grp + 1]),
                       [pk, "gsc", "modT"], ["hT"])

    def rms_scale(self, xt, xk, out, ok, gtab=None):
        self.A(lambda e: e.activation(out=self.sq, in_=xt, func=AF.Square, accum_out=self.stat[:, 0:1]), [xk], ["sq", "stat"])
        self.V(lambda e: e.tensor_scalar(out=self.stat[:, 1:2], in0=self.stat[:, 0:1], scalar1=1.0 / D, scalar2=EPS,
                                         op0=ALU.mult, op1=ALU.add), ["stat"], ["stat"])
        self.A(lambda e: e.activation(out=self.stat[:, 3:4], in_=self.stat[:, 1:2], func=AF.Sqrt), ["stat"], ["stat"])
        self.V(lambda e: e.reciprocal(out=self.stat[:, 2:3], in_=self.stat[:, 3:4]), ["stat"], ["stat"])
        if gtab is None:
            self.V(lambda e: e.tensor_scalar(out=out, in0=xt, scalar1=self.stat[:, 2:3], scalar2=None, op0=ALU.mult),
                   [xk, "stat"], [ok])
        else:
            self.V(lambda e: e.scalar_tensor_tensor(out=out, in0=xt, scalar=self.stat[:, 2:3], in1=gtab,
                                                    op0=ALU.mult, op1=ALU.mult), [xk, "stat", "fgb"], [ok])

    def epilogue(self, i, grp, kdim, w_out, last):
        t0, n = self.trange(grp)
        kc = kdim // 128
        if kdim > D:
            self.arena_reset()
            self.wres = self.asb("wres", [128, 16, 1024], BF16)
            wrk = ["wres"]
        else:
            self.wres = self.SS.bitcast(BF16).rearrange("p (k n) -> p k n", k=8)
            wrk = ["SS", "SS2"]
        self.ldc(self.wres[:, 0:kc, :], w_out.rearrange("(k p) n -> p k n", p=128), w=wrk)
        yb = self.xn
        for tt in range(n // 128):
            b = tt % 2
            xt = self.xt[b]
            src = self.xres[t0 + tt * 128: t0 + (tt + 1) * 128, :]
            if i == self.layers[0]:
                src = self.xs[tt * 128:(tt + 1) * 128, :] if grp == 1 else self.xp[tt * 128:(tt + 1) * 128, :]
                rk = []
            else:
                rk = ["xres"]
            self.ld(xt, src, r=rk, w=[f"xt{b}"])
            yt = self.ylt[b]
            ysrc, ykey = self.ysrc
            self.ld(yt[:, 0:kc, :], ysrc[0:kc, :, t0 + tt * 128: t0 + (tt + 1) * 128].rearrange("k p t -> p k t"),
                    r=[ykey], w=["ylt"], q="sync")
            for h in range(2):
                pt = self.ps[4 + h]
                for k in range(kc):
                    self.T(lambda e: e.matmul(pt, lhsT=yt[:, k, :], rhs=self.wres[:, k, h * 512:(h + 1) * 512],
                                              start=(k == 0), stop=(k == kc - 1)), ["ylt"] + wrk, [f"ps{4 + h}"])
                self.V(lambda e: e.tensor_tensor(out=self.sq[:, h * 512:(h + 1) * 512], in0=pt,
                                                 in1=self.gt_bc[:, grp, h * 512:(h + 1) * 512], op=ALU.mult),
                       [f"ps{4 + h}", "gt_bc"], ["sq"])
            self.V(lambda e: e.tensor_tensor(out=xt, in0=xt, in1=self.sq, op=ALU.add), [f"xt{b}", "sq"], [f"xt{b}"])
            if not last:
                self.ld(self.xres[t0 + tt * 128: t0 + (tt + 1) * 128, :], xt, r=[f"xt{b}"], w=["xres"])
            else:
                ot = self.xt[1 - b]
                self.rms_scale(xt, f"xt{b}", ot, f"xt{1 - b}", gtab=self.fgb)
                dst = self.ys if grp == 1 else self.yp
                self.ld(dst[tt * 128:(tt + 1) * 128, :], ot, r=[f"xt{1 - b}"], w=["ys" if grp == 1 else "yp"])

    def E(self, eng, fn, r=(), w=()):
        return self.fw.op(eng, fn, r, w)

    def sin_of(self, out, x, shift, eng="vector"):
        y, yi, yf = self.tr_y, self.tr_yi, self.tr_yf
        self.E(eng, lambda e: e.tensor_scalar(out=y, in0=x, scalar1=1.0 / TWO_PI, scalar2=shift / TWO_PI + 8.0,
                                              op0=ALU.mult, op1=ALU.add), ["trx"], ["try"])
        self.E(eng, lambda e: e.tensor_copy(out=yi, in_=y), ["try"], ["tryi"])
        self.E(eng, lambda e: e.tensor_copy(out=yf, in_=yi), ["tryi"], ["tryf"])
        self.E(eng, lambda e: e.tensor_tensor(out=y, in0=y, in1=yf, op=ALU.subtract), ["try", "tryf"], ["try"])
        self.E(eng, lambda e: e.tensor_scalar(out=yf, in0=y, scalar1=0.5, scalar2=None, op0=ALU.is_gt), ["try"], ["tryf"])
        self.E(eng, lambda e: e.tensor_tensor(out=y, in0=y, in1=yf, op=ALU.subtract), ["try", "tryf"], ["try"])
        self.E(eng, lambda e: e.tensor_scalar(out=yf, in0=y, scalar1=-0.5, scalar2=None, op0=ALU.is_lt), ["try"], ["tryf"])
        self.E(eng, lambda e: e.tensor_tensor(out=y, in0=y, in1=yf, op=ALU.add), ["try", "tryf"], ["try"])
        self.A(lambda e: e.activation(out=out, in_=y, func=AF.Sin, scale=6.283185), ["try"], ["trx"])

    def s5_alloc(self):
        sb = self.asb
        self.wgl = [sb(f"wgl{i}", [128, 8, 128], BF16) for i in range(2)]
        self.LR = sb("LR", [128, 128]); self.LI = sb("LI", [128, 128]); self.DT = sb("DT", [128, 128])
        self.ANG = sb("ANG", [128, 128]); self.AR = sb("AR", [128, 128])
        self.SN = sb("SN", [128, 128]); self.CS = sb("CS", [128, 128])
        self.tr_y = sb("tr_y", [128, 128]); self.tr_yi = sb("tr_yi", [128, 128], I32); self.tr_yf = sb("tr_yf", [128, 128])
        self.PR = sb("PR", [128, 9, 128]); self.PI = sb("PI", [128, 9, 128])
        self.FR = sb("FR", [128, 128]); self.FI = sb("FI", [128, 128])
        self.t1 = sb("t1", [128, 256]); self.t2 = sb("t2", [128, 256]); self.t3 = sb("t3", [128, 256])
        self.BRk = sb("BRk", [128, 2, 8, 16]); self.BIk = sb("BIk", [128, 2, 8, 16])
        self.bbr = sb("bbr", [128, 2, 8, 16]); self.bbi = sb("bbi", [128, 2, 8, 16])
        self.CRn = sb("CRn", [128, 2, 2, 64]); self.CIn = sb("CIn", [128, 2, 2, 64])
        self.CRk = sb("CRk", [128, 2, 8, 16]); self.CIk = sb("CIk", [128, 2, 8, 16])
        self.BA = sb("BA", [128, 8, 2, 128]); self.CC = sb("CC", [128, 2, 128])
        self.CAr = sb("CAr", [128, 9, 2, 128], BF16); self.CAi = sb("CAi", [128, 9, 2, 128], BF16)
        self.Tsb = sb("Tsb", [128, 16, 128], BF16)
        self.LW = sb("LW", [128, 16, 2, 128], BF16)
        self.LM = sb("LM", [128, 4, 2, 2, 2, 128], BF16)
        self.Kc = sb("Kc", [128, 15, 128], BF16)
        self.SS = sb("SS", [128, 8 * 2 * 256]); self.HP = sb("HP", [128, 8 * 2 * 256], BF16)
        self.HH = sb("HH", [128, 64]); self.HT1 = sb("HT1", [128, 64]); self.HU1 = sb("HU1", [128, 64])
        self.A1 = sb("A1", [128, 64]); self.A2 = sb("A2", [128, 64])
        self.HL = sb("HL", [128, 256]); self.HLT1 = sb("HLT1", [128, 256]); self.HLU1 = sb("HLU1", [128, 256])
        self.A1f = sb("A1f", [128, 256]); self.A2f = sb("A2f", [128, 256])
        self.PWp = sb("PWp", [128, 256]); self.PW = sb("PW", [128, 256]); self.HST = sb("HST", [128, 256])
        self.Hc = sb("Hc", [128, 16]); self.Hc0 = sb("Hc0", [128, 16]); self.HcT = sb("HcT", [128, 16]); self.HcU = sb("HcU", [128, 16])
        self.B1 = sb("B1", [128, 16]); self.B2 = sb("B2", [128, 16])
        self.TC1 = sb("TC1", [128, 512]); self.TC2 = sb("TC2", [128, 512])
        self.h0T = sb("h0T", [128, 2, 2, 32]); self.FS = sb("FS", [128, 4, 2, 2, 32]); self.FSo = sb("FSo", [128, 128])
        self.dcol = sb("dcol", [128, 8]); self.bgT = sb("bgT", [128, 8])
        self.u8 = sb("u8", [128, 8, LS // 8], BF16); self.g_k = sb("g_k", [128, LS], BF16)
        self.t1g = sb("t1g", [128, 256]); self.t2g = sb("t2g", [128, 256])
        self.gblk = self.wst[0]
        self.sgm = self.SS[:, 0:512]; self.slu = self.SS[:, 512:1024]; self.yb = [self.HP[:, i * 512:(i + 1) * 512] for i in range(2)]
        self.V(lambda e: e.memset(self.LM, 0.0), [], ["LM"])

    def s5_layer_prep(self, js):
        V, A = self.V, self.A
        for half in range(2):
            hs = slice(half * 64, half * 64 + 64)
            self.ld(self.LR[hs, :], self.s5_lam_re[js].rearrange("d g p -> p (d g)"), w=["LR"], allow_slow_non_contiguous=True)
            self.ld(self.LI[hs, :], self.s5_lam_im[js].rearrange("d g p -> p (d g)"), w=["LI"], allow_slow_non_contiguous=True)
        self.ld(self.DT, self.s5_log_step[js].rearrange("d g -> (d g)").partition_broadcast(128), w=["DT"])
        self.ld(self.dcol, self.s5_d[js].rearrange("(k p) -> p k", p=128), w=["dcol"], allow_slow_non_contiguous=True)
        self.ld(self.bgT, self.s5_b_glu[js].rearrange("(k p) -> p k", p=128), w=["bgT"], allow_slow_non_contiguous=True)
        A(lambda e: e.activation(out=self.DT, in_=self.DT, func=AF.Exp), ["DT"], ["DT"])
        V(lambda e: e.tensor_tensor(out=self.ANG, in0=self.LI, in1=self.DT, op=ALU.mult), ["LI", "DT"], ["trx", "ANG"])
        V(lambda e: e.tensor_tensor(out=self.AR, in0=self.LR, in1=self.DT, op=ALU.mult), ["LR", "DT"], ["AR"])
        A(lambda e: e.activation(out=self.AR, in_=self.AR, func=AF.Exp), ["AR"], ["AR"])
        self.sin_of(self.SN, self.ANG, 0.0)
        self.sin_of(self.CS, self.ANG, math.pi / 2)
        PR, PI = self.PR, self.PI
        V(lambda e: e.memset(PR[:, 0], 1.0), [], ["PR"])
        V(lambda e: e.memset(PI[:, 0], 0.0), [], ["PI"])
        V(lambda e: e.tensor_tensor(out=PR[:, 1], in0=self.AR, in1=self.CS, op=ALU.mult), ["AR", "trx"], ["PR"])
        V(lambda e: e.tensor_tensor(out=PI[:, 1], in0=self.AR, in1=self.SN, op=ALU.mult), ["AR", "trx"], ["PI"])
        t1, t2 = self.t1[:, 0:128], self.t2[:, 0:128]
        for m in range(2, 9):
            V(lambda e: e.tensor_tensor(out=t1, in0=PR[:, m - 1], in1=PR[:, 1], op=ALU.mult), ["PR"], ["t1"])
            V(lambda e: e.tensor_tensor(out=t2, in0=PI[:, m - 1], in1=PI[:, 1], op=ALU.mult), ["PI"], ["t2"])
            V(lambda e: e.tensor_tensor(out=PR[:, m], in0=t1, in1=t2, op=ALU.subtract), ["t1", "t2"], ["PR"])
            V(lambda e: e.tensor_tensor(out=t1, in0=PR[:, m - 1], in1=PI[:, 1], op=ALU.mult), ["PR", "PI"], ["t1"])
            V(lambda e: e.tensor_tensor(out=t2, in0=PI[:, m - 1], in1=PR[:, 1], op=ALU.mult), ["PR", "PI"], ["t2"])
            V(lambda e: e.tensor_tensor(out=PI[:, m], in0=t1, in1=t2, op=ALU.add), ["t1", "t2"], ["PI"])
        nr, den = self.SN, self.CS
        V(lambda e: e.tensor_scalar(out=nr, in0=PR[:, 1], scalar1=-1.0, scalar2=None, op0=ALU.add), ["PR"], ["trx"])
        V(lambda e: e.tensor_tensor(out=t1, in0=self.LR, in1=self.LR, op=ALU.mult), ["LR"], ["t1"])
        V(lambda e: e.tensor_tensor(out=t2, in0=self.LI, in1=self.LI, op=ALU.mult), ["LI"], ["t2"])
        V(lambda e: e.tensor_tensor(out=den, in0=t1, in1=t2, op=ALU.add), ["t1", "t2"], ["trx"])
        V(lambda e: e.reciprocal(out=den, in_=den), ["trx"], ["trx"])
        V(lambda e: e.tensor_tensor(out=t1, in0=nr, in1=self.LR, op=ALU.mult), ["trx", "LR"], ["t1"])
        V(lambda e: e.tensor_tensor(out=t2, in0=PI[:, 1], in1=self.LI, op=ALU.mult), ["PI", "LI"], ["t2"])
        V(lambda e: e.tensor_tensor(out=t1, in0=t1, in1=t2, op=ALU.add), ["t1", "t2"], ["t1"])
        V(lambda e: e.tensor_tensor(out=self.FR, in0=t1, in1=den, op=ALU.mult), ["t1", "trx"], ["FR"])
        V(lambda e: e.tensor_tensor(out=t1, in0=PI[:, 1], in1=self.LR, op=ALU.mult), ["PI", "LR"], ["t1"])
        V(lambda e: e.tensor_tensor(out=t2, in0=nr, in1=self.LI, op=ALU.mult), ["trx", "LI"], ["t2"])
        V(lambda e: e.tensor_tensor(out=t1, in0=t1, in1=t2, op=ALU.subtract), ["t1", "t2"], ["t1"])
        V(lambda e: e.tensor_tensor(out=self.FI, in0=t1, in1=den, op=ALU.mult), ["t1", "trx"], ["FI"])
        self.ld(self.FSo, self.st5[js], w=["FSo"])
        pt = self.ps[1][:, 0:128]
        self.T(lambda e: e.transpose(pt, self.FSo, self.identf), ["FSo", "identf"], ["ps1"])
        V(lambda e: e.tensor_copy(out=self.h0T.rearrange("p d x g -> p (d x g)"), in_=pt), ["ps1"], ["h0T"])

    def bcg(self, tab, m, k):
        a = tab[:, m, :].rearrange("p (d g) -> p d g", d=2)[:, :, 8 * k:8 * k + 8]
        return a.unsqueeze(3).to_broadcast([128, 2, 8, 16])

    def bcf(self, tab, k):
        a = tab.rearrange("p (d g) -> p d g", d=2)[:, :, 8 * k:8 * k + 8]
        return a.unsqueeze(3).to_broadcast([128, 2, 8, 16])

    def cmul(self, eng, outr, outi, ar, ai, br, bi, rk, wk, hs_r=slice(0, 128), hs_i=slice(0, 128), negi=False):
        if eng == "gpsimd":
            t1 = self.t1g.rearrange("p (d g h) -> p d g h", d=2, g=8)
            t2 = self.t2g.rearrange("p (d g h) -> p d g h", d=2, g=8)
            k1, k2 = "t1g", "t2g"
        else:
            t1 = self.t1.rearrange("p (d g h) -> p d g h", d=2, g=8)
            t2 = self.t2.rearrange("p (d g h) -> p d g h", d=2, g=8)
            k1, k2 = "t1", "t2"
        E = self.E
        s = hs_r
        E(eng, lambda e: e.tensor_tensor(out=t1[s], in0=ar[s], in1=br[s], op=ALU.mult), rk, [k1])
        E(eng, lambda e: e.tensor_tensor(out=t2[s], in0=ai[s], in1=bi[s], op=ALU.mult), rk, [k2])
        E(eng, lambda e: e.tensor_tensor(out=outr[s], in0=t1[s], in1=t2[s], op=ALU.subtract), [k1, k2], wk)
        s = hs_i
        E(eng, lambda e: e.tensor_tensor(out=t1[s], in0=ar[s], in1=bi[s], op=ALU.mult), rk, [k1])
        E(eng, lambda e: e.tensor_tensor(out=t2[s], in0=ai[s], in1=br[s], op=ALU.mult), rk, [k2])
        if negi:
            E(eng, lambda e: e.tensor_tensor(out=t1[s], in0=t1[s], in1=t2[s], op=ALU.add), [k1, k2], [k1])
            E(eng, lambda e: e.tensor_scalar(out=outi[s], in0=t1[s], scalar1=-1.0, scalar2=None, op0=ALU.mult), [k1], wk)
        else:
            E(eng, lambda e: e.tensor_tensor(out=outi[s], in0=t1[s], in1=t2[s], op=ALU.add), [k1, k2], wk)

    def s5_scan1(self, k, seng, SSv, HPv, SQs, ncg, ncs, nseq, A1s, A2s, grp):
        E = self.E
        HHv = lambda t: t.rearrange("p (x q d s) -> p x q d s", x=2, q=4, d=2)
        HHs, T1s, U1s = self.HH[:, 0:16 * nseq], self.HT1[:, 0:16 * nseq], self.HU1[:, 0:16 * nseq]
        if grp == 1:
            E(seng, lambda e: e.tensor_copy(out=HHv(HHs)[:, :, :, :, 0].rearrange("p x q d -> p d x q"),
                                            in_=self.h0T[:, :, :, 4 * k:4 * k + 4]), ["h0T"], ["HH"])
        else:
            E(seng, lambda e: e.memset(HHs, 0.0), [], ["HH"])
        hsw = AP(HHs, HHs.offset + 8 * nseq, [[HHs.ap[0][0], 128], [-8 * nseq, 2], [1, 8 * nseq]])
        hfl = HHs.rearrange("p (x r) -> p x r", x=2)
        a2f = A2s.rearrange("p (x r) -> p x r", x=2)
        u1f = U1s.rearrange("p (x r) -> p x r", x=2)
        hh4 = HHs.rearrange("p (xq d s) -> p xq d s", d=2, s=nseq)
        for i in range(ncs):
            def colap(t):
                return AP(t, t.offset + i, [[t.ap[0][0], 128], [SQs, 8], [ncg + ncs - 1 - 2 * i, 2], [ncs, nseq]])
            E(seng, lambda e: e.tensor_copy(out=colap(HPv), in_=hh4), ["HH"], ["HP"])
            E(seng, lambda e: e.tensor_tensor(out=T1s, in0=A1s, in1=HHs, op=ALU.mult), ["A1", "HH"], ["HT1"])
            E(seng, lambda e: e.tensor_tensor(out=u1f, in0=a2f, in1=hsw, op=ALU.mult), ["A2", "HH"], ["HU1"])
            E(seng, lambda e: e.tensor_tensor(out=T1s, in0=T1s, in1=U1s, op=ALU.add), ["HT1", "HU1"], ["HT1"])
            E(seng, lambda e: e.tensor_tensor(out=hh4, in0=T1s.rearrange("p (xq d s) -> p xq d s", d=2, s=nseq),
                                              in1=colap(SSv), op=ALU.add), ["HT1", "SS"], ["HH"])
        if grp == 0:
            for s in range(nseq):
                E(seng, lambda e: e.tensor_copy(out=self.FS[:, s, :, :, 4 * k:4 * k + 4],
                                                in_=HHv(HHs)[:, :, :, :, s].rearrange("p x q d -> p d x q")), ["HH"], ["FS"])

    def s5_scan2(self, k, seng, SSv, HPv, SQs, ncg, A1s, A2s):
        E = self.E
        MB = 16
        ps_ = SSv.ap[0][0]
        HL, T1, U1, A1f, A2f = self.HL, self.HLT1, self.HLU1, self.A1f, self.A2f
        PWp, PW, HST = self.PWp, self.PW, self.HST
        Hc, Hc0, HcT, HcU, B1, B2 = self.Hc, self.Hc0, self.HcT, self.HcU, self.B1, self.B2
        TC1, TC2 = self.TC1, self.TC2
        E(seng, lambda e: e.tensor_copy(out=A1f.rearrange("p (a b) -> p a b", b=MB), in_=A1s.unsqueeze(2).to_broadcast([128, 16, MB])), ["A1"], ["A1f"])
        E(seng, lambda e: e.tensor_copy(out=A2f.rearrange("p (a b) -> p a b", b=MB), in_=A2s.unsqueeze(2).to_broadcast([128, 16, MB])), ["A2"], ["A2f"])
        PWv = PWp.rearrange("p (x g i) -> p x g i", x=2, g=8)
        E(seng, lambda e: e.tensor_copy(out=PWv[:, 0, :, 0], in_=A1s[:, 0:8]), ["A1"], ["PWp"])
        E(seng, lambda e: e.tensor_copy(out=PWv[:, 1, :, 0], in_=A2s[:, 8:16]), ["A2"], ["PWp"])
        ln = 1
        t1 = TC1[:, 0:64].rearrange("p (g i) -> p g i", g=8)
        t2 = TC2[:, 0:64].rearrange("p (g i) -> p g i", g=8)
        while ln < MB:
            mr = PWv[:, 0, :, ln - 1:ln].to_broadcast([128, 8, ln])
            mi = PWv[:, 1, :, ln - 1:ln].to_broadcast([128, 8, ln])
            ar, ai = PWv[:, 0, :, 0:ln], PWv[:, 1, :, 0:ln]
            E(seng, lambda e: e.tensor_tensor(out=t1[:, :, 0:ln], in0=ar, in1=mr, op=ALU.mult), ["PWp"], ["TC1"])
            E(seng, lambda e: e.tensor_tensor(out=t2[:, :, 0:ln], in0=ai, in1=mi, op=ALU.mult), ["PWp"], ["TC2"])
            E(seng, lambda e: e.tensor_tensor(out=PWv[:, 0, :, ln:2 * ln], in0=t1[:, :, 0:ln], in1=t2[:, :, 0:ln], op=ALU.subtract), ["TC1", "TC2"], ["PWp"])
            E(seng, lambda e: e.tensor_tensor(out=t1[:, :, 0:ln], in0=ar, in1=mi, op=ALU.mult), ["PWp"], ["TC1"])
            E(seng, lambda e: e.tensor_tensor(out=t2[:, :, 0:ln], in0=ai, in1=mr, op=ALU.mult), ["PWp"], ["TC2"])
            E(seng, lambda e: e.tensor_tensor(out=PWv[:, 1, :, ln:2 * ln], in0=t1[:, :, 0:ln], in1=t2[:, :, 0:ln], op=ALU.add), ["TC1", "TC2"], ["PWp"])
            ln *= 2
        PW5 = PW.rearrange("p (x q d i) -> p x q d i", x=2, q=4, d=2)
        PWp5 = PWp.rearrange("p (x q d i) -> p x q d i", x=2, q=4, d=2)
        for x in range(2):
            E(seng, lambda e: e.tensor_copy(out=PW5[:, x, :, 0, :], in_=PWp5[:, x, :, 0, :]), ["PWp"], ["PW"])
            E(seng, lambda e: e.tensor_copy(out=PW5[:, x, :, 1, :], in_=PWp5[:, x, :, 1, ::-1]), ["PWp"], ["PW"])
        B1v = B1.rearrange("p (x g) -> p x g", x=2)
        B2v = B2.rearrange("p (x g) -> p x g", x=2)
        for x in range(2):
            E(seng, lambda e: e.tensor_copy(out=B1v[:, x, :], in_=PWv[:, 0, :, MB - 1]), ["PWp"], ["B1"])
        E(seng, lambda e: e.tensor_scalar(out=B2v[:, 0, :], in0=PWv[:, 1, :, MB - 1], scalar1=-1.0, scalar2=None, op0=ALU.mult), ["PWp"], ["B2"])
        E(seng, lambda e: e.tensor_copy(out=B2v[:, 1, :], in_=PWv[:, 1, :, MB - 1]), ["PWp"], ["B2"])
        E(seng, lambda e: e.memset(HL, 0.0), [], ["HL"])
        hsw = AP(HL, HL.offset + 128, [[HL.ap[0][0], 128], [-128, 2], [1, 128]])
        a2f = A2f.rearrange("p (x r) -> p x r", x=2)
        u1f = U1.rearrange("p (x r) -> p x r", x=2)
        hl4 = HL.rearrange("p (g d b) -> p g d b", g=8, d=2)
        t14 = T1.rearrange("p (g d b) -> p g d b", g=8, d=2)
        for i in range(MB):
            col = AP(SSv, SSv.offset + i, [[ps_, 128], [SQs, 8], [ncg + MB - 1 - 2 * i, 2], [MB, MB]])
            E(seng, lambda e: e.tensor_tensor(out=T1, in0=A1f, in1=HL, op=ALU.mult), ["A1f", "HL"], ["HLT1"])
            E(seng, lambda e: e.tensor_tensor(out=u1f, in0=a2f, in1=hsw, op=ALU.mult), ["A2f", "HL"], ["HLU1"])
            E(seng, lambda e: e.tensor_tensor(out=T1, in0=T1, in1=U1, op=ALU.add), ["HLT1", "HLU1"], ["HLT1"])
            E(seng, lambda e: e.tensor_tensor(out=hl4, in0=t14, in1=col, op=ALU.add), ["HLT1", "SS"], ["HL"])
            E(seng, lambda e: e.tensor_copy(out=col, in_=hl4), ["HL"], ["SS"])
        E(seng, lambda e: e.tensor_copy(out=Hc.rearrange("p (x q d) -> p d x q", x=2, q=4), in_=self.h0T[:, :, :, 4 * k:4 * k + 4]), ["h0T"], ["Hc"])
        E(seng, lambda e: e.tensor_copy(out=Hc0, in_=Hc), ["Hc"], ["Hc0"])
        hcsw = AP(Hc, Hc.offset + 8, [[Hc.ap[0][0], 128], [-8, 2], [1, 8]])
        b2f = B2.rearrange("p (x r) -> p x r", x=2)
        hcuf = HcU.rearrange("p (x r) -> p x r", x=2)
        hc2 = Hc.rearrange("p (g d) -> p g d", d=2)
        hct2 = HcT.rearrange("p (g d) -> p g d", d=2)
        for b in range(MB):
            hpos = AP(HST, HST.offset + b, [[HST.ap[0][0], 128], [2 * MB, 8], [MB + MB - 1 - 2 * b, 2]])
            E(seng, lambda e: e.tensor_copy(out=hpos, in_=hc2), ["Hc"], ["HST"])
            if b == MB - 1:
                break
            send = AP(SSv, SSv.offset + MB * b + MB - 1, [[ps_, 128], [SQs, 8], [ncg + MB * (MB - 1 - b) - (MB * b + MB - 1), 2]])
            E(seng, lambda e: e.tensor_tensor(out=HcT, in0=B1, in1=Hc, op=ALU.mult), ["B1", "Hc"], ["HcT"])
            E(seng, lambda e: e.tensor_tensor(out=hcuf, in0=b2f, in1=hcsw, op=ALU.mult), ["B2", "Hc"], ["HcU"])
            E(seng, lambda e: e.tensor_tensor(out=HcT, in0=HcT, in1=HcU, op=ALU.add), ["HcT", "HcU"], ["HcT"])
            E(seng, lambda e: e.tensor_tensor(out=hc2, in0=hct2, in1=send, op=ALU.add), ["HcT", "SS"], ["Hc"])
        HST4 = HST.rearrange("p (x q d b) -> p x q d b", x=2, q=4, d=2)
        c1 = TC1.rearrange("p (q b i) -> p q b i", q=2, b=MB)
        c2 = TC2.rearrange("p (q b i) -> p q b i", q=2, b=MB)
        for d in range(2):
          for qh in range(2):
            qs = slice(2 * qh, 2 * qh + 2)
            pr = PW5[:, 0, qs, d, :].unsqueeze(2).to_broadcast([128, 2, MB, MB])
            pi = PW5[:, 1, qs, d, :].unsqueeze(2).to_broadcast([128, 2, MB, MB])
            hr = HST4[:, 0, qs, d, :].unsqueeze(3).to_broadcast([128, 2, MB, MB])
            hi = HST4[:, 1, qs, d, :].unsqueeze(3).to_broadcast([128, 2, MB, MB])
            sre = AP(SSv, SSv.offset + 2 * qh * SQs + d * ncg, [[ps_, 128], [SQs, 2], [MB, MB], [1, MB]])
            sim = AP(SSv, SSv.offset + (4 + 2 * qh) * SQs + d * ncg, [[ps_, 128], [SQs, 2], [MB, MB], [1, MB]])
            E(seng, lambda e: e.tensor_tensor(out=c1, in0=pr, in1=hr, op=ALU.mult), ["PW", "HST"], ["TC1"])
            E(seng, lambda e: e.tensor_tensor(out=c2, in0=pi, in1=hi, op=ALU.mult), ["PW", "HST"], ["TC2"])
            E(seng, lambda e: e.tensor_tensor(out=c1, in0=c1, in1=c2, op=ALU.subtract), ["TC1", "TC2"], ["TC1"])
            E(seng, lambda e: e.tensor_tensor(out=sre, in0=sre, in1=c1, op=ALU.add), ["SS", "TC1"], ["SS"])
            E(seng, lambda e: e.tensor_tensor(out=c1, in0=pr, in1=hi, op=ALU.mult), ["PW", "HST"], ["TC1"])
            E(seng, lambda e: e.tensor_tensor(out=c2, in0=pi, in1=hr, op=ALU.mult), ["PW", "HST"], ["TC2"])
            E(seng, lambda e: e.tensor_tensor(out=c1, in0=c1, in1=c2, op=ALU.add), ["TC1", "TC2"], ["TC1"])
            E(seng, lambda e: e.tensor_tensor(out=sim, in0=sim, in1=c1, op=ALU.add), ["SS", "TC1"], ["SS"])
        ss3 = SSv.rearrange("p (g d c) -> p g d c", g=8, d=2)
        hp3 = HPv.rearrange("p (g d c) -> p g d c", g=8, d=2)
        h03 = Hc0.rearrange("p (g d) -> p g d", d=2)
        self.A(lambda e: e.activation(out=hp3[:, :, 0, 1:ncg], in_=ss3[:, :, 0, 0:ncg - 1], func=AF.Copy), ["SS"], ["HP"])
        self.A(lambda e: e.activation(out=hp3[:, :, 1, 0:ncg - 1], in_=ss3[:, :, 1, 1:ncg], func=AF.Copy), ["SS"], ["HP"])
        E(seng, lambda e: e.tensor_copy(out=hp3[:, :, 0, 0:1], in_=h03[:, :, 0:1]), ["Hc0"], ["HP"])
        E(seng, lambda e: e.tensor_copy(out=hp3[:, :, 1, ncg - 1:ncg], in_=h03[:, :, 1:2]), ["Hc0"], ["HP"])

    def s5_mixer(self, js, grp):
        if grp == 1:
            self.arena_reset()
            self.s5_alloc()
            self.s5_layer_prep(js)
        else:
            self.V(lambda e: e.memset(self.LM, 0.0), [], ["LM"])
        t0, n = self.trange(grp)
        nseq = 1 if grp == 1 else NPS
        ncs = (n // nseq) // 8
        ncg = n // 8
        V, A, T, E = self.V, self.A, self.T, self.E
        H0, H1 = slice(0, 64), slice(64, 128)
        def ld_wu(k_):
            self.ldc(self.wgs[k_ % 2], self.s5_w_in[js][:, k_ * 128:(k_ + 1) * 128].rearrange("(k p) n -> p k n", p=128), w=[f"wgs{k_ % 2}"])
        ld_wu(0)
        for k in range(8):
            wu, wuk = self.wgs[k % 2], f"wgs{k % 2}"
            if k + 1 < 8:
                ld_wu(k + 1)
            eng = "vector"
            seng = "vector"
            for tb in range(n // 512):
                pt = self.ps[0]
                for kk in range(8):
                    T(lambda e: e.matmul(pt, lhsT=wu[:, kk, :],
                                         rhs=self.hT[:, kk, tb * 512:(tb + 1) * 512], start=(kk == 0), stop=(kk == 7)),
                      [wuk, "hT"], ["ps0"])
                A(lambda e: e.activation(out=self.u8[:, :, tb * 64:(tb + 1) * 64].rearrange("p j c -> p c j"),
                                         in_=pt.rearrange("p (c j) -> p c j", j=8), func=AF.Copy), ["ps0"], ["u_k"])
            if grp == 1:
                for half in range(2):
                    hs = slice(half * 64, half * 64 + 64)
                    for d in range(2):
                        self.ld(self.BRk[hs, d], self.s5_b_re[js, d, 8 * k:8 * k + 8].rearrange("g p h -> p g h"), w=["BRk"])
                        self.ld(self.BIk[hs, d], self.s5_b_im[js, d, 8 * k:8 * k + 8].rearrange("g p h -> p g h"), w=["BIk"])
                for dup in range(2):
                    self.ld(self.CRn[:, :, dup, :], self.s5_c_re[js][:, 8 * k:8 * k + 8].rearrange("d g h p -> (g h) d p"), w=["CRn"])
                    self.ld(self.CIn[:, :, dup, :], self.s5_c_im[js][:, 8 * k:8 * k + 8].rearrange("d g h p -> (g h) d p"), w=["CIn"])
                for (cn, cnk, ck, ckk) in ((self.CRn, "CRn", self.CRk, "CRk"), (self.CIn, "CIn", self.CIk, "CIk")):
                    for d in range(2):
                        pt = self.ps[1][:, 0:128]
                        src = cn[:, d].rearrange("p a c -> p (a c)")
                        T(lambda e: e.transpose(pt, src, self.identf), [cnk, "identf"], ["ps1"])
                        A(lambda e: e.activation(out=ck[:, d].rearrange("p g h -> p (g h)"), in_=pt, func=AF.Copy), ["ps1"], [ckk])
                self.cmul(eng, self.bbr, self.bbi, self.bcf(self.FR, k), self.bcf(self.FI, k), self.BRk, self.BIk,
                          ["FR", "FI", "BRk", "BIk"], ["bb"])
                BAv = self.BA.rearrange("p m d (g h) -> p m d g h", g=8)
                for m in range(8):
                    self.cmul(eng, BAv[:, m], BAv[:, m], self.bcg(self.PR, m, k), self.bcg(self.PI, m, k), self.bbr, self.bbi,
                              ["PR", "PI", "bb"], ["BA"], hs_r=H0, hs_i=H1)
                CCv = self.CC.rearrange("p d (g h) -> p d g h", g=8)
                V(lambda e: e.tensor_copy(out=CCv[H0], in_=self.CRk[H0]), ["CRk"], ["CC"])
                V(lambda e: e.tensor_scalar(out=CCv[H1], in0=self.CIk[H1], scalar1=-1.0, scalar2=None, op0=ALU.mult), ["CIk"], ["CC"])
                CArv = self.CAr.rearrange("p m d (g h) -> p m d g h", g=8)
                CAiv = self.CAi.rearrange("p m d (g h) -> p m d g h", g=8)
                for m in range(1, 9):
                    self.cmul("gpsimd", CArv[:, m], CAiv[:, m], self.bcg(self.PR, m, k), self.bcg(self.PI, m, k), self.CRk, self.CIk,
                              ["PR", "PI", "CRk", "CIk"], ["CA"], negi=True)
                for tau in range(8):
                    for d in range(2):
                        pt = self.ps[1][:, 0:128]
                        if tau == 0:
                            T(lambda e: e.matmul(pt, lhsT=self.BA[:, 0, d], rhs=self.CC[:, d], start=(d == 0), stop=(d == 1)),
                              ["BA", "CC"], ["ps1"])
                            if d == 0:
                                continue
                            tt = self.t3[:, 0:128]
                            V(lambda e: e.tensor_tensor(out=tt, in0=pt, in1=self.bdmask, op=ALU.mult), ["ps1", "bdmask"], ["t3"])
                            V(lambda e: e.scalar_tensor_tensor(out=self.Kc[:, 7], in0=self.identf, scalar=self.dcol[:, k:k + 1],
                                                               in1=tt, op0=ALU.mult, op1=ALU.add), ["t3", "identf", "dcol"], ["Kc"])
                        else:
                            T(lambda e: e.matmul(pt, lhsT=self.BA[:, tau, d], rhs=self.CC[:, d], start=True, stop=True),
                              ["BA", "CC"], ["ps1"])
                            idx = 7 + tau if d == 0 else 7 - tau
                            V(lambda e: e.tensor_tensor(out=self.Kc[:, idx], in0=pt, in1=self.bdmask, op=ALU.mult),
                              ["ps1", "bdmask"], ["Kc"])
                for g4 in range(4):
                    bank, bk = (self.ps[1], "ps1") if g4 % 2 == 0 else (self.ps[0], "ps0")
                    for ii in range(4):
                        idx = g4 * 4 + ii
                        T(lambda e: e.transpose(bank[:, ii * 128:(ii + 1) * 128], self.BA[:, idx // 2, idx % 2], self.identf),
                          ["BA", "identf"], [bk])
                    A(lambda e: e.activation(out=self.Tsb[:, g4 * 4:(g4 + 1) * 4].rearrange("p a b -> p (a b)"), in_=bank, func=AF.Copy),
                      [bk], ["Tsb"])
                self.ld(self.Tsb_scr[js, k], self.Tsb.rearrange("p a b -> p (a b)"), r=["Tsb"], w=[("s5c", k)])
                self.ld(self.Kc_scr[js, k], self.Kc.rearrange("p a b -> p (a b)"), r=["Kc"], w=[("s5c", k)])
                self.ld(self.CA_scr[js, k, :, 0], self.CAr.rearrange("p m d c -> p (m d c)"), r=["CA"], w=[("s5c", k)])
                self.ld(self.CA_scr[js, k, :, 1], self.CAi.rearrange("p m d c -> p (m d c)"), r=["CA"], w=[("s5c", k)])
            else:
                self.ld(self.Tsb.rearrange("p a b -> p (a b)"), self.Tsb_scr[js, k], r=[("s5c", k)], w=["Tsb"])
                self.ld(self.Kc.rearrange("p a b -> p (a b)"), self.Kc_scr[js, k], r=[("s5c", k)], w=["Kc"])
                self.ld(self.CAr.rearrange("p m d c -> p (m d c)"), self.CA_scr[js, k, :, 0], r=[("s5c", k)], w=["CA"])
                self.ld(self.CAi.rearrange("p m d c -> p (m d c)"), self.CA_scr[js, k, :, 1], r=[("s5c", k)], w=["CA"])
            HHv = lambda t: t.rearrange("p (x q d s) -> p x q d s", x=2, q=4, d=2)
            A1s, A2s = self.A1[:, 0:16 * nseq], self.A2[:, 0:16 * nseq]
            A1v, A2v = HHv(A1s), HHv(A2s)
            for hf, hsl in ((0, H0), (1, H1)):
                pr8 = self.PR[hsl, 8, :].rearrange("p (d g) -> p d g", d=2)[:, :, 8 * k + hf:8 * k + 8:2]
                pi8 = self.PI[hsl, 8, :].rearrange("p (d g) -> p d g", d=2)[:, :, 8 * k + hf:8 * k + 8:2]
                for s in range(nseq):
                    for x in range(2):
                        V(lambda e: e.tensor_copy(out=A1v[hsl, x, :, :, s].rearrange("p q d -> p d q"), in_=pr8), ["PR"], ["A1"])
                    V(lambda e: e.tensor_scalar(out=A2v[hsl, 0, :, :, s].rearrange("p q d -> p d q"), in0=pi8, scalar1=-1.0,
                                                scalar2=None, op0=ALU.mult), ["PI"], ["A2"])
                    V(lambda e: e.tensor_copy(out=A2v[hsl, 1, :, :, s].rearrange("p q d -> p d q"), in_=pi8), ["PI"], ["A2"])
            SQs = 2 * ncg
            SSv = self.SS[:, 0:16 * ncg]
            HPv = self.HP[:, 0:16 * ncg]
            for q in range(4):
                for x in range(2):
                    in0 = self.Tsb[:, :, x * 64:(x + 1) * 64].unsqueeze(2).to_broadcast([128, 16, 2, 64])
                    in1 = self.pmask[:, 2 * q:2 * q + 2].unsqueeze(1).unsqueeze(3).to_broadcast([128, 16, 2, 64])
                    outv = self.LW[:, :, x, :].rearrange("p m (a c) -> p m a c", a=2)
                    V(lambda e: e.tensor_tensor(out=outv, in0=in0, in1=in1, op=ALU.mult), ["Tsb", "pmask"], ["LW"])
                for d in range(2):
                    for x in range(2):
                        pt = self.ps[2 + x][:, 0:ncg]
                        for j in range(8):
                            m = 7 - j if d == 0 else j
                            T(lambda e: e.matmul(pt, lhsT=self.LW[:, m * 2 + d, x, :], rhs=self.u8[:, j, 0:ncg],
                                                 start=(j == 0), stop=(j == 7)), ["LW", "u_k"], [f"ps{2 + x}"])
                        off = (x * 4 + q) * SQs + d * ncg
                        A(lambda e: e.activation(out=SSv[:, off:off + ncg], in_=pt, func=AF.Copy), [f"ps{2 + x}"], ["SS"])
            if grp == 1:
                self.s5_scan2(k, seng, SSv, HPv, SQs, ncg, A1s, A2s)
            else:
                self.s5_scan1(k, seng, SSv, HPv, SQs, ncg, ncs, nseq, A1s, A2s, grp)
            for jh in range(4):
                for d in range(2):
                    for x, ca in ((0, self.CAr), (1, self.CAi)):
                        for hf, hsl in ((0, H0), (1, H1)):
                            lm = self.LM[hsl, :, d, x, :, :]
                            outv = AP(lm, lm.offset + 16 * hf, [[lm.ap[0][0], 64], [lm.ap[1][0] + 32, 4], [lm.ap[2][0], 2], [1, 16]])
                            if d == 0:
                                m0, ms = 2 * jh + 1, 1
                            else:
                                m0, ms = 8 - 2 * jh, -1
                            cam = ca[hsl, m0, d, :]
                            mstride = ca.ap[1][0]
                            inv = AP(cam, cam.offset + 16 * hf, [[cam.ap[0][0], 64], [32, 4], [ms * mstride, 2], [1, 16]])
                            V(lambda e: e.tensor_copy(out=outv, in_=inv), ["CA"], ["LM"])
                for jj in range(2):
                    j = jh * 2 + jj
                    pt = self.ps[4 + jj][:, 0:ncg]
                    pk = f"ps{4 + jj}"
                    for j2 in range(8):
                        T(lambda e: e.matmul(pt, lhsT=self.Kc[:, j - j2 + 7, :], rhs=self.u8[:, j2, 0:ncg],
                                             start=(j2 == 0), stop=False), ["Kc", "u_k"], [pk])
                    cnt = 0
                    for q in range(4):
                        for d in range(2):
                            for x in range(2):
                                off = (x * 4 + q) * SQs + d * ncg
                                cnt += 1
                                T(lambda e: e.matmul(pt, lhsT=self.LM[:, q, d, x, jj, :], rhs=HPv[:, off:off + ncg],
                                                     start=False, stop=(cnt == 16)), ["LM", "HP"], [pk])
                    A(lambda e: e.activation(out=self.g_k[:, j:n:8], in_=pt, func=AF.Gelu_apprx_tanh), [pk], ["g_k"])
            self.ld(self.gscr[k, :, t0:t0 + n], self.g_k[:, 0:n], r=["g_k"], w=[("gscr", grp)])
        if grp == 0:
            for s in range(nseq):
                pt = self.ps[1][:, 0:128]
                T(lambda e: e.transpose(pt, self.FS[:, s].rearrange("p d x g -> p (d x g)"), self.identf), ["FS", "identf"], ["ps1"])
                V(lambda e: e.tensor_copy(out=self.FSo, in_=pt), ["ps1"], ["FSo"])
                self.ld(self.ns5[s, js], self.FSo, r=["FSo"], w=["ns5"])
        lwv = self.LW.rearrange("p a b c -> p (a b c)")
        wglu = AP(lwv, lwv.offset, [[lwv.ap[0][0], 128], [1024, 8], [1, 1024]])
        self.ldc(wglu, self.s5_w_glu[js].rearrange("(k p) n -> p k n", p=128), w=["LW", "LM"])
        steps = [(tb, nn) for tb in range(n // 512) for nn in range(8)]
        def load_gate(i):
            nn_ = steps[i][1]
            self.ldc(self.wgs[i % 2], self.s5_w_in[js][:, D + nn_ * 128:D + (nn_ + 1) * 128].rearrange("(k p) n -> p k n", p=128),
                     w=[f"wgs{i % 2}"])
        load_gate(0)
        for si, (tb, nn) in enumerate(steps):
            ts = slice(tb * 512, (tb + 1) * 512)
            if nn == 0:
                self.ld(self.gblk, self.gscr[0:8, :, t0 + tb * 512:t0 + (tb + 1) * 512].rearrange("k p t -> p k t"),
                        r=[("gscr", grp)], w=["wst0"])
            if si + 1 < len(steps):
                load_gate(si + 1)
            pz, pg = self.ps[0 + 2 * (nn % 2)], self.ps[1 + 2 * (nn % 2)]
            pzk, pgk = f"ps{0 + 2 * (nn % 2)}", f"ps{1 + 2 * (nn % 2)}"
            for kk in range(8):
                T(lambda e: e.matmul(pz, lhsT=wglu[:, kk, nn * 128:(nn + 1) * 128], rhs=self.gblk[:, kk, :],
                                     start=(kk == 0), stop=(kk == 7)), ["LW", "LM", "wst0"], [pzk])
            A(lambda e: e.activation(out=self.sgm, in_=pz, func=AF.Sigmoid, bias=self.bgT[:, nn:nn + 1]), [pzk, "bgT"], ["SS"])
            wg, wgk = self.wgs[si % 2], f"wgs{si % 2}"
            for kk in range(8):
                T(lambda e: e.matmul(pg, lhsT=wg[:, kk, :], rhs=self.hT[:, kk, ts],
                                     start=(kk == 0), stop=(kk == 7)), [wgk, "hT"], [pgk])
            A(lambda e: e.activation(out=self.slu, in_=pg, func=AF.Silu), [pgk], ["SS2"])
            yb = self.yb[nn % 2]
            ybk = f"yb{nn % 2}"
            V(lambda e: e.tensor_tensor(out=self.sgm, in0=self.sgm, in1=self.gblk[:, nn, :], op=ALU.mult), ["SS", "wst0"], ["SS"])
            V(lambda e: e.tensor_tensor(out=yb, in0=self.sgm, in1=self.slu, op=ALU.mult), ["SS", "SS2"], [ybk, "HP"])
            self.ld(self.g2scr[nn, :, t0 + tb * 512:t0 + (tb + 1) * 512], yb, r=[ybk], w=[("g2scr", grp)])
        self.ysrc = (self.g2scr, ("g2scr", grp))
        return D


def host_consts():
    r = np.arange(128)
    pm = np.zeros((128, 8), np.float32)
    for q in range(4):
        for qq in range(2):
            pm[:, 2 * q + qq] = ((r // 16) == 2 * q + qq)
    bd = ((r[:, None] // 16) == (r[None, :] // 16)).astype(np.float32)
    t = np.arange(2048)
    row, col = t // 64, t % 64
    inv = (10000.0 ** (-np.arange(32, dtype=np.float32) / 32)).astype(np.float32)
    ang = np.concatenate([row[:, None].astype(np.float32) * inv[None], col[:, None].astype(np.float32) * inv[None]], 1)
    jj = r[:, None].astype(np.float32)
    ii = r[None, :].astype(np.float32)
    retE = np.stack([np.maximum(ii - jj, 0.0), np.maximum(jj - ii, 0.0)]).astype(np.float32)
    retM = np.stack([(ii >= jj), (jj > ii)]).astype(np.float32)
    retqe = np.stack([r + 1.0, 128.0 - r]).astype(np.float32)
    retke = np.stack([127.0 - r, r * 1.0], 1).astype(np.float32)
    extra = {
        "c_ropeC": np.cos(ang).astype(np.float32).reshape(16, 128, 64),
        "c_ropeS": np.sin(ang).astype(np.float32).reshape(16, 128, 64),
        "c_retE": retE, "c_retM": retM, "c_retqe": retqe, "c_retke": retke,
    }
    extra.update(hy_consts())
    return extra | {
        "c_identb": np.eye(128, dtype=np.float32).astype(ml_dtypes.bfloat16),
        "c_identf": np.eye(128, dtype=np.float32),
        "c_pmask": pm,
        "c_bdmask": bd,
    }


_CONSTS = None


def make_in_maps(prog, inputs):
    global _CONSTS
    if _CONSTS is None:
        _CONSTS = host_consts()
    consts = _CONSTS
    maps = []
    for c in range(8):
        m = {}
        for name in prog.inputs:
            if name in consts:
                m[name] = consts[name]
            elif name == "xs":
                m[name] = np.ascontiguousarray(inputs["x_sample"][c])
            elif name == "xp":
                m[name] = np.ascontiguousarray(inputs["x_prompt"][4 * c:4 * c + 4].reshape(NPS * LP, D))
            elif name == "cvec":
                m[name] = np.ascontiguousarray(np.stack([inputs["c_ctx"], inputs["c"][c]], 0))
            elif name == "st5":
                m[name] = np.ascontiguousarray(inputs["state_s5"][c].reshape(2, 128, 128))
            elif name == "stret":
                m[name] = np.ascontiguousarray(inputs["state_ret"][c, 0])
            else:
                a = np.asarray(inputs[name])
                shp = prog.inputs[name][0]
                m[name] = np.ascontiguousarray(a.reshape(shp))
        maps.append(m)
    return maps


_PROG = None


def kernel(**inputs):
    global _PROG
    inputs = {k: np.asarray(v) for k, v in inputs.items()}
    if _PROG is None:
        _PROG = K()
    prog = _PROG
    res = run_bass_kernel_spmd(prog.nc, make_in_maps(prog, inputs), core_ids=list(range(8)))
    rs = res.results
    y_prompt = np.concatenate([r["yp"].reshape(NPS, LP, D) for r in rs], 0)
    y_sample = np.stack([r["ys"] for r in rs], 0)
    ns5 = np.concatenate([r["ns5"].reshape(NPS, 2, 2, 2, 64, 64) for r in rs], 0)
    nret = np.concatenate([r["nret"][:, None].reshape(NPS, 1, 2, 8, 128, 256) for r in rs], 0)
    return (y_prompt.astype(np.float32), y_sample.astype(np.float32), ns5.astype(np.float32), nret.astype(np.float32))


def _ret_decl(self):
    self.ret_w_in = self.din("ret_w_in", [1, D, 6 * D])
    self.ret_decay_logit = self.din("ret_decay_logit", [1, 2, 8])
    self.ret_w_out = self.din("ret_w_out", [1, 2 * D, D])
    self.c_ropeC = self.din("c_ropeC", [16, 128, 64])
    self.c_ropeS = self.din("c_ropeS", [16, 128, 64])
    self.c_retE = self.din("c_retE", [2, 128, 128])
    self.c_retM = self.din("c_retM", [2, 128, 128])
    self.c_retqe = self.din("c_retqe", [2, 128])
    self.c_retke = self.din("c_retke", [128, 2])
    self.qT_scr = self.dscr("qT_scr", [8, 128, NT], BF16)
    self.kT_scr = self.dscr("kT_scr", [8, 128, NT], BF16)
    self.ktok_scr = self.dscr("ktok_scr", [NT, D], BF16)
    self.v_scr = self.dscr("v_scr", [NT, 2 * D], BF16)
    self.gate_scr = self.dscr("gate_scr", [NT, 2 * D], BF16)
    self.of_scr = self.dscr("of_scr", [NT, 2 * D])


def _ret_mixer(self, grp):
    V, A, T, G = self.V, self.A, self.T, self.G
    t0, n = self.trange(grp)
    nseq = 1 if grp == 1 else NPS
    L = n // nseq
    nch = L // 128
    self.arena_reset()
    sb = self.asb
    wblk = [sb(f"rwb{i}", [128, 8, 512], BF16) for i in range(2)]
    lgt = sb("lgt", [128, 16]); kdt = sb("kdt", [128, 16]); cdt = sb("cdt", [128, 16]); ke = sb("ke", [128, 2])
    Et = sb("Et", [128, 2, 128]); Mt = sb("Mt", [128, 2, 128]); qe = sb("qe", [128, 2, 128])
    Dtab = sb("Dtab", [128, 16, 128]); qdtab = sb("qdtab", [128, 16, 128])
    rc = sb("rc", [128, 64]); rs = sb("rs", [128, 64])
    pq = sb("pq", [128, 512]); pq2 = sb("pq2", [128, 512]); pt1 = sb("pt1", [128, 512])
    pbf = [sb(f"pbf{i}", [128, 512], BF16) for i in range(2)]
    trb = sb("trb", [128, 4, 128], BF16)
    S = sb("S", [128, 8, 256]); Sb = sb("Sb", [128, 8, 256], BF16)
    qTc = [sb(f"qTc{i}", [128, 8, 128], BF16) for i in range(2)]
    kTc = [sb(f"kTc{i}", [128, 8, 128], BF16) for i in range(2)]
    ktc = [sb(f"ktc{i}", [128, 1024], BF16) for i in range(2)]
    vc = [sb(f"vc{i}", [128, 2048], BF16) for i in range(2)]
    gc = sb("gc", [128, 2048], BF16)
    ofc = sb("ofc", [128, 2048])
    ot = sb("ot", [128, 2048])
    ybf = sb("ybf", [128, 2048], BF16)
    yTt = sb("yTt", [128, 16, 128], BF16)
    attb2 = [sb(f"attb{i}", [128, 128], BF16) for i in range(2)]
    qd2 = [sb(f"qd{i}", [128, 128], BF16) for i in range(2)]
    kd2 = [sb(f"kd{i}", [128, 128], BF16) for i in range(2)]
    rst = sb("rst", [128, 24])
    self.ld(lgt, self.ret_decay_logit[0].rearrange("d h -> (d h)").partition_broadcast(128), w=["lgt"])
    self.ld(ke, self.c_retke, w=["ke"])
    for d in range(2):
        self.ld(Et[:, d], self.c_retE[d], w=["Et"])
        self.ld(Mt[:, d], self.c_retM[d], w=["Mt"])
        self.ld(qe[:, d], self.c_retqe[d].partition_broadcast(128), w=["qe"])
    A(lambda e: e.activation(out=lgt, in_=lgt, func=AF.Exp, scale=-1.0), ["lgt"], ["lgt"])
    V(lambda e: e.tensor_scalar(out=lgt, in0=lgt, scalar1=1.0, scalar2=None, op0=ALU.add), ["lgt"], ["lgt"])
    A(lambda e: e.activation(out=lgt, in_=lgt, func=AF.Ln), ["lgt"], ["lgt"])
    V(lambda e: e.tensor_scalar(out=lgt, in0=lgt, scalar1=-1.0, scalar2=None, op0=ALU.mult), ["lgt"], ["lgt"])
    for d in range(2):
        for h in range(8):
            c = d * 8 + h
            A(lambda e: e.activation(out=Dtab[:, c], in_=Et[:, d], func=AF.Exp, scale=lgt[:, c:c + 1]), ["Et", "lgt"], ["Dtab"])
            V(lambda e: e.tensor_tensor(out=Dtab[:, c], in0=Dtab[:, c], in1=Mt[:, d], op=ALU.mult), ["Dtab", "Mt"], ["Dtab"])
            A(lambda e: e.activation(out=qdtab[:, c], in_=qe[:, d], func=AF.Exp, scale=lgt[:, c:c + 1]), ["qe", "lgt"], ["qdtab"])
            A(lambda e: e.activation(out=kdt[:, c:c + 1], in_=ke[:, d:d + 1], func=AF.Exp, scale=lgt[:, c:c + 1]), ["ke", "lgt"], ["kdt"])
    A(lambda e: e.activation(out=cdt, in_=lgt, func=AF.Exp, scale=128.0), ["lgt"], ["cdt"])
    gk = lambda nm: (nm, grp)
    def ld_wb(cb_):
        self.ldc(wblk[cb_ % 2], self.ret_w_in[0][:, cb_ * 512:(cb_ + 1) * 512].rearrange("(k p) n -> p k n", p=128), w=[f"rwb{cb_ % 2}"])
    ld_wb(0)
    for cb in range(12):
        wb, wk = wblk[cb % 2], f"rwb{cb % 2}"
        if cb + 1 < 12:
            ld_wb(cb + 1)
        for tt in range(n // 128):
            ts = slice(tt * 128, (tt + 1) * 128)
            gts = slice(t0 + tt * 128, t0 + (tt + 1) * 128)
            pp = self.ps[tt % 2]
            pk = f"ps{tt % 2}"
            for kk in range(8):
                T(lambda e: e.matmul(pp, lhsT=self.hT[:, kk, ts], rhs=wb[:, kk, :], start=(kk == 0), stop=(kk == 7)),
                  ["hT", wk], [pk])
            ob = pbf[tt % 2]
            obk = f"pbf{tt % 2}"
            if cb < 4:
                isk = cb >= 2
                sc = (128.0 ** -0.5) if isk else 1.0
                if grp == 1:
                    if True:
                        self.ld(rc, self.c_ropeC[tt], w=["rc"])
                        self.ld(rs, self.c_ropeS[tt], w=["rs"])
                    A(lambda e: e.activation(out=pq, in_=pp, func=AF.Copy, scale=sc), [pk], ["pq"])
                    v5 = lambda t: t.rearrange("p (h a b f) -> p h a b f", h=4, a=2, b=2)
                    x1 = v5(pq)[:, :, :, 0, :]
                    x2 = v5(pq)[:, :, :, 1, :]
                    cosb = rc.rearrange("p (a f) -> p a f", a=2).unsqueeze(1).to_broadcast([128, 4, 2, 32])
                    sinb = rs.rearrange("p (a f) -> p a f", a=2).unsqueeze(1).to_broadcast([128, 4, 2, 32])
                    o1 = v5(pq2)[:, :, :, 0, :]
                    o2 = v5(pq2)[:, :, :, 1, :]
                    u1 = v5(pt1)[:, :, :, 0, :]
                    u2 = v5(pt1)[:, :, :, 1, :]
                    V(lambda e: e.tensor_tensor(out=o1, in0=x1, in1=cosb, op=ALU.mult), ["pq", "rc"], ["pq2"])
                    V(lambda e: e.tensor_tensor(out=u1, in0=x2, in1=sinb, op=ALU.mult), ["pq", "rs"], ["pt1"])
                    G(lambda e: e.tensor_tensor(out=o2, in0=x1, in1=sinb, op=ALU.mult), ["pq", "rs"], ["pq2b"])
                    G(lambda e: e.tensor_tensor(out=u2, in0=x2, in1=cosb, op=ALU.mult), ["pq", "rc"], ["pt1b"])
                    V(lambda e: e.tensor_tensor(out=v5(ob)[:, :, :, 0, :], in0=o1, in1=u1, op=ALU.subtract), ["pq2", "pt1"], [obk])
                    V(lambda e: e.tensor_tensor(out=v5(ob)[:, :, :, 1, :], in0=o2, in1=u2, op=ALU.add), ["pq2b", "pt1b"], [obk])
                else:
                    A(lambda e: e.activation(out=ob, in_=pp, func=AF.Copy, scale=sc), [pk], [obk])
                if isk:
                    self.ld(self.ktok_scr[gts, (cb - 2) * 512:(cb - 1) * 512], ob, r=[obk], w=[gk("ktok")])
                for hh in range(4):
                    ptr = self.ps[2].bitcast(BF16)[:, hh * 128:(hh + 1) * 128]
                    T(lambda e: e.transpose(ptr, ob[:, hh * 128:(hh + 1) * 128], self.identb), [obk, "identb"], ["ps2"])
                V(lambda e: e.tensor_copy(out=trb.rearrange("p a b -> p (a b)"), in_=self.ps[2].bitcast(BF16)[:, 0:512]), ["ps2"], ["trb"])
                dst = self.kT_scr if isk else self.qT_scr
                h0 = (cb % 2) * 4
                self.ld(dst[h0:h0 + 4, :, gts].rearrange("h p t -> p h t"), trb, r=["trb"], w=[gk("kT" if isk else "qT")])
            elif cb < 8:
                A(lambda e: e.activation(out=ob, in_=pp, func=AF.Copy), [pk], [obk])
                self.ld(self.v_scr[gts, (cb - 4) * 512:(cb - 3) * 512], ob, r=[obk], w=[gk("v")])
            else:
                A(lambda e: e.activation(out=ob, in_=pp, func=AF.Silu), [pk], [obk])
                self.ld(self.gate_scr[gts, (cb - 8) * 512:(cb - 7) * 512], ob, r=[obk], w=[gk("gate")])
    for s in range(nseq):
        for d in range(2):
            if grp == 1:
                self.ld(S, self.stret[d].rearrange("h p e -> p h e"), w=["S"])
            else:
                V(lambda e: e.memset(S, 0.0), [], ["S"])
            V(lambda e: e.tensor_copy(out=Sb, in_=S), ["S"], ["Sb"])
            order = range(nch) if d == 0 else range(nch - 1, -1, -1)
            for ci, c in enumerate(order):
                b = ci % 2
                ts = slice(t0 + s * L + c * 128, t0 + s * L + (c + 1) * 128)
                self.ld(qTc[b], self.qT_scr[:, :, ts].rearrange("h p t -> p h t"), r=[gk("qT")], w=[f"qTc{b}"])
                self.ld(kTc[b], self.kT_scr[:, :, ts].rearrange("h p t -> p h t"), r=[gk("kT")], w=[f"kTc{b}"])
                self.ld(ktc[b], self.ktok_scr[ts, :], r=[gk("ktok")], w=[f"ktc{b}"])
                self.ld(vc[b], self.v_scr[ts, :], r=[gk("v")], w=[f"vc{b}"])
                if d == 1:
                    self.ld(ofc, self.of_scr[ts, :], r=[gk("of")], w=["ofc"])
                    self.ld(gc, self.gate_scr[ts, :], r=[gk("gate")], w=["gc"])
                for h in range(8):
                    cI = d * 8 + h
                    hb = h % 2
                    attb, qd, kd = attb2[hb], qd2[hb], kd2[hb]
                    attk, qdk, kdk = f"attb{hb}", f"qd{hb}", f"kd{hb}"
                    pak = "ps3" if hb == 0 else "ps0"
                    pa = (self.ps[3] if hb == 0 else self.ps[0])[:, 0:128]
                    T(lambda e: e.matmul(pa, lhsT=kTc[b][:, h, :], rhs=qTc[b][:, h, :], start=True, stop=True),
                      [f"kTc{b}", f"qTc{b}"], [pak])
                    V(lambda e: e.tensor_tensor(out=attb, in0=pa, in1=Dtab[:, cI], op=ALU.mult), [pak, "Dtab"], [attk])
                    G(lambda e: e.tensor_tensor(out=qd, in0=qTc[b][:, h, :], in1=qdtab[:, cI], op=ALU.mult), [f"qTc{b}", "qdtab"], [qdk])
                    A(lambda e: e.activation(out=kd, in_=ktc[b][:, h * 128:(h + 1) * 128], func=AF.Copy, scale=kdt[:, cI:cI + 1]),
                      [f"ktc{b}", "kdt"], [kdk])
                    po = self.ps[4 + (h % 2)][:, 0:256]
                    pok = f"ps{4 + (h % 2)}"
                    T(lambda e: e.matmul(po, lhsT=attb, rhs=vc[b][:, h * 256:(h + 1) * 256], start=True, stop=False),
                      [attk, f"vc{b}"], [pok])
                    T(lambda e: e.matmul(po, lhsT=qd, rhs=Sb[:, h, :], start=False, stop=True), [qdk, "Sb"], [pok])
                    psu = self.ps[6 + (h % 2)][:, 0:256]
                    psk = f"ps{6 + (h % 2)}"
                    T(lambda e: e.matmul(psu, lhsT=kd, rhs=vc[b][:, h * 256:(h + 1) * 256], start=True, stop=True),
                      [kdk, f"vc{b}"], [psk])
                    if d == 0:
                        A(lambda e: e.activation(out=ot[:, h * 256:(h + 1) * 256], in_=po, func=AF.Copy), [pok], ["ot"])
                    else:
                        V(lambda e: e.tensor_tensor(out=ot[:, h * 256:(h + 1) * 256], in0=po, in1=ofc[:, h * 256:(h + 1) * 256],
                                                    op=ALU.add), [pok, "ofc"], ["ot"])
                    V(lambda e: e.scalar_tensor_tensor(out=S[:, h, :], in0=S[:, h, :], scalar=cdt[:, cI:cI + 1], in1=psu,
                                                       op0=ALU.mult, op1=ALU.add), ["S", "cdt", psk], ["S"])
                    A(lambda e: e.activation(out=Sb[:, h, :], in_=S[:, h, :], func=AF.Copy), ["S"], ["Sb"])
                if d == 0:
                    self.ld(self.of_scr[ts, :], ot, r=["ot"], w=[gk("of")])
                else:
                    for h in range(8):
                        A(lambda e: e.activation(out=ofc[:, h * 256:(h + 1) * 256], in_=ot[:, h * 256:(h + 1) * 256], func=AF.Square,
                                                 accum_out=rst[:, h:h + 1]), ["ot"], ["ofc", "rst"])
                    V(lambda e: e.tensor_scalar(out=rst[:, 8:16], in0=rst[:, 0:8], scalar1=1.0 / 256, scalar2=EPS,
                                                op0=ALU.mult, op1=ALU.add), ["rst"], ["rst"])
                    A(lambda e: e.activation(out=rst[:, 8:16], in_=rst[:, 8:16], func=AF.Sqrt), ["rst"], ["rst"])
                    V(lambda e: e.reciprocal(out=rst[:, 16:24], in_=rst[:, 8:16]), ["rst"], ["rst"])
                    for h in range(8):
                        V(lambda e: e.scalar_tensor_tensor(out=ybf[:, h * 256:(h + 1) * 256], in0=ot[:, h * 256:(h + 1) * 256],
                                                           scalar=rst[:, 16 + h:17 + h], in1=gc[:, h * 256:(h + 1) * 256],
                                                           op0=ALU.mult, op1=ALU.mult), ["ot", "rst", "gc"], ["ybf"])
                    for k4 in range(4):
                        for kk in range(4):
                            k = k4 * 4 + kk
                            ptr = self.ps[2].bitcast(BF16)[:, kk * 128:(kk + 1) * 128]
                            T(lambda e: e.transpose(ptr, ybf[:, k * 128:(k + 1) * 128], self.identb), ["ybf", "identb"], ["ps2"])
                        A(lambda e: e.activation(out=yTt[:, k4 * 4:(k4 + 1) * 4, :].rearrange("p a b -> p (a b)"),
                                                 in_=self.ps[2].bitcast(BF16)[:, 0:512], func=AF.Copy), ["ps2"], ["yTt"])
                    self.ld(self.gscr[0:16, :, ts].rearrange("k p t -> p k t"), yTt, r=["yTt"], w=[("gscr", grp)])
            if grp == 0:
                self.ld(self.nret[s, d].rearrange("h p e -> p h e"), S, r=["S"], w=["nret"])
    self.ysrc = (self.gscr, ("gscr", grp))
    return 2 * D


K.ret_decl = _ret_decl
K.ret_mixer = _ret_mixer


HY_FT = {2048: 17, 256: 3}


def _hy_decl(self):
    self.hy_w_in = self.din("hy_w_in", [1, D, 8 * D])
    self.hy_conv_w = self.din("hy_conv_w", [1, 3, 6 * D])
    self.hy_conv_b = self.din("hy_conv_b", [1, 6 * D])
    self.hy_f_w1 = self.din("hy_f_w1", [1, 33, 64])
    self.hy_f_b1 = self.din("hy_f_b1", [1, 64])
    self.hy_f_w2 = self.din("hy_f_w2", [1, 64, 64])
    self.hy_f_b2 = self.din("hy_f_b2", [1, 64])
    self.hy_f_w3 = self.din("hy_f_w3", [1, 64, 8 * D])
    self.hy_skip = self.din("hy_skip", [1, 2, 2 * D])
    self.hy_w_out = self.din("hy_w_out", [1, 2 * D, D])
    self.c_absd = self.din("c_absd", [2 * D])
    self.c_ones = self.din("c_ones", [128, 128])
    self.hyc = {}
    for L in (2048, 256):
        FT = HY_FT[L]
        self.hyc[L] = dict(
            feat=self.din(f"c_feat{L}", [33, L]),
            tneg=self.din(f"c_tneg{L}", [128, L // 128]),
            C=self.din(f"c_C{L}", [FT, 128, L // 128, 128], BF16), S=self.din(f"c_S{L}", [FT, 128, L // 128, 128], BF16),
            IC=self.din(f"c_IC{L}", [L // min(512, L), 128, FT, min(512, L)], BF16),
            IS=self.din(f"c_IS{L}", [L // min(512, L), 128, FT, min(512, L)], BF16))
    self.vT_scr = self.dscr("vT_scr", [16, 128, NT])
    self.x1T_scr = self.dscr("x1T_scr", [16, 128, NT])
    self.x2T_scr = self.dscr("x2T_scr", [16, 128, NT])
    self.z1T_scr = self.dscr("z1T_scr", [16, 128, NT])
    self.sgT_scr = self.dscr("sgT_scr", [16, 128, NT], BF16)
    self.ztok_scr = self.dscr("ztok_scr", [2, NT, 2 * D], BF16)
    self.Eo_scr = self.dscr("Eo_scr", [2, 2, 2048, 2 * D], BF16)
    self.KH_scr = self.dscr("KH_scr", [2, 2, 17 * 128, 2 * D])


def _sin_any(self, out, x, rk, wk, tmpf, tmpi, tmpk, biasp):
    V, A = self.V, self.A
    V(lambda e: e.tensor_scalar(out=out, in0=x, scalar1=biasp, scalar2=1.0 / TWO_PI, op0=ALU.add, op1=ALU.mult), rk, wk)
    V(lambda e: e.tensor_copy(out=tmpi, in_=out), wk, [tmpk + "i"])
    V(lambda e: e.tensor_copy(out=tmpf, in_=tmpi), [tmpk + "i"], [tmpk])
    V(lambda e: e.tensor_tensor(out=out, in0=out, in1=tmpf, op=ALU.subtract), wk + [tmpk], wk)
    V(lambda e: e.tensor_scalar(out=tmpf, in0=out, scalar1=0.5, scalar2=None, op0=ALU.is_gt), wk, [tmpk])
    V(lambda e: e.tensor_tensor(out=out, in0=out, in1=tmpf, op=ALU.subtract), wk + [tmpk], wk)
    V(lambda e: e.tensor_scalar(out=tmpf, in0=out, scalar1=-0.5, scalar2=None, op0=ALU.is_lt), wk, [tmpk])
    V(lambda e: e.tensor_tensor(out=out, in0=out, in1=tmpf, op=ALU.add), wk + [tmpk], wk)
    A(lambda e: e.activation(out=out, in_=out, func=AF.Sin, scale=6.283185), wk, wk)


def _hy_filters(self, grp):
    V, A, T, G = self.V, self.A, self.T, self.G
    L = LS if grp == 1 else LP
    FT, LT = HY_FT[L], L // 128
    hc = self.hyc[L]
    self.arena_reset()
    sb = self.asb
    w1 = sb("hw1", [33, 64]); w2 = sb("hw2", [64, 64]); b1 = sb("hb1", [64, 1]); b2 = sb("hb2", [64, 1])
    feat = sb("hfeat", [33, L]); z1 = sb("hz1", [64, L]); z2 = sb("hz2", [64, L])
    tf = sb("htf", [64, 512]); ti = sb("hti", [64, 512], I32)
    w3b = [sb(f"hw3{i}", [64, 512], BF16) for i in range(2)]
    z2b = None
    absd = sb("habsd", [128, 2 * D]); tneg = sb("htneg", [128, LT]); ones = sb("hones", [128, 128], BF16)
    wins = [sb(f"hwin{i}", [128, 512]) for i in range(2)]
    fds = [[sb(f"hfd{j}{i}", [128, 512]) for i in range(2)] for j in range(2)]
    fabs = [sb(f"hfab{i}", [128, 512], BF16) for i in range(2)]
    ebs = [[sb(f"heb{j}{i}", [128, 512], BF16) for i in range(2)] for j in range(2)]
    rn = sb("hrn", [128, 2, 2 * D])
    Eb = sb("hE", [128, LT, 512], BF16); Ob = sb("hO", [128, LT, 512], BF16)
    Cs = [sb(f"hCs{i}", [128, LT, 128], BF16) for i in range(2)]
    Ss = [sb(f"hSs{i}", [128, LT, 128], BF16) for i in range(2)]
    ko = [fds[0][0], fds[0][1]]
    self.ld(w1, self.hy_f_w1[0], w=["hw1"]); self.ld(w2, self.hy_f_w2[0], w=["hw2"])
    self.ld(b1, self.hy_f_b1[0].rearrange("(p o) -> p o", o=1), w=["hb1"])
    self.ld(b2, self.hy_f_b2[0].rearrange("(p o) -> p o", o=1), w=["hb2"])
    self.ld(feat, hc["feat"], w=["hfeat"])
    self.ld(absd, self.c_absd.partition_broadcast(128), w=["habsd"])
    self.ld(tneg, hc["tneg"], w=["htneg"])
    self.ldc(ones, self.c_ones, w=["hones"])
    V(lambda e: e.tensor_scalar(out=b1, in0=b1, scalar1=16.0 * math.pi, scalar2=None, op0=ALU.add), ["hb1"], ["hb1"])
    V(lambda e: e.tensor_scalar(out=b2, in0=b2, scalar1=16.0 * math.pi, scalar2=None, op0=ALU.add), ["hb2"], ["hb2"])
    BW = min(512, L)
    for (src, srck, wt, wtk, bb, bbk, dst, dstk, kdim) in ((feat, "hfeat", w1, "hw1", b1, "hb1", z1, "hz1", 33),
                                                          (z1, "hz1", w2, "hw2", b2, "hb2", z2, "hz2", 64)):
        for tb in range(L // BW):
            pp = self.ps[0][0:64, 0:BW]
            T(lambda e: e.matmul(pp, lhsT=wt[0:kdim, :], rhs=src[0:kdim, tb * BW:(tb + 1) * BW], start=True, stop=True),
              [srck, wtk], ["ps0"])
            self.sin_any(dst[:, tb * BW:(tb + 1) * BW], pp, ["ps0", bbk], [dstk], tf[:, 0:BW], ti[:, 0:BW], "htf", bb[:, 0:1])
    z2b = z1.bitcast(BF16)[:, 0:L]
    A(lambda e: e.activation(out=z2b, in_=z2, func=AF.Copy), ["hz2", "hz1"], ["hz1"])
    for o in range(2):
        for cb in range(4):
            cs = slice(cb * 512, (cb + 1) * 512)
            for dr in range(2):
                col0 = dr * 4096 + o * 2048 + cb * 512
                self.ldc(w3b[dr], self.hy_f_w3[0][:, col0:col0 + 512], w=[f"hw3{dr}"])
            pacc = self.ps[3]
            for lt in range(LT):
                pb_ = lt % 2
                win, fd, eb = wins[pb_], fds[pb_], ebs[pb_]
                wink = f"hwin{pb_}"
                A(lambda e: e.activation(out=win, in_=absd[:, cs], func=AF.Exp, scale=tneg[:, lt:lt + 1]), ["habsd", "htneg"], [wink])
                for dr in range(2):
                    fab, fabk = fabs[dr], f"hfab{dr}"
                    pf = self.ps[(1 + dr) if pb_ == 0 else (5 + dr)]
                    pfk = f"ps{(1 + dr) if pb_ == 0 else (5 + dr)}"
                    T(lambda e: e.matmul(pf, lhsT=z2b[:, lt * 128:(lt + 1) * 128], rhs=w3b[dr], start=True, stop=True),
                      ["hz1", f"hw3{dr}"], [pfk])
                    V(lambda e: e.tensor_tensor(out=fd[dr], in0=pf, in1=win, op=ALU.mult), [pfk, wink], [f"hfd{pb_}{dr}"])
                    A(lambda e: e.activation(out=fab, in_=fd[dr], func=AF.Abs), [f"hfd{pb_}{dr}"], [fabk])
                    T(lambda e: e.matmul(pacc, lhsT=ones, rhs=fab, start=(lt == 0 and dr == 0), stop=(lt == LT - 1 and dr == 1)),
                      ["hones", fabk], ["ps3"])
                if lt == 0:
                    V(lambda e: e.memset(fd[1][0:1, :], 0.0), [], [f"hfd{pb_}1"])
                V(lambda e: e.tensor_tensor(out=eb[0], in0=fd[0], in1=fd[1], op=ALU.add), [f"hfd{pb_}0", f"hfd{pb_}1"], [f"heb{pb_}0"])
                V(lambda e: e.tensor_tensor(out=eb[1], in0=fd[1], in1=fd[0], op=ALU.subtract), [f"hfd{pb_}0", f"hfd{pb_}1"], [f"heb{pb_}1"])
                for eo in range(2):
                    self.ld(self.Eo_scr[o, eo, lt * 128:(lt + 1) * 128, cs], eb[eo], r=[f"heb{pb_}{eo}"], w=[("Eo", grp)])
            V(lambda e: e.tensor_scalar(out=rn[:, o, cs], in0=pacc, scalar1=EPS, scalar2=None, op0=ALU.add), ["ps3"], ["hrn"])
            V(lambda e: e.reciprocal(out=rn[:, o, cs], in_=rn[:, o, cs]), ["hrn"], ["hrn"])
    for o in range(2):
        for cb in range(4):
            cs = slice(cb * 512, (cb + 1) * 512)
            self.ld(Eb, self.Eo_scr[o, 0, 0:L, cs].rearrange("(lt p) c -> p lt c", p=128), r=[("Eo", grp)], w=["hE"])
            self.ld(Ob, self.Eo_scr[o, 1, 0:L, cs].rearrange("(lt p) c -> p lt c", p=128), r=[("Eo", grp)], w=["hO"])
            for ft in range(FT):
                b = ft % 2
                fs = slice(ft * 128, (ft + 1) * 128)
                self.ld(Cs[b], hc["C"][ft], w=[f"hCs{b}"])
                self.ld(Ss[b], hc["S"][ft], w=[f"hSs{b}"])
                for ri, (tab, tabk, dat, datk) in enumerate(((Cs[b], f"hCs{b}", Eb, "hE"), (Ss[b], f"hSs{b}", Ob, "hO"))):
                    pk_ = self.ps[4 + ri]
                    for lt in range(LT):
                        T(lambda e: e.matmul(pk_, lhsT=tab[:, lt, :], rhs=dat[:, lt, :], start=(lt == 0), stop=(lt == LT - 1)),
                          [tabk, datk], [f"ps{4 + ri}"])
                    V(lambda e: e.tensor_tensor(out=ko[ri], in0=pk_, in1=rn[:, o, cs], op=ALU.mult), [f"ps{4 + ri}", "hrn"], [f"hfd0{ri}"])
                    self.ld(self.KH_scr[o, ri, fs, cs], ko[ri], r=[f"hfd0{ri}"], w=[("KH", grp)])


def _hy_mixer(self, grp):
    V, A, T, G = self.V, self.A, self.T, self.G
    t0, n = self.trange(grp)
    nseq = 1 if grp == 1 else NPS
    L = n // nseq
    FT, LT = HY_FT[L], L // 128
    hc = self.hyc[L]
    self.hy_filters(grp)
    self.arena_reset()
    sb = self.asb
    wch = [sb(f"ywc{i}", [128, 8, 128], BF16) for i in range(2)]
    PBs = [sb(f"yPB{i}", [128, nseq, L + 2]) for i in range(2)]
    cvs = [sb(f"ycv{i}", [128, nseq, L]) for i in range(2)]
    cvbs = [sb(f"ycvb{i}", [128, n], BF16) for i in range(2)]
    cwT = sb("ycwT", [128, 3, 48]); cbT = sb("ycbT", [128, 48]); skT = sb("yskT", [128, 2, 16])
    ztt = sb("yztt", [128, n // 128, 128], BF16)
    for j in range(3):
        self.ld(cwT[:, j, :], self.hy_conv_w[0, j].rearrange("(c p) -> p c", p=128), w=["ycwT"], allow_slow_non_contiguous=True)
    self.ld(cbT, self.hy_conv_b[0].rearrange("(c p) -> p c", p=128), w=["ycbT"], allow_slow_non_contiguous=True)
    for o in range(2):
        self.ld(skT[:, o, :], self.hy_skip[0, o].rearrange("(c p) -> p c", p=128), w=["yskT"], allow_slow_non_contiguous=True)
    for i in range(2):
        V(lambda e: e.memset(PBs[i], 0.0), [], [f"yPB{i}"])
    def ld_wch(c_):
        self.ldc(wch[c_ % 2], self.hy_w_in[0][:, c_ * 128:(c_ + 1) * 128].rearrange("(k p) n -> p k n", p=128), w=[f"ywc{c_ % 2}"])
    ld_wch(0)
    pend_tr = []
    for c in range(64):
        PB, cv, cvb = PBs[c % 2], cvs[c % 2], cvbs[c % 2]
        PBk, cvk, cvbk = f"yPB{c % 2}", f"ycv{c % 2}", f"ycvb{c % 2}"
        wc, wck = wch[c % 2], f"ywc{c % 2}"
        if c + 1 < 64:
            ld_wch(c + 1)
        for tb in range(n // 512):
            pp = self.ps[tb % 2]
            pk = f"ps{tb % 2}"
            for kk in range(8):
                T(lambda e: e.matmul(pp, lhsT=wc[:, kk, :], rhs=self.hT[:, kk, tb * 512:(tb + 1) * 512], start=(kk == 0), stop=(kk == 7)),
                  [wck, "hT"], [pk])
            if c < 48:
                if grp == 1:
                    dstp = PB[:, 0, 1 + tb * 512:1 + (tb + 1) * 512]
                    srcp = pp
                else:
                    dstp = PB[:, 2 * tb:2 * tb + 2, 1:L + 1]
                    srcp = pp.rearrange("p (s t) -> p s t", s=2)
                A(lambda e: e.activation(out=dstp, in_=srcp, func=AF.Copy), [pk], [PBk])
            else:
                A(lambda e: e.activation(out=cvb[:, tb * 512:(tb + 1) * 512], in_=pp, func=AF.Silu), [pk], [cvbk])
        while pend_tr:
            pend_tr.pop(0)()
        if c >= 48:
            self.ld(self.sgT_scr[c - 48, :, t0:t0 + n], cvb, r=[cvbk], w=[("sgT", grp)])
            continue
        A(lambda e: e.activation(out=cv, in_=PB[:, :, 1:L + 1], func=AF.Identity, scale=cwT[:, 1, c:c + 1], bias=cbT[:, c:c + 1]),
          [PBk, "ycwT", "ycbT"], [cvk])
        V(lambda e: e.scalar_tensor_tensor(out=cv, in0=PB[:, :, 0:L], scalar=cwT[:, 0, c:c + 1], in1=cv, op0=ALU.mult, op1=ALU.add),
          [PBk, "ycwT", cvk], [cvk])
        V(lambda e: e.scalar_tensor_tensor(out=cv, in0=PB[:, :, 2:L + 2], scalar=cwT[:, 2, c:c + 1], in1=cv, op0=ALU.mult, op1=ALU.add),
          [PBk, "ycwT", cvk], [cvk])
        cvf = cv.rearrange("p s t -> p (s t)")
        dst = (self.vT_scr, self.x1T_scr, self.x2T_scr)[c // 16]
        self.ld(dst[c % 16, :, t0:t0 + n], cvf, r=[cvk], w=[(("vT", "x1T", "x2T")[c // 16], grp)])
        if c < 16:
            V(lambda e: e.tensor_copy(out=cvb, in_=cvf), [cvk], [cvbk])

            def do_tr(c=c, cvb=cvb, cvbk=cvbk):
                for t4 in range(n // 512):
                    for kk in range(4):
                        tt = t4 * 4 + kk
                        ptr = self.ps[2].bitcast(BF16)[:, kk * 128:(kk + 1) * 128]
                        T(lambda e: e.transpose(ptr, cvb[:, tt * 128:(tt + 1) * 128], self.identb), [cvbk, "identb"], ["ps2"])
                    A(lambda e: e.activation(out=ztt[:, t4 * 4:(t4 + 1) * 4, :].rearrange("p a b -> p (a b)"),
                                             in_=self.ps[2].bitcast(BF16)[:, 0:512], func=AF.Copy), ["ps2"], ["yztt"])
                self.ld(self.ztok_scr[0, t0:t0 + n, c * 128:(c + 1) * 128].rearrange("(tt p) c -> p tt c", p=128), ztt,
                        r=["yztt"], w=[("ztok0", grp)])
            pend_tr.append(do_tr)
    while pend_tr:
        pend_tr.pop(0)()
    self.arena_reset()
    sb = self.asb
    TBW = min(512, L)
    NTB = L // TBW
    zt = sb("yzt", [128, LT, 512], BF16)
    Cs = [sb(f"yCs{i}", [128, LT, 128], BF16) for i in range(2)]
    Ss = [sb(f"ySs{i}", [128, LT, 128], BF16) for i in range(2)]
    Yh = sb("yYh", [128, FT, 2, 512], BF16)
    ICs = sb("yIC", [128, FT, TBW], BF16); ISs = sb("yIS", [128, FT, TBW], BF16)
    kre = [sb(f"ykre{i}", [128, 512]) for i in range(2)]; kim = [sb(f"ykim{i}", [128, 512]) for i in range(2)]
    u1 = sb("yu1", [128, 512]); u2 = sb("yu2", [128, 512])
    NB2 = 2 if L == LP else 1
    tas = [sb(f"yta{i}", [128, TBW]) for i in range(NB2)]; txs = [sb(f"ytx{i}", [128, TBW]) for i in range(NB2)]
    tgs = [sb(f"ytg{i}", [128, TBW], BF16) for i in range(NB2)]; tos = [sb(f"yto{i}", [128, TBW]) for i in range(NB2)]
    tobs = [sb(f"ytob{i}", [128, TBW], BF16) for i in range(NB2)]
    ztt2s = [sb(f"yztt2{i}", [128, TBW // 128, 128], BF16) for i in range(NB2)]
    skT = sb("yskT2", [128, 2, 16])
    for o in range(2):
        self.ld(skT[:, o, :], self.hy_skip[0, o].rearrange("(c p) -> p c", p=128), w=["yskT2"], allow_slow_non_contiguous=True)
    small = (L == LP)
    if small:
        Call = sb("yCall", [128, FT, LT, 128], BF16); Sall = sb("ySall", [128, FT, LT, 128], BF16)
        kra = sb("ykra", [128, FT, 512]); kia = sb("ykia", [128, FT, 512])
        for ft in range(FT):
            self.ld(Call[:, ft], hc["C"][ft], w=["yCall"])
            self.ld(Sall[:, ft], hc["S"][ft], w=["ySall"])
        self.ld(ICs, hc["IC"][0], w=["yIC"])
        self.ld(ISs, hc["IS"][0], w=["yIS"])
    for o in range(2):
        zprev = (self.vT_scr, ("vT", grp)) if o == 0 else (self.z1T_scr, ("z1T", grp))
        xg = (self.x1T_scr, ("x1T", grp)) if o == 0 else (self.x2T_scr, ("x2T", grp))
        for cb in range(4):
          cs = slice(cb * 512, (cb + 1) * 512)
          if small:
              self.ld(kra, self.KH_scr[o, 0, 0:FT * 128, cs].rearrange("(ft p) c -> p ft c", p=128), r=[("KH", grp)], w=["ykra"])
              self.ld(kia, self.KH_scr[o, 1, 0:FT * 128, cs].rearrange("(ft p) c -> p ft c", p=128), r=[("KH", grp)], w=["ykia"])
          for s in range(nseq):
                tq = t0 + s * L
                self.ld(zt, self.ztok_scr[o, tq:tq + L, cs].rearrange("(lt p) c -> p lt c", p=128), r=[(f"ztok{o}", grp)], w=["yzt"])
                for ft in range(FT):
                    b = ft % 2
                    fs = slice(ft * 128, (ft + 1) * 128)
                    if small:
                        Csb, Ssb, krb, kib = Call[:, ft], Sall[:, ft], kra[:, ft], kia[:, ft]
                        Ck, Sk, krk, kik = "yCall", "ySall", "ykra", "ykia"
                    else:
                        Csb, Ssb, krb, kib = Cs[b], Ss[b], kre[b], kim[b]
                        Ck, Sk, krk, kik = f"yCs{b}", f"ySs{b}", f"ykre{b}", f"ykim{b}"
                        self.ld(Cs[b], hc["C"][ft], w=[Ck])
                        self.ld(Ss[b], hc["S"][ft], w=[Sk])
                        self.ld(kre[b], self.KH_scr[o, 0, fs, cs], r=[("KH", grp)], w=[krk])
                        self.ld(kim[b], self.KH_scr[o, 1, fs, cs], r=[("KH", grp)], w=[kik])
                    pA, pB = self.ps[0 + 2 * b], self.ps[1 + 2 * b]
                    pAk, pBk = f"ps{0 + 2 * b}", f"ps{1 + 2 * b}"
                    for lt in range(LT):
                        T(lambda e: e.matmul(pA, lhsT=Csb[:, lt, :], rhs=zt[:, lt, :], start=(lt == 0), stop=(lt == LT - 1)),
                          [Ck, "yzt"], [pAk])
                    for lt in range(LT):
                        T(lambda e: e.matmul(pB, lhsT=Ssb[:, lt, :], rhs=zt[:, lt, :], start=(lt == 0), stop=(lt == LT - 1)),
                          [Sk, "yzt"], [pBk])
                    V(lambda e: e.tensor_tensor(out=u1, in0=pA, in1=krb, op=ALU.mult), [pAk, krk], ["yu1"])
                    V(lambda e: e.tensor_tensor(out=u2, in0=pB, in1=kib, op=ALU.mult), [pBk, kik], ["yu2"])
                    G(lambda e: e.tensor_tensor(out=Yh[:, ft, 0, :], in0=u1, in1=u2, op=ALU.add), ["yu1", "yu2"], ["yYh"])
                    V(lambda e: e.tensor_tensor(out=u1, in0=pA, in1=kib, op=ALU.mult), [pAk, kik], ["yu1"])
                    V(lambda e: e.tensor_tensor(out=u2, in0=pB, in1=krb, op=ALU.mult), [pBk, krk], ["yu2"])
                    V(lambda e: e.tensor_tensor(out=Yh[:, ft, 1, :], in0=u1, in1=u2, op=ALU.subtract), ["yu1", "yu2"], ["yYh"])
                for tb in range(NTB):
                    tsl = slice(tb * TBW, (tb + 1) * TBW)
                    gsl = slice(tq + tb * TBW, tq + (tb + 1) * TBW)
                    if not small:
                        self.ld(ICs, hc["IC"][tb], w=["yIC"])
                        self.ld(ISs, hc["IS"][tb], w=["yIS"])
                    for cc in range(4):
                        ch = cb * 4 + cc
                        bi_ = cc % NB2
                        ta, tx, tg, to, tob, ztt2 = tas[bi_], txs[bi_], tgs[bi_], tos[bi_], tobs[bi_], ztt2s[bi_]
                        tak, txk, tgk, tok, tobk, zt2k = f"yta{bi_}", f"ytx{bi_}", f"ytg{bi_}", f"yto{bi_}", f"ytob{bi_}", f"yztt2{bi_}"
                        pz = self.ps[4 + (cc % 2)][:, 0:TBW]
                        pzk = f"ps{4 + (cc % 2)}"
                        self.ld(ta, zprev[0][ch, :, gsl], r=[zprev[1]], w=[tak])
                        self.ld(tx, xg[0][ch, :, gsl], r=[xg[1]], w=[txk])
                        if o == 1:
                            self.ld(tg, self.sgT_scr[ch, :, gsl], r=[("sgT", grp)], w=[tgk])
                        for ft in range(FT):
                            T(lambda e: e.matmul(pz, lhsT=Yh[:, ft, 0, cc * 128:(cc + 1) * 128], rhs=ICs[:, ft, :], start=(ft == 0), stop=False),
                              ["yYh", "yIC"], [pzk])
                            T(lambda e: e.matmul(pz, lhsT=Yh[:, ft, 1, cc * 128:(cc + 1) * 128], rhs=ISs[:, ft, :], start=False, stop=(ft == FT - 1)),
                              ["yYh", "yIS"], [pzk])
                        V(lambda e: e.scalar_tensor_tensor(out=ta, in0=ta, scalar=skT[:, o, ch:ch + 1], in1=pz, op0=ALU.mult, op1=ALU.add),
                          [tak, "yskT2", pzk], [tak])
                        if o == 0:
                            G(lambda e: e.tensor_tensor(out=to, in0=ta, in1=tx, op=ALU.mult), [tak, txk], [tok])
                            self.ld(self.z1T_scr[ch, :, gsl], to, r=[tok], w=[("z1T", grp)])
                            A(lambda e: e.activation(out=tob, in_=to, func=AF.Copy), [tok], [tobk])
                            for kk in range(TBW // 128):
                                ptr = self.ps[6 + bi_].bitcast(BF16)[:, kk * 128:(kk + 1) * 128]
                                T(lambda e: e.transpose(ptr, tob[:, kk * 128:(kk + 1) * 128], self.identb), [tobk, "identb"], ["ps6" if bi_ == 0 else "ps7"])
                            A(lambda e: e.activation(out=ztt2.rearrange("p a b -> p (a b)"), in_=self.ps[6 + bi_].bitcast(BF16)[:, 0:TBW], func=AF.Copy),
                              ["ps6" if bi_ == 0 else "ps7"], [zt2k])
                            self.ld(self.ztok_scr[1, gsl, ch * 128:(ch + 1) * 128].rearrange("(tt p) c -> p tt c", p=128), ztt2,
                                    r=[zt2k], w=[("ztok1", grp)])
                        else:
                            G(lambda e: e.tensor_tensor(out=to, in0=ta, in1=tx, op=ALU.mult), [tak, txk], [tok])
                            G(lambda e: e.tensor_tensor(out=tob, in0=to, in1=tg, op=ALU.mult), [tok, tgk], [tobk])
                            self.ld(self.gscr[ch, :, gsl], tob, r=[tobk], w=[("gscr", grp)])
    self.ysrc = (self.gscr, ("gscr", grp))
    return 2 * D


K.hy_decl = _hy_decl
K.sin_any = _sin_any
K.hy_filters = _hy_filters
K.hy_mixer = _hy_mixer


def hy_consts():
    out = {}
    HY_BANDS = 16
    min_decay = math.log(1e-2) / 1.5
    max_decay = math.log(1e-2) / 0.3
    out["c_absd"] = np.abs(np.linspace(min_decay, max_decay, 2048, dtype=np.float32)).astype(np.float32)
    out["c_ones"] = np.ones((128, 128), np.float32)
    for L in (2048, 256):
        FT = HY_FT[L]
        N = 2 * L
        t = np.linspace(0.0, 1.0, L, dtype=np.float32)[:, None]
        w = (2.0 * np.float32(math.pi) * np.arange(L, dtype=np.float32)[:, None] / np.float32(L)).astype(np.float32)
        f = np.linspace(1e-4, HY_BANDS - 1.0, HY_BANDS, dtype=np.float32)[None, :]
        fw_ = (f * w).astype(np.float32)
        feat = np.concatenate([t, np.cos(fw_), -np.sin(fw_)], -1).astype(np.float32)
        out[f"c_feat{L}"] = np.ascontiguousarray(feat.T)
        out[f"c_tneg{L}"] = np.ascontiguousarray((-t[:, 0]).reshape(L // 128, 128).T)
        tt = np.arange(L, dtype=np.int64)[:, None]
        ff = np.arange(FT * 128, dtype=np.int64)[None, :]
        ang = 2.0 * np.pi * ((tt * ff) % N).astype(np.float64) / N
        valid = (ff <= L).astype(np.float64)
        C = np.cos(ang) * valid
        S = np.sin(ang) * valid
        wf = np.where((ff == 0) | (ff == L), 1.0, 2.0) * valid / N
        LT = L // 128
        TBW = min(512, L)
        def fwd_tile(M):
            return np.ascontiguousarray(M.reshape(LT, 128, FT, 128).transpose(2, 1, 0, 3)).astype(np.float32).astype(ml_dtypes.bfloat16)
        def inv_tile(M):
            return np.ascontiguousarray(M.reshape(FT, 128, L // TBW, TBW).transpose(2, 1, 0, 3)).astype(np.float32).astype(ml_dtypes.bfloat16)
        out[f"c_C{L}"] = fwd_tile(C)
        out[f"c_S{L}"] = fwd_tile(S)
        out[f"c_IC{L}"] = inv_tile((C * wf).T)
        out[f"c_IS{L}"] = inv_tile((-S * wf).T)
    return out
```

```python
import math
import numpy as np
import ml_dtypes
import concourse.bass as bass
import concourse.mybir as mybir
from concourse.bass_utils import run_bass_kernel_spmd

F32 = mybir.dt.float32
BF16 = mybir.dt.bfloat16
I32 = mybir.dt.int32
ALU = mybir.AluOpType
AF = mybir.ActivationFunctionType
AX = mybir.AxisListType

D = 1024
LS = 2048
LP = 256
NPS = 4
NT = LS + NPS * LP
EPS = 1e-6
TWO_PI = 2.0 * math.pi
ARENA_W = 31500

SAME_ENG_SYNC = True


class _PEProxy:
    def __init__(self, eng):
        self.eng = eng
        self.stop = True

    def matmul(self, *a, **kw):
        self.stop = bool(kw.get("stop", True))
        return self.eng.matmul(*a, **kw)

    def transpose(self, *a, **kw):
        self.stop = True
        return self.eng.transpose(*a, **kw)


class Fw:
    def __init__(self, nc, n_dma_sems=20):
        self.nc = nc
        self.engs = {}
        for name in ("tensor", "vector", "scalar", "gpsimd", "sync"):
            e = getattr(nc, name)
            self.engs[name] = dict(eng=e, sem=nc.alloc_semaphore("s_" + name), count=0, seen={})
        self.dma_pool = {}
        for q in ("sync", "gpsimd", "scalar"):
            self.dma_pool[q] = dict(
                sems=[nc.alloc_semaphore(f"d_{q}_{i}") for i in range(n_dma_sems)],
                vals=[0] * n_dma_sems, nxt=0)
        self.bufs = {}
        self.sem_owner = {id(E["sem"]): name for name, E in self.engs.items()}

    def _st(self, key):
        s = self.bufs.get(key)
        if s is None:
            s = dict(w=None, r=[])
            self.bufs[key] = s
        return s

    def _deps(self, reads, writes):
        deps = []
        for k in reads:
            s = self._st(k)
            if s["w"] is not None:
                deps.append(s["w"])
        for k in writes:
            s = self._st(k)
            if s["w"] is not None:
                deps.append(s["w"])
            deps.extend(s["r"])
        return deps

    def _wait(self, E, deps):
        best = {}
        for (sem, val) in deps:
            if sem is E["sem"] and not SAME_ENG_SYNC:
                continue
            k = id(sem)
            if k not in best or best[k][1] < val:
                best[k] = (sem, val)
        for k, (sem, val) in best.items():
            if E["seen"].get(k, 0) < val:
                E["eng"].wait_ge(sem, val)
                E["seen"][k] = val

    def _mark(self, tok, reads, writes):
        for k in reads:
            r = self._st(k)["r"]
            r.append(tok)
            if len(r) > 12:
                best = {}
                for (sem, val) in r:
                    if id(sem) not in best or best[id(sem)][1] < val:
                        best[id(sem)] = (sem, val)
                r[:] = list(best.values())
        for k in writes:
            s = self._st(k)
            s["w"] = tok
            s["r"] = []

    def op(self, eng, fn, reads=(), writes=()):
        E = self.engs[eng]
        self._wait(E, self._deps(reads, writes))
        if eng == "tensor":
            px = _PEProxy(E["eng"])
            ins = fn(px)
            if not px.stop:
                E.setdefault("pend", []).append((tuple(reads), tuple(writes)))
                return ins
            pend = E.get("pend", [])
            E["pend"] = []
            E["count"] += 1
            ins.then_inc(E["sem"], 1)
            tok = (E["sem"], E["count"])
            for (r, w) in pend:
                self._mark(tok, r, w)
            self._mark(tok, reads, writes)
            return ins
        ins = fn(E["eng"])
        E["count"] += 1
        ins.then_inc(E["sem"], 1)
        self._mark((E["sem"], E["count"]), reads, writes)
        return ins

    def dma(self, q, out, in_, reads=(), writes=(), **kw):
        E = self.engs[q]
        P = self.dma_pool[q]
        i = P["nxt"]
        P["nxt"] = (i + 1) % len(P["sems"])
        sem = P["sems"][i]
        deps = self._deps(reads, writes)
        if P["vals"][i] > 0:
            deps.append((sem, P["vals"][i]))
        self._wait(E, deps)
        ins = E["eng"].dma_start(out=out, in_=in_, **kw)
        P["vals"][i] += 16
        ins.then_inc(sem, 16)
        tok = (sem, P["vals"][i])
        self._mark(tok, reads, writes)
        return tok

    def barrier(self):
        toks = [(E["sem"], E["count"]) for E in self.engs.values() if E["count"] > 0]
        for P in self.dma_pool.values():
            for sem, val in zip(P["sems"], P["vals"]):
                if val > 0:
                    toks.append((sem, val))
        for E in self.engs.values():
            self._wait(E, toks)

    def finish(self, out_keys):
        E = self.engs["sync"]
        deps = []
        for k in out_keys:
            s = self._st(k)
            if s["w"] is not None:
                deps.append(s["w"])
        self._wait(E, deps)


def AP(t, off, dims):
    return bass.AP(t.tensor if hasattr(t, "tensor") else t, off, [list(d) for d in dims])


class K:
    def __init__(self, layers=(0, 1, 2, 3), final=True, dbg=False):
        self.dbg = dbg
        self.layers = layers
        self.final = final
        nc = self.nc = bass.Bass("TRN2", target_bir_lowering=False)
        self.fw = Fw(nc)
        self.inputs = {}
        self.build()

    def din(self, name, shape, dt=F32):
        t = self.nc.dram_tensor(name, list(shape), dt, kind="ExternalInput").ap()
        self.inputs[name] = (tuple(shape), dt)
        return t

    def dout(self, name, shape, dt=F32):
        return self.nc.dram_tensor(name, list(shape), dt, kind="ExternalOutput").ap()

    def dscr(self, name, shape, dt=F32):
        return self.nc.dram_tensor(name, list(shape), dt, kind="Internal").ap()

    def sb(self, name, shape, dt=F32):
        return self.nc.alloc_sbuf_tensor(name, list(shape), dt).ap()

    def arena_reset(self):
        self.fw.barrier()
        self.aoff = 0

    def asb(self, name, shape, dt=F32):
        n = 1
        for x in shape[1:]:
            n *= x
        words = n if dt in (F32, I32) else (n + 1) // 2
        words = (words + 7) // 8 * 8
        assert self.aoff + words <= ARENA_W, (name, self.aoff, words)
        v = self.arena[0:shape[0], self.aoff:self.aoff + words]
        self.aoff += words
        if dt not in (F32,):
            v = v.bitcast(dt)
        v = v[:, 0:n]
        if len(shape) > 2:
            names = " ".join(f"a{i}" for i in range(len(shape) - 1))
            kw = {f"a{i}": shape[i + 1] for i in range(len(shape) - 2)}
            v = v.rearrange(f"p ({names}) -> p {names}", **kw)
        return v

    def V(self, fn, r=(), w=()):
        return self.fw.op("vector", fn, r, w)

    def G(self, fn, r=(), w=()):
        return self.fw.op("gpsimd", fn, r, w)

    def A(self, fn, r=(), w=()):
        return self.fw.op("scalar", fn, r, w)

    def T(self, fn, r=(), w=()):
        return self.fw.op("tensor", fn, r, w)

    def ld(self, out, in_, r=(), w=(), q="sync", **kw):
        if q == "sync" and r and "DRam" in type(out.tensor).__name__ and "DRam" not in type(in_.tensor).__name__:
            engs = set()
            for k in r:
                st = self.fw.bufs.get(k)
                if st is None or st["w"] is None:
                    continue
                engs.add(self.fw.sem_owner.get(id(st["w"][0]), "dma"))
            if len(engs) == 1:
                e = engs.pop()
                if e in ("scalar", "gpsimd"):
                    q = e
                elif e == "vector":
                    q = "scalar"
        return self.fw.dma(q, out, in_, r, w, **kw)

    def ldc(self, out, in_, r=(), w=()):
        return self.fw.dma("gpsimd", out, in_, r, w)

    def build(self):
        nc = self.nc
        self.xs = self.din("xs", [LS, D])
        self.xp = self.din("xp", [NPS * LP, D])
        self.cvec = self.din("cvec", [2, D])
        self.st5 = self.din("st5", [2, 128, 128])
        self.stret = self.din("stret", [2, 8, 128, 256])
        self.norm_g = self.din("norm_g", [4, D])
        self.mod_w = self.din("mod_w", [4, D, 3 * D])
        self.mod_b = self.din("mod_b", [4, 3 * D])
        self.s5_w_in = self.din("s5_w_in", [2, D, 2 * D])
        self.s5_lam_re = self.din("s5_lam_re", [2, 2, 64, 64])
        self.s5_lam_im = self.din("s5_lam_im", [2, 2, 64, 64])
        self.s5_log_step = self.din("s5_log_step", [2, 2, 64])
        self.s5_b_re = self.din("s5_b_re", [2, 2, 64, 64, 16])
        self.s5_b_im = self.din("s5_b_im", [2, 2, 64, 64, 16])
        self.s5_c_re = self.din("s5_c_re", [2, 2, 64, 16, 64])
        self.s5_c_im = self.din("s5_c_im", [2, 2, 64, 16, 64])
        self.s5_d = self.din("s5_d", [2, D])
        self.s5_w_glu = self.din("s5_w_glu", [2, D, D])
        self.s5_b_glu = self.din("s5_b_glu", [2, D])
        self.s5_w_out = self.din("s5_w_out", [2, D, D])
        self.final_g = self.din("final_g", [D])
        self.ret_decl()
        self.hy_decl()
        self.c_identb = self.din("c_identb", [128, 128], BF16)
        self.c_identf = self.din("c_identf", [128, 128])
        self.c_pmask = self.din("c_pmask", [128, 8])
        self.c_bdmask = self.din("c_bdmask", [128, 128])
        self.ys = self.dout("ys", [LS, D])
        self.yp = self.dout("yp", [NPS * LP, D])
        self.ns5 = self.dout("ns5", [NPS, 2, 128, 128])
        self.nret = self.dout("nret", [NPS, 2, 8, 128, 256])
        self.xres = (self.dout if self.dbg else self.dscr)("xres", [NT, D])
        self.gscr = self.dscr("gscr", [16, 128, NT], BF16)
        self.g2scr = self.dscr("g2scr", [8, 128, NT], BF16)
        self.Tsb_scr = self.dscr("Tsb_scr", [2, 8, 128, 2048], BF16)
        self.Kc_scr = self.dscr("Kc_scr", [2, 8, 128, 1920], BF16)
        self.CA_scr = self.dscr("CA_scr", [2, 8, 128, 2, 2304], BF16)
        self.identb = self.sb("identb", [128, 128], BF16)
        self.identf = self.sb("identf", [128, 128])
        self.pmask = self.sb("pmask", [128, 8])
        self.bdmask = self.sb("bdmask", [128, 128])
        self.hT = self.sb("hT", [128, 8, LS], BF16)
        self.arena = self.sb("arena", [128, ARENA_W])
        self.aoff = 0
        self.wgs = [self.sb(f"wgs{i}", [128, 8, 128], BF16) for i in range(2)]
        self.wst = [self.sb("wst0", [128, 8, 512], BF16)] * 2
        self.xt = [self.sb(f"xt{i}", [128, D]) for i in range(2)]
        self.xn = [self.sb(f"xn{i}", [128, D], BF16) for i in range(2)]
        self.sq = self.sb("sq", [128, D])
        self.stat = self.sb("stat", [128, 8])
        self.modT = self.sb("modT", [128, 4, 24, 2])
        self.gsc = self.sb("gsc", [128, 4, 8, 2])
        self.gt_bc = self.sb("gt_bc", [128, 2, D])
        self.cT = self.sb("cT", [128, 8, 2])
        self.cTb = self.sb("cTb", [128, 8, 2], BF16)
        self.cTrep = self.sb("cTrep", [128, 2, 8, 128], BF16)
        self.ngT = self.sb("ngT", [128, 4, 8])
        self.mbT = self.sb("mbT", [128, 4, 24])
        self.mbg = None
        self.fgb = self.sb("fgb", [128, D])
        self.ylt = [self.sb("ylt", [128, 16, 128], BF16)] * 2
        self.ps = [nc.alloc_psum_tensor(f"ps{i}", [128, 512], F32).ap() for i in range(8)]

        f = self.fw
        self.ld(self.identb, self.c_identb, w=["identb"])
        self.ld(self.identf, self.c_identf, w=["identf"])
        self.ld(self.pmask, self.c_pmask, w=["pmask"])
        self.ld(self.bdmask, self.c_bdmask, w=["bdmask"])
        self.ld(self.fgb, self.final_g.partition_broadcast(128), w=["fgb"])

        self.mod_stage()
        out_keys = []
        nl = len(self.layers)
        for li, i in enumerate(self.layers):
            last = (li == nl - 1) and self.final
            first = (li == 0)
            self.gate_table(i)
            for grp in (1, 0):
                kind = i % 3
                self.prologue(i, grp, first)
                if kind == 0:
                    kdim = self.s5_mixer(i // 3, grp)
                    w_out = self.s5_w_out[i // 3]
                elif kind == 1:
                    kdim = self.ret_mixer(grp)
                    w_out = self.ret_w_out[0]
                else:
                    kdim = self.hy_mixer(grp)
                    w_out = self.hy_w_out[0]
                self.epilogue(i, grp, kdim, w_out, last)
        out_keys = ["ys", "yp", "ns5", "nret", "xres"]
        f.finish(out_keys)

    def trange(self, grp):
        return (0, LS) if grp == 1 else (LS, NPS * LP)

    def mod_stage(self):
        for r in range(2):
            self.ld(self.cT[:, :, r], self.cvec[r].rearrange("(k p) -> p k", p=128), w=["cT"], allow_slow_non_contiguous=True)
        self.A(lambda e: e.activation(out=self.cTb, in_=self.cT, func=AF.Silu), ["cT"], ["cTb"])
        for r in range(2):
            self.V(lambda e: e.tensor_copy(out=self.cTrep[:, r], in_=self.cTb[:, :, r:r + 1].to_broadcast([128, 8, 128])),
                   ["cTb"], ["cTrep"])
        for l in range(4):
            self.ld(self.ngT[:, l, :], self.norm_g[l].rearrange("(k p) -> p k", p=128), w=["ngT"], allow_slow_non_contiguous=True)
            self.ld(self.mbT[:, l, :], self.mod_b[l].rearrange("(k p) -> p k", p=128), w=["mbT"], allow_slow_non_contiguous=True)
        for i in self.layers:
            for half in range(4):
                wt = self.wst[half % 2]
                wk = "wst0"
                self.ldc(wt, self.mod_w[i, :, half * 512:(half + 1) * 512].rearrange("(k p) n -> p k n", p=128), w=[wk])
                for cc in range(4):
                    ch = half * 4 + cc
                    pt = self.ps[0][:, 0:2]
                    for k in range(8):
                        self.T(lambda e: e.matmul(pt, lhsT=wt[:, k, cc * 128:(cc + 1) * 128], rhs=self.cTb[:, k, :],
                                                  start=(k == 0), stop=(k == 7)), [wk, "cTb"], ["ps0"])
                    self.V(lambda e: e.tensor_tensor(out=self.modT[:, i, ch, :], in0=pt,
                                                     in1=self.mbT[:, i, ch:ch + 1].to_broadcast([128, 2]), op=ALU.add),
                           ["ps0", "mbT"], ["modT"])
            self.V(lambda e: e.tensor_scalar(out=self.gsc[:, i], in0=self.modT[:, i, 8:16, :], scalar1=1.0, scalar2=None,
                                             op0=ALU.add), ["modT"], ["gsc"])
            self.V(lambda e: e.tensor_tensor(out=self.gsc[:, i], in0=self.gsc[:, i],
                                             in1=self.ngT[:, i, :].unsqueeze(2).to_broadcast([128, 8, 2]), op=ALU.mult),
                   ["gsc", "ngT"], ["gsc"])

    def gate_table(self, i):
        self.mbg = self.sq
        self.ld(self.mbg, self.mod_b[i, 2 * D:3 * D].partition_broadcast(128), w=["sq"])
        for half in range(2):
            wt = self.wst[half % 2]
            wk = "wst0"
            self.ldc(wt, self.mod_w[i, :, 2 * D + half * 512: 2 * D + (half + 1) * 512].rearrange("(k p) n -> p k n", p=128),
                     w=[wk])
            for r in range(2):
                pt = self.ps[1]
                for k in range(8):
                    self.T(lambda e: e.matmul(pt, lhsT=self.cTrep[:, r, k, :], rhs=wt[:, k, :],
                                              start=(k == 0), stop=(k == 7)), [wk, "cTrep"], ["ps1"])
                self.V(lambda e: e.tensor_tensor(out=self.gt_bc[:, r, half * 512:(half + 1) * 512], in0=pt,
                                                 in1=self.mbg[:, half * 512:(half + 1) * 512], op=ALU.add),
                       ["ps1", "sq"], ["gt_bc"])

    def prologue(self, i, grp, first):
        t0, n = self.trange(grp)
        for tt in range(n // 128):
            b = tt % 2
            xt, xn = self.xt[b], self.xn[b]
            if first:
                src = self.xs[tt * 128:(tt + 1) * 128, :] if grp == 1 else self.xp[tt * 128:(tt + 1) * 128, :]
                rk = []
            else:
                src = self.xres[t0 + tt * 128: t0 + (tt + 1) * 128, :]
                rk = ["xres"]
            self.ld(xt, src, r=rk, w=[f"xt{b}"])
            self.rms_scale(xt, f"xt{b}", xn, f"xn{b}")
            for k in range(8):
                pt = self.ps[2 + (k % 2)].bitcast(BF16)[:, 0:128]
                pk = f"ps{2 + (k % 2)}"
                self.T(lambda e: e.transpose(pt, xn[:, k * 128:(k + 1) * 128], self.identb), [f"xn{b}", "identb"], [pk])
                self.A(lambda e: e.activation(out=self.hT[:, k, tt * 128:(tt + 1) * 128], in_=pt, func=AF.Identity,
                                              scale=self.gsc[:, i, k, grp:grp + 1], bias=self.modT[:, i, k, grp:grp + 1]),
                       [pk, "gsc", "modT"], ["hT"])

    def rms_scale(self, xt, xk, out, ok, gtab=None):
        self.A(lambda e: e.activation(out=self.sq, in_=xt, func=AF.Square, accum_out=self.stat[:, 0:1]), [xk], ["sq", "stat"])
        self.V(lambda e: e.tensor_scalar(out=self.stat[:, 1:2], in0=self.stat[:, 0:1], scalar1=1.0 / D, scalar2=EPS,
                                         op0=ALU.mult, op1=ALU.add), ["stat"], ["stat"])
        self.A(lambda e: e.activation(out=self.stat[:, 3:4], in_=self.stat[:, 1:2], func=AF.Sqrt), ["stat"], ["stat"])
        self.V(lambda e: e.reciprocal(out=self.stat[:, 2:3], in_=self.stat[:, 3:4]), ["stat"], ["stat"])
        if gtab is None:
            self.V(lambda e: e.tensor_scalar(out=out, in0=xt, scalar1=self.stat[:, 2:3], scalar2=None, op0=ALU.mult),
                   [xk, "stat"], [ok])
        else:
            self.V(lambda e: e.scalar_tensor_tensor(out=out, in0=xt, scalar=self.stat[:, 2:3], in1=gtab,
                                                    op0=ALU.mult, op1=ALU.mult), [xk, "stat", "fgb"], [ok])

    def epilogue(self, i, grp, kdim, w_out, last):
        t0, n = self.trange(grp)
        kc = kdim // 128
        if kdim > D:
            self.arena_reset()
            self.wres = self.asb("wres", [128, 16, 1024], BF16)
            wrk = ["wres"]
        else:
            self.wres = self.SS.bitcast(BF16).rearrange("p (k n) -> p k n", k=8)
            wrk = ["SS", "SS2"]
        self.ldc(self.wres[:, 0:kc, :], w_out.rearrange("(k p) n -> p k n", p=128), w=wrk)
        yb = self.xn
        for tt in range(n // 128):
            b = tt % 2
            xt = self.xt[b]
            src = self.xres[t0 + tt * 128: t0 + (tt + 1) * 128, :]
            if i == self.layers[0]:
                src = self.xs[tt * 128:(tt + 1) * 128, :] if grp == 1 else self.xp[tt * 128:(tt + 1) * 128, :]
                rk = []
            else:
                rk = ["xres"]
            self.ld(xt, src, r=rk, w=[f"xt{b}"])
            yt = self.ylt[b]
            ysrc, ykey = self.ysrc
            self.ld(yt[:, 0:kc, :], ysrc[0:kc, :, t0 + tt * 128: t0 + (tt + 1) * 128].rearrange("k p t -> p k t"),
                    r=[ykey], w=["ylt"], q="sync")
            for h in range(2):
                pt = self.ps[4 + h]
                for k in range(kc):
                    self.T(lambda e: e.matmul(pt, lhsT=yt[:, k, :], rhs=self.wres[:, k, h * 512:(h + 1) * 512],
                                              start=(k == 0), stop=(k == kc - 1)), ["ylt"] + wrk, [f"ps{4 + h}"])
                self.V(lambda e: e.tensor_tensor(out=self.sq[:, h * 512:(h + 1) * 512], in0=pt,
                                                 in1=self.gt_bc[:, grp, h * 512:(h + 1) * 512], op=ALU.mult),
                       [f"ps{4 + h}", "gt_bc"], ["sq"])
            self.V(lambda e: e.tensor_tensor(out=xt, in0=xt, in1=self.sq, op=ALU.add), [f"xt{b}", "sq"], [f"xt{b}"])
            if not last:
                self.ld(self.xres[t0 + tt * 128: t0 + (tt + 1) * 128, :], xt, r=[f"xt{b}"], w=["xres"])
            else:
                ot = self.xt[1 - b]
                self.rms_scale(xt, f"xt{b}", ot, f"xt{1 - b}", gtab=self.fgb)
                dst = self.ys if grp == 1 else self.yp
                self.ld(dst[tt * 128:(tt + 1) * 128, :], ot, r=[f"xt{1 - b}"], w=["ys" if grp == 1 else "yp"])

    def E(self, eng, fn, r=(), w=()):
        return self.fw.op(eng, fn, r, w)

    def sin_of(self, out, x, shift, eng="vector"):
        y, yi, yf = self.tr_y, self.tr_yi, self.tr_yf
        self.E(eng, lambda e: e.tensor_scalar(out=y, in0=x, scalar1=1.0 / TWO_PI, scalar2=shift / TWO_PI + 8.0,
                                              op0=ALU.mult, op1=ALU.add), ["trx"], ["try"])
        self.E(eng, lambda e: e.tensor_copy(out=yi, in_=y), ["try"], ["tryi"])
        self.E(eng, lambda e: e.tensor_copy(out=yf, in_=yi), ["tryi"], ["tryf"])
        self.E(eng, lambda e: e.tensor_tensor(out=y, in0=y, in1=yf, op=ALU.subtract), ["try", "tryf"], ["try"])
        self.E(eng, lambda e: e.tensor_scalar(out=yf, in0=y, scalar1=0.5, scalar2=None, op0=ALU.is_gt), ["try"], ["tryf"])
        self.E(eng, lambda e: e.tensor_tensor(out=y, in0=y, in1=yf, op=ALU.subtract), ["try", "tryf"], ["try"])
        self.E(eng, lambda e: e.tensor_scalar(out=yf, in0=y, scalar1=-0.5, scalar2=None, op0=ALU.is_lt), ["try"], ["tryf"])
        self.E(eng, lambda e: e.tensor_tensor(out=y, in0=y, in1=yf, op=ALU.add), ["try", "tryf"], ["try"])
        self.A(lambda e: e.activation(out=out, in_=y, func=AF.Sin, scale=6.283185), ["try"], ["trx"])

    def s5_alloc(self):
        sb = self.asb
        self.wgl = [sb(f"wgl{i}", [128, 8, 128], BF16) for i in range(2)]
        self.LR = sb("LR", [128, 128]); self.LI = sb("LI", [128, 128]); self.DT = sb("DT", [128, 128])
        self.ANG = sb("ANG", [128, 128]); self.AR = sb("AR", [128, 128])
        self.SN = sb("SN", [128, 128]); self.CS = sb("CS", [128, 128])
        self.tr_y = sb("tr_y", [128, 128]); self.tr_yi = sb("tr_yi", [128, 128], I32); self.tr_yf = sb("tr_yf", [128, 128])
        self.PR = sb("PR", [128, 9, 128]); self.PI = sb("PI", [128, 9, 128])
        self.FR = sb("FR", [128, 128]); self.FI = sb("FI", [128, 128])
        self.t1 = sb("t1", [128, 256]); self.t2 = sb("t2", [128, 256]); self.t3 = sb("t3", [128, 256])
        self.BRk = sb("BRk", [128, 2, 8, 16]); self.BIk = sb("BIk", [128, 2, 8, 16])
        self.bbr = sb("bbr", [128, 2, 8, 16]); self.bbi = sb("bbi", [128, 2, 8, 16])
        self.CRn = sb("CRn", [128, 2, 2, 64]); self.CIn = sb("CIn", [128, 2, 2, 64])
        self.CRk = sb("CRk", [128, 2, 8, 16]); self.CIk = sb("CIk", [128, 2, 8, 16])
        self.BA = sb("BA", [128, 8, 2, 128]); self.CC = sb("CC", [128, 2, 128])
        self.CAr = sb("CAr", [128, 9, 2, 128], BF16); self.CAi = sb("CAi", [128, 9, 2, 128], BF16)
        self.Tsb = sb("Tsb", [128, 16, 128], BF16)
        self.LW = sb("LW", [128, 16, 2, 128], BF16)
        self.LM = sb("LM", [128, 4, 2, 2, 2, 128], BF16)
        self.Kc = sb("Kc", [128, 15, 128], BF16)
        self.SS = sb("SS", [128, 8 * 2 * 256]); self.HP = sb("HP", [128, 8 * 2 * 256], BF16)
        self.HH = sb("HH", [128, 64]); self.HT1 = sb("HT1", [128, 64]); self.HU1 = sb("HU1", [128, 64])
        self.A1 = sb("A1", [128, 64]); self.A2 = sb("A2", [128, 64])
        self.HL = sb("HL", [128, 256]); self.HLT1 = sb("HLT1", [128, 256]); self.HLU1 = sb("HLU1", [128, 256])
        self.A1f = sb("A1f", [128, 256]); self.A2f = sb("A2f", [128, 256])
        self.PWp = sb("PWp", [128, 256]); self.PW = sb("PW", [128, 256]); self.HST = sb("HST", [128, 256])
        self.Hc = sb("Hc", [128, 16]); self.Hc0 = sb("Hc0", [128, 16]); self.HcT = sb("HcT", [128, 16]); self.HcU = sb("HcU", [128, 16])
        self.B1 = sb("B1", [128, 16]); self.B2 = sb("B2", [128, 16])
        self.TC1 = sb("TC1", [128, 512]); self.TC2 = sb("TC2", [128, 512])
        self.h0T = sb("h0T", [128, 2, 2, 32]); self.FS = sb("FS", [128, 4, 2, 2, 32]); self.FSo = sb("FSo", [128, 128])
        self.dcol = sb("dcol", [128, 8]); self.bgT = sb("bgT", [128, 8])
        self.u8 = sb("u8", [128, 8, LS // 8], BF16); self.g_k = sb("g_k", [128, LS], BF16)
        self.t1g = sb("t1g", [128, 256]); self.t2g = sb("t2g", [128, 256])
        self.gblk = self.wst[0]
        self.sgm = self.SS[:, 0:512]; self.slu = self.SS[:, 512:1024]; self.yb = [self.HP[:, i * 512:(i + 1) * 512] for i in range(2)]
        self.V(lambda e: e.memset(self.LM, 0.0), [], ["LM"])
        self.V(lambda e: e.memset(self.CAr[:, 0], 0.0), [], ["CA"])
        self.V(lambda e: e.memset(self.CAi[:, 0], 0.0), [], ["CA"])

    def s5_layer_prep(self, js):
        V, A = self.V, self.A
        for half in range(2):
            hs = slice(half * 64, half * 64 + 64)
            self.ld(self.LR[hs, :], self.s5_lam_re[js].rearrange("d g p -> p (d g)"), w=["LR"], allow_slow_non_contiguous=True)
            self.ld(self.LI[hs, :], self.s5_lam_im[js].rearrange("d g p -> p (d g)"), w=["LI"], allow_slow_non_contiguous=True)
        self.ld(self.DT, self.s5_log_step[js].rearrange("d g -> (d g)").partition_broadcast(128), w=["DT"])
        self.ld(self.dcol, self.s5_d[js].rearrange("(k p) -> p k", p=128), w=["dcol"], allow_slow_non_contiguous=True)
        self.ld(self.bgT, self.s5_b_glu[js].rearrange("(k p) -> p k", p=128), w=["bgT"], allow_slow_non_contiguous=True)
        A(lambda e: e.activation(out=self.DT, in_=self.DT, func=AF.Exp), ["DT"], ["DT"])
        V(lambda e: e.tensor_tensor(out=self.ANG, in0=self.LI, in1=self.DT, op=ALU.mult), ["LI", "DT"], ["trx", "ANG"])
        V(lambda e: e.tensor_tensor(out=self.AR, in0=self.LR, in1=self.DT, op=ALU.mult), ["LR", "DT"], ["AR"])
        A(lambda e: e.activation(out=self.AR, in_=self.AR, func=AF.Exp), ["AR"], ["AR"])
        self.sin_of(self.SN, self.ANG, 0.0)
        self.sin_of(self.CS, self.ANG, math.pi / 2)
        PR, PI = self.PR, self.PI
        V(lambda e: e.memset(PR[:, 0], 1.0), [], ["PR"])
        V(lambda e: e.memset(PI[:, 0], 0.0), [], ["PI"])
        V(lambda e: e.tensor_tensor(out=PR[:, 1], in0=self.AR, in1=self.CS, op=ALU.mult), ["AR", "trx"], ["PR"])
        V(lambda e: e.tensor_tensor(out=PI[:, 1], in0=self.AR, in1=self.SN, op=ALU.mult), ["AR", "trx"], ["PI"])
        t1, t2 = self.t1[:, 0:128], self.t2[:, 0:128]
        for m in range(2, 9):
            V(lambda e: e.tensor_tensor(out=t1, in0=PR[:, m - 1], in1=PR[:, 1], op=ALU.mult), ["PR"], ["t1"])
            V(lambda e: e.tensor_tensor(out=t2, in0=PI[:, m - 1], in1=PI[:, 1], op=ALU.mult), ["PI"], ["t2"])
            V(lambda e: e.tensor_tensor(out=PR[:, m], in0=t1, in1=t2, op=ALU.subtract), ["t1", "t2"], ["PR"])
            V(lambda e: e.tensor_tensor(out=t1, in0=PR[:, m - 1], in1=PI[:, 1], op=ALU.mult), ["PR", "PI"], ["t1"])
            V(lambda e: e.tensor_tensor(out=t2, in0=PI[:, m - 1], in1=PR[:, 1], op=ALU.mult), ["PR", "PI"], ["t2"])
            V(lambda e: e.tensor_tensor(out=PI[:, m], in0=t1, in1=t2, op=ALU.add), ["t1", "t2"], ["PI"])
        nr, den = self.SN, self.CS
        V(lambda e: e.tensor_scalar(out=nr, in0=PR[:, 1], scalar1=-1.0, scalar2=None, op0=ALU.add), ["PR"], ["trx"])
        V(lambda e: e.tensor_tensor(out=t1, in0=self.LR, in1=self.LR, op=ALU.mult), ["LR"], ["t1"])
        V(lambda e: e.tensor_tensor(out=t2, in0=self.LI, in1=self.LI, op=ALU.mult), ["LI"], ["t2"])
        V(lambda e: e.tensor_tensor(out=den, in0=t1, in1=t2, op=ALU.add), ["t1", "t2"], ["trx"])
        V(lambda e: e.reciprocal(out=den, in_=den), ["trx"], ["trx"])
        V(lambda e: e.tensor_tensor(out=t1, in0=nr, in1=self.LR, op=ALU.mult), ["trx", "LR"], ["t1"])
        V(lambda e: e.tensor_tensor(out=t2, in0=PI[:, 1], in1=self.LI, op=ALU.mult), ["PI", "LI"], ["t2"])
        V(lambda e: e.tensor_tensor(out=t1, in0=t1, in1=t2, op=ALU.add), ["t1", "t2"], ["t1"])
        V(lambda e: e.tensor_tensor(out=self.FR, in0=t1, in1=den, op=ALU.mult), ["t1", "trx"], ["FR"])
        V(lambda e: e.tensor_tensor(out=t1, in0=PI[:, 1], in1=self.LR, op=ALU.mult), ["PI", "LR"], ["t1"])
        V(lambda e: e.tensor_tensor(out=t2, in0=nr, in1=self.LI, op=ALU.mult), ["trx", "LI"], ["t2"])
        V(lambda e: e.tensor_tensor(out=t1, in0=t1, in1=t2, op=ALU.subtract), ["t1", "t2"], ["t1"])
        V(lambda e: e.tensor_tensor(out=self.FI, in0=t1, in1=den, op=ALU.mult), ["t1", "trx"], ["FI"])
        self.ld(self.FSo, self.st5[js], w=["FSo"])
        pt = self.ps[1][:, 0:128]
        self.T(lambda e: e.transpose(pt, self.FSo, self.identf), ["FSo", "identf"], ["ps1"])
        V(lambda e: e.tensor_copy(out=self.h0T.rearrange("p d x g -> p (d x g)"), in_=pt), ["ps1"], ["h0T"])

    def bcg(self, tab, m, k):
        a = tab[:, m, :].rearrange("p (d g) -> p d g", d=2)[:, :, 8 * k:8 * k + 8]
        return a.unsqueeze(3).to_broadcast([128, 2, 8, 16])

    def bcf(self, tab, k):
        a = tab.rearrange("p (d g) -> p d g", d=2)[:, :, 8 * k:8 * k + 8]
        return a.unsqueeze(3).to_broadcast([128, 2, 8, 16])

    def cmul(self, eng, outr, outi, ar, ai, br, bi, rk, wk, hs_r=slice(0, 128), hs_i=slice(0, 128), negi=False):
        if eng == "gpsimd":
            t1 = self.t1g.rearrange("p (d g h) -> p d g h", d=2, g=8)
            t2 = self.t2g.rearrange("p (d g h) -> p d g h", d=2, g=8)
            k1, k2 = "t1g", "t2g"
        else:
            t1 = self.t1.rearrange("p (d g h) -> p d g h", d=2, g=8)
            t2 = self.t2.rearrange("p (d g h) -> p d g h", d=2, g=8)
            k1, k2 = "t1", "t2"
        E = self.E
        s = hs_r
        E(eng, lambda e: e.tensor_tensor(out=t1[s], in0=ar[s], in1=br[s], op=ALU.mult), rk, [k1])
        E(eng, lambda e: e.tensor_tensor(out=t2[s], in0=ai[s], in1=bi[s], op=ALU.mult), rk, [k2])
        E(eng, lambda e: e.tensor_tensor(out=outr[s], in0=t1[s], in1=t2[s], op=ALU.subtract), [k1, k2], wk)
        s = hs_i
        E(eng, lambda e: e.tensor_tensor(out=t1[s], in0=ar[s], in1=bi[s], op=ALU.mult), rk, [k1])
        E(eng, lambda e: e.tensor_tensor(out=t2[s], in0=ai[s], in1=br[s], op=ALU.mult), rk, [k2])
        if negi:
            E(eng, lambda e: e.tensor_tensor(out=t1[s], in0=t1[s], in1=t2[s], op=ALU.add), [k1, k2], [k1])
            E(eng, lambda e: e.tensor_scalar(out=outi[s], in0=t1[s], scalar1=-1.0, scalar2=None, op0=ALU.mult), [k1], wk)
        else:
            E(eng, lambda e: e.tensor_tensor(out=outi[s], in0=t1[s], in1=t2[s], op=ALU.add), [k1, k2], wk)

    def s5_scan1(self, k, seng, SSv, HPv, SQs, ncg, ncs, nseq, A1s, A2s, grp):
        E = self.E
        HHv = lambda t: t.rearrange("p (x q d s) -> p x q d s", x=2, q=4, d=2)
        HHs, T1s, U1s = self.HH[:, 0:16 * nseq], self.HT1[:, 0:16 * nseq], self.HU1[:, 0:16 * nseq]
        if grp == 1:
            E(seng, lambda e: e.tensor_copy(out=HHv(HHs)[:, :, :, :, 0].rearrange("p x q d -> p d x q"),
                                            in_=self.h0T[:, :, :, 4 * k:4 * k + 4]), ["h0T"], ["HH"])
        else:
            E(seng, lambda e: e.memset(HHs, 0.0), [], ["HH"])
        hsw = AP(HHs, HHs.offset + 8 * nseq, [[HHs.ap[0][0], 128], [-8 * nseq, 2], [1, 8 * nseq]])
        hfl = HHs.rearrange("p (x r) -> p x r", x=2)
        a2f = A2s.rearrange("p (x r) -> p x r", x=2)
        u1f = U1s.rearrange("p (x r) -> p x r", x=2)
        hh4 = HHs.rearrange("p (xq d s) -> p xq d s", d=2, s=nseq)
        for i in range(ncs):
            def colap(t):
                return AP(t, t.offset + i, [[t.ap[0][0], 128], [SQs, 8], [ncg + ncs - 1 - 2 * i, 2], [ncs, nseq]])
            E(seng, lambda e: e.tensor_copy(out=colap(HPv), in_=hh4), ["HH"], ["HP"])
            E(seng, lambda e: e.tensor_tensor(out=T1s, in0=A1s, in1=HHs, op=ALU.mult), ["A1", "HH"], ["HT1"])
            E(seng, lambda e: e.tensor_tensor(out=u1f, in0=a2f, in1=hsw, op=ALU.mult), ["A2", "HH"], ["HU1"])
            E(seng, lambda e: e.tensor_tensor(out=T1s, in0=T1s, in1=U1s, op=ALU.add), ["HT1", "HU1"], ["HT1"])
            E(seng, lambda e: e.tensor_tensor(out=hh4, in0=T1s.rearrange("p (xq d s) -> p xq d s", d=2, s=nseq),
                                              in1=colap(SSv), op=ALU.add), ["HT1", "SS"], ["HH"])
        if grp == 0:
            for s in range(nseq):
                E(seng, lambda e: e.tensor_copy(out=self.FS[:, s, :, :, 4 * k:4 * k + 4],
                                                in_=HHv(HHs)[:, :, :, :, s].rearrange("p x q d -> p d x q")), ["HH"], ["FS"])

    def s5_scan2(self, k, seng, SSv, HPv, SQs, ncg, A1s, A2s):
        E = self.E
        MB = 16
        ps_ = SSv.ap[0][0]
        HL, T1, U1, A1f, A2f = self.HL, self.HLT1, self.HLU1, self.A1f, self.A2f
        PWp, PW, HST = self.PWp, self.PW, self.HST
        Hc, Hc0, HcT, HcU, B1, B2 = self.Hc, self.Hc0, self.HcT, self.HcU, self.B1, self.B2
        TC1, TC2 = self.TC1, self.TC2
        E(seng, lambda e: e.tensor_copy(out=A1f.rearrange("p (a b) -> p a b", b=MB), in_=A1s.unsqueeze(2).to_broadcast([128, 16, MB])), ["A1"], ["A1f"])
        E(seng, lambda e: e.tensor_copy(out=A2f.rearrange("p (a b) -> p a b", b=MB), in_=A2s.unsqueeze(2).to_broadcast([128, 16, MB])), ["A2"], ["A2f"])
        PWv = PWp.rearrange("p (x g i) -> p x g i", x=2, g=8)
        E(seng, lambda e: e.tensor_copy(out=PWv[:, 0, :, 0], in_=A1s[:, 0:8]), ["A1"], ["PWp"])
        E(seng, lambda e: e.tensor_copy(out=PWv[:, 1, :, 0], in_=A2s[:, 8:16]), ["A2"], ["PWp"])
        ln = 1
        t1 = TC1[:, 0:64].rearrange("p (g i) -> p g i", g=8)
        t2 = TC2[:, 0:64].rearrange("p (g i) -> p g i", g=8)
        while ln < MB:
            mr = PWv[:, 0, :, ln - 1:ln].to_broadcast([128, 8, ln])
            mi = PWv[:, 1, :, ln - 1:ln].to_broadcast([128, 8, ln])
            ar, ai = PWv[:, 0, :, 0:ln], PWv[:, 1, :, 0:ln]
            E(seng, lambda e: e.tensor_tensor(out=t1[:, :, 0:ln], in0=ar, in1=mr, op=ALU.mult), ["PWp"], ["TC1"])
            E(seng, lambda e: e.tensor_tensor(out=t2[:, :, 0:ln], in0=ai, in1=mi, op=ALU.mult), ["PWp"], ["TC2"])
            E(seng, lambda e: e.tensor_tensor(out=PWv[:, 0, :, ln:2 * ln], in0=t1[:, :, 0:ln], in1=t2[:, :, 0:ln], op=ALU.subtract), ["TC1", "TC2"], ["PWp"])
            E(seng, lambda e: e.tensor_tensor(out=t1[:, :, 0:ln], in0=ar, in1=mi, op=ALU.mult), ["PWp"], ["TC1"])
            E(seng, lambda e: e.tensor_tensor(out=t2[:, :, 0:ln], in0=ai, in1=mr, op=ALU.mult), ["PWp"], ["TC2"])
            E(seng, lambda e: e.tensor_tensor(out=PWv[:, 1, :, ln:2 * ln], in0=t1[:, :, 0:ln], in1=t2[:, :, 0:ln], op=ALU.add), ["TC1", "TC2"], ["PWp"])
            ln *= 2
        PW5 = PW.rearrange("p (x q d i) -> p x q d i", x=2, q=4, d=2)
        PWp5 = PWp.rearrange("p (x q d i) -> p x q d i", x=2, q=4, d=2)
        for x in range(2):
            E(seng, lambda e: e.tensor_copy(out=PW5[:, x, :, 0, :], in_=PWp5[:, x, :, 0, :]), ["PWp"], ["PW"])
            E(seng, lambda e: e.tensor_copy(out=PW5[:, x, :, 1, :], in_=PWp5[:, x, :, 1, ::-1]), ["PWp"], ["PW"])
        B1v = B1.rearrange("p (x g) -> p x g", x=2)
        B2v = B2.rearrange("p (x g) -> p x g", x=2)
        for x in range(2):
            E(seng, lambda e: e.tensor_copy(out=B1v[:, x, :], in_=PWv[:, 0, :, MB - 1]), ["PWp"], ["B1"])
        E(seng, lambda e: e.tensor_scalar(out=B2v[:, 0, :], in0=PWv[:, 1, :, MB - 1], scalar1=-1.0, scalar2=None, op0=ALU.mult), ["PWp"], ["B2"])
        E(seng, lambda e: e.tensor_copy(out=B2v[:, 1, :], in_=PWv[:, 1, :, MB - 1]), ["PWp"], ["B2"])
        E(seng, lambda e: e.memset(HL, 0.0), [], ["HL"])
        hsw = AP(HL, HL.offset + 128, [[HL.ap[0][0], 128], [-128, 2], [1, 128]])
        a2f = A2f.rearrange("p (x r) -> p x r", x=2)
        u1f = U1.rearrange("p (x r) -> p x r", x=2)
        hl4 = HL.rearrange("p (g d b) -> p g d b", g=8, d=2)
        t14 = T1.rearrange("p (g d b) -> p g d b", g=8, d=2)
        for i in range(MB):
            col = AP(SSv, SSv.offset + i, [[ps_, 128], [SQs, 8], [ncg + MB - 1 - 2 * i, 2], [MB, MB]])
            E(seng, lambda e: e.tensor_tensor(out=T1, in0=A1f, in1=HL, op=ALU.mult), ["A1f", "HL"], ["HLT1"])
            E(seng, lambda e: e.tensor_tensor(out=u1f, in0=a2f, in1=hsw, op=ALU.mult), ["A2f", "HL"], ["HLU1"])
            E(seng, lambda e: e.tensor_tensor(out=T1, in0=T1, in1=U1, op=ALU.add), ["HLT1", "HLU1"], ["HLT1"])
            E(seng, lambda e: e.tensor_tensor(out=hl4, in0=t14, in1=col, op=ALU.add), ["HLT1", "SS"], ["HL"])
            E(seng, lambda e: e.tensor_copy(out=col, in_=hl4), ["HL"], ["SS"])
        E(seng, lambda e: e.tensor_copy(out=Hc.rearrange("p (x q d) -> p d x q", x=2, q=4), in_=self.h0T[:, :, :, 4 * k:4 * k + 4]), ["h0T"], ["Hc"])
        E(seng, lambda e: e.tensor_copy(out=Hc0, in_=Hc), ["Hc"], ["Hc0"])
        hcsw = AP(Hc, Hc.offset + 8, [[Hc.ap[0][0], 128], [-8, 2], [1, 8]])
        b2f = B2.rearrange("p (x r) -> p x r", x=2)
        hcuf = HcU.rearrange("p (x r) -> p x r", x=2)
        hc2 = Hc.rearrange("p (g d) -> p g d", d=2)
        hct2 = HcT.rearrange("p (g d) -> p g d", d=2)
        for b in range(MB):
            hpos = AP(HST, HST.offset + b, [[HST.ap[0][0], 128], [2 * MB, 8], [MB + MB - 1 - 2 * b, 2]])
            E(seng, lambda e: e.tensor_copy(out=hpos, in_=hc2), ["Hc"], ["HST"])
            if b == MB - 1:
                break
            send = AP(SSv, SSv.offset + MB * b + MB - 1, [[ps_, 128], [SQs, 8], [ncg + MB * (MB - 1 - b) - (MB * b + MB - 1), 2]])
            E(seng, lambda e: e.tensor_tensor(out=HcT, in0=B1, in1=Hc, op=ALU.mult), ["B1", "Hc"], ["HcT"])
            E(seng, lambda e: e.tensor_tensor(out=hcuf, in0=b2f, in1=hcsw, op=ALU.mult), ["B2", "Hc"], ["HcU"])
            E(seng, lambda e: e.tensor_tensor(out=HcT, in0=HcT, in1=HcU, op=ALU.add), ["HcT", "HcU"], ["HcT"])
            E(seng, lambda e: e.tensor_tensor(out=hc2, in0=hct2, in1=send, op=ALU.add), ["HcT", "SS"], ["Hc"])
        HST4 = HST.rearrange("p (x q d b) -> p x q d b", x=2, q=4, d=2)
        c1 = TC1.rearrange("p (q b i) -> p q b i", q=2, b=MB)
        c2 = TC2.rearrange("p (q b i) -> p q b i", q=2, b=MB)
        for d in range(2):
          for qh in range(2):
            qs = slice(2 * qh, 2 * qh + 2)
            pr = PW5[:, 0, qs, d, :].unsqueeze(2).to_broadcast([128, 2, MB, MB])
            pi = PW5[:, 1, qs, d, :].unsqueeze(2).to_broadcast([128, 2, MB, MB])
            hr = HST4[:, 0, qs, d, :].unsqueeze(3).to_broadcast([128, 2, MB, MB])
            hi = HST4[:, 1, qs, d, :].unsqueeze(3).to_broadcast([128, 2, MB, MB])
            sre = AP(SSv, SSv.offset + 2 * qh * SQs + d * ncg, [[ps_, 128], [SQs, 2], [MB, MB], [1, MB]])
            sim = AP(SSv, SSv.offset + (4 + 2 * qh) * SQs + d * ncg, [[ps_, 128], [SQs, 2], [MB, MB], [1, MB]])
            E(seng, lambda e: e.tensor_tensor(out=c1, in0=pr, in1=hr, op=ALU.mult), ["PW", "HST"], ["TC1"])
            E(seng, lambda e: e.tensor_tensor(out=c2, in0=pi, in1=hi, op=ALU.mult), ["PW", "HST"], ["TC2"])
            E(seng, lambda e: e.tensor_tensor(out=c1, in0=c1, in1=c2, op=ALU.subtract), ["TC1", "TC2"], ["TC1"])
            E(seng, lambda e: e.tensor_tensor(out=sre, in0=sre, in1=c1, op=ALU.add), ["SS", "TC1"], ["SS"])
            E(seng, lambda e: e.tensor_tensor(out=c1, in0=pr, in1=hi, op=ALU.mult), ["PW", "HST"], ["TC1"])
            E(seng, lambda e: e.tensor_tensor(out=c2, in0=pi, in1=hr, op=ALU.mult), ["PW", "HST"], ["TC2"])
            E(seng, lambda e: e.tensor_tensor(out=c1, in0=c1, in1=c2, op=ALU.add), ["TC1", "TC2"], ["TC1"])
            E(seng, lambda e: e.tensor_tensor(out=sim, in0=sim, in1=c1, op=ALU.add), ["SS", "TC1"], ["SS"])
        ss3 = SSv.rearrange("p (g d c) -> p g d c", g=8, d=2)
        hp3 = HPv.rearrange("p (g d c) -> p g d c", g=8, d=2)
        h03 = Hc0.rearrange("p (g d) -> p g d", d=2)
        self.A(lambda e: e.activation(out=hp3[:, :, 0, 1:ncg], in_=ss3[:, :, 0, 0:ncg - 1], func=AF.Copy), ["SS"], ["HP"])
        self.A(lambda e: e.activation(out=hp3[:, :, 1, 0:ncg - 1], in_=ss3[:, :, 1, 1:ncg], func=AF.Copy), ["SS"], ["HP"])
        E(seng, lambda e: e.tensor_copy(out=hp3[:, :, 0, 0:1], in_=h03[:, :, 0:1]), ["Hc0"], ["HP"])
        E(seng, lambda e: e.tensor_copy(out=hp3[:, :, 1, ncg - 1:ncg], in_=h03[:, :, 1:2]), ["Hc0"], ["HP"])

    def s5_mixer(self, js, grp):
        if grp == 1:
            self.arena_reset()
            self.s5_alloc()
            self.s5_layer_prep(js)
        else:
            self.V(lambda e: e.memset(self.LM, 0.0), [], ["LM"])
        t0, n = self.trange(grp)
        nseq = 1 if grp == 1 else NPS
        ncs = (n // nseq) // 8
        ncg = n // 8
        V, A, T, E = self.V, self.A, self.T, self.E
        H0, H1 = slice(0, 64), slice(64, 128)
        def ld_wu(k_):
            self.ldc(self.wgs[k_ % 2], self.s5_w_in[js][:, k_ * 128:(k_ + 1) * 128].rearrange("(k p) n -> p k n", p=128), w=[f"wgs{k_ % 2}"])
        ld_wu(0)
        for k in range(8):
            wu, wuk = self.wgs[k % 2], f"wgs{k % 2}"
            if k + 1 < 8:
                ld_wu(k + 1)
            eng = "vector"
            seng = "vector"
            for tb in range(n // 512):
                pt = self.ps[0]
                for kk in range(8):
                    T(lambda e: e.matmul(pt, lhsT=wu[:, kk, :],
                                         rhs=self.hT[:, kk, tb * 512:(tb + 1) * 512], start=(kk == 0), stop=(kk == 7)),
                      [wuk, "hT"], ["ps0"])
                A(lambda e: e.activation(out=self.u8[:, :, tb * 64:(tb + 1) * 64].rearrange("p j c -> p c j"),
                                         in_=pt.rearrange("p (c j) -> p c j", j=8), func=AF.Copy), ["ps0"], ["u_k"])
            if grp == 1:
                for half in range(2):
                    hs = slice(half * 64, half * 64 + 64)
                    for d in range(2):
                        self.ld(self.BRk[hs, d], self.s5_b_re[js, d, 8 * k:8 * k + 8].rearrange("g p h -> p g h"), w=["BRk"])
                        self.ld(self.BIk[hs, d], self.s5_b_im[js, d, 8 * k:8 * k + 8].rearrange("g p h -> p g h"), w=["BIk"])
                for dup in range(2):
                    self.ld(self.CRn[:, :, dup, :], self.s5_c_re[js][:, 8 * k:8 * k + 8].rearrange("d g h p -> (g h) d p"), w=["CRn"])
                    self.ld(self.CIn[:, :, dup, :], self.s5_c_im[js][:, 8 * k:8 * k + 8].rearrange("d g h p -> (g h) d p"), w=["CIn"])
                for (cn, cnk, ck, ckk) in ((self.CRn, "CRn", self.CRk, "CRk"), (self.CIn, "CIn", self.CIk, "CIk")):
                    for d in range(2):
                        pt = self.ps[1][:, 0:128]
                        src = cn[:, d].rearrange("p a c -> p (a c)")
                        T(lambda e: e.transpose(pt, src, self.identf), [cnk, "identf"], ["ps1"])
                        A(lambda e: e.activation(out=ck[:, d].rearrange("p g h -> p (g h)"), in_=pt, func=AF.Copy), ["ps1"], [ckk])
                self.cmul(eng, self.bbr, self.bbi, self.bcf(self.FR, k), self.bcf(self.FI, k), self.BRk, self.BIk,
                          ["FR", "FI", "BRk", "BIk"], ["bb"])
                BAv = self.BA.rearrange("p m d (g h) -> p m d g h", g=8)
                for m in range(8):
                    self.cmul(eng, BAv[:, m], BAv[:, m], self.bcg(self.PR, m, k), self.bcg(self.PI, m, k), self.bbr, self.bbi,
                              ["PR", "PI", "bb"], ["BA"], hs_r=H0, hs_i=H1)
                CCv = self.CC.rearrange("p d (g h) -> p d g h", g=8)
                V(lambda e: e.tensor_copy(out=CCv[H0], in_=self.CRk[H0]), ["CRk"], ["CC"])
                V(lambda e: e.tensor_scalar(out=CCv[H1], in0=self.CIk[H1], scalar1=-1.0, scalar2=None, op0=ALU.mult), ["CIk"], ["CC"])
                CArv = self.CAr.rearrange("p m d (g h) -> p m d g h", g=8)
                CAiv = self.CAi.rearrange("p m d (g h) -> p m d g h", g=8)
                for m in range(1, 9):
                    self.cmul("gpsimd", CArv[:, m], CAiv[:, m], self.bcg(self.PR, m, k), self.bcg(self.PI, m, k), self.CRk, self.CIk,
                              ["PR", "PI", "CRk", "CIk"], ["CA"], negi=True)
                for tau in range(8):
                    for d in range(2):
                        pt = self.ps[1][:, 0:128]
                        if tau == 0:
                            T(lambda e: e.matmul(pt, lhsT=self.BA[:, 0, d], rhs=self.CC[:, d], start=(d == 0), stop=(d == 1)),
                              ["BA", "CC"], ["ps1"])
                            if d == 0:
                                continue
                            tt = self.t3[:, 0:128]
                            V(lambda e: e.tensor_tensor(out=tt, in0=pt, in1=self.bdmask, op=ALU.mult), ["ps1", "bdmask"], ["t3"])
                            V(lambda e: e.scalar_tensor_tensor(out=self.Kc[:, 7], in0=self.identf, scalar=self.dcol[:, k:k + 1],
                                                               in1=tt, op0=ALU.mult, op1=ALU.add), ["t3", "identf", "dcol"], ["Kc"])
                        else:
                            T(lambda e: e.matmul(pt, lhsT=self.BA[:, tau, d], rhs=self.CC[:, d], start=True, stop=True),
                              ["BA", "CC"], ["ps1"])
                            idx = 7 + tau if d == 0 else 7 - tau
                            V(lambda e: e.tensor_tensor(out=self.Kc[:, idx], in0=pt, in1=self.bdmask, op=ALU.mult),
                              ["ps1", "bdmask"], ["Kc"])
                for g4 in range(4):
                    bank, bk = (self.ps[1], "ps1") if g4 % 2 == 0 else (self.ps[0], "ps0")
                    for ii in range(4):
                        idx = g4 * 4 + ii
                        T(lambda e: e.transpose(bank[:, ii * 128:(ii + 1) * 128], self.BA[:, idx // 2, idx % 2], self.identf),
                          ["BA", "identf"], [bk])
                    A(lambda e: e.activation(out=self.Tsb[:, g4 * 4:(g4 + 1) * 4].rearrange("p a b -> p (a b)"), in_=bank, func=AF.Copy),
                      [bk], ["Tsb"])
                self.ld(self.Tsb_scr[js, k], self.Tsb.rearrange("p a b -> p (a b)"), r=["Tsb"], w=[("s5c", k)])
                self.ld(self.Kc_scr[js, k], self.Kc.rearrange("p a b -> p (a b)"), r=["Kc"], w=[("s5c", k)])
                self.ld(self.CA_scr[js, k, :, 0], self.CAr.rearrange("p m d c -> p (m d c)"), r=["CA"], w=[("s5c", k)])
                self.ld(self.CA_scr[js, k, :, 1], self.CAi.rearrange("p m d c -> p (m d c)"), r=["CA"], w=[("s5c", k)])
            else:
                self.ld(self.Tsb.rearrange("p a b -> p (a b)"), self.Tsb_scr[js, k], r=[("s5c", k)], w=["Tsb"])
                self.ld(self.Kc.rearrange("p a b -> p (a b)"), self.Kc_scr[js, k], r=[("s5c", k)], w=["Kc"])
                self.ld(self.CAr.rearrange("p m d c -> p (m d c)"), self.CA_scr[js, k, :, 0], r=[("s5c", k)], w=["CA"])
                self.ld(self.CAi.rearrange("p m d c -> p (m d c)"), self.CA_scr[js, k, :, 1], r=[("s5c", k)], w=["CA"])
            HHv = lambda t: t.rearrange("p (x q d s) -> p x q d s", x=2, q=4, d=2)
            A1s, A2s = self.A1[:, 0:16 * nseq], self.A2[:, 0:16 * nseq]
            A1v, A2v = HHv(A1s), HHv(A2s)
            for hf, hsl in ((0, H0), (1, H1)):
                pr8 = self.PR[hsl, 8, :].rearrange("p (d g) -> p d g", d=2)[:, :, 8 * k + hf:8 * k + 8:2]
                pi8 = self.PI[hsl, 8, :].rearrange("p (d g) -> p d g", d=2)[:, :, 8 * k + hf:8 * k + 8:2]
                for s in range(nseq):
                    for x in range(2):
                        V(lambda e: e.tensor_copy(out=A1v[hsl, x, :, :, s].rearrange("p q d -> p d q"), in_=pr8), ["PR"], ["A1"])
                    V(lambda e: e.tensor_scalar(out=A2v[hsl, 0, :, :, s].rearrange("p q d -> p d q"), in0=pi8, scalar1=-1.0,
                                                scalar2=None, op0=ALU.mult), ["PI"], ["A2"])
                    V(lambda e: e.tensor_copy(out=A2v[hsl, 1, :, :, s].rearrange("p q d -> p d q"), in_=pi8), ["PI"], ["A2"])
            SQs = 2 * ncg
            SSv = self.SS[:, 0:16 * ncg]
            HPv = self.HP[:, 0:16 * ncg]
            for q in range(4):
                for x in range(2):
                    in0 = self.Tsb[:, :, x * 64:(x + 1) * 64].unsqueeze(2).to_broadcast([128, 16, 2, 64])
                    in1 = self.pmask[:, 2 * q:2 * q + 2].unsqueeze(1).unsqueeze(3).to_broadcast([128, 16, 2, 64])
                    outv = self.LW[:, :, x, :].rearrange("p m (a c) -> p m a c", a=2)
                    V(lambda e: e.tensor_tensor(out=outv, in0=in0, in1=in1, op=ALU.mult), ["Tsb", "pmask"], ["LW"])
                for d in range(2):
                    for x in range(2):
                        pt = self.ps[2 + x][:, 0:ncg]
                        for j in range(8):
                            m = 7 - j if d == 0 else j
                            T(lambda e: e.matmul(pt, lhsT=self.LW[:, m * 2 + d, x, :], rhs=self.u8[:, j, 0:ncg],
                                                 start=(j == 0), stop=(j == 7)), ["LW", "u_k"], [f"ps{2 + x}"])
                        off = (x * 4 + q) * SQs + d * ncg
                        A(lambda e: e.activation(out=SSv[:, off:off + ncg], in_=pt, func=AF.Copy), [f"ps{2 + x}"], ["SS"])
            if grp == 1:
                self.s5_scan2(k, seng, SSv, HPv, SQs, ncg, A1s, A2s)
            else:
                self.s5_scan1(k, seng, SSv, HPv, SQs, ncg, ncs, nseq, A1s, A2s, grp)
            for jh in range(4):
                for d in range(2):
                    for x, ca in ((0, self.CAr), (1, self.CAi)):
                        for hf, hsl in ((0, H0), (1, H1)):
                            lm = self.LM[hsl, :, d, x, :, :]
                            outv = AP(lm, lm.offset + 16 * hf, [[lm.ap[0][0], 64], [lm.ap[1][0] + 32, 4], [lm.ap[2][0], 2], [1, 16]])
                            if d == 0:
                                m0, ms = 2 * jh + 1, 1
                            else:
                                m0, ms = 8 - 2 * jh, -1
                            cam = ca[hsl, m0, d, :]
                            mstride = ca.ap[1][0]
                            inv = AP(cam, cam.offset + 16 * hf, [[cam.ap[0][0], 64], [32, 4], [ms * mstride, 2], [1, 16]])
                            V(lambda e: e.tensor_copy(out=outv, in_=inv), ["CA"], ["LM"])
                for jj in range(2):
                    j = jh * 2 + jj
                    pt = self.ps[4 + jj][:, 0:ncg]
                    pk = f"ps{4 + jj}"
                    for j2 in range(8):
                        T(lambda e: e.matmul(pt, lhsT=self.Kc[:, j - j2 + 7, :], rhs=self.u8[:, j2, 0:ncg],
                                             start=(j2 == 0), stop=False), ["Kc", "u_k"], [pk])
                    cnt = 0
                    for q in range(4):
                        for d in range(2):
                            for x in range(2):
                                off = (x * 4 + q) * SQs + d * ncg
                                cnt += 1
                                T(lambda e: e.matmul(pt, lhsT=self.LM[:, q, d, x, jj, :], rhs=HPv[:, off:off + ncg],
                                                     start=False, stop=(cnt == 16)), ["LM", "HP"], [pk])
                    A(lambda e: e.activation(out=self.g_k[:, j:n:8], in_=pt, func=AF.Gelu_apprx_tanh), [pk], ["g_k"])
            self.ld(self.gscr[k, :, t0:t0 + n], self.g_k[:, 0:n], r=["g_k"], w=[("gscr", grp)])
        if grp == 0:
            for s in range(nseq):
                pt = self.ps[1][:, 0:128]
                T(lambda e: e.transpose(pt, self.FS[:, s].rearrange("p d x g -> p (d x g)"), self.identf), ["FS", "identf"], ["ps1"])
                V(lambda e: e.tensor_copy(out=self.FSo, in_=pt), ["ps1"], ["FSo"])
                self.ld(self.ns5[s, js], self.FSo, r=["FSo"], w=["ns5"])
        lwv = self.LW.rearrange("p a b c -> p (a b c)")
        wglu = AP(lwv, lwv.offset, [[lwv.ap[0][0], 128], [1024, 8], [1, 1024]])
        self.ldc(wglu, self.s5_w_glu[js].rearrange("(k p) n -> p k n", p=128), w=["LW", "LM"])
        steps = [(tb, nn) for tb in range(n // 512) for nn in range(8)]
        def load_gate(i):
            nn_ = steps[i][1]
            self.ldc(self.wgs[i % 2], self.s5_w_in[js][:, D + nn_ * 128:D + (nn_ + 1) * 128].rearrange("(k p) n -> p k n", p=128),
                     w=[f"wgs{i % 2}"])
        load_gate(0)
        for si, (tb, nn) in enumerate(steps):
            ts = slice(tb * 512, (tb + 1) * 512)
            if nn == 0:
                self.ld(self.gblk, self.gscr[0:8, :, t0 + tb * 512:t0 + (tb + 1) * 512].rearrange("k p t -> p k t"),
                        r=[("gscr", grp)], w=["wst0"])
            if si + 1 < len(steps):
                load_gate(si + 1)
            pz, pg = self.ps[0 + 2 * (nn % 2)], self.ps[1 + 2 * (nn % 2)]
            pzk, pgk = f"ps{0 + 2 * (nn % 2)}", f"ps{1 + 2 * (nn % 2)}"
            for kk in range(8):
                T(lambda e: e.matmul(pz, lhsT=wglu[:, kk, nn * 128:(nn + 1) * 128], rhs=self.gblk[:, kk, :],
                                     start=(kk == 0), stop=(kk == 7)), ["LW", "LM", "wst0"], [pzk])
            A(lambda e: e.activation(out=self.sgm, in_=pz, func=AF.Sigmoid, bias=self.bgT[:, nn:nn + 1]), [pzk, "bgT"], ["SS"])
            wg, wgk = self.wgs[si % 2], f"wgs{si % 2}"
            for kk in range(8):
                T(lambda e: e.matmul(pg, lhsT=wg[:, kk, :], rhs=self.hT[:, kk, ts],
                                     start=(kk == 0), stop=(kk == 7)), [wgk, "hT"], [pgk])
            A(lambda e: e.activation(out=self.slu, in_=pg, func=AF.Silu), [pgk], ["SS2"])
            yb = self.yb[nn % 2]
            ybk = f"yb{nn % 2}"
            V(lambda e: e.tensor_tensor(out=self.sgm, in0=self.sgm, in1=self.gblk[:, nn, :], op=ALU.mult), ["SS", "wst0"], ["SS"])
            V(lambda e: e.tensor_tensor(out=yb, in0=self.sgm, in1=self.slu, op=ALU.mult), ["SS", "SS2"], [ybk, "HP"])
            self.ld(self.g2scr[nn, :, t0 + tb * 512:t0 + (tb + 1) * 512], yb, r=[ybk], w=[("g2scr", grp)])
        self.ysrc = (self.g2scr, ("g2scr", grp))
        return D


def host_consts():
    r = np.arange(128)
    pm = np.zeros((128, 8), np.float32)
    for q in range(4):
        for qq in range(2):
            pm[:, 2 * q + qq] = ((r // 16) == 2 * q + qq)
    bd = ((r[:, None] // 16) == (r[None, :] // 16)).astype(np.float32)
    t = np.arange(2048)
    row, col = t // 64, t % 64
    inv = (10000.0 ** (-np.arange(32, dtype=np.float32) / 32)).astype(np.float32)
    ang = np.concatenate([row[:, None].astype(np.float32) * inv[None], col[:, None].astype(np.float32) * inv[None]], 1)
    jj = r[:, None].astype(np.float32)
    ii = r[None, :].astype(np.float32)
    retE = np.stack([np.maximum(ii - jj, 0.0), np.maximum(jj - ii, 0.0)]).astype(np.float32)
    retM = np.stack([(ii >= jj), (jj > ii)]).astype(np.float32)
    retqe = np.stack([r + 1.0, 128.0 - r]).astype(np.float32)
    retke = np.stack([127.0 - r, r * 1.0], 1).astype(np.float32)
    extra = {
        "c_ropeC": np.cos(ang).astype(np.float32).reshape(16, 128, 64),
        "c_ropeS": np.sin(ang).astype(np.float32).reshape(16, 128, 64),
        "c_retE": retE, "c_retM": retM, "c_retqe": retqe, "c_retke": retke,
    }
    extra.update(hy_consts())
    return extra | {
        "c_identb": np.eye(128, dtype=np.float32).astype(ml_dtypes.bfloat16),
        "c_identf": np.eye(128, dtype=np.float32),
        "c_pmask": pm,
        "c_bdmask": bd,
    }


_CONSTS = None


def make_in_maps(prog, inputs):
    global _CONSTS
    if _CONSTS is None:
        _CONSTS = host_consts()
    consts = _CONSTS
    maps = []
    for c in range(8):
        m = {}
        for name in prog.inputs:
            if name in consts:
                m[name] = consts[name]
            elif name == "xs":
                m[name] = np.ascontiguousarray(inputs["x_sample"][c])
            elif name == "xp":
                m[name] = np.ascontiguousarray(inputs["x_prompt"][4 * c:4 * c + 4].reshape(NPS * LP, D))
            elif name == "cvec":
                m[name] = np.ascontiguousarray(np.stack([inputs["c_ctx"], inputs["c"][c]], 0))
            elif name == "st5":
                m[name] = np.ascontiguousarray(inputs["state_s5"][c].reshape(2, 128, 128))
            elif name == "stret":
                m[name] = np.ascontiguousarray(inputs["state_ret"][c, 0])
            else:
                a = np.asarray(inputs[name])
                shp = prog.inputs[name][0]
                m[name] = np.ascontiguousarray(a.reshape(shp))
        maps.append(m)
    return maps


_PROG = None


def kernel(**inputs):
    global _PROG
    inputs = {k: np.asarray(v) for k, v in inputs.items()}
    if _PROG is None:
        _PROG = K()
    prog = _PROG
    res = run_bass_kernel_spmd(prog.nc, make_in_maps(prog, inputs), core_ids=list(range(8)))
    rs = res.results
    y_prompt = np.concatenate([r["yp"].reshape(NPS, LP, D) for r in rs], 0)
    y_sample = np.stack([r["ys"] for r in rs], 0)
    ns5 = np.concatenate([r["ns5"].reshape(NPS, 2, 2, 2, 64, 64) for r in rs], 0)
    nret = np.concatenate([r["nret"][:, None].reshape(NPS, 1, 2, 8, 128, 256) for r in rs], 0)
    return (y_prompt.astype(np.float32), y_sample.astype(np.float32), ns5.astype(np.float32), nret.astype(np.float32))


def _ret_decl(self):
    self.ret_w_in = self.din("ret_w_in", [1, D, 6 * D])
    self.ret_decay_logit = self.din("ret_decay_logit", [1, 2, 8])
    self.ret_w_out = self.din("ret_w_out", [1, 2 * D, D])
    self.c_ropeC = self.din("c_ropeC", [16, 128, 64])
    self.c_ropeS = self.din("c_ropeS", [16, 128, 64])
    self.c_retE = self.din("c_retE", [2, 128, 128])
    self.c_retM = self.din("c_retM", [2, 128, 128])
    self.c_retqe = self.din("c_retqe", [2, 128])
    self.c_retke = self.din("c_retke", [128, 2])
    self.qT_scr = self.dscr("qT_scr", [8, 128, NT], BF16)
    self.kT_scr = self.dscr("kT_scr", [8, 128, NT], BF16)
    self.ktok_scr = self.dscr("ktok_scr", [NT, D], BF16)
    self.v_scr = self.dscr("v_scr", [NT, 2 * D], BF16)
    self.gate_scr = self.dscr("gate_scr", [NT, 2 * D], BF16)
    self.of_scr = self.dscr("of_scr", [NT, 2 * D])


def _ret_mixer(self, grp):
    V, A, T, G = self.V, self.A, self.T, self.G
    t0, n = self.trange(grp)
    nseq = 1 if grp == 1 else NPS
    L = n // nseq
    nch = L // 128
    self.arena_reset()
    sb = self.asb
    wblk = [sb(f"rwb{i}", [128, 8, 512], BF16) for i in range(2)]
    lgt = sb("lgt", [128, 16]); kdt = sb("kdt", [128, 16]); cdt = sb("cdt", [128, 16]); ke = sb("ke", [128, 2])
    Et = sb("Et", [128, 2, 128]); Mt = sb("Mt", [128, 2, 128]); qe = sb("qe", [128, 2, 128])
    Dtab = sb("Dtab", [128, 16, 128]); qdtab = sb("qdtab", [128, 16, 128])
    rc = sb("rc", [128, 64]); rs = sb("rs", [128, 64])
    pq = sb("pq", [128, 512]); pq2 = sb("pq2", [128, 512]); pt1 = sb("pt1", [128, 512])
    pbf = [sb(f"pbf{i}", [128, 512], BF16) for i in range(2)]
    trb = sb("trb", [128, 4, 128], BF16)
    S = sb("S", [128, 8, 256]); Sb = sb("Sb", [128, 8, 256], BF16)
    qTc = [sb(f"qTc{i}", [128, 8, 128], BF16) for i in range(2)]
    kTc = [sb(f"kTc{i}", [128, 8, 128], BF16) for i in range(2)]
    ktc = [sb(f"ktc{i}", [128, 1024], BF16) for i in range(2)]
    vc = [sb(f"vc{i}", [128, 2048], BF16) for i in range(2)]
    gc = sb("gc", [128, 2048], BF16)
    ofc = sb("ofc", [128, 2048])
    ot = sb("ot", [128, 2048])
    ybf = sb("ybf", [128, 2048], BF16)
    yTt = sb("yTt", [128, 16, 128], BF16)
    attb2 = [sb(f"attb{i}", [128, 128], BF16) for i in range(2)]
    qd2 = [sb(f"qd{i}", [128, 128], BF16) for i in range(2)]
    kd2 = [sb(f"kd{i}", [128, 128], BF16) for i in range(2)]
    rst = sb("rst", [128, 24])
    self.ld(lgt, self.ret_decay_logit[0].rearrange("d h -> (d h)").partition_broadcast(128), w=["lgt"])
    self.ld(ke, self.c_retke, w=["ke"])
    for d in range(2):
        self.ld(Et[:, d], self.c_retE[d], w=["Et"])
        self.ld(Mt[:, d], self.c_retM[d], w=["Mt"])
        self.ld(qe[:, d], self.c_retqe[d].partition_broadcast(128), w=["qe"])
    A(lambda e: e.activation(out=lgt, in_=lgt, func=AF.Exp, scale=-1.0), ["lgt"], ["lgt"])
    V(lambda e: e.tensor_scalar(out=lgt, in0=lgt, scalar1=1.0, scalar2=None, op0=ALU.add), ["lgt"], ["lgt"])
    A(lambda e: e.activation(out=lgt, in_=lgt, func=AF.Ln), ["lgt"], ["lgt"])
    V(lambda e: e.tensor_scalar(out=lgt, in0=lgt, scalar1=-1.0, scalar2=None, op0=ALU.mult), ["lgt"], ["lgt"])
    for d in range(2):
        for h in range(8):
            c = d * 8 + h
            A(lambda e: e.activation(out=Dtab[:, c], in_=Et[:, d], func=AF.Exp, scale=lgt[:, c:c + 1]), ["Et", "lgt"], ["Dtab"])
            V(lambda e: e.tensor_tensor(out=Dtab[:, c], in0=Dtab[:, c], in1=Mt[:, d], op=ALU.mult), ["Dtab", "Mt"], ["Dtab"])
            A(lambda e: e.activation(out=qdtab[:, c], in_=qe[:, d], func=AF.Exp, scale=lgt[:, c:c + 1]), ["qe", "lgt"], ["qdtab"])
            A(lambda e: e.activation(out=kdt[:, c:c + 1], in_=ke[:, d:d + 1], func=AF.Exp, scale=lgt[:, c:c + 1]), ["ke", "lgt"], ["kdt"])
    A(lambda e: e.activation(out=cdt, in_=lgt, func=AF.Exp, scale=128.0), ["lgt"], ["cdt"])
    gk = lambda nm: (nm, grp)
    def ld_wb(cb_):
        self.ldc(wblk[cb_ % 2], self.ret_w_in[0][:, cb_ * 512:(cb_ + 1) * 512].rearrange("(k p) n -> p k n", p=128), w=[f"rwb{cb_ % 2}"])
    ld_wb(0)
    for cb in range(12):
        wb, wk = wblk[cb % 2], f"rwb{cb % 2}"
        if cb + 1 < 12:
            ld_wb(cb + 1)
        for tt in range(n // 128):
            ts = slice(tt * 128, (tt + 1) * 128)
            gts = slice(t0 + tt * 128, t0 + (tt + 1) * 128)
            pp = self.ps[tt % 2]
            pk = f"ps{tt % 2}"
            for kk in range(8):
                T(lambda e: e.matmul(pp, lhsT=self.hT[:, kk, ts], rhs=wb[:, kk, :], start=(kk == 0), stop=(kk == 7)),
                  ["hT", wk], [pk])
            ob = pbf[tt % 2]
            obk = f"pbf{tt % 2}"
            if cb < 4:
                isk = cb >= 2
                sc = (128.0 ** -0.5) if isk else 1.0
                if grp == 1:
                    if True:
                        self.ld(rc, self.c_ropeC[tt], w=["rc"])
                        self.ld(rs, self.c_ropeS[tt], w=["rs"])
                    A(lambda e: e.activation(out=pq, in_=pp, func=AF.Copy, scale=sc), [pk], ["pq"])
                    v5 = lambda t: t.rearrange("p (h a b f) -> p h a b f", h=4, a=2, b=2)
                    x1 = v5(pq)[:, :, :, 0, :]
                    x2 = v5(pq)[:, :, :, 1, :]
                    cosb = rc.rearrange("p (a f) -> p a f", a=2).unsqueeze(1).to_broadcast([128, 4, 2, 32])
                    sinb = rs.rearrange("p (a f) -> p a f", a=2).unsqueeze(1).to_broadcast([128, 4, 2, 32])
                    o1 = v5(pq2)[:, :, :, 0, :]
                    o2 = v5(pq2)[:, :, :, 1, :]
                    u1 = v5(pt1)[:, :, :, 0, :]
                    u2 = v5(pt1)[:, :, :, 1, :]
                    V(lambda e: e.tensor_tensor(out=o1, in0=x1, in1=cosb, op=ALU.mult), ["pq", "rc"], ["pq2"])
                    V(lambda e: e.tensor_tensor(out=u1, in0=x2, in1=sinb, op=ALU.mult), ["pq", "rs"], ["pt1"])
                    G(lambda e: e.tensor_tensor(out=o2, in0=x1, in1=sinb, op=ALU.mult), ["pq", "rs"], ["pq2b"])
                    G(lambda e: e.tensor_tensor(out=u2, in0=x2, in1=cosb, op=ALU.mult), ["pq", "rc"], ["pt1b"])
                    V(lambda e: e.tensor_tensor(out=v5(ob)[:, :, :, 0, :], in0=o1, in1=u1, op=ALU.subtract), ["pq2", "pt1"], [obk])
                    V(lambda e: e.tensor_tensor(out=v5(ob)[:, :, :, 1, :], in0=o2, in1=u2, op=ALU.add), ["pq2b", "pt1b"], [obk])
                else:
                    A(lambda e: e.activation(out=ob, in_=pp, func=AF.Copy, scale=sc), [pk], [obk])
                if isk:
                    self.ld(self.ktok_scr[gts, (cb - 2) * 512:(cb - 1) * 512], ob, r=[obk], w=[gk("ktok")])
                for hh in range(4):
                    ptr = self.ps[2].bitcast(BF16)[:, hh * 128:(hh + 1) * 128]
                    T(lambda e: e.transpose(ptr, ob[:, hh * 128:(hh + 1) * 128], self.identb), [obk, "identb"], ["ps2"])
                V(lambda e: e.tensor_copy(out=trb.rearrange("p a b -> p (a b)"), in_=self.ps[2].bitcast(BF16)[:, 0:512]), ["ps2"], ["trb"])
                dst = self.kT_scr if isk else self.qT_scr
                h0 = (cb % 2) * 4
                self.ld(dst[h0:h0 + 4, :, gts].rearrange("h p t -> p h t"), trb, r=["trb"], w=[gk("kT" if isk else "qT")])
            elif cb < 8:
                A(lambda e: e.activation(out=ob, in_=pp, func=AF.Copy), [pk], [obk])
                self.ld(self.v_scr[gts, (cb - 4) * 512:(cb - 3) * 512], ob, r=[obk], w=[gk("v")])
            else:
                A(lambda e: e.activation(out=ob, in_=pp, func=AF.Silu), [pk], [obk])
                self.ld(self.gate_scr[gts, (cb - 8) * 512:(cb - 7) * 512], ob, r=[obk], w=[gk("gate")])
    for s in range(nseq):
        for d in range(2):
            if grp == 1:
                self.ld(S, self.stret[d].rearrange("h p e -> p h e"), w=["S"])
            else:
                V(lambda e: e.memset(S, 0.0), [], ["S"])
            V(lambda e: e.tensor_copy(out=Sb, in_=S), ["S"], ["Sb"])
            order = range(nch) if d == 0 else range(nch - 1, -1, -1)
            for ci, c in enumerate(order):
                b = ci % 2
                ts = slice(t0 + s * L + c * 128, t0 + s * L + (c + 1) * 128)
                self.ld(qTc[b], self.qT_scr[:, :, ts].rearrange("h p t -> p h t"), r=[gk("qT")], w=[f"qTc{b}"])
                self.ld(kTc[b], self.kT_scr[:, :, ts].rearrange("h p t -> p h t"), r=[gk("kT")], w=[f"kTc{b}"])
                self.ld(ktc[b], self.ktok_scr[ts, :], r=[gk("ktok")], w=[f"ktc{b}"])
                self.ld(vc[b], self.v_scr[ts, :], r=[gk("v")], w=[f"vc{b}"])
                if d == 1:
                    self.ld(ofc, self.of_scr[ts, :], r=[gk("of")], w=["ofc"])
                    self.ld(gc, self.gate_scr[ts, :], r=[gk("gate")], w=["gc"])
                for h in range(8):
                    cI = d * 8 + h
                    hb = h % 2
                    attb, qd, kd = attb2[hb], qd2[hb], kd2[hb]
                    attk, qdk, kdk = f"attb{hb}", f"qd{hb}", f"kd{hb}"
                    pak = "ps3" if hb == 0 else "ps0"
                    pa = (self.ps[3] if hb == 0 else self.ps[0])[:, 0:128]
                    T(lambda e: e.matmul(pa, lhsT=kTc[b][:, h, :], rhs=qTc[b][:, h, :], start=True, stop=True),
                      [f"kTc{b}", f"qTc{b}"], [pak])
                    V(lambda e: e.tensor_tensor(out=attb, in0=pa, in1=Dtab[:, cI], op=ALU.mult), [pak, "Dtab"], [attk])
                    G(lambda e: e.tensor_tensor(out=qd, in0=qTc[b][:, h, :], in1=qdtab[:, cI], op=ALU.mult), [f"qTc{b}", "qdtab"], [qdk])
                    A(lambda e: e.activation(out=kd, in_=ktc[b][:, h * 128:(h + 1) * 128], func=AF.Copy, scale=kdt[:, cI:cI + 1]),
                      [f"ktc{b}", "kdt"], [kdk])
                    po = self.ps[4 + (h % 2)][:, 0:256]
                    pok = f"ps{4 + (h % 2)}"
                    T(lambda e: e.matmul(po, lhsT=attb, rhs=vc[b][:, h * 256:(h + 1) * 256], start=True, stop=False),
                      [attk, f"vc{b}"], [pok])
                    T(lambda e: e.matmul(po, lhsT=qd, rhs=Sb[:, h, :], start=False, stop=True), [qdk, "Sb"], [pok])
                    psu = self.ps[6 + (h % 2)][:, 0:256]
                    psk = f"ps{6 + (h % 2)}"
                    T(lambda e: e.matmul(psu, lhsT=kd, rhs=vc[b][:, h * 256:(h + 1) * 256], start=True, stop=True),
                      [kdk, f"vc{b}"], [psk])
                    if d == 0:
                        A(lambda e: e.activation(out=ot[:, h * 256:(h + 1) * 256], in_=po, func=AF.Copy), [pok], ["ot"])
                    else:
                        V(lambda e: e.tensor_tensor(out=ot[:, h * 256:(h + 1) * 256], in0=po, in1=ofc[:, h * 256:(h + 1) * 256],
                                                    op=ALU.add), [pok, "ofc"], ["ot"])
                    V(lambda e: e.scalar_tensor_tensor(out=S[:, h, :], in0=S[:, h, :], scalar=cdt[:, cI:cI + 1], in1=psu,
                                                       op0=ALU.mult, op1=ALU.add), ["S", "cdt", psk], ["S"])
                    A(lambda e: e.activation(out=Sb[:, h, :], in_=S[:, h, :], func=AF.Copy), ["S"], ["Sb"])
                if d == 0:
                    self.ld(self.of_scr[ts, :], ot, r=["ot"], w=[gk("of")])
                else:
                    for h in range(8):
                        A(lambda e: e.activation(out=ofc[:, h * 256:(h + 1) * 256], in_=ot[:, h * 256:(h + 1) * 256], func=AF.Square,
                                                 accum_out=rst[:, h:h + 1]), ["ot"], ["ofc", "rst"])
                    V(lambda e: e.tensor_scalar(out=rst[:, 8:16], in0=rst[:, 0:8], scalar1=1.0 / 256, scalar2=EPS,
                                                op0=ALU.mult, op1=ALU.add), ["rst"], ["rst"])
                    A(lambda e: e.activation(out=rst[:, 8:16], in_=rst[:, 8:16], func=AF.Sqrt), ["rst"], ["rst"])
                    V(lambda e: e.reciprocal(out=rst[:, 16:24], in_=rst[:, 8:16]), ["rst"], ["rst"])
                    for h in range(8):
                        V(lambda e: e.scalar_tensor_tensor(out=ybf[:, h * 256:(h + 1) * 256], in0=ot[:, h * 256:(h + 1) * 256],
                                                           scalar=rst[:, 16 + h:17 + h], in1=gc[:, h * 256:(h + 1) * 256],
                                                           op0=ALU.mult, op1=ALU.mult), ["ot", "rst", "gc"], ["ybf"])
                    for k4 in range(4):
                        for kk in range(4):
                            k = k4 * 4 + kk
                            ptr = self.ps[2].bitcast(BF16)[:, kk * 128:(kk + 1) * 128]
                            T(lambda e: e.transpose(ptr, ybf[:, k * 128:(k + 1) * 128], self.identb), ["ybf", "identb"], ["ps2"])
                        A(lambda e: e.activation(out=yTt[:, k4 * 4:(k4 + 1) * 4, :].rearrange("p a b -> p (a b)"),
                                                 in_=self.ps[2].bitcast(BF16)[:, 0:512], func=AF.Copy), ["ps2"], ["yTt"])
                    self.ld(self.gscr[0:16, :, ts].rearrange("k p t -> p k t"), yTt, r=["yTt"], w=[("gscr", grp)])
            if grp == 0:
                self.ld(self.nret[s, d].rearrange("h p e -> p h e"), S, r=["S"], w=["nret"])
    self.ysrc = (self.gscr, ("gscr", grp))
    return 2 * D


K.ret_decl = _ret_decl
K.ret_mixer = _ret_mixer


HY_FT = {2048: 17, 256: 3}


def _hy_decl(self):
    self.hy_w_in = self.din("hy_w_in", [1, D, 8 * D])
    self.hy_conv_w = self.din("hy_conv_w", [1, 3, 6 * D])
    self.hy_conv_b = self.din("hy_conv_b", [1, 6 * D])
    self.hy_f_w1 = self.din("hy_f_w1", [1, 33, 64])
    self.hy_f_b1 = self.din("hy_f_b1", [1, 64])
    self.hy_f_w2 = self.din("hy_f_w2", [1, 64, 64])
    self.hy_f_b2 = self.din("hy_f_b2", [1, 64])
    self.hy_f_w3 = self.din("hy_f_w3", [1, 64, 8 * D])
    self.hy_skip = self.din("hy_skip", [1, 2, 2 * D])
    self.hy_w_out = self.din("hy_w_out", [1, 2 * D, D])
    self.c_absd = self.din("c_absd", [2 * D])
    self.c_ones = self.din("c_ones", [128, 128])
    self.hyc = {}
    for L in (2048, 256):
        FT = HY_FT[L]
        self.hyc[L] = dict(
            feat=self.din(f"c_feat{L}", [33, L]),
            tneg=self.din(f"c_tneg{L}", [128, L // 128]),
            C=self.din(f"c_C{L}", [FT, 128, L // 128, 128], BF16), S=self.din(f"c_S{L}", [FT, 128, L // 128, 128], BF16),
            IC=self.din(f"c_IC{L}", [L // min(512, L), 128, FT, min(512, L)], BF16),
            IS=self.din(f"c_IS{L}", [L // min(512, L), 128, FT, min(512, L)], BF16))
    self.vT_scr = self.dscr("vT_scr", [16, 128, NT])
    self.x1T_scr = self.dscr("x1T_scr", [16, 128, NT])
    self.x2T_scr = self.dscr("x2T_scr", [16, 128, NT])
    self.z1T_scr = self.dscr("z1T_scr", [16, 128, NT])
    self.sgT_scr = self.dscr("sgT_scr", [16, 128, NT], BF16)
    self.ztok_scr = self.dscr("ztok_scr", [2, NT, 2 * D], BF16)
    self.Eo_scr = self.dscr("Eo_scr", [2, 2, 2048, 2 * D], BF16)
    self.KH_scr = self.dscr("KH_scr", [2, 2, 17 * 128, 2 * D])


def _sin_any(self, out, x, rk, wk, tmpf, tmpi, tmpk, biasp):
    V, A = self.V, self.A
    V(lambda e: e.tensor_scalar(out=out, in0=x, scalar1=biasp, scalar2=1.0 / TWO_PI, op0=ALU.add, op1=ALU.mult), rk, wk)
    V(lambda e: e.tensor_copy(out=tmpi, in_=out), wk, [tmpk + "i"])
    V(lambda e: e.tensor_copy(out=tmpf, in_=tmpi), [tmpk + "i"], [tmpk])
    V(lambda e: e.tensor_tensor(out=out, in0=out, in1=tmpf, op=ALU.subtract), wk + [tmpk], wk)
    V(lambda e: e.tensor_scalar(out=tmpf, in0=out, scalar1=0.5, scalar2=None, op0=ALU.is_gt), wk, [tmpk])
    V(lambda e: e.tensor_tensor(out=out, in0=out, in1=tmpf, op=ALU.subtract), wk + [tmpk], wk)
    V(lambda e: e.tensor_scalar(out=tmpf, in0=out, scalar1=-0.5, scalar2=None, op0=ALU.is_lt), wk, [tmpk])
    V(lambda e: e.tensor_tensor(out=out, in0=out, in1=tmpf, op=ALU.add), wk + [tmpk], wk)
    A(lambda e: e.activation(out=out, in_=out, func=AF.Sin, scale=6.283185), wk, wk)


def _hy_filters(self, grp):
    V, A, T, G = self.V, self.A, self.T, self.G
    L = LS if grp == 1 else LP
    FT, LT = HY_FT[L], L // 128
    hc = self.hyc[L]
    self.arena_reset()
    sb = self.asb
    w1 = sb("hw1", [33, 64]); w2 = sb("hw2", [64, 64]); b1 = sb("hb1", [64, 1]); b2 = sb("hb2", [64, 1])
    feat = sb("hfeat", [33, L]); z1 = sb("hz1", [64, L]); z2 = sb("hz2", [64, L])
    tf = sb("htf", [64, 512]); ti = sb("hti", [64, 512], I32)
    w3b = [sb(f"hw3{i}", [64, 512], BF16) for i in range(2)]
    z2b = None
    absd = sb("habsd", [128, 2 * D]); tneg = sb("htneg", [128, LT]); ones = sb("hones", [128, 128], BF16)
    wins = [sb(f"hwin{i}", [128, 512]) for i in range(2)]
    fds = [[sb(f"hfd{j}{i}", [128, 512]) for i in range(2)] for j in range(2)]
    fabs = [sb(f"hfab{i}", [128, 512], BF16) for i in range(2)]
    ebs = [[sb(f"heb{j}{i}", [128, 512], BF16) for i in range(2)] for j in range(2)]
    rn = sb("hrn", [128, 2, 2 * D])
    Eb = sb("hE", [128, LT, 512], BF16); Ob = sb("hO", [128, LT, 512], BF16)
    Cs = [sb(f"hCs{i}", [128, LT, 128], BF16) for i in range(2)]
    Ss = [sb(f"hSs{i}", [128, LT, 128], BF16) for i in range(2)]
    ko = [fds[0][0], fds[0][1]]
    self.ld(w1, self.hy_f_w1[0], w=["hw1"]); self.ld(w2, self.hy_f_w2[0], w=["hw2"])
    self.ld(b1, self.hy_f_b1[0].rearrange("(p o) -> p o", o=1), w=["hb1"])
    self.ld(b2, self.hy_f_b2[0].rearrange("(p o) -> p o", o=1), w=["hb2"])
    self.ld(feat, hc["feat"], w=["hfeat"])
    self.ld(absd, self.c_absd.partition_broadcast(128), w=["habsd"])
    self.ld(tneg, hc["tneg"], w=["htneg"])
    self.ldc(ones, self.c_ones, w=["hones"])
    V(lambda e: e.tensor_scalar(out=b1, in0=b1, scalar1=16.0 * math.pi, scalar2=None, op0=ALU.add), ["hb1"], ["hb1"])
    V(lambda e: e.tensor_scalar(out=b2, in0=b2, scalar1=16.0 * math.pi, scalar2=None, op0=ALU.add), ["hb2"], ["hb2"])
    BW = min(512, L)
    for (src, srck, wt, wtk, bb, bbk, dst, dstk, kdim) in ((feat, "hfeat", w1, "hw1", b1, "hb1", z1, "hz1", 33),
                                                          (z1, "hz1", w2, "hw2", b2, "hb2", z2, "hz2", 64)):
        for tb in range(L // BW):
            pp = self.ps[0][0:64, 0:BW]
            T(lambda e: e.matmul(pp, lhsT=wt[0:kdim, :], rhs=src[0:kdim, tb * BW:(tb + 1) * BW], start=True, stop=True),
              [srck, wtk], ["ps0"])
            self.sin_any(dst[:, tb * BW:(tb + 1) * BW], pp, ["ps0", bbk], [dstk], tf[:, 0:BW], ti[:, 0:BW], "htf", bb[:, 0:1])
    z2b = z1.bitcast(BF16)[:, 0:L]
    A(lambda e: e.activation(out=z2b, in_=z2, func=AF.Copy), ["hz2", "hz1"], ["hz1"])
    for o in range(2):
        for cb in range(4):
            cs = slice(cb * 512, (cb + 1) * 512)
            for dr in range(2):
                col0 = dr * 4096 + o * 2048 + cb * 512
                self.ldc(w3b[dr], self.hy_f_w3[0][:, col0:col0 + 512], w=[f"hw3{dr}"])
            pacc = self.ps[3]
            for lt in range(LT):
                pb_ = lt % 2
                win, fd, eb = wins[pb_], fds[pb_], ebs[pb_]
                wink = f"hwin{pb_}"
                A(lambda e: e.activation(out=win, in_=absd[:, cs], func=AF.Exp, scale=tneg[:, lt:lt + 1]), ["habsd", "htneg"], [wink])
                for dr in range(2):
                    fab, fabk = fabs[dr], f"hfab{dr}"
                    pf = self.ps[(1 + dr) if pb_ == 0 else (5 + dr)]
                    pfk = f"ps{(1 + dr) if pb_ == 0 else (5 + dr)}"
                    T(lambda e: e.matmul(pf, lhsT=z2b[:, lt * 128:(lt + 1) * 128], rhs=w3b[dr], start=True, stop=True),
                      ["hz1", f"hw3{dr}"], [pfk])
                    V(lambda e: e.tensor_tensor(out=fd[dr], in0=pf, in1=win, op=ALU.mult), [pfk, wink], [f"hfd{pb_}{dr}"])
                    A(lambda e: e.activation(out=fab, in_=fd[dr], func=AF.Abs), [f"hfd{pb_}{dr}"], [fabk])
                    T(lambda e: e.matmul(pacc, lhsT=ones, rhs=fab, start=(lt == 0 and dr == 0), stop=(lt == LT - 1 and dr == 1)),
                      ["hones", fabk], ["ps3"])
                if lt == 0:
                    V(lambda e: e.memset(fd[1][0:1, :], 0.0), [], [f"hfd{pb_}1"])
                V(lambda e: e.tensor_tensor(out=eb[0], in0=fd[0], in1=fd[1], op=ALU.add), [f"hfd{pb_}0", f"hfd{pb_}1"], [f"heb{pb_}0"])
                V(lambda e: e.tensor_tensor(out=eb[1], in0=fd[1], in1=fd[0], op=ALU.subtract), [f"hfd{pb_}0", f"hfd{pb_}1"], [f"heb{pb_}1"])
                for eo in range(2):
                    self.ld(self.Eo_scr[o, eo, lt * 128:(lt + 1) * 128, cs], eb[eo], r=[f"heb{pb_}{eo}"], w=[("Eo", grp)])
            V(lambda e: e.tensor_scalar(out=rn[:, o, cs], in0=pacc, scalar1=EPS, scalar2=None, op0=ALU.add), ["ps3"], ["hrn"])
            V(lambda e: e.reciprocal(out=rn[:, o, cs], in_=rn[:, o, cs]), ["hrn"], ["hrn"])
    for o in range(2):
        for cb in range(4):
            cs = slice(cb * 512, (cb + 1) * 512)
            self.ld(Eb, self.Eo_scr[o, 0, 0:L, cs].rearrange("(lt p) c -> p lt c", p=128), r=[("Eo", grp)], w=["hE"])
            self.ld(Ob, self.Eo_scr[o, 1, 0:L, cs].rearrange("(lt p) c -> p lt c", p=128), r=[("Eo", grp)], w=["hO"])
            for ft in range(FT):
                b = ft % 2
                fs = slice(ft * 128, (ft + 1) * 128)
                self.ld(Cs[b], hc["C"][ft], w=[f"hCs{b}"])
                self.ld(Ss[b], hc["S"][ft], w=[f"hSs{b}"])
                for ri, (tab, tabk, dat, datk) in enumerate(((Cs[b], f"hCs{b}", Eb, "hE"), (Ss[b], f"hSs{b}", Ob, "hO"))):
                    pk_ = self.ps[4 + ri]
                    for lt in range(LT):
                        T(lambda e: e.matmul(pk_, lhsT=tab[:, lt, :], rhs=dat[:, lt, :], start=(lt == 0), stop=(lt == LT - 1)),
                          [tabk, datk], [f"ps{4 + ri}"])
                    V(lambda e: e.tensor_tensor(out=ko[ri], in0=pk_, in1=rn[:, o, cs], op=ALU.mult), [f"ps{4 + ri}", "hrn"], [f"hfd0{ri}"])
                    self.ld(self.KH_scr[o, ri, fs, cs], ko[ri], r=[f"hfd0{ri}"], w=[("KH", grp)])


def _hy_mixer(self, grp):
    V, A, T, G = self.V, self.A, self.T, self.G
    t0, n = self.trange(grp)
    nseq = 1 if grp == 1 else NPS
    L = n // nseq
    FT, LT = HY_FT[L], L // 128
    hc = self.hyc[L]
    self.hy_filters(grp)
    self.arena_reset()
    sb = self.asb
    wch = [sb(f"ywc{i}", [128, 8, 128], BF16) for i in range(2)]
    PBs = [sb(f"yPB{i}", [128, nseq, L + 2]) for i in range(2)]
    cvs = [sb(f"ycv{i}", [128, nseq, L]) for i in range(2)]
    cvbs = [sb(f"ycvb{i}", [128, n], BF16) for i in range(2)]
    cwT = sb("ycwT", [128, 3, 48]); cbT = sb("ycbT", [128, 48]); skT = sb("yskT", [128, 2, 16])
    ztt = sb("yztt", [128, n // 128, 128], BF16)
    for j in range(3):
        self.ld(cwT[:, j, :], self.hy_conv_w[0, j].rearrange("(c p) -> p c", p=128), w=["ycwT"], allow_slow_non_contiguous=True)
    self.ld(cbT, self.hy_conv_b[0].rearrange("(c p) -> p c", p=128), w=["ycbT"], allow_slow_non_contiguous=True)
    for o in range(2):
        self.ld(skT[:, o, :], self.hy_skip[0, o].rearrange("(c p) -> p c", p=128), w=["yskT"], allow_slow_non_contiguous=True)
    for i in range(2):
        V(lambda e: e.memset(PBs[i], 0.0), [], [f"yPB{i}"])
    def ld_wch(c_):
        self.ldc(wch[c_ % 2], self.hy_w_in[0][:, c_ * 128:(c_ + 1) * 128].rearrange("(k p) n -> p k n", p=128), w=[f"ywc{c_ % 2}"])
    ld_wch(0)
    pend_tr = []
    for c in range(64):
        PB, cv, cvb = PBs[c % 2], cvs[c % 2], cvbs[c % 2]
        PBk, cvk, cvbk = f"yPB{c % 2}", f"ycv{c % 2}", f"ycvb{c % 2}"
        wc, wck = wch[c % 2], f"ywc{c % 2}"
        if c + 1 < 64:
            ld_wch(c + 1)
        for tb in range(n // 512):
            pp = self.ps[tb % 2]
            pk = f"ps{tb % 2}"
            for kk in range(8):
                T(lambda e: e.matmul(pp, lhsT=wc[:, kk, :], rhs=self.hT[:, kk, tb * 512:(tb + 1) * 512], start=(kk == 0), stop=(kk == 7)),
                  [wck, "hT"], [pk])
            if c < 48:
                if grp == 1:
                    dstp = PB[:, 0, 1 + tb * 512:1 + (tb + 1) * 512]
                    srcp = pp
                else:
                    dstp = PB[:, 2 * tb:2 * tb + 2, 1:L + 1]
                    srcp = pp.rearrange("p (s t) -> p s t", s=2)
                A(lambda e: e.activation(out=dstp, in_=srcp, func=AF.Copy), [pk], [PBk])
            else:
                A(lambda e: e.activation(out=cvb[:, tb * 512:(tb + 1) * 512], in_=pp, func=AF.Silu), [pk], [cvbk])
        while pend_tr:
            pend_tr.pop(0)()
        if c >= 48:
            self.ld(self.sgT_scr[c - 48, :, t0:t0 + n], cvb, r=[cvbk], w=[("sgT", grp)])
            continue
        A(lambda e: e.activation(out=cv, in_=PB[:, :, 1:L + 1], func=AF.Identity, scale=cwT[:, 1, c:c + 1], bias=cbT[:, c:c + 1]),
          [PBk, "ycwT", "ycbT"], [cvk])
        V(lambda e: e.scalar_tensor_tensor(out=cv, in0=PB[:, :, 0:L], scalar=cwT[:, 0, c:c + 1], in1=cv, op0=ALU.mult, op1=ALU.add),
          [PBk, "ycwT", cvk], [cvk])
        V(lambda e: e.scalar_tensor_tensor(out=cv, in0=PB[:, :, 2:L + 2], scalar=cwT[:, 2, c:c + 1], in1=cv, op0=ALU.mult, op1=ALU.add),
          [PBk, "ycwT", cvk], [cvk])
        cvf = cv.rearrange("p s t -> p (s t)")
        dst = (self.vT_scr, self.x1T_scr, self.x2T_scr)[c // 16]
        self.ld(dst[c % 16, :, t0:t0 + n], cvf, r=[cvk], w=[(("vT", "x1T", "x2T")[c // 16], grp)])
        if c < 16:
            V(lambda e: e.tensor_copy(out=cvb, in_=cvf), [cvk], [cvbk])

            def do_tr(c=c, cvb=cvb, cvbk=cvbk):
                for t4 in range(n // 512):
                    for kk in range(4):
                        tt = t4 * 4 + kk
                        ptr = self.ps[2].bitcast(BF16)[:, kk * 128:(kk + 1) * 128]
                        T(lambda e: e.transpose(ptr, cvb[:, tt * 128:(tt + 1) * 128], self.identb), [cvbk, "identb"], ["ps2"])
                    A(lambda e: e.activation(out=ztt[:, t4 * 4:(t4 + 1) * 4, :].rearrange("p a b -> p (a b)"),
                                             in_=self.ps[2].bitcast(BF16)[:, 0:512], func=AF.Copy), ["ps2"], ["yztt"])
                self.ld(self.ztok_scr[0, t0:t0 + n, c * 128:(c + 1) * 128].rearrange("(tt p) c -> p tt c", p=128), ztt,
                        r=["yztt"], w=[("ztok0", grp)])
            pend_tr.append(do_tr)
    while pend_tr:
        pend_tr.pop(0)()
    self.arena_reset()
    sb = self.asb
    TBW = min(512, L)
    NTB = L // TBW
    zt = sb("yzt", [128, LT, 512], BF16)
    Cs = [sb(f"yCs{i}", [128, LT, 128], BF16) for i in range(2)]
    Ss = [sb(f"ySs{i}", [128, LT, 128], BF16) for i in range(2)]
    Yh = sb("yYh", [128, FT, 2, 512], BF16)
    ICs = sb("yIC", [128, FT, TBW], BF16); ISs = sb("yIS", [128, FT, TBW], BF16)
    kre = [sb(f"ykre{i}", [128, 512]) for i in range(2)]; kim = [sb(f"ykim{i}", [128, 512]) for i in range(2)]
    u1 = sb("yu1", [128, 512]); u2 = sb("yu2", [128, 512])
    NB2 = 2 if L == LP else 1
    tas = [sb(f"yta{i}", [128, TBW]) for i in range(NB2)]; txs = [sb(f"ytx{i}", [128, TBW]) for i in range(NB2)]
    tgs = [sb(f"ytg{i}", [128, TBW], BF16) for i in range(NB2)]; tos = [sb(f"yto{i}", [128, TBW]) for i in range(NB2)]
    tobs = [sb(f"ytob{i}", [128, TBW], BF16) for i in range(NB2)]
    ztt2s = [sb(f"yztt2{i}", [128, TBW // 128, 128], BF16) for i in range(NB2)]
    skT = sb("yskT2", [128, 2, 16])
    for o in range(2):
        self.ld(skT[:, o, :], self.hy_skip[0, o].rearrange("(c p) -> p c", p=128), w=["yskT2"], allow_slow_non_contiguous=True)
    small = (L == LP)
    if small:
        Call = sb("yCall", [128, FT, LT, 128], BF16); Sall = sb("ySall", [128, FT, LT, 128], BF16)
        kra = sb("ykra", [128, FT, 512]); kia = sb("ykia", [128, FT, 512])
        for ft in range(FT):
            self.ld(Call[:, ft], hc["C"][ft], w=["yCall"])
            self.ld(Sall[:, ft], hc["S"][ft], w=["ySall"])
        self.ld(ICs, hc["IC"][0], w=["yIC"])
        self.ld(ISs, hc["IS"][0], w=["yIS"])
    for o in range(2):
        zprev = (self.vT_scr, ("vT", grp)) if o == 0 else (self.z1T_scr, ("z1T", grp))
        xg = (self.x1T_scr, ("x1T", grp)) if o == 0 else (self.x2T_scr, ("x2T", grp))
        for cb in range(4):
          cs = slice(cb * 512, (cb + 1) * 512)
          if small:
              self.ld(kra, self.KH_scr[o, 0, 0:FT * 128, cs].rearrange("(ft p) c -> p ft c", p=128), r=[("KH", grp)], w=["ykra"])
              self.ld(kia, self.KH_scr[o, 1, 0:FT * 128, cs].rearrange("(ft p) c -> p ft c", p=128), r=[("KH", grp)], w=["ykia"])
          for s in range(nseq):
                tq = t0 + s * L
                self.ld(zt, self.ztok_scr[o, tq:tq + L, cs].rearrange("(lt p) c -> p lt c", p=128), r=[(f"ztok{o}", grp)], w=["yzt"])
                for ft in range(FT):
                    b = ft % 2
                    fs = slice(ft * 128, (ft + 1) * 128)
                    if small:
                        Csb, Ssb, krb, kib = Call[:, ft], Sall[:, ft], kra[:, ft], kia[:, ft]
                        Ck, Sk, krk, kik = "yCall", "ySall", "ykra", "ykia"
                    else:
                        Csb, Ssb, krb, kib = Cs[b], Ss[b], kre[b], kim[b]
                        Ck, Sk, krk, kik = f"yCs{b}", f"ySs{b}", f"ykre{b}", f"ykim{b}"
                        self.ld(Cs[b], hc["C"][ft], w=[Ck])
                        self.ld(Ss[b], hc["S"][ft], w=[Sk])
                        self.ld(kre[b], self.KH_scr[o, 0, fs, cs], r=[("KH", grp)], w=[krk])
                        self.ld(kim[b], self.KH_scr[o, 1, fs, cs], r=[("KH", grp)], w=[kik])
                    pA, pB = self.ps[0 + 2 * b], self.ps[1 + 2 * b]
                    pAk, pBk = f"ps{0 + 2 * b}", f"ps{1 + 2 * b}"
                    for lt in range(LT):
                        T(lambda e: e.matmul(pA, lhsT=Csb[:, lt, :], rhs=zt[:, lt, :], start=(lt == 0), stop=(lt == LT - 1)),
                          [Ck, "yzt"], [pAk])
                    for lt in range(LT):
                        T(lambda e: e.matmul(pB, lhsT=Ssb[:, lt, :], rhs=zt[:, lt, :], start=(lt == 0), stop=(lt == LT - 1)),
                          [Sk, "yzt"], [pBk])
                    V(lambda e: e.tensor_tensor(out=u1, in0=pA, in1=krb, op=ALU.mult), [pAk, krk], ["yu1"])
                    V(lambda e: e.tensor_tensor(out=u2, in0=pB, in1=kib, op=ALU.mult), [pBk, kik], ["yu2"])
                    G(lambda e: e.tensor_tensor(out=Yh[:, ft, 0, :], in0=u1, in1=u2, op=ALU.add), ["yu1", "yu2"], ["yYh"])
                    V(lambda e: e.tensor_tensor(out=u1, in0=pA, in1=kib, op=ALU.mult), [pAk, kik], ["yu1"])
                    V(lambda e: e.tensor_tensor(out=u2, in0=pB, in1=krb, op=ALU.mult), [pBk, krk], ["yu2"])
                    V(lambda e: e.tensor_tensor(out=Yh[:, ft, 1, :], in0=u1, in1=u2, op=ALU.subtract), ["yu1", "yu2"], ["yYh"])
                for tb in range(NTB):
                    tsl = slice(tb * TBW, (tb + 1) * TBW)
                    gsl = slice(tq + tb * TBW, tq + (tb + 1) * TBW)
                    if not small:
                        self.ld(ICs, hc["IC"][tb], w=["yIC"])
                        self.ld(ISs, hc["IS"][tb], w=["yIS"])
                    for cc in range(4):
                        ch = cb * 4 + cc
                        bi_ = cc % NB2
                        ta, tx, tg, to, tob, ztt2 = tas[bi_], txs[bi_], tgs[bi_], tos[bi_], tobs[bi_], ztt2s[bi_]
                        tak, txk, tgk, tok, tobk, zt2k = f"yta{bi_}", f"ytx{bi_}", f"ytg{bi_}", f"yto{bi_}", f"ytob{bi_}", f"yztt2{bi_}"
                        pz = self.ps[4 + (cc % 2)][:, 0:TBW]
                        pzk = f"ps{4 + (cc % 2)}"
                        self.ld(ta, zprev[0][ch, :, gsl], r=[zprev[1]], w=[tak])
                        self.ld(tx, xg[0][ch, :, gsl], r=[xg[1]], w=[txk])
                        if o == 1:
                            self.ld(tg, self.sgT_scr[ch, :, gsl], r=[("sgT", grp)], w=[tgk])
                        for ft in range(FT):
                            T(lambda e: e.matmul(pz, lhsT=Yh[:, ft, 0, cc * 128:(cc + 1) * 128], rhs=ICs[:, ft, :], start=(ft == 0), stop=False),
                              ["yYh", "yIC"], [pzk])
                            T(lambda e: e.matmul(pz, lhsT=Yh[:, ft, 1, cc * 128:(cc + 1) * 128], rhs=ISs[:, ft, :], start=False, stop=(ft == FT - 1)),
                              ["yYh", "yIS"], [pzk])
                        V(lambda e: e.scalar_tensor_tensor(out=ta, in0=ta, scalar=skT[:, o, ch:ch + 1], in1=pz, op0=ALU.mult, op1=ALU.add),
                          [tak, "yskT2", pzk], [tak])
                        if o == 0:
                            G(lambda e: e.tensor_tensor(out=to, in0=ta, in1=tx, op=ALU.mult), [tak, txk], [tok])
                            self.ld(self.z1T_scr[ch, :, gsl], to, r=[tok], w=[("z1T", grp)])
                            A(lambda e: e.activation(out=tob, in_=to, func=AF.Copy), [tok], [tobk])
                            for kk in range(TBW // 128):
                                ptr = self.ps[6 + bi_].bitcast(BF16)[:, kk * 128:(kk + 1) * 128]
                                T(lambda e: e.transpose(ptr, tob[:, kk * 128:(kk + 1) * 128], self.identb), [tobk, "identb"], ["ps6" if bi_ == 0 else "ps7"])
                            A(lambda e: e.activation(out=ztt2.rearrange("p a b -> p (a b)"), in_=self.ps[6 + bi_].bitcast(BF16)[:, 0:TBW], func=AF.Copy),
                              ["ps6" if bi_ == 0 else "ps7"], [zt2k])
                            self.ld(self.ztok_scr[1, gsl, ch * 128:(ch + 1) * 128].rearrange("(tt p) c -> p tt c", p=128), ztt2,
                                    r=[zt2k], w=[("ztok1", grp)])
                        else:
                            G(lambda e: e.tensor_tensor(out=to, in0=ta, in1=tx, op=ALU.mult), [tak, txk], [tok])
                            G(lambda e: e.tensor_tensor(out=tob, in0=to, in1=tg, op=ALU.mult), [tok, tgk], [tobk])
                            self.ld(self.gscr[ch, :, gsl], tob, r=[tobk], w=[("gscr", grp)])
    self.ysrc = (self.gscr, ("gscr", grp))
    return 2 * D


K.hy_decl = _hy_decl
K.sin_any = _sin_any
K.hy_filters = _hy_filters
K.hy_mixer = _hy_mixer


def hy_consts():
    out = {}
    HY_BANDS = 16
    min_decay = math.log(1e-2) / 1.5
    max_decay = math.log(1e-2) / 0.3
    out["c_absd"] = np.abs(np.linspace(min_decay, max_decay, 2048, dtype=np.float32)).astype(np.float32)
    out["c_ones"] = np.ones((128, 128), np.float32)
    for L in (2048, 256):
        FT = HY_FT[L]
        N = 2 * L
        t = np.linspace(0.0, 1.0, L, dtype=np.float32)[:, None]
        w = (2.0 * np.float32(math.pi) * np.arange(L, dtype=np.float32)[:, None] / np.float32(L)).astype(np.float32)
        f = np.linspace(1e-4, HY_BANDS - 1.0, HY_BANDS, dtype=np.float32)[None, :]
        fw_ = (f * w).astype(np.float32)
        feat = np.concatenate([t, np.cos(fw_), -np.sin(fw_)], -1).astype(np.float32)
        out[f"c_feat{L}"] = np.ascontiguousarray(feat.T)
        out[f"c_tneg{L}"] = np.ascontiguousarray((-t[:, 0]).reshape(L // 128, 128).T)
        tt = np.arange(L, dtype=np.int64)[:, None]
        ff = np.arange(FT * 128, dtype=np.int64)[None, :]
        ang = 2.0 * np.pi * ((tt * ff) % N).astype(np.float64) / N
        valid = (ff <= L).astype(np.float64)
        C = np.cos(ang) * valid
        S = np.sin(ang) * valid
        wf = np.where((ff == 0) | (ff == L), 1.0, 2.0) * valid / N
        LT = L // 128
        TBW = min(512, L)
        def fwd_tile(M):
            return np.ascontiguousarray(M.reshape(LT, 128, FT, 128).transpose(2, 1, 0, 3)).astype(np.float32).astype(ml_dtypes.bfloat16)
        def inv_tile(M):
            return np.ascontiguousarray(M.reshape(FT, 128, L // TBW, TBW).transpose(2, 1, 0, 3)).astype(np.float32).astype(ml_dtypes.bfloat16)
        out[f"c_C{L}"] = fwd_tile(C)
        out[f"c_S{L}"] = fwd_tile(S)
        out[f"c_IC{L}"] = inv_tile((C * wf).T)
        out[f"c_IS{L}"] = inv_tile((-S * wf).T)
    return out
```
